# Optimizing a Trainium2 kernel written in Bass

```python
import jax, jax.numpy as jnp
from jax import lax
import numpy as np

D_MODEL = 1024
BATCH = 2
SEQ = 8192
DEPTH = 1

GRID_W = 64
CTX_LEN = 256
EPS = 1e-6
N_MOD = 6
F_GROUPS = 8
F_GROUP_DIM = 128
F_WIDTH = F_GROUPS * F_GROUP_DIM
SSD_HEADS = 32
SSD_HEAD_DIM = 64
SSD_INNER = SSD_HEADS * SSD_HEAD_DIM
SSD_GROUPS = 4
SSD_HPG = SSD_HEADS // SSD_GROUPS
SSD_STATE = 128
SSD_BC = SSD_GROUPS * SSD_STATE
SSD_CONV = 3
SSD_CHUNK = 128
XBC_WIDTH = SSD_INNER + 2 * SSD_BC
D_FF = 2816
FFN_CONV = 3
IN_WIDTHS = (F_WIDTH, XBC_WIDTH, SSD_INNER, 2 * SSD_HEADS, D_MODEL, D_MODEL)
IN_TOTAL = F_WIDTH + XBC_WIDTH + SSD_INNER + 2 * SSD_HEADS + 2 * D_MODEL

kernel_name = "hybrid_fnet_ssd_convffn_dit_block"


def _rmsnorm(x, g):
    x32 = x.astype(jnp.float32)
    y = x32 * lax.rsqrt(jnp.mean(x32 * x32, axis=-1, keepdims=True) + EPS)
    return (y * g.astype(jnp.float32)).astype(x.dtype)


def _adaln(cond, w_mod, b_mod):
    m = jax.nn.silu(cond) @ w_mod + b_mod
    m = m.reshape(cond.shape[:-1] + (N_MOD, D_MODEL))
    return [m[..., k:k + 1, :] for k in range(N_MOD)]


def _modulate(x, shift, scale):
    return x * (1.0 + scale) + shift


def _split_in(proj):
    idx = np.cumsum(IN_WIDTHS)[:-1].tolist()
    return jnp.split(proj, idx, axis=-1)


def _dwconv_seq(u, w, b):
    k = w.shape[0]
    out = lax.conv_general_dilated(
        u, w[:, None, :].astype(u.dtype), window_strides=(1,), padding=[(k // 2, k // 2)],
        dimension_numbers=("NWC", "WIO", "NWC"), feature_group_count=u.shape[-1])
    return out + b.astype(u.dtype)


def _dwconv_grid(u, w, b):
    bsz, length, ch = u.shape
    rows = length // GRID_W
    k = w.shape[0]
    img = u.reshape(bsz, rows, GRID_W, ch)
    out = lax.conv_general_dilated(
        img, w[:, :, None, :].astype(u.dtype), window_strides=(1, 1),
        padding=[(k // 2, k // 2), (k // 2, k // 2)],
        dimension_numbers=("NHWC", "HWIO", "NHWC"), feature_group_count=ch)
    return out.reshape(bsz, length, ch) + b.astype(u.dtype)


def _segsum(a):
    t = a.shape[-1]
    rep = jnp.broadcast_to(a[..., :, None], a.shape + (t,))
    strict = jnp.tril(jnp.ones((t, t), dtype=bool), -1)
    ss = jnp.cumsum(jnp.where(strict, rep, 0.0), axis=-2)
    incl = jnp.tril(jnp.ones((t, t), dtype=bool))
    return jnp.where(incl, ss, -jnp.inf)


def _ssd_chunked(xh, dt, a, bm, cm, h0):
    bsz, length, g, hg, p = xh.shape
    n = bm.shape[-1]
    q = SSD_CHUNK
    nc = length // q
    xd = (xh * dt[..., None]).reshape(bsz, nc, q, g, hg, p)
    la = (dt * a).reshape(bsz, nc, q, g, hg).transpose(0, 3, 4, 1, 2)
    bc = bm.reshape(bsz, nc, q, g, n)
    cc = cm.reshape(bsz, nc, q, g, n)
    la_cum = jnp.cumsum(la, axis=-1)
    lmat = jnp.exp(_segsum(la))
    scores = jnp.einsum("bclgn,bcsgn->bgcls", cc, bc)
    y_diag = jnp.einsum("bgcls,bghcls,bcsghp->bclghp", scores, lmat, xd)
    decay_states = jnp.exp(la_cum[..., -1:] - la_cum)
    states = jnp.einsum("bcsgn,bghcs,bcsghp->bcghpn", bc, decay_states, xd)
    states = jnp.concatenate([h0[:, None], states], axis=1)
    chunk_a = jnp.pad(la_cum[..., -1], ((0, 0), (0, 0), (0, 0), (1, 0)))
    decay_chunk = jnp.exp(_segsum(chunk_a))
    new_states = jnp.einsum("bghzc,bcghpn->bzghpn", decay_chunk, states)
    enter, final = new_states[:, :-1], new_states[:, -1]
    y_off = jnp.einsum("bclgn,bcghpn,bghcl->bclghp", cc, enter, jnp.exp(la_cum))
    y = (y_diag + y_off).reshape(bsz, length, g, hg, p)
    return y, final


def _ssd_prep(xbc, dt_raw, conv_w, conv_b, dt_bias):
    bsz, length = xbc.shape[:2]
    xbc = jax.nn.silu(_dwconv_seq(xbc, conv_w, conv_b)).astype(jnp.float32)
    xs, bm, cm = jnp.split(xbc, [SSD_INNER, SSD_INNER + SSD_BC], axis=-1)
    xh = xs.reshape(bsz, length, SSD_GROUPS, SSD_HPG, SSD_HEAD_DIM)
    bm = bm.reshape(bsz, length, SSD_GROUPS, SSD_STATE)
    cm = cm.reshape(bsz, length, SSD_GROUPS, SSD_STATE)
    dt = jax.nn.softplus(dt_raw.astype(jnp.float32).reshape(bsz, length, 2, SSD_GROUPS, SSD_HPG)
                         + dt_bias.astype(jnp.float32).reshape(2, SSD_GROUPS, SSD_HPG))
    return xh, bm, cm, dt


def _ssd_bidir(xh, bm, cm, dt, a, h0_f, h0_b):
    y_f, s_f = _ssd_chunked(xh, dt[:, :, 0], a[0], bm, cm, h0_f)
    flip = lambda t: jnp.flip(t, axis=1)
    y_b, s_b = _ssd_chunked(flip(xh), flip(dt[:, :, 1]), a[1], flip(bm), flip(cm), h0_b)
    return y_f + flip(y_b), s_f, s_b


def _mixer_merge(u_f, y, xh, z, g_f, g_s, d_skip, ssd_norm_g, w_fa, w_sb, w_o):
    bsz, length = u_f.shape[:2]
    dtype = u_f.dtype
    uf = u_f.astype(jnp.float32).reshape(bsz, length, F_GROUPS, F_GROUP_DIM)
    mixed = jnp.fft.fft2(uf, axes=(1, 3), norm="ortho").real
    branch_f = mixed.reshape(bsz, length, F_WIDTH).astype(dtype) @ w_fa
    ys = y + d_skip.astype(jnp.float32).reshape(SSD_GROUPS, SSD_HPG)[:, :, None] * xh
    ys = ys.reshape(bsz, length, SSD_GROUPS, SSD_HPG * SSD_HEAD_DIM)
    ys = ys * jax.nn.silu(z.astype(jnp.float32)).reshape(bsz, length, SSD_GROUPS, SSD_HPG * SSD_HEAD_DIM)
    ys = ys * lax.rsqrt(jnp.mean(ys * ys, axis=-1, keepdims=True) + EPS)
    ys = (ys.reshape(bsz, length, SSD_INNER) * ssd_norm_g.astype(jnp.float32)).astype(dtype)
    branch_s = ys @ w_sb
    merged = jax.nn.sigmoid(g_f) * branch_f + jax.nn.sigmoid(g_s) * branch_s
    return merged @ w_o


def _conv_ffn(h, w_up, conv_w, conv_b, w_down, on_grid):
    u = h @ w_up
    if on_grid:
        u = _dwconv_grid(u, conv_w, conv_b)
    else:
        u = _dwconv_seq(u, conv_w[FFN_CONV // 2], conv_b)
    gate, val = jnp.split(u, 2, axis=-1)
    return (jax.nn.silu(gate) * val) @ w_down


def setup_inputs(seed: int = 0) -> dict:
    key = jax.random.key(seed)
    ks = jax.random.split(key, 24)
    f32 = jnp.float32
    nrm = lambda k, shape, scale: jax.random.normal(k, shape, f32) * scale
    dt0 = jnp.exp(jax.random.uniform(ks[9], (DEPTH, 2, SSD_HEADS), f32, np.log(1e-3), np.log(1e-1)))
    return {
        "x": nrm(ks[0], (BATCH, SEQ, D_MODEL), 1.0),
        "c": nrm(ks[1], (BATCH, D_MODEL), 1.0),
        "ctx": nrm(ks[2], (BATCH, CTX_LEN, D_MODEL), 1.0),
        "c_ctx": nrm(ks[3], (D_MODEL,), 1.0),
        "w_mod": nrm(ks[4], (DEPTH, D_MODEL, N_MOD * D_MODEL), D_MODEL ** -0.5),
        "b_mod": nrm(ks[5], (DEPTH, N_MOD * D_MODEL), 0.01),
        "norm1_g": 1.0 + nrm(ks[6], (DEPTH, D_MODEL), 0.01),
        "w_in": nrm(ks[7], (DEPTH, D_MODEL, IN_TOTAL), D_MODEL ** -0.5),
        "conv_ssd_w": nrm(ks[8], (DEPTH, SSD_CONV, XBC_WIDTH), SSD_CONV ** -0.5),
        "conv_ssd_b": nrm(ks[10], (DEPTH, XBC_WIDTH), 0.01),
        "dt_bias": dt0 + jnp.log(-jnp.expm1(-dt0)),
        "a_log": jnp.log(jax.random.uniform(ks[11], (DEPTH, 2, SSD_HEADS), f32, 1.0, 16.0)),
        "d_skip": 1.0 + nrm(ks[12], (DEPTH, SSD_HEADS), 0.01),
        "ssd_norm_g": 1.0 + nrm(ks[13], (DEPTH, SSD_INNER), 0.01),
        "w_fa": nrm(ks[14], (DEPTH, F_WIDTH, D_MODEL), F_WIDTH ** -0.5),
        "w_sb": nrm(ks[15], (DEPTH, SSD_INNER, D_MODEL), SSD_INNER ** -0.5),
        "w_o": nrm(ks[16], (DEPTH, D_MODEL, D_MODEL), D_MODEL ** -0.5),
        "norm2_g": 1.0 + nrm(ks[17], (DEPTH, D_MODEL), 0.01),
        "w_up": nrm(ks[18], (DEPTH, D_MODEL, 2 * D_FF), D_MODEL ** -0.5),
        "conv_ffn_w": nrm(ks[19], (DEPTH, FFN_CONV, FFN_CONV, 2 * D_FF), 1.0 / FFN_CONV),
        "conv_ffn_b": nrm(ks[20], (DEPTH, 2 * D_FF), 0.01),
        "w_down": nrm(ks[21], (DEPTH, D_FF, D_MODEL), D_FF ** -0.5),
        "final_g": 1.0 + nrm(ks[22], (D_MODEL,), 0.01),
    }


def reference(x, c, ctx, c_ctx, w_mod, b_mod, norm1_g, w_in, conv_ssd_w, conv_ssd_b, dt_bias, a_log,
              d_skip, ssd_norm_g, w_fa, w_sb, w_o, norm2_g, w_up, conv_ffn_w, conv_ffn_b, w_down, final_g):
    bsz = x.shape[0]
    lat, cstream = x, ctx
    for l in range(DEPTH):
        last = l == DEPTH - 1
        m_lat = _adaln(c, w_mod[l], b_mod[l])
        m_ctx = _adaln(c_ctx, w_mod[l], b_mod[l])
        a = -jnp.exp(a_log[l].astype(jnp.float32)).reshape(2, SSD_GROUPS, SSD_HPG)
        h_lat = _modulate(_rmsnorm(lat, norm1_g[l]), m_lat[0], m_lat[1])
        h_ctx = _modulate(_rmsnorm(cstream, norm1_g[l]), m_ctx[0], m_ctx[1])
        f_l, xbc_l, z_l, dt_l, gf_l, gs_l = _split_in(h_lat @ w_in[l])
        f_c, xbc_c, z_c, dt_c, gf_c, gs_c = _split_in(h_ctx @ w_in[l])
        xh_c, b_c, c_c, dts_c = _ssd_prep(xbc_c, dt_c, conv_ssd_w[l], conv_ssd_b[l], dt_bias[l])
        zeros = jnp.zeros((bsz, SSD_GROUPS, SSD_HPG, SSD_HEAD_DIM, SSD_STATE), jnp.float32)
        y_c, s_f, s_b = _ssd_bidir(xh_c, b_c, c_c, dts_c, a, zeros, zeros)
        xh_l, b_l, c_l, dts_l = _ssd_prep(xbc_l, dt_l, conv_ssd_w[l], conv_ssd_b[l], dt_bias[l])
        y_l, _, _ = _ssd_bidir(xh_l, b_l, c_l, dts_l, a, s_f, s_b)
        mix_lat = _mixer_merge(f_l, y_l, xh_l, z_l, gf_l, gs_l, d_skip[l], ssd_norm_g[l],
                               w_fa[l], w_sb[l], w_o[l])
        lat = lat + m_lat[2] * mix_lat
        h2_lat = _modulate(_rmsnorm(lat, norm2_g[l]), m_lat[3], m_lat[4])
        lat = lat + m_lat[5] * _conv_ffn(h2_lat, w_up[l], conv_ffn_w[l], conv_ffn_b[l], w_down[l], True)
        if not last:
            mix_ctx = _mixer_merge(f_c, y_c, xh_c, z_c, gf_c, gs_c, d_skip[l], ssd_norm_g[l],
                                   w_fa[l], w_sb[l], w_o[l])
            cstream = cstream + m_ctx[2] * mix_ctx
            h2_ctx = _modulate(_rmsnorm(cstream, norm2_g[l]), m_ctx[3], m_ctx[4])
            cstream = cstream + m_ctx[5] * _conv_ffn(h2_ctx, w_up[l], conv_ffn_w[l], conv_ffn_b[l],
                                                     w_down[l], False)
    return _rmsnorm(lat, final_g)
```

```python
import os
from contextlib import ExitStack
import numpy as np
import concourse.bass as bass
import concourse.mybir as mybir
from concourse.bass_utils import run_bass_kernel_spmd

F32 = mybir.dt.float32
BF16 = mybir.dt.bfloat16
AF = mybir.ActivationFunctionType
OP = mybir.AluOpType

D = 1024
SEQ = 8192
NCH = 64
NEXT = 17
EXT = NEXT * 128
EPS = 1e-6
BIG = 30000.0
NSLOT = 68
DFF = 2816
NFB = 22

STOP = os.environ.get("MK_STOP", "")
DEBUG = bool(STOP)


class Buf:
    def __init__(self, t, nparts=1):
        self.t = t
        self.n = nparts
        self.w = [None] * nparts
        self.r = [[] for _ in range(nparts)]
        self.excl = False

    def __getitem__(self, idx):
        return self.t[idx]


def _parts(items):
    out = []
    for it in items:
        if it is None:
            continue
        if isinstance(it, Buf):
            out.extend((it, i) for i in range(it.n))
        else:
            b, idx = it
            if isinstance(idx, int):
                out.append((b, idx))
            else:
                out.extend((b, i) for i in idx)
    return out


class Prog:
    def __init__(self, nc, es):
        self.nc = nc
        self.E = {}
        self.semid = 0
        for name, eng in (("pe", nc.tensor), ("act", nc.scalar), ("dve", nc.vector), ("pool", nc.gpsimd), ("sp", nc.sync)):
            sem = es.enter_context(nc.semaphore("s_" + name))
            self.E[name] = dict(name=name, eng=eng, sem=(self._sid(), sem), count=0, waited={}, pool=[], ndma=0)
        for name, n in (("sp", 8), ("pool", 6), ("act", 4)):
            for i in range(n):
                sem = es.enter_context(nc.semaphore("d_%s%d" % (name, i)))
                self.E[name]["pool"].append((self._sid(), sem))
        self.ninst = 0

    def _sid(self):
        self.semid += 1
        return self.semid

    def _wait(self, E, tok):
        (sid, sem), val, _ = tok
        if E["waited"].get(sid, 0) >= val:
            return
        E["eng"].wait_ge(sem, val)
        E["waited"][sid] = val

    def _collect(self, en, reads, writes):
        toks = []
        for b, i in _parts(reads):
            if b.w[i] is not None:
                toks.append(b.w[i])
            if b.excl:
                toks.extend(t for t in b.r[i] if t[2] != en)
        for b, i in _parts(writes):
            if b.w[i] is not None:
                toks.append(b.w[i])
            toks.extend(b.r[i])
        res = []
        for t in toks:
            if en == "pe" and t[2] == "pe":
                continue
            res.append(t)
        return res

    def _update(self, reads, writes, tok):
        for b, i in _parts(reads):
            b.r[i].append(tok)
            if len(b.r[i]) > 24:
                last = {}
                for t in b.r[i]:
                    k = t[0][0]
                    if k not in last or last[k][1] < t[1]:
                        last[k] = t
                b.r[i] = list(last.values())
        for b, i in _parts(writes):
            b.w[i] = tok
            b.r[i] = []

    def op(self, en, fn, reads=(), writes=()):
        E = self.E[en]
        for t in self._collect(en, reads, writes):
            self._wait(E, t)
        ins = fn(E["eng"])
        E["count"] += 1
        ins.then_inc(E["sem"][1], 1)
        tok = (E["sem"], E["count"], en)
        self._update(reads, writes, tok)
        self.ninst += 1
        return tok

    def dma(self, qn, out, in_, reads=(), writes=(), **kw):
        Q = self.E[qn]
        i = Q["ndma"]
        P = len(Q["pool"])
        sem = Q["pool"][i % P]
        val = 16 * (i // P + 1)
        if i >= P:
            self._wait(Q, (sem, val - 16, "dma"))
        for t in self._collect("dma", reads, writes):
            self._wait(Q, t)
        Q["eng"].dma_start(out=out, in_=in_, **kw).then_inc(sem[1], 16)
        Q["ndma"] += 1
        tok = (sem, val, "dma")
        self._update(reads, writes, tok)
        return tok

    def all_tokens(self):
        toks = []
        for E in self.E.values():
            if E["count"]:
                toks.append((E["sem"], E["count"], E["name"]))
            P = len(E["pool"])
            for j in range(min(P, E["ndma"])):
                n = (E["ndma"] - 1 - j) // P + 1
                toks.append((E["pool"][j], 16 * n, "dma"))
        return toks

    def barrier(self):
        toks = self.all_tokens()
        for E in self.E.values():
            for t in toks:
                self._wait(E, t)


def bc(ap, shape):
    return ap.broadcast_to(shape)


class Ctx:
    pass


def build_program():
    nc = bass.Bass("TRN2", target_bir_lowering=False)
    g = Ctx()
    g.nc = nc

    def din(name, shape, dt=F32):
        return nc.dram_tensor(name, list(shape), dt, kind="ExternalInput").ap()

    def dscr(name, shape, dt):
        kind = "ExternalOutput" if DEBUG else "Internal"
        return nc.dram_tensor(name, list(shape), dt, kind=kind).ap()

    g.xb = din("xb", [SEQ, D])
    g.ctxb = din("ctxb", [256, D])
    g.xext = din("xext", [EXT + 128, D])
    g.w_mod = din("w_mod", [D, 6 * D])
    g.w_in = din("w_in", [D, 8256])
    g.w_fa = din("w_fa", [D, D])
    g.w_sb = din("w_sb", [2048, D])
    g.w_o = din("w_o", [D, D])
    g.w_up = din("w_up", [D, 2 * DFF])
    g.w_down = din("w_down", [DFF, D])
    g.cpp = din("cpp", [128, CPP_N])
    g.cbc = din("cbc", [128, CBC_N])
    g.cbg = din("cbg", [128, CBG_N])
    g.cmk = din("cmk", [128, 8 * 128])
    g.t1 = din("t1", [64, 128])
    g.t2 = din("t2", [128, 2 * 64 * 68])
    g.tcs = din("tcs", [128, 256])
    g.out = nc.dram_tensor("out", [2048, D], F32, kind="ExternalOutput").ap()
    g.U = dscr("U", [NCH, 128, D], BF16)
    g.Y = dscr("Y", [128, 128, D], BF16)
    g.XT = dscr("XT", [128, 8 * 2 * EXT], BF16)
    g.MS = dscr("MS", [NEXT, 128, D], BF16)
    g.YF = dscr("YF", [NEXT, 128, 2048], BF16)
    g.YT = dscr("YT", [NEXT, 128, 2048], BF16)
    g.L1 = dscr("L1", [NEXT, 128, D], F32)
    g.SFB = dscr("SFB", [2, 128, 2048], F32)

    with ExitStack() as es:
        P = Prog(nc, es)
        g.P = P
        g.uid = 0
        phase_setup(g, es)
        if STOP != "setup":
            phase_far(g)
        if STOP not in ("setup", "far"):
            phase_fnet(g)
        if STOP not in ("setup", "far", "fnet"):
            phase_own(g, 0)
            phase_own(g, 1)
        if STOP not in ("setup", "far", "fnet", "own"):
            phase_merge(g)
        if STOP not in ("setup", "far", "fnet", "own", "merge"):
            phase_ffn(g)
        P.barrier()
    return nc


def sb(g, es, name, shape, dt, nparts=1):
    g.uid += 1
    t = es.enter_context(g.nc.sbuf_tensor("%s_%d" % (name, g.uid), list(shape), dt))
    return Buf(t, nparts)


def ps(g, es, name, shape, dt=F32, nparts=1):
    g.uid += 1
    t = es.enter_context(g.nc.psum_tensor("%s_%d" % (name, g.uid), list(shape), dt))
    b = Buf(t, nparts)
    b.excl = True
    return b


def _layout(items):
    off = {}
    o = 0
    for k, n in items:
        off[k] = (o, n)
        o += n
    return off, o


CPP, CPP_N = _layout([("c", 8), ("cctx", 8), ("bmod", 48), ("n1g", 8), ("n2g", 8), ("cw_ssd", 72), ("cb_ssd", 24),
                      ("cw_ffn", 9 * 44), ("cb_ffn", 44), ("emask", NEXT), ("fmask", NSLOT * 2)])
CBC, CBC_N = _layout([("dt_bias", 64), ("a_log", 64), ("d_skip", 32), ("emask_bc", 256), ("halo_v", 128)])
CBG, CBG_N = _layout([("ssd_g", 2048), ("final_g", 1024), ("bmod_g1", 1024), ("bmod_g2", 1024)])
MK = {k: i for i, k in enumerate(["ones", "le", "ge", "gt", "lt", "ident", "pen_f", "pen_b"])}


def cpp(g, key, j=None, n=1):
    o, _ = CPP[key]
    if j is None:
        return g.cpp_t[:, o:o + CPP[key][1]]
    return g.cpp_t[:, o + j:o + j + n]


def cbcv(g, key, a=0, n=None):
    o, m = CBC[key]
    if n is None:
        n = m
    return g.cbc_t[:, o + a:o + a + n]


def mk32(g, key):
    i = MK[key]
    return g.cmk_t[:, i * 128:(i + 1) * 128]


def mk16(g, key):
    i = MK[key]
    return g.cmkb_t[:, i * 128:(i + 1) * 128]


def phase_setup(g, es):
    nc, P = g.nc, g.P
    g.cpp_t = sb(g, es, "cpp", [128, CPP_N], F32)
    g.cbc_t = sb(g, es, "cbc", [128, CBC_N], F32)
    g.cmk_t = sb(g, es, "cmk", [128, 8 * 128], F32)
    g.cmkb_t = sb(g, es, "cmkb", [128, 8 * 128], BF16)
    g.modv = sb(g, es, "modv", [128, 8 * 8], F32)
    g.gbc = sb(g, es, "gbc", [128, 2 * D], F32)
    g.nega = sb(g, es, "nega", [128, 64], F32)
    g.Sf = sb(g, es, "Sf", [128, 2048], F32)
    g.Sb = sb(g, es, "Sb", [128, 2048], F32)
    P.dma("sp", g.cpp_t[:], g.cpp, writes=[g.cpp_t])
    P.dma("sp", g.cbc_t[:], g.cbc, writes=[g.cbc_t])
    P.dma("sp", g.cmk_t[:], g.cmk, writes=[g.cmk_t])
    P.dma("pool", g.cmkb_t[:], g.cmk, writes=[g.cmkb_t])
    with ExitStack() as ls:
        sc = sb(g, ls, "sc", [128, 8, 2], F32)
        screp = sb(g, ls, "screp", [128, 8, 128], F32)
        modT = sb(g, ls, "modT", [128, 48, 2], F32)
        wm = [sb(g, ls, "wm%d" % i, [128, 8, 1024], F32) for i in range(2)]
        pm = ps(g, ls, "pm", [128, 8, 2], F32)
        pg = ps(g, ls, "pg", [128, 512], F32)
        bg = sb(g, ls, "bg", [128, 2 * D], F32)
        P.dma("sp", bg[:], g.cbg[:, CBG["bmod_g1"][0]:CBG["bmod_g1"][0] + 2 * D], writes=[bg])
        P.op("act", lambda e: e.activation(out=sc[:, :, 0], in_=cpp(g, "c"), func=AF.Silu), [g.cpp_t], [sc])
        P.op("act", lambda e: e.activation(out=sc[:, :, 1], in_=cpp(g, "cctx"), func=AF.Silu), [g.cpp_t], [sc])
        P.op("dve", lambda e: e.tensor_copy(out=screp[:], in_=bc(sc[:, :, 0:1], [128, 8, 128])), [sc], [screp])
        wv = g.w_mod.rearrange("(kb p) n -> p kb n", p=128)
        for j in range(6):
            w = wm[j % 2]
            P.dma("sp", w[:], wv[:, :, j * 1024:(j + 1) * 1024], writes=[w])
            def mm(e, w=w):
                ins = None
                for fb in range(8):
                    for kb in range(8):
                        ins = e.matmul(pm[:, fb, :], lhsT=w[:, kb, fb * 128:(fb + 1) * 128], rhs=sc[:, kb, :],
                                       start=(kb == 0), stop=(kb == 7))
                return ins
            P.op("pe", mm, [w, sc], [pm])
            bo = CPP["bmod"][0] + j * 8
            P.op("dve", lambda e, j=j, bo=bo: e.tensor_tensor(
                out=modT[:, j * 8:(j + 1) * 8, :], in0=pm[:], in1=bc(g.cpp_t[:, bo:bo + 8].unsqueeze(2), [128, 8, 2]),
                op=OP.add), [pm, g.cpp_t], [modT])
            if j in (2, 5):
                gi = 0 if j == 2 else 1
                for hf in range(2):
                    def mg(e, w=w, hf=hf):
                        ins = None
                        for kb in range(8):
                            ins = e.matmul(pg[:], lhsT=screp[:, kb, :], rhs=w[:, kb, hf * 512:(hf + 1) * 512],
                                           start=(kb == 0), stop=(kb == 7))
                        return ins
                    P.op("pe", mg, [w, screp], [pg])
                    P.op("dve", lambda e, gi=gi, hf=hf: e.tensor_tensor(
                        out=g.gbc[:, gi * D + hf * 512: gi * D + (hf + 1) * 512], in0=pg[:],
                        in1=bg[:, gi * D + hf * 512: gi * D + (hf + 1) * 512], op=OP.add), [pg, bg], [g.gbc])
        mv = g.modv
        def mkA(dst, scale_j, which, gkey):
            P.op("dve", lambda e: e.scalar_tensor_tensor(
                out=mv[:, dst * 8:(dst + 1) * 8], in0=modT[:, scale_j * 8:(scale_j + 1) * 8, which], scalar=1.0,
                in1=cpp(g, gkey), op0=OP.add, op1=OP.mult), [modT, g.cpp_t], [mv])

        def mkB(dst, shift_j, which):
            P.op("dve", lambda e: e.tensor_copy(out=mv[:, dst * 8:(dst + 1) * 8],
                                                 in_=modT[:, shift_j * 8:(shift_j + 1) * 8, which]), [modT], [mv])
        mkA(0, 1, 0, "n1g"); mkB(1, 0, 0)
        mkA(2, 1, 1, "n1g"); mkB(3, 0, 1)
        mkA(4, 4, 0, "n2g"); mkB(5, 3, 0)
        P.op("act", lambda e: e.activation(out=g.nega[:], in_=cbcv(g, "a_log"), func=AF.Exp), [g.cbc_t], [g.nega])
        P.op("dve", lambda e: e.tensor_scalar(out=g.nega[:], in0=g.nega[:], scalar1=-1.0, scalar2=None, op0=OP.mult),
             [g.nega], [g.nega])
        P.barrier()


def modA(g, i, kb):
    return g.modv[:, i * 8 + kb:i * 8 + kb + 1]


def alloc_chunk_bufs(g, es, nfb):
    c = Ctx()
    c.xt = [sb(g, es, "xt%d" % i, [128, D], F32) for i in range(2)]
    c.junk = sb(g, es, "junk", [128, D], BF16)
    c.st = [sb(g, es, "st%d" % i, [128, 4], F32) for i in range(2)]
    c.xn = [sb(g, es, "xn%d" % i, [128, D], BF16) for i in range(2)]
    c.hTe = [sb(g, es, "hTe%d" % i, [128, 8, 130], BF16) for i in range(3)]
    c.pA = ps(g, es, "pA", [128, 8, 128], BF16)
    c.nfb = nfb
    return c


def prep(g, c, k, src_rows, ai, bi, vmask=None, hT=None):
    P = g.P
    i2 = k % 2
    xt, st, xn = c.xt[i2], c.st[i2], c.xn[i2]
    if hT is None:
        hT = c.hTe[k % 3]
    P.dma("sp", xt[:], src_rows, writes=[xt])
    P.op("act", lambda e: e.activation(out=c.junk[:], in_=xt[:], func=AF.Square, accum_out=st[:, 0:1]),
         [xt], [c.junk, st])
    P.op("dve", lambda e: e.tensor_scalar(out=st[:, 1:2], in0=st[:, 0:1], scalar1=1.0 / D, scalar2=EPS,
                                           op0=OP.mult, op1=OP.add), [st], [st])
    P.op("act", lambda e: e.activation(out=st[:, 2:3], in_=st[:, 1:2], func=AF.Sqrt), [st], [st])
    P.op("dve", lambda e: e.reciprocal(out=st[:, 3:4], in_=st[:, 2:3]), [st], [st])
    P.op("act", lambda e: e.activation(out=xn[:], in_=xt[:], func=AF.Copy, scale=st[:, 3:4]), [xt, st], [xn])

    def tr(e):
        ins = None
        for kb in range(8):
            ins = e.transpose(out=c.pA[:, kb, :], in_=xn[:, kb * 128:(kb + 1) * 128], identity=mk16(g, "ident"))
        return ins
    P.op("pe", tr, [xn, g.cmkb_t], [c.pA])
    for kb in range(8):
        eng = "dve" if kb % 2 else "act"
        if eng == "act":
            P.op("act", lambda e, kb=kb: e.activation(out=hT[:, kb, 1:129], in_=c.pA[:, kb, :], func=AF.Identity,
                                                      scale=modA(g, ai, kb), bias=modA(g, bi, kb)),
                 [c.pA, g.modv], [hT])
        else:
            P.op("dve", lambda e, kb=kb: e.tensor_scalar(out=hT[:, kb, 1:129], in0=c.pA[:, kb, :],
                                                          scalar1=modA(g, ai, kb), scalar2=modA(g, bi, kb),
                                                          op0=OP.mult, op1=OP.add), [c.pA, g.modv], [hT])
    if vmask is not None:
        P.op("pool", lambda e: e.tensor_tensor(out=hT[:, :, 1:129], in0=hT[:, :, 1:129],
                                               in1=bc(vmask.unsqueeze(1), [128, 8, 128]), op=OP.mult),
             [hT, g.cbc_t], [hT])
    return xt


def halo_link(g, c, k, has_left):
    P = g.P
    cur = c.hTe[k % 3]
    if has_left:
        prv = c.hTe[(k - 1) % 3]
        P.op("pool", lambda e: e.tensor_copy(out=cur[:, :, 0:1], in_=prv[:, :, 128:129]), [prv], [cur])
        P.op("pool", lambda e: e.tensor_copy(out=prv[:, :, 129:130], in_=cur[:, :, 1:2]), [cur], [prv])
    else:
        P.op("pool", lambda e: e.memset(cur[:, :, 0:1], 0.0), [], [cur])


def halo_zero_right(g, c, k):
    cur = c.hTe[k % 3]
    g.P.op("pool", lambda e: e.memset(cur[:, :, 129:130], 0.0), [], [cur])


def fm_proj_conv(g, c, s, hT, W, nfb, cw_off, steps=None):
    P = g.P
    steps = steps if steps is not None else []
    nb = len(s.pxb)
    groups = [list(range(a, min(a + 3, nfb))) for a in range(0, nfb, 3)]
    xc, xb2 = s.xc, s.xb2
    for gi, fbs in enumerate(groups):
        pb = s.pxb[gi % nb]

        def mm(e, fbs=fbs, pb=pb):
            ins = None
            for j, fb in enumerate(fbs):
                for kb in range(8):
                    ins = e.matmul(pb[:, j * 130:(j + 1) * 130], lhsT=W[:, kb, fb * 128:(fb + 1) * 128], rhs=hT[:, kb, :],
                                   start=(kb == 0), stop=(kb == 7))
            return ins
        P.op("pe", mm, [W, hT], [pb])
        for j, fb in enumerate(fbs):
            cf = cw_off + fb
            P.op("act", lambda e, fb=fb, j=j, pb=pb, cf=cf: e.activation(
                out=xc[:, fb, :], in_=pb[:, j * 130:j * 130 + 128], func=AF.Identity,
                scale=cpp(g, "cw_ssd", 0 * 24 + cf), bias=cpp(g, "cb_ssd", cf)), [pb, g.cpp_t], [(xc, gi)])
        for j, fb in enumerate(fbs):
            cf = cw_off + fb
            P.op("dve", lambda e, fb=fb, j=j, pb=pb, cf=cf: e.tensor_scalar(
                out=xb2[:, fb, :], in0=pb[:, j * 130 + 2:j * 130 + 130], scalar1=cpp(g, "cw_ssd", 2 * 24 + cf),
                scalar2=None, op0=OP.mult), [pb, g.cpp_t], [(xb2, gi)])
        for j, fb in enumerate(fbs):
            cf = cw_off + fb
            P.op("dve", lambda e, fb=fb, j=j, pb=pb, cf=cf: e.scalar_tensor_tensor(
                out=xc[:, fb, :], in0=pb[:, j * 130 + 1:j * 130 + 129], scalar=cpp(g, "cw_ssd", 1 * 24 + cf),
                in1=xc[:, fb, :], op0=OP.mult, op1=OP.add), [pb, g.cpp_t, (xc, gi)], [(xc, gi)])
        f0, f1 = fbs[0], fbs[-1] + 1
        P.op("pool", lambda e, f0=f0, f1=f1: e.tensor_tensor(out=xc[:, f0:f1, :], in0=xc[:, f0:f1, :],
                                                             in1=xb2[:, f0:f1, :], op=OP.add),
             [(xc, gi), (xb2, gi)], [(xc, gi)])
        P.op("act", lambda e, f0=f0, f1=f1: e.activation(out=s.xcs[:, f0:f1, :], in_=xc[:, f0:f1, :], func=AF.Silu),
             [(xc, gi)], [(s.xcs, gi)])
        if steps:
            steps.pop(0)()
    while steps:
        steps.pop(0)()


def to_token_major(g, c, s, nblk):
    P = g.P
    for r0 in range(0, nblk, 8):
        n = min(8, nblk - r0)

        def tr(e, r0=r0, n=n):
            ins = None
            for j in range(n):
                ins = e.transpose(out=c.pA[:, j, :], in_=s.xcs[:, r0 + j, :], identity=mk16(g, "ident"))
            return ins
        P.op("pe", tr, [s.xcs, g.cmkb_t], [c.pA])
        P.op("act", lambda e, r0=r0, n=n: e.activation(
            out=s.xtok[:, r0 * 128:(r0 + n) * 128], in_=c.pA[:, 0:n, :], func=AF.Copy), [c.pA], [s.xtok])


def dt_steps(g, s, hT, Wdt, ncol, bias_ap, nega_ap, mask_ap):
    P = g.P

    def s1():
        def mm(e):
            ins = None
            for kb in range(8):
                ins = e.matmul(s.pD[:, 0:ncol], lhsT=hT[:, kb, 1:129], rhs=Wdt[:, kb, 0:ncol], start=(kb == 0), stop=(kb == 7))
            return ins
        P.op("pe", mm, [hT, Wdt], [s.pD])
        P.op("dve", lambda e: e.tensor_tensor(out=s.dtm[:, 0:ncol], in0=s.pD[:, 0:ncol], in1=bias_ap, op=OP.add),
             [s.pD, g.cbc_t], [s.dtm])

    def s2():
        P.op("act", lambda e: e.activation(out=s.dtm[:, 0:ncol], in_=s.dtm[:, 0:ncol], func=AF.Exp), [s.dtm], [s.dtm])

    def s3():
        P.op("act", lambda e: e.activation(out=s.dtm[:, 0:ncol], in_=s.dtm[:, 0:ncol], func=AF.Ln, bias=1.0), [s.dtm], [s.dtm])

    def s4():
        dv = s.dtm[:, 0:ncol].rearrange("p (a b) -> p a b", b=32)
        P.op("dve", lambda e: e.tensor_tensor(out=dv, in0=dv, in1=mask_ap, op=OP.mult), [s.dtm, g.cpp_t], [s.dtm])

    def s5():
        P.op("dve", lambda e: e.tensor_tensor(out=s.la[:, 0:ncol], in0=s.dtm[:, 0:ncol], in1=nega_ap, op=OP.mult),
             [s.dtm, g.nega], [s.la])
    return [s1, s2, s3, s4, s5]


def state_contrib(g, s, wexp_ap, xdd, on_group):
    P = g.P
    P.op("dve", lambda e: e.tensor_tensor(
        out=xdd[:].rearrange("p (h d) -> p h d", d=64), in0=s.xtok[:, 0:2048].rearrange("p (h d) -> p h d", d=64),
        in1=bc(wexp_ap.unsqueeze(2), [128, 32, 64]), op=OP.mult), [s.xtok, s.wx], [xdd])
    for gi in range(4):
        P.op("pe", lambda e, gi=gi: e.matmul(s.pH[:], lhsT=s.xtok[:, 2048 + gi * 128:2048 + (gi + 1) * 128],
                                             rhs=xdd[:, gi * 512:(gi + 1) * 512], start=True, stop=True),
             [s.xtok, xdd], [s.pH])
        on_group(gi, s.pH)


def load_w_cols(g, W, col0, ncols, dst0=0):
    wv = g.w_in.rearrange("(kb p) n -> p kb n", p=128)
    for a in range(0, ncols, 512):
        n = min(512, ncols - a)
        g.P.dma("pool", W[:, :, dst0 + a:dst0 + a + n], wv[:, :, col0 + a:col0 + a + n], writes=[W])


def phase_far(g):
    nc, P = g.nc, g.P
    with ExitStack() as es:
        c = alloc_chunk_bufs(g, es, 20)
        s = Ctx()
        Wf = sb(g, es, "Wf", [128, 8, 1024], BF16)
        Wxb = sb(g, es, "Wxb", [128, 8, 2560], BF16)
        Wdt = sb(g, es, "Wdt", [128, 8, 64], BF16)
        load_w_cols(g, Wf, 0, 1024)
        load_w_cols(g, Wxb, 1024, 2560)
        load_w_cols(g, Wdt, 6144, 64)
        s.pxb = [ps(g, es, "pxb%d" % i, [128, 512], F32) for i in range(3)]
        s.pD = ps(g, es, "pD", [128, 512], F32)
        s.pH = ps(g, es, "pH", [128, 512], F32)
        pf = [ps(g, es, "pf%d" % i, [128, 512], F32) for i in range(2)]
        s.xc = sb(g, es, "xc", [128, 20, 128], F32, nparts=8)
        s.xb2 = sb(g, es, "xb2", [128, 20, 128], BF16, nparts=8)
        s.xcs = sb(g, es, "xcs", [128, 20, 128], BF16, nparts=8)
        s.pD2 = sb(g, es, "pD2", [128, 192], F32)
        s.xtok = sb(g, es, "xtok", [128, 2560], BF16)
        s.dtm = sb(g, es, "dtm", [128, 64], F32)
        s.la = sb(g, es, "la", [128, 64], F32)
        s.wx = sb(g, es, "wx", [128, 64], F32)
        s.sg = sb(g, es, "sg", [128, 64], F32)
        Rb = sb(g, es, "Rb", [128, 32], F32)
        dec = sb(g, es, "dec", [128, 32], F32)
        xdd = [sb(g, es, "xdd%d" % i, [128, 2048], BF16) for i in range(2)]
        ub = [sb(g, es, "ub%d" % i, [128, D], BF16) for i in range(2)]
        P.op("dve", lambda e: e.memset(g.Sf[:], 0.0), [], [g.Sf])
        P.op("dve", lambda e: e.memset(g.Sb[:], 0.0), [], [g.Sb])
        P.op("dve", lambda e: e.memset(Rb[:], 0.0), [], [Rb])
        slots = [("c", 0), ("c", 1)] + [("l", i) for i in range(NCH)] + [("c", 0), ("c", 1)]
        first = {0, 2, 66}
        last = {1, 65, 67}

        def do_prep(k):
            kind, i = slots[k]
            if kind == "c":
                prep(g, c, k, g.ctxb[i * 128:(i + 1) * 128, :], 2, 3)
            else:
                prep(g, c, k, g.xb[i * 128:(i + 1) * 128, :], 0, 1)
            halo_link(g, c, k, k not in first)
            if k in last:
                halo_zero_right(g, c, k)
        do_prep(0)
        for k in range(NSLOT):
            if k + 1 < NSLOT:
                do_prep(k + 1)
            kind, i = slots[k]
            hT = c.hTe[k % 3]
            if kind == "l":
                u = ub[i % 2]
                for hf in range(2):
                    def mm(e, hf=hf):
                        ins = None
                        for kb in range(8):
                            ins = e.matmul(pf[hf][:], lhsT=hT[:, kb, 1:129], rhs=Wf[:, kb, hf * 512:(hf + 1) * 512],
                                           start=(kb == 0), stop=(kb == 7))
                        return ins
                    P.op("pe", mm, [hT, Wf], [pf[hf]])
                    P.op("act", lambda e, hf=hf, u=u: e.activation(out=u[:, hf * 512:(hf + 1) * 512], in_=pf[hf][:],
                                                                   func=AF.Copy), [pf[hf]], [u])
                P.dma("sp", g.U[i], u[:], reads=[u])
            fo = CPP["fmask"][0] + 2 * k
            mask_ap = bc(g.cpp_t[:, fo:fo + 2].unsqueeze(2), [128, 2, 32])
            steps = dt_steps(g, s, hT, Wdt, 64, cbcv(g, "dt_bias"), g.nega[:], mask_ap)

            def t1():
                def segs(e):
                    e.matmul(s.pD[:, 64:96], lhsT=mk32(g, "gt"), rhs=s.la[:, 0:32], start=True, stop=True)
                    e.matmul(s.pD[:, 96:128], lhsT=mk32(g, "lt"), rhs=s.la[:, 32:64], start=True, stop=True)
                    return e.matmul(s.pD[:, 128:192], lhsT=mk32(g, "ones"), rhs=s.la[:, 0:64], start=True, stop=True)
                P.op("pe", segs, [s.la, g.cmk_t], [s.pD])

            def t2b():
                P.op("pool", lambda e: e.tensor_copy(out=s.sg[:, 0:32], in_=s.pD2[:, 64:96]), [s.pD2], [s.sg])
                P.op("pool", lambda e: e.tensor_tensor(out=s.sg[:, 32:64], in0=s.pD2[:, 96:128], in1=Rb[:], op=OP.add),
                     [s.pD2, Rb], [s.sg])
                P.op("pool", lambda e: e.tensor_tensor(out=Rb[:], in0=Rb[:], in1=s.pD2[:, 160:192], op=OP.add),
                     [s.pD2, Rb], [Rb])

            def t3():
                P.op("act", lambda e: e.activation(out=s.wx[:], in_=s.sg[:], func=AF.Exp), [s.sg], [s.wx])
                P.op("act", lambda e: e.activation(out=dec[:], in_=s.pD2[:, 128:160], func=AF.Exp), [s.pD2], [dec])

            def t4():
                P.op("pool", lambda e: e.tensor_tensor(out=s.wx[:], in0=s.wx[:], in1=s.dtm[:], op=OP.mult),
                     [s.wx, s.dtm], [s.wx])
                P.op("pool", lambda e: e.tensor_tensor(
                    out=g.Sf[:].rearrange("p (h d) -> p h d", d=64), in0=g.Sf[:].rearrange("p (h d) -> p h d", d=64),
                    in1=bc(dec[:].unsqueeze(2), [128, 32, 64]), op=OP.mult), [g.Sf, dec], [g.Sf])

            def t2a():
                P.op("dve", lambda e: e.tensor_copy(out=s.pD2[:], in_=s.pD[:, 0:192]), [s.pD], [s.pD2])
            steps += [t1, t2a, t2b, t3, t4]
            fm_proj_conv(g, c, s, hT, Wxb, 20, 0, steps)
            to_token_major(g, c, s, 20)
            x3 = s.xtok[:, 0:2048].rearrange("p (h d) -> p h d", d=64)
            P.op("dve", lambda e: e.tensor_tensor(out=xdd[0][:].rearrange("p (h d) -> p h d", d=64), in0=x3,
                                                  in1=bc(s.wx[:, 0:32].unsqueeze(2), [128, 32, 64]), op=OP.mult),
                 [s.xtok, s.wx], [xdd[0]])
            P.op("pool", lambda e: e.tensor_tensor(out=xdd[1][:].rearrange("p (h d) -> p h d", d=64), in0=x3,
                                                   in1=bc(s.wx[:, 32:64].unsqueeze(2), [128, 32, 64]), op=OP.mult),
                 [s.xtok, s.wx], [xdd[1]])
            banks = [s.pxb[0], s.pxb[1], s.pxb[2], s.pH]
            for di, (xd_, S_) in enumerate(((xdd[0], g.Sf), (xdd[1], g.Sb))):
                for gi in range(4):
                    pst = banks[gi]
                    P.op("pe", lambda e, gi=gi, pst=pst, xd_=xd_: e.matmul(
                        pst[:], lhsT=s.xtok[:, 2048 + gi * 128:2048 + (gi + 1) * 128],
                        rhs=xd_[:, gi * 512:(gi + 1) * 512], start=True, stop=True), [s.xtok, xd_], [pst])
                    P.op("dve", lambda e, gi=gi, pst=pst, S_=S_: e.tensor_tensor(
                        out=S_[:, gi * 512:(gi + 1) * 512], in0=S_[:, gi * 512:(gi + 1) * 512], in1=pst[:], op=OP.add),
                        [S_, pst], [S_])
        if DEBUG:
            P.dma("sp", g.SFB[0], g.Sf[:], reads=[g.Sf])
            P.dma("sp", g.SFB[1], g.Sb[:], reads=[g.Sb])
        P.barrier()


def load_w_gen(g, W, src, nkb, ncols):
    wv = src.rearrange("(kb p) n -> p kb n", p=128)
    for a in range(0, ncols, 512):
        n = min(512, ncols - a)
        g.P.dma("pool", W[:, :, a:a + n], wv[:, :, a:a + n], writes=[W])


def phase_fnet(g):
    nc, P = g.nc, g.P
    with ExitStack() as es:
        T1 = sb(g, es, "T1", [64, 128], BF16)
        P.dma("pool", T1[:], g.t1, writes=[T1])
        V = [sb(g, es, "V%d" % i, [64, 4, D], BF16) for i in range(2)]
        Yt = [sb(g, es, "Yt%d" % i, [128, 4, D], BF16) for i in range(2)]
        p1 = [ps(g, es, "p1_%d" % i, [128, 512], F32) for i in range(4)]
        cnt = 0
        for tg in range(32):
            v, yt = V[tg % 2], Yt[tg % 2]
            P.dma("sp", v[:], g.U[:, tg * 4:(tg + 1) * 4, :], writes=[v])
            for t in range(4):
                for hf in range(2):
                    pp = p1[cnt % 4]
                    P.op("pe", lambda e, pp=pp, t=t, hf=hf, v=v: e.matmul(
                        pp[:], lhsT=T1[:], rhs=v[:, t, hf * 512:(hf + 1) * 512], start=True, stop=True), [T1, v], [pp])
                    if cnt % 2:
                        P.op("act", lambda e, pp=pp, t=t, hf=hf, yt=yt: e.activation(
                            out=yt[:, t, hf * 512:(hf + 1) * 512], in_=pp[:], func=AF.Copy), [pp], [yt])
                    else:
                        P.op("dve", lambda e, pp=pp, t=t, hf=hf, yt=yt: e.tensor_copy(
                            out=yt[:, t, hf * 512:(hf + 1) * 512], in_=pp[:]), [pp], [yt])
                    cnt += 1
            P.dma("sp", g.Y[:, tg * 4:(tg + 1) * 4, :], yt[:], reads=[yt])
        P.barrier()
    with ExitStack() as es:
        T2 = sb(g, es, "T2", [128, 2 * 64 * 68], BF16)
        for a in range(0, 2 * 64 * 68, 1088):
            P.dma("pool", T2[:, a:a + 1088], g.t2[:, a:a + 1088], writes=[T2])
        Yk = [sb(g, es, "Yk%d" % i, [128, 2, D], BF16) for i in range(2)]
        XTs = sb(g, es, "XTs", [128, 8, 2, EXT], BF16)
        p2f = [ps(g, es, "p2_%d" % i, [128, 512], F32) for i in range(4)]
        yv = g.Y.rearrange("(ri k) t c -> k t ri c", ri=2)
        xv = XTs[:].rearrange("p c r (j k) -> p c r j k", k=64)
        for k1 in range(64):
            yk = Yk[k1 % 2]
            P.dma("sp", yk[:], yv[k1], writes=[yk])
            for cg in range(2):
                ppb = p2f[(k1 * 2 + cg) % 4]
                pp = ppb[:, 0:272].rearrange("p (c k) -> p c k", k=68)

                def mm(e, pp=pp, cg=cg, yk=yk, k1=k1):
                    ins = None
                    for cb in range(4):
                        cbx = cg * 4 + cb
                        e.matmul(pp[:, cb, :], lhsT=yk[:, 0, cbx * 128:(cbx + 1) * 128],
                                 rhs=T2[:, k1 * 68:(k1 + 1) * 68], start=True, stop=False)
                        ins = e.matmul(pp[:, cb, :], lhsT=yk[:, 1, cbx * 128:(cbx + 1) * 128],
                                       rhs=T2[:, (64 + k1) * 68:(64 + k1 + 1) * 68], start=False, stop=True)
                    return ins
                P.op("pe", mm, [yk, T2], [ppb])
                for ri in range(2):
                    if (k1 + cg) % 2:
                        P.op("act", lambda e, pp=pp, cg=cg, ri=ri, k1=k1: e.activation(
                            out=xv[:, cg * 4:(cg + 1) * 4, ri, :, k1], in_=pp[:, :, ri * 34:(ri + 1) * 34],
                            func=AF.Copy), [ppb], [XTs])
                    else:
                        P.op("dve", lambda e, pp=pp, cg=cg, ri=ri, k1=k1: e.tensor_copy(
                            out=xv[:, cg * 4:(cg + 1) * 4, ri, :, k1], in_=pp[:, :, ri * 34:(ri + 1) * 34]),
                            [ppb], [XTs])
        for cb in range(8):
            P.dma("sp", g.XT[:, cb * 2 * EXT:(cb + 1) * 2 * EXT].rearrange("p (r t) -> p r t", r=2), XTs[:, cb, :, :],
                  reads=[XTs])
        P.barrier()


def phase_own(g, d):
    nc, P = g.nc, g.P
    with ExitStack() as es:
        c = alloc_chunk_bufs(g, es, 24)
        s = Ctx()
        hH = sb(g, es, "hH", [128, 8, 130], BF16)
        W = sb(g, es, "Wxbc", [128, 8, 3072], BF16)
        Wdt = sb(g, es, "Wdt", [128, 8, 32], BF16)
        load_w_cols(g, W, 1024, 3072)
        load_w_cols(g, Wdt, 6144 + 32 * d, 32)
        s.pxb = [ps(g, es, "pxb%d" % i, [128, 512], F32) for i in range(2)]
        s.pD = ps(g, es, "pD", [128, 512], F32)
        s.pH = ps(g, es, "pH", [128, 512], F32)
        psc = ps(g, es, "psc", [128, 4, 128], F32)
        pL = [ps(g, es, "pL%d" % i, [128, 4, 128], F32) for i in range(2)]
        s.xc = sb(g, es, "xc", [128, 24, 128], F32, nparts=8)
        s.xb2 = sb(g, es, "xb2", [128, 24, 128], BF16, nparts=8)
        s.xcs = sb(g, es, "xcs", [128, 24, 128], BF16, nparts=8)
        s.xtok = sb(g, es, "xtok", [128, 2560], BF16)
        s.dtm = sb(g, es, "dtm", [128, 32], F32)
        s.la = sb(g, es, "la", [128, 32], F32)
        s.wx = sb(g, es, "wx", [128, 32], F32)
        lab = sb(g, es, "lab", [128, 32], BF16)
        nlab = sb(g, es, "nlab", [128, 32], BF16)
        ecum = sb(g, es, "ecum", [128, 32], F32)
        dec = sb(g, es, "dec", [128, 32], F32)
        xd = sb(g, es, "xd", [128, 2048], BF16)
        xdd = sb(g, es, "xdd", [128, 2048], BF16)
        Sbf = sb(g, es, "Sbf", [128, 2048], BF16)
        Dt = [sb(g, es, "Dt%d" % i, [128, 8, 128], BF16) for i in range(2)]
        Lx = [sb(g, es, "Lx%d" % i, [128, 8, 128], BF16) for i in range(2)]
        G = [sb(g, es, "G%d" % i, [128, 8, 128], BF16) for i in range(2)]
        yo = sb(g, es, "yo", [128, 512], F32)
        ytile = sb(g, es, "ytile", [128, 2048], BF16)
        tmp = sb(g, es, "tmp", [128, 2048], BF16)
        yfl = sb(g, es, "yfl", [128, 2048], BF16)
        S = g.Sf if d == 0 else g.Sb
        mxk = "le" if d == 0 else "ge"
        sgk = "gt" if d == 0 else "lt"
        penk = "pen_f" if d == 0 else "pen_b"
        order = list(range(NEXT)) if d == 0 else list(range(NEXT - 1, -1, -1))

        prep(g, c, 0, g.xext[EXT:EXT + 128, :], 0, 1, vmask=cbcv(g, "halo_v"), hT=hH)

        def do_prep(ci, prev_ci):
            hT = c.hTe[ci % 3]
            vm = None
            if ci == 0:
                vm = cbcv(g, "emask_bc", 0, 128)
            if ci == NEXT - 1:
                vm = cbcv(g, "emask_bc", 128, 128)
            prep(g, c, ci, g.xext[ci * 128:(ci + 1) * 128, :], 0, 1, vmask=vm)
            if prev_ci is not None:
                nb = c.hTe[prev_ci % 3]
                if ci == prev_ci + 1:
                    P.op("pool", lambda e: e.tensor_copy(out=hT[:, :, 0:1], in_=nb[:, :, 128:129]), [nb], [hT])
                    P.op("pool", lambda e: e.tensor_copy(out=nb[:, :, 129:130], in_=hT[:, :, 1:2]), [hT], [nb])
                else:
                    P.op("pool", lambda e: e.tensor_copy(out=hT[:, :, 129:130], in_=nb[:, :, 1:2]), [nb], [hT])
                    P.op("pool", lambda e: e.tensor_copy(out=nb[:, :, 0:1], in_=hT[:, :, 128:129]), [hT], [nb])
            if ci == 0:
                P.op("pool", lambda e: e.tensor_copy(out=hT[:, :, 0:1], in_=hH[:, :, 1:2]), [hH], [hT])
            if ci == NEXT - 1:
                P.op("pool", lambda e: e.tensor_copy(out=hT[:, :, 129:130], in_=hH[:, :, 2:3]), [hH], [hT])

        do_prep(order[0], None)
        for oi, ci in enumerate(order):
            if oi + 1 < NEXT:
                do_prep(order[oi + 1], ci)
            hT = c.hTe[ci % 3]
            if d == 1:
                P.dma("sp", yfl[:], g.YF[ci], writes=[yfl])
            mask_ap = bc(cpp(g, "emask", ci).unsqueeze(2), [128, 1, 32])
            steps = dt_steps(g, s, hT, Wdt, 32, cbcv(g, "dt_bias", 32 * d, 32), g.nega[:, 32 * d:32 * (d + 1)], mask_ap)

            def u1():
                P.op("pool", lambda e: e.tensor_copy(out=lab[:], in_=s.la[:]), [s.la], [lab])
                P.op("pool", lambda e: e.tensor_scalar(out=nlab[:], in0=lab[:], scalar1=-1.0, scalar2=None, op0=OP.mult),
                     [lab], [nlab])

                def segs(e):
                    e.matmul(s.pD[:, 64:96], lhsT=mk32(g, mxk), rhs=s.la[:], start=True, stop=True)
                    e.matmul(s.pD[:, 96:128], lhsT=mk32(g, sgk), rhs=s.la[:], start=True, stop=True)
                    return e.matmul(s.pD[:, 128:160], lhsT=mk32(g, "ones"), rhs=s.la[:], start=True, stop=True)
                P.op("pe", segs, [s.la, g.cmk_t], [s.pD])

            def u2():
                P.op("act", lambda e: e.activation(out=ecum[:], in_=s.pD[:, 64:96], func=AF.Exp), [s.pD], [ecum])
                P.op("act", lambda e: e.activation(out=s.wx[:], in_=s.pD[:, 96:128], func=AF.Exp), [s.pD], [s.wx])
                P.op("act", lambda e: e.activation(out=dec[:], in_=s.pD[:, 128:160], func=AF.Exp), [s.pD], [dec])

            def u3():
                P.op("pool", lambda e: e.tensor_tensor(out=s.wx[:], in0=s.wx[:], in1=s.dtm[:], op=OP.mult),
                     [s.wx, s.dtm], [s.wx])
            steps += [u1, u2, u3]
            fm_proj_conv(g, c, s, hT, W, 24, 0, steps)
            to_token_major(g, c, s, 20)
            x3 = s.xtok[:, 0:2048].rearrange("p (h d) -> p h d", d=64)
            P.op("pool", lambda e: e.tensor_tensor(out=xd[:].rearrange("p (h d) -> p h d", d=64), in0=x3,
                                                   in1=bc(s.dtm[:].unsqueeze(2), [128, 32, 64]), op=OP.mult),
                 [s.xtok, s.dtm], [xd])
            P.op("pool", lambda e: e.tensor_tensor(out=xdd[:].rearrange("p (h d) -> p h d", d=64), in0=x3,
                                                   in1=bc(s.wx[:].unsqueeze(2), [128, 32, 64]), op=OP.mult),
                 [s.xtok, s.wx], [xdd])
            P.op("act", lambda e: e.activation(out=Sbf[:], in_=S[:], func=AF.Copy), [S], [Sbf])

            def sc(e):
                ins = None
                for gi in range(4):
                    ins = e.matmul(psc[:, gi, :], lhsT=s.xcs[:, 16 + gi, :], rhs=s.xcs[:, 20 + gi, :], start=True, stop=True)
                return ins
            P.op("pe", sc, [s.xcs], [psc])
            for gi in range(4):
                dt_, lx, gg = Dt[gi % 2], Lx[gi % 2], G[gi % 2]
                P.op("pool", lambda e, gi=gi, dt_=dt_: e.tensor_tensor(
                    out=dt_[:], in0=bc(lab[:, gi * 8:(gi + 1) * 8].unsqueeze(2), [128, 8, 128]),
                    in1=bc(mk16(g, mxk).unsqueeze(1), [128, 8, 128]), op=OP.mult), [lab, g.cmkb_t], [dt_])
                for hh in range(2):
                    def mmL(e, gi=gi, hh=hh, dt_=dt_):
                        e.matmul(pL[hh][:], lhsT=mk16(g, "ones"), rhs=dt_[:, hh * 4:(hh + 1) * 4, :], start=True, stop=False)
                        e.matmul(pL[hh][:], lhsT=mk16(g, mxk),
                                 rhs=bc(nlab[:, gi * 8 + hh * 4:gi * 8 + hh * 4 + 4].unsqueeze(2), [128, 4, 128]),
                                 start=False, stop=False)
                        return e.matmul(pL[hh][:], lhsT=mk16(g, "ident"),
                                        rhs=bc(mk16(g, penk).unsqueeze(1), [128, 4, 128]), start=False, stop=True)
                    P.op("pe", mmL, [dt_, nlab, g.cmkb_t], [pL[hh]])
                    P.op("act", lambda e, hh=hh, lx=lx: e.activation(out=lx[:, hh * 4:(hh + 1) * 4, :], in_=pL[hh][:],
                                                                   func=AF.Exp), [pL[hh]], [lx])
                P.op("dve", lambda e, gi=gi, lx=lx, gg=gg: e.tensor_tensor(
                    out=gg[:], in0=lx[:], in1=bc(psc[:, gi, :].unsqueeze(1), [128, 8, 128]), op=OP.mult),
                    [lx, psc], [gg])

                def mmy(e, gi=gi, gg=gg):
                    ins = None
                    for h in range(8):
                        hh = gi * 8 + h
                        ins = e.matmul(s.pH[:, h * 64:(h + 1) * 64], lhsT=gg[:, h, :], rhs=xd[:, hh * 64:(hh + 1) * 64],
                                       start=True, stop=True)
                    return ins
                P.op("pe", mmy, [gg, xd], [s.pH])
                P.op("pe", lambda e, gi=gi: e.matmul(s.pxb[0][:], lhsT=s.xcs[:, 20 + gi, :],
                                                     rhs=Sbf[:, gi * 512:(gi + 1) * 512], start=True, stop=True),
                     [s.xcs, Sbf], [s.pxb[0]])
                P.op("dve", lambda e, gi=gi: e.tensor_tensor(
                    out=yo[:].rearrange("p (h d) -> p h d", d=64), in0=s.pxb[0][:].rearrange("p (h d) -> p h d", d=64),
                    in1=bc(ecum[:, gi * 8:(gi + 1) * 8].unsqueeze(2), [128, 8, 64]), op=OP.mult),
                    [s.pxb[0], ecum], [yo])
                P.op("dve", lambda e, gi=gi: e.tensor_tensor(out=ytile[:, gi * 512:(gi + 1) * 512], in0=yo[:],
                                                              in1=s.pH[:], op=OP.add), [yo, s.pH], [ytile])
                P.op("pe", lambda e, gi=gi: e.matmul(s.pxb[1][:], lhsT=s.xtok[:, 2048 + gi * 128:2048 + (gi + 1) * 128],
                                                     rhs=xdd[:, gi * 512:(gi + 1) * 512], start=True, stop=True),
                     [s.xtok, xdd], [s.pxb[1]])
                P.op("dve", lambda e, gi=gi: e.tensor_tensor(
                    out=S[:, gi * 512:(gi + 1) * 512].rearrange("p (h d) -> p h d", d=64),
                    in0=S[:, gi * 512:(gi + 1) * 512].rearrange("p (h d) -> p h d", d=64),
                    in1=bc(dec[:, gi * 8:(gi + 1) * 8].unsqueeze(2), [128, 8, 64]), op=OP.mult), [S, dec], [S])
                P.op("dve", lambda e, gi=gi: e.tensor_tensor(out=S[:, gi * 512:(gi + 1) * 512],
                                                              in0=S[:, gi * 512:(gi + 1) * 512], in1=s.pxb[1][:],
                                                              op=OP.add), [S, s.pxb[1]], [S])
            if d == 0:
                P.op("pool", lambda e: e.tensor_tensor(out=tmp[:].rearrange("p (h d) -> p h d", d=64), in0=x3,
                                                       in1=bc(cbcv(g, "d_skip").unsqueeze(2), [128, 32, 64]),
                                                       op=OP.mult), [s.xtok, g.cbc_t], [tmp])
                P.op("pool", lambda e: e.tensor_tensor(out=tmp[:], in0=tmp[:], in1=ytile[:], op=OP.add),
                     [tmp, ytile], [tmp])
                P.dma("sp", g.YF[ci], tmp[:], reads=[tmp])
            else:
                P.op("pool", lambda e: e.tensor_tensor(out=tmp[:], in0=yfl[:], in1=ytile[:], op=OP.add),
                     [yfl, ytile], [tmp])
                P.dma("sp", g.YT[ci], tmp[:], reads=[tmp])
        P.barrier()


def phase_merge(g):
    phase_merge_a(g)
    phase_merge_b(g)


def phase_merge_a(g):
    nc, P = g.nc, g.P
    with ExitStack() as es:
        c = alloc_chunk_bufs(g, es, 0)
        Wz = sb(g, es, "Wz", [128, 8, 2048], BF16)
        Wgs = sb(g, es, "Wgs", [128, 8, 1024], BF16)
        Wsb = sb(g, es, "Wsb", [128, 16, 1024], BF16)
        load_w_cols(g, Wz, 4096, 2048)
        load_w_cols(g, Wgs, 7232, 1024)
        load_w_gen(g, Wsb, g.w_sb, 16, 1024)
        sg = sb(g, es, "ssdg", [128, 2048], F32)
        P.dma("sp", sg[:], g.cbg[:, CBG["ssd_g"][0]:CBG["ssd_g"][0] + 2048], writes=[sg])
        pz = ps(g, es, "pz", [128, 512], F32)
        pb0 = ps(g, es, "pb0", [128, 2, 512], F32)
        pb1 = ps(g, es, "pb1", [128, 2, 512], F32)
        yt = [sb(g, es, "yt%d" % i, [128, 2048], BF16) for i in range(2)]
        zs = sb(g, es, "zs", [128, 512], F32)
        t = sb(g, es, "t", [128, 512], F32)
        jk = sb(g, es, "jk", [128, 512], BF16)
        st2 = sb(g, es, "st2", [128, 16], F32)
        ysn = sb(g, es, "ysn", [128, 512], BF16)
        ysnT = sb(g, es, "ysnT", [128, 16, 128], BF16)
        sgs = sb(g, es, "sgs", [128, 1024], F32)
        ms = [sb(g, es, "ms%d" % i, [128, 1024], BF16) for i in range(2)]
        for ci in range(NEXT):
            hT = c.hTe[ci % 3]
            prep(g, c, ci, g.xext[ci * 128:(ci + 1) * 128, :], 0, 1)
            y = yt[ci % 2]
            P.dma("sp", y[:], g.YT[ci], writes=[y])
            for gi in range(4):
                def mm(e, gi=gi):
                    ins = None
                    for kb in range(8):
                        ins = e.matmul(pz[:], lhsT=hT[:, kb, 1:129], rhs=Wz[:, kb, gi * 512:(gi + 1) * 512],
                                       start=(kb == 0), stop=(kb == 7))
                    return ins
                P.op("pe", mm, [hT, Wz], [pz])
                P.op("act", lambda e: e.activation(out=zs[:], in_=pz[:], func=AF.Silu), [pz], [zs])
                P.op("dve", lambda e, gi=gi: e.tensor_tensor(out=t[:], in0=zs[:], in1=y[:, gi * 512:(gi + 1) * 512],
                                                              op=OP.mult), [zs, y], [t])
                P.op("act", lambda e, gi=gi: e.activation(out=jk[:], in_=t[:], func=AF.Square,
                                                          accum_out=st2[:, gi:gi + 1]), [t], [jk, st2])
                P.op("dve", lambda e, gi=gi: e.tensor_scalar(out=st2[:, 4 + gi:5 + gi], in0=st2[:, gi:gi + 1],
                                                              scalar1=1.0 / 512, scalar2=EPS, op0=OP.mult, op1=OP.add),
                     [st2], [st2])
                P.op("act", lambda e, gi=gi: e.activation(out=st2[:, 8 + gi:9 + gi], in_=st2[:, 4 + gi:5 + gi],
                                                          func=AF.Sqrt), [st2], [st2])
                P.op("dve", lambda e, gi=gi: e.reciprocal(out=st2[:, 12 + gi:13 + gi], in_=st2[:, 8 + gi:9 + gi]),
                     [st2], [st2])
                P.op("dve", lambda e, gi=gi: e.scalar_tensor_tensor(
                    out=ysn[:], in0=t[:], scalar=st2[:, 12 + gi:13 + gi], in1=sg[:, gi * 512:(gi + 1) * 512],
                    op0=OP.mult, op1=OP.mult), [t, st2, sg], [ysn])

                def tr(e):
                    ins = None
                    for j in range(4):
                        ins = e.transpose(out=c.pA[:, j, :], in_=ysn[:, j * 128:(j + 1) * 128], identity=mk16(g, "ident"))
                    return ins
                P.op("pe", tr, [ysn, g.cmkb_t], [c.pA])
                P.op("act", lambda e, gi=gi: e.activation(out=ysnT[:, gi * 4:(gi + 1) * 4, :], in_=c.pA[:, 0:4, :],
                                                          func=AF.Copy), [c.pA], [ysnT])
            for hf in range(2):
                def mms(e, hf=hf):
                    ins = None
                    for kb in range(16):
                        ins = e.matmul(pb0[:, hf, :], lhsT=ysnT[:, kb, :], rhs=Wsb[:, kb, hf * 512:(hf + 1) * 512],
                                       start=(kb == 0), stop=(kb == 15))
                    return ins
                P.op("pe", mms, [ysnT, Wsb], [pb0])

                def mmg(e, hf=hf):
                    ins = None
                    for kb in range(8):
                        ins = e.matmul(pb1[:, hf, :], lhsT=hT[:, kb, 1:129], rhs=Wgs[:, kb, hf * 512:(hf + 1) * 512],
                                       start=(kb == 0), stop=(kb == 7))
                    return ins
                P.op("pe", mmg, [hT, Wgs], [pb1])
            P.op("act", lambda e: e.activation(out=sgs[:], in_=pb1[:].rearrange("p a b -> p (a b)"), func=AF.Sigmoid),
                 [pb1], [sgs])
            m = ms[ci % 2]
            P.op("dve", lambda e, m=m: e.tensor_tensor(out=m[:], in0=sgs[:], in1=pb0[:].rearrange("p a b -> p (a b)"),
                                                        op=OP.mult), [sgs, pb0], [m])
            P.dma("sp", g.MS[ci], m[:], reads=[m])
        P.barrier()


def phase_merge_b(g):
    nc, P = g.nc, g.P
    with ExitStack() as es:
        c = alloc_chunk_bufs(g, es, 0)
        Wgf = sb(g, es, "Wgf", [128, 8, 1024], BF16)
        Wfa = sb(g, es, "Wfa", [128, 8, 1024], BF16)
        Wo = sb(g, es, "Wo", [128, 8, 1024], BF16)
        Tcs = sb(g, es, "Tcs", [128, 256], BF16)
        load_w_cols(g, Wgf, 6208, 1024)
        load_w_gen(g, Wfa, g.w_fa, 8, 1024)
        load_w_gen(g, Wo, g.w_o, 8, 1024)
        P.dma("pool", Tcs[:], g.tcs, writes=[Tcs])
        pb0 = ps(g, es, "pb0", [128, 2, 512], F32)
        pb1 = ps(g, es, "pb1", [128, 2, 512], F32)
        pb2 = ps(g, es, "pb2", [128, 2, 512], F32)
        xtc = [sb(g, es, "xtc%d" % i, [128, 8, 2, 128], BF16) for i in range(2)]
        msl = [sb(g, es, "msl%d" % i, [128, 1024], BF16) for i in range(2)]
        mixT = sb(g, es, "mixT", [128, 8, 128], BF16)
        sgf = sb(g, es, "sgf", [128, 1024], F32)
        tmp = sb(g, es, "tmpm", [128, 1024], F32)
        mrg = sb(g, es, "mrg", [128, 1024], BF16)
        mrgT = sb(g, es, "mrgT", [128, 8, 128], BF16)
        l1 = [sb(g, es, "l1_%d" % i, [128, 1024], F32) for i in range(2)]
        xtv = g.XT.rearrange("p (c r t) -> p c r t", c=8, r=2)
        for ci in range(NEXT):
            hT = c.hTe[ci % 3]
            xt = prep(g, c, ci, g.xext[ci * 128:(ci + 1) * 128, :], 0, 1)
            xc_, m = xtc[ci % 2], msl[ci % 2]
            P.dma("sp", xc_[:], xtv[:, :, :, ci * 128:(ci + 1) * 128], writes=[xc_])
            P.dma("sp", m[:], g.MS[ci], writes=[m])
            for cg in range(2):
                def mmx(e, cg=cg):
                    e.matmul(pb2[:, cg, :], lhsT=Tcs[:, 0:128], rhs=xc_[:, cg * 4:(cg + 1) * 4, 0, :], start=True, stop=False)
                    return e.matmul(pb2[:, cg, :], lhsT=Tcs[:, 128:256], rhs=xc_[:, cg * 4:(cg + 1) * 4, 1, :],
                                    start=False, stop=True)
                P.op("pe", mmx, [Tcs, xc_], [pb2])
            P.op("act", lambda e: e.activation(out=mixT[:].rearrange("p a b -> p (a b)"),
                                               in_=pb2[:].rearrange("p a b -> p (a b)"), func=AF.Copy), [pb2], [mixT])
            for hf in range(2):
                def mmf(e, hf=hf):
                    ins = None
                    for kb in range(8):
                        ins = e.matmul(pb0[:, hf, :], lhsT=mixT[:, kb, :], rhs=Wfa[:, kb, hf * 512:(hf + 1) * 512],
                                       start=(kb == 0), stop=(kb == 7))
                    return ins
                P.op("pe", mmf, [mixT, Wfa], [pb0])

                def mmg(e, hf=hf):
                    ins = None
                    for kb in range(8):
                        ins = e.matmul(pb1[:, hf, :], lhsT=hT[:, kb, 1:129], rhs=Wgf[:, kb, hf * 512:(hf + 1) * 512],
                                       start=(kb == 0), stop=(kb == 7))
                    return ins
                P.op("pe", mmg, [hT, Wgf], [pb1])
            P.op("act", lambda e: e.activation(out=sgf[:], in_=pb1[:].rearrange("p a b -> p (a b)"), func=AF.Sigmoid),
                 [pb1], [sgf])
            P.op("dve", lambda e: e.tensor_tensor(out=tmp[:], in0=sgf[:], in1=pb0[:].rearrange("p a b -> p (a b)"),
                                                   op=OP.mult), [sgf, pb0], [tmp])
            P.op("dve", lambda e, m=m: e.tensor_tensor(out=mrg[:], in0=tmp[:], in1=m[:], op=OP.add), [tmp, m], [mrg])

            def tr(e):
                ins = None
                for kb in range(8):
                    ins = e.transpose(out=c.pA[:, kb, :], in_=mrg[:, kb * 128:(kb + 1) * 128], identity=mk16(g, "ident"))
                return ins
            P.op("pe", tr, [mrg, g.cmkb_t], [c.pA])
            P.op("act", lambda e: e.activation(out=mrgT[:], in_=c.pA[:], func=AF.Copy), [c.pA], [mrgT])
            for hf in range(2):
                def mmo(e, hf=hf):
                    ins = None
                    for kb in range(8):
                        ins = e.matmul(pb2[:, hf, :], lhsT=mrgT[:, kb, :], rhs=Wo[:, kb, hf * 512:(hf + 1) * 512],
                                       start=(kb == 0), stop=(kb == 7))
                    return ins
                P.op("pe", mmo, [mrgT, Wo], [pb2])
            l = l1[ci % 2]
            P.op("dve", lambda e, l=l: e.tensor_tensor(out=l[:], in0=pb2[:].rearrange("p a b -> p (a b)"),
                                                        in1=g.gbc[:, 0:D], op=OP.mult), [pb2, g.gbc], [l])
            P.op("pool", lambda e, l=l, xt=xt: e.tensor_tensor(out=l[:], in0=l[:], in1=xt[:], op=OP.add), [l, xt], [l])
            P.dma("sp", g.L1[ci], l[:], reads=[l])
        P.barrier()


def phase_ffn(g):
    nc, P = g.nc, g.P
    with ExitStack() as es:
        c = alloc_chunk_bufs(g, es, 0)
        h2T = sb(g, es, "h2T", [128, 8, EXT], BF16, nparts=NEXT)
        Wd = sb(g, es, "Wd", [128, NFB, 1024], BF16)
        load_w_gen(g, Wd, g.w_down, NFB, 1024)
        fg = sb(g, es, "fg", [128, 1024], F32)
        P.dma("sp", fg[:], g.cbg[:, CBG["final_g"][0]:CBG["final_g"][0] + 1024], writes=[fg])
        for ci in range(NEXT):
            i2 = ci % 2
            xt, st, xn = c.xt[i2], c.st[i2], c.xn[i2]
            P.dma("sp", xt[:], g.L1[ci], writes=[xt])
            P.op("act", lambda e: e.activation(out=c.junk[:], in_=xt[:], func=AF.Square, accum_out=st[:, 0:1]),
                 [xt], [c.junk, st])
            P.op("dve", lambda e: e.tensor_scalar(out=st[:, 1:2], in0=st[:, 0:1], scalar1=1.0 / D, scalar2=EPS,
                                                   op0=OP.mult, op1=OP.add), [st], [st])
            P.op("act", lambda e: e.activation(out=st[:, 2:3], in_=st[:, 1:2], func=AF.Sqrt), [st], [st])
            P.op("dve", lambda e: e.reciprocal(out=st[:, 3:4], in_=st[:, 2:3]), [st], [st])
            P.op("act", lambda e: e.activation(out=xn[:], in_=xt[:], func=AF.Copy, scale=st[:, 3:4]), [xt, st], [xn])

            def tr(e):
                ins = None
                for kb in range(8):
                    ins = e.transpose(out=c.pA[:, kb, :], in_=xn[:, kb * 128:(kb + 1) * 128], identity=mk16(g, "ident"))
                return ins
            P.op("pe", tr, [xn, g.cmkb_t], [c.pA])
            for kb in range(8):
                P.op("dve", lambda e, kb=kb, ci=ci: e.tensor_scalar(
                    out=h2T[:, kb, ci * 128:(ci + 1) * 128], in0=c.pA[:, kb, :], scalar1=modA(g, 4, kb),
                    scalar2=modA(g, 5, kb), op0=OP.mult, op1=OP.add), [c.pA, g.modv], [(h2T, ci)])
            if ci in (0, NEXT - 1):
                vm = cbcv(g, "emask_bc", 0 if ci == 0 else 128, 128)
                P.op("pool", lambda e, ci=ci, vm=vm: e.tensor_tensor(
                    out=h2T[:, :, ci * 128:(ci + 1) * 128], in0=h2T[:, :, ci * 128:(ci + 1) * 128],
                    in1=bc(vm.unsqueeze(1), [128, 8, 128]), op=OP.mult), [(h2T, ci), g.cbc_t], [(h2T, ci)])
        NB = 4
        pu = [ps(g, es, "pu%d" % i, [128, 512], F32) for i in range(2)]
        pd = ps(g, es, "pd", [128, 2, 512], F32)
        aT = sb(g, es, "aT", [128, NFB, 512], BF16, nparts=NFB)
        wu = [sb(g, es, "wu%d" % i, [128, 8, 2, 128], BF16) for i in range(3)]
        ug = [sb(g, es, "ug%d" % i, [128, 10, 64], F32) for i in range(2)]
        acc = [sb(g, es, "acc%d" % i, [128, 8, 64], F32) for i in range(2)]
        acc2 = [sb(g, es, "acc2_%d" % i, [128, 8, 64], F32) for i in range(2)]
        tmpc = [sb(g, es, "tmpc%d" % i, [128, 8, 64], F32) for i in range(2)]
        sgl = sb(g, es, "sgl", [128, 512], F32)
        lt = [sb(g, es, "lt%d" % i, [128, 1024], F32) for i in range(2)]
        yy = [sb(g, es, "yy%d" % i, [128, 1024], F32) for i in range(2)]
        jk = c.junk
        st = [sb(g, es, "stf%d" % i, [128, 4], F32) for i in range(2)]
        wuv = g.w_up.rearrange("(kb p) (gv n) -> p kb gv n", p=128, gv=2)
        l1f = g.L1.rearrange("c p d -> (c p) d")
        cnt = 0
        nitem = NB * NFB

        def issue_w(i):
            if i < nitem:
                fb_ = i % NFB
                w_ = wu[i % 3]
                for gv_ in range(2):
                    P.dma("pool", w_[:, :, gv_, :], wuv[:, :, gv_, fb_ * 128:(fb_ + 1) * 128], writes=[w_])
        issue_w(0)
        issue_w(1)
        for blk in range(NB):
            base = blk * 512
            hparts = [(h2T, i) for i in range(base // 128, (base + 640 + 127) // 128)]
            for fb in range(NFB):
                w = wu[cnt % 3]
                issue_w(cnt + 2)
                cnt += 1
                for gv in range(2):
                    u = ug[gv]
                    a = acc[gv]
                    for j in range(2):
                        def mm(e, j=j, gv=gv, w=w):
                            ins = None
                            for kb in range(8):
                                ins = e.matmul(pu[j][:, 0:320], lhsT=w[:, kb, gv, :],
                                               rhs=h2T[:, kb, base + j * 320:base + (j + 1) * 320],
                                               start=(kb == 0), stop=(kb == 7))
                            return ins
                        P.op("pe", mm, [w] + hparts, [pu[j]])
                        P.op("act", lambda e, j=j, u=u: e.activation(
                            out=u[:].rearrange("p r c -> p (r c)")[:, j * 320:(j + 1) * 320], in_=pu[j][:, 0:320],
                            func=AF.Copy), [pu[j]], [u])
                    cf = gv * NFB + fb
                    wt = lambda t: cpp(g, "cw_ffn", t * 44 + cf)
                    P.op("act", lambda e, u=u, a=a, cf=cf: e.activation(
                        out=a[:], in_=u[:, 1:9, :], func=AF.Identity, scale=cpp(g, "cw_ffn", 4 * 44 + cf),
                        bias=cpp(g, "cb_ffn", cf)), [u, g.cpp_t], [a])
                    a2, tp = acc2[gv], tmpc[gv]
                    for (kh, kw) in ((0, 0), (0, 2), (2, 0), (2, 2)):
                        dy, dx = kh - 1, kw - 1
                        c0, c1 = max(0, -dx), 64 - max(0, dx)
                        P.op("dve", lambda e, u=u, a=a, dy=dy, dx=dx, c0=c0, c1=c1, t=kh * 3 + kw, cf=cf:
                             e.scalar_tensor_tensor(out=a[:, :, c0:c1], in0=u[:, 1 + dy:9 + dy, c0 + dx:c1 + dx],
                                                    scalar=cpp(g, "cw_ffn", t * 44 + cf), in1=a[:, :, c0:c1],
                                                    op0=OP.mult, op1=OP.add), [u, a, g.cpp_t], [a])
                    for n_, (kh, kw) in enumerate(((0, 1), (2, 1), (1, 0), (1, 2))):
                        dy, dx = kh - 1, kw - 1
                        c0, c1 = max(0, -dx), 64 - max(0, dx)
                        dst = a2 if n_ == 0 else tp
                        P.op("pool", lambda e, u=u, dst=dst, dy=dy, dx=dx, c0=c0, c1=c1, t=kh * 3 + kw, cf=cf:
                             e.tensor_scalar(out=dst[:, :, c0:c1], in0=u[:, 1 + dy:9 + dy, c0 + dx:c1 + dx],
                                             scalar1=cpp(g, "cw_ffn", t * 44 + cf), scalar2=0.0, op0=OP.mult, op1=OP.add),
                             [u, g.cpp_t], [dst])
                        if n_ > 0:
                            P.op("pool", lambda e, a2=a2, tp=tp, c0=c0, c1=c1: e.tensor_tensor(
                                out=a2[:, :, c0:c1], in0=a2[:, :, c0:c1], in1=tp[:, :, c0:c1], op=OP.add), [a2, tp], [a2])
                    P.op("pool", lambda e, a=a, a2=a2: e.tensor_tensor(out=a[:], in0=a[:], in1=a2[:], op=OP.add),
                         [a, a2], [a])
                P.op("act", lambda e: e.activation(out=sgl[:], in_=acc[0][:].rearrange("p r c -> p (r c)"), func=AF.Silu),
                     [acc[0]], [sgl])
                P.op("dve", lambda e, fb=fb: e.tensor_tensor(out=aT[:, fb, :], in0=sgl[:],
                                                              in1=acc[1][:].rearrange("p r c -> p (r c)"), op=OP.mult),
                     [sgl, acc[1]], [(aT, fb)])
            for tcn in range(4):
                o0 = blk * 512 + tcn * 128
                i2 = (blk * 4 + tcn) % 2
                l, y, s4 = lt[i2], yy[i2], st[i2]
                o = y
                P.dma("sp", l[:], l1f[o0 + 64:o0 + 64 + 128, :], writes=[l])
                for hf in range(2):
                    def mmd(e, hf=hf, tcn=tcn):
                        ins = None
                        for fb in range(NFB):
                            ins = e.matmul(pd[:, hf, :], lhsT=aT[:, fb, tcn * 128:(tcn + 1) * 128],
                                           rhs=Wd[:, fb, hf * 512:(hf + 1) * 512], start=(fb == 0), stop=(fb == NFB - 1))
                        return ins
                    P.op("pe", mmd, [aT, Wd], [pd])
                P.op("dve", lambda e, y=y: e.tensor_tensor(out=y[:], in0=pd[:].rearrange("p a b -> p (a b)"),
                                                            in1=g.gbc[:, D:2 * D], op=OP.mult), [pd, g.gbc], [y])
                P.op("pool", lambda e, y=y, l=l: e.tensor_tensor(out=y[:], in0=y[:], in1=l[:], op=OP.add), [y, l], [y])
                P.op("act", lambda e, y=y, s4=s4: e.activation(out=jk[:], in_=y[:], func=AF.Square,
                                                               accum_out=s4[:, 0:1]), [y], [jk, s4])
                P.op("dve", lambda e, s4=s4: e.tensor_scalar(out=s4[:, 1:2], in0=s4[:, 0:1], scalar1=1.0 / D,
                                                              scalar2=EPS, op0=OP.mult, op1=OP.add), [s4], [s4])
                P.op("act", lambda e, s4=s4: e.activation(out=s4[:, 2:3], in_=s4[:, 1:2], func=AF.Sqrt), [s4], [s4])
                P.op("dve", lambda e, s4=s4: e.reciprocal(out=s4[:, 3:4], in_=s4[:, 2:3]), [s4], [s4])
                P.op("dve", lambda e, y=y, s4=s4, o=o: e.scalar_tensor_tensor(
                    out=o[:], in0=y[:], scalar=s4[:, 3:4], in1=fg[:], op0=OP.mult, op1=OP.mult), [y, s4, fg], [y])
                P.dma("sp", g.out[o0:o0 + 128, :], o[:], reads=[o])
        P.barrier()


def _pm(v):
    v = np.asarray(v, np.float32)
    return np.ascontiguousarray(v.reshape(-1, 128).T)


def _rb(v):
    v = np.asarray(v, np.float32).reshape(1, -1)
    return np.ascontiguousarray(np.broadcast_to(v, (128, v.shape[1])))


def _const_tables():
    k = np.arange(128)[:, None]
    m = np.arange(128)[None, :]
    mats = [np.ones((128, 128)), k <= m, k >= m, k > m, k < m, k == m,
            np.where(m < k, -BIG, 0.0), np.where(m > k, -BIG, 0.0)]
    cmk = np.concatenate([np.asarray(a, np.float32) for a in mats], axis=1)
    t1i = np.arange(64)[:, None] * np.arange(64)[None, :]
    th = 2 * np.pi * t1i / 64.0
    t1 = np.concatenate([np.cos(th), -np.sin(th)], axis=1).astype(np.float32)
    j = np.arange(128)[:, None] * np.arange(128)[None, :]
    thc = 2 * np.pi * j / 128.0
    tcs = (np.concatenate([np.cos(thc), np.sin(thc)], axis=1) / 1024.0).astype(np.float32)
    return cmk, t1, tcs


def _t2_tables(q):
    t2 = np.arange(128, dtype=np.float64)[:, None, None]
    k1 = np.arange(64, dtype=np.float64)[None, :, None]
    k2 = (32 * q - 1 + np.arange(34, dtype=np.float64))[None, None, :]
    kk = np.mod(k1 + 64 * k2, 8192)
    th = 2 * np.pi * np.mod(kk * t2, 8192) / 8192.0
    Mr, Mi = np.cos(th), -np.sin(th)
    ta = np.concatenate([Mr, Mi], axis=2)
    tb = np.concatenate([-Mi, Mr], axis=2)
    return np.concatenate([ta.reshape(128, -1), tb.reshape(128, -1)], axis=1).astype(np.float32)


_CACHE = {}


def kernel(x, c, ctx, c_ctx, w_mod, b_mod, norm1_g, w_in, conv_ssd_w, conv_ssd_b, dt_bias, a_log,
           d_skip, ssd_norm_g, w_fa, w_sb, w_o, norm2_g, w_up, conv_ffn_w, conv_ffn_b, w_down, final_g):
    f = lambda a: np.asarray(a, np.float32)
    x, c, ctx, c_ctx = f(x), f(c), f(ctx), f(c_ctx)
    cmk, t1, tcs = _const_tables()
    in_maps = []
    bm = f(b_mod)[0]
    for core in range(8):
        b, q = divmod(core, 4)
        e0 = 2048 * q - 64
        xext = np.zeros((EXT + 128, D), np.float32)
        lo, hi = max(e0, 0), min(e0 + EXT, SEQ)
        xext[lo - e0:hi - e0] = x[b, lo:hi]
        hv = np.zeros(2, np.float32)
        if e0 - 1 >= 0:
            xext[EXT] = x[b, e0 - 1]; hv[0] = 1
        if e0 + EXT < SEQ:
            xext[EXT + 1] = x[b, e0 + EXT]; hv[1] = 1
        tok = e0 + np.arange(EXT)
        valid = ((tok >= 0) & (tok < SEQ)).astype(np.float32)
        fmask = np.zeros((NSLOT, 128, 2), np.float32)
        fmask[0:2, :, 0] = 1
        fmask[66:68, :, 1] = 1
        lt = np.arange(SEQ).reshape(NCH, 128)
        fmask[2:66, :, 0] = (lt < e0)
        fmask[2:66, :, 1] = (lt >= e0 + EXT)
        cpp_a = np.zeros((128, CPP_N), np.float32)

        def put(key, arr):
            o, n = CPP[key]
            cpp_a[:, o:o + n] = arr
        put("c", _pm(c[b])); put("cctx", _pm(c_ctx)); put("bmod", _pm(bm))
        put("n1g", _pm(f(norm1_g)[0])); put("n2g", _pm(f(norm2_g)[0]))
        put("cw_ssd", np.concatenate([_pm(f(conv_ssd_w)[0, t]) for t in range(3)], axis=1))
        put("cb_ssd", _pm(f(conv_ssd_b)[0]))
        cfw = f(conv_ffn_w)[0].reshape(9, 2 * DFF)
        put("cw_ffn", np.concatenate([_pm(cfw[t]) for t in range(9)], axis=1))
        put("cb_ffn", _pm(f(conv_ffn_b)[0]))
        put("emask", valid.reshape(NEXT, 128).T)
        put("fmask", fmask.transpose(1, 0, 2).reshape(128, NSLOT * 2))
        cbc_a = np.zeros((128, CBC_N), np.float32)

        def putb(key, arr):
            o, n = CBC[key]
            cbc_a[:, o:o + n] = arr
        putb("dt_bias", _rb(f(dt_bias)[0].reshape(-1))); putb("a_log", _rb(f(a_log)[0].reshape(-1)))
        putb("d_skip", _rb(f(d_skip)[0]))
        cbg_a = np.concatenate([_rb(f(ssd_norm_g)[0]), _rb(f(final_g)), _rb(bm[2048:3072]), _rb(bm[5120:6144])], axis=1)
        putb("emask_bc", _rb(np.concatenate([valid[:128], valid[-128:]])))
        hvb = np.zeros(128, np.float32); hvb[0:2] = hv
        putb("halo_v", _rb(hvb))
        in_maps.append(dict(
            xb=np.ascontiguousarray(x[b]), ctxb=np.ascontiguousarray(ctx[b]), xext=xext,
            w_mod=f(w_mod)[0], w_in=f(w_in)[0], w_fa=f(w_fa)[0], w_sb=f(w_sb)[0], w_o=f(w_o)[0],
            w_up=f(w_up)[0], w_down=f(w_down)[0], cpp=cpp_a, cbc=cbc_a, cbg=cbg_a, cmk=cmk, t1=t1, t2=_t2_tables(q), tcs=tcs))
    if "nc" not in _CACHE:
        _CACHE["nc"] = build_program()
    res = run_bass_kernel_spmd(_CACHE["nc"], in_maps, core_ids=list(range(8)))
    if DEBUG:
        _CACHE["res"] = res
    out = np.zeros((2, SEQ, D), np.float32)
    for core in range(8):
        b, q = divmod(core, 4)
        out[b, 2048 * q:2048 * (q + 1)] = res.results[core]["out"]
    return out
```

```python
import os
from contextlib import ExitStack
import numpy as np
import concourse.bass as bass
import concourse.mybir as mybir
from concourse.bass_utils import run_bass_kernel_spmd

F32 = mybir.dt.float32
BF16 = mybir.dt.bfloat16
AF = mybir.ActivationFunctionType
OP = mybir.AluOpType

D = 1024
SEQ = 8192
NCH = 64
NEXT = 17
EXT = NEXT * 128
EPS = 1e-6
BIG = 30000.0
NSLOT = 68
DFF = 2816
NFB = 22

STOP = os.environ.get("MK_STOP", "")
DEBUG = bool(STOP)


class Buf:
    def __init__(self, t, nparts=1):
        self.t = t
        self.n = nparts
        self.w = [None] * nparts
        self.r = [[] for _ in range(nparts)]
        self.excl = False

    def __getitem__(self, idx):
        return self.t[idx]


def _parts(items):
    out = []
    for it in items:
        if it is None:
            continue
        if isinstance(it, Buf):
            out.extend((it, i) for i in range(it.n))
        else:
            b, idx = it
            if isinstance(idx, int):
                out.append((b, idx))
            else:
                out.extend((b, i) for i in idx)
    return out


class Prog:
    def __init__(self, nc, es):
        self.nc = nc
        self.E = {}
        self.semid = 0
        for name, eng in (("pe", nc.tensor), ("act", nc.scalar), ("dve", nc.vector), ("pool", nc.gpsimd), ("sp", nc.sync)):
            sem = es.enter_context(nc.semaphore("s_" + name))
            self.E[name] = dict(name=name, eng=eng, sem=(self._sid(), sem), count=0, waited={}, pool=[], ndma=0)
        for name, n in (("sp", 8), ("pool", 6), ("act", 4)):
            for i in range(n):
                sem = es.enter_context(nc.semaphore("d_%s%d" % (name, i)))
                self.E[name]["pool"].append((self._sid(), sem))
        self.ninst = 0

    def _sid(self):
        self.semid += 1
        return self.semid

    def _wait(self, E, tok):
        (sid, sem), val, _ = tok
        if E["waited"].get(sid, 0) >= val:
            return
        E["eng"].wait_ge(sem, val)
        E["waited"][sid] = val

    def _collect(self, en, reads, writes):
        toks = []
        for b, i in _parts(reads):
            if b.w[i] is not None:
                toks.append(b.w[i])
            if b.excl:
                toks.extend(t for t in b.r[i] if t[2] != en)
        for b, i in _parts(writes):
            if b.w[i] is not None:
                toks.append(b.w[i])
            toks.extend(b.r[i])
        res = []
        for t in toks:
            if en == "pe" and t[2] == "pe":
                continue
            res.append(t)
        return res

    def _update(self, reads, writes, tok):
        for b, i in _parts(reads):
            b.r[i].append(tok)
            if len(b.r[i]) > 24:
                last = {}
                for t in b.r[i]:
                    k = t[0][0]
                    if k not in last or last[k][1] < t[1]:
                        last[k] = t
                b.r[i] = list(last.values())
        for b, i in _parts(writes):
            b.w[i] = tok
            b.r[i] = []

    def op(self, en, fn, reads=(), writes=()):
        E = self.E[en]
        for t in self._collect(en, reads, writes):
            self._wait(E, t)
        ins = fn(E["eng"])
        E["count"] += 1
        ins.then_inc(E["sem"][1], 1)
        tok = (E["sem"], E["count"], en)
        self._update(reads, writes, tok)
        self.ninst += 1
        return tok

    def dma(self, qn, out, in_, reads=(), writes=(), **kw):
        Q = self.E[qn]
        i = Q["ndma"]
        P = len(Q["pool"])
        sem = Q["pool"][i % P]
        val = 16 * (i // P + 1)
        if i >= P:
            self._wait(Q, (sem, val - 16, "dma"))
        for t in self._collect("dma", reads, writes):
            self._wait(Q, t)
        Q["eng"].dma_start(out=out, in_=in_, **kw).then_inc(sem[1], 16)
        Q["ndma"] += 1
        tok = (sem, val, "dma")
        self._update(reads, writes, tok)
        return tok

    def all_tokens(self):
        toks = []
        for E in self.E.values():
            if E["count"]:
                toks.append((E["sem"], E["count"], E["name"]))
            P = len(E["pool"])
            for j in range(min(P, E["ndma"])):
                n = (E["ndma"] - 1 - j) // P + 1
                toks.append((E["pool"][j], 16 * n, "dma"))
        return toks

    def barrier(self):
        toks = self.all_tokens()
        for E in self.E.values():
            for t in toks:
                self._wait(E, t)


def bc(ap, shape):
    return ap.broadcast_to(shape)


class Ctx:
    pass


def build_program():
    nc = bass.Bass("TRN2", target_bir_lowering=False)
    g = Ctx()
    g.nc = nc

    def din(name, shape, dt=F32):
        return nc.dram_tensor(name, list(shape), dt, kind="ExternalInput").ap()

    def dscr(name, shape, dt):
        kind = "ExternalOutput" if DEBUG else "Internal"
        return nc.dram_tensor(name, list(shape), dt, kind=kind).ap()

    g.xb = din("xb", [SEQ, D])
    g.ctxb = din("ctxb", [256, D])
    g.xext = din("xext", [EXT + 128, D])
    g.w_mod = din("w_mod", [D, 6 * D])
    g.w_in = din("w_in", [D, 8256])
    g.w_fa = din("w_fa", [D, D])
    g.w_sb = din("w_sb", [2048, D])
    g.w_o = din("w_o", [D, D])
    g.w_up = din("w_up", [D, 2 * DFF])
    g.w_down = din("w_down", [DFF, D])
    g.cpp = din("cpp", [128, CPP_N])
    g.cbc = din("cbc", [128, CBC_N])
    g.cbg = din("cbg", [128, CBG_N])
    g.cbrow = din("cbrow", [1, 3072])
    g.cmk = din("cmk", [128, 8 * 128])
    g.t1 = din("t1", [64, 128])
    g.t2 = din("t2", [128, 2 * 64 * 68])
    g.tcs = din("tcs", [128, 256])
    g.out = nc.dram_tensor("out", [2048, D], F32, kind="ExternalOutput").ap()
    g.U = dscr("U", [NCH, 128, D], BF16)
    g.Y = dscr("Y", [128, 128, D], BF16)
    g.XT = dscr("XT", [128, 8 * 2 * EXT], BF16)
    g.MS = dscr("MS", [NEXT, 128, D], BF16)
    g.YF = dscr("YF", [NEXT, 128, 2048], BF16)
    g.YT = dscr("YT", [NEXT, 128, 2048], BF16)
    g.L1 = dscr("L1", [NEXT, 128, D], F32)
    g.SFB = dscr("SFB", [2, 128, 2048], F32)

    with ExitStack() as es:
        P = Prog(nc, es)
        g.P = P
        g.uid = 0
        phase_setup(g, es)
        with ExitStack() as es2:
            alloc_conv_consts(g, es2)
            if STOP != "setup":
                phase_far(g)
            if STOP not in ("setup", "far"):
                phase_fnet(g)
            if STOP not in ("setup", "far", "fnet"):
                phase_own(g, 0)
                phase_own(g, 1)
            P.barrier()
        if STOP not in ("setup", "far", "fnet", "own"):
            phase_merge(g)
        if STOP not in ("setup", "far", "fnet", "own", "merge"):
            phase_ffn(g)
        P.barrier()
    return nc


def sb(g, es, name, shape, dt, nparts=1):
    g.uid += 1
    t = es.enter_context(g.nc.sbuf_tensor("%s_%d" % (name, g.uid), list(shape), dt))
    return Buf(t, nparts)


def ps(g, es, name, shape, dt=F32, nparts=1):
    g.uid += 1
    t = es.enter_context(g.nc.psum_tensor("%s_%d" % (name, g.uid), list(shape), dt))
    b = Buf(t, nparts)
    b.excl = True
    return b


def _layout(items):
    off = {}
    o = 0
    for k, n in items:
        off[k] = (o, n)
        o += n
    return off, o


CPP, CPP_N = _layout([("c", 8), ("cctx", 8), ("bmod", 48), ("n1g", 8), ("n2g", 8), ("cw_ssd", 72), ("cb_ssd", 24),
                      ("cw_ffn", 9 * 44), ("cb_ffn", 44), ("emask", NEXT), ("fmask", NSLOT * 2)])
CBC, CBC_N = _layout([("dt_bias", 64), ("a_log", 64), ("d_skip", 32), ("emask_bc", 256), ("halo_v", 128)])
CBG, CBG_N = _layout([("ssd_g", 2048), ("final_g", 1024), ("bmod_g1", 1024), ("bmod_g2", 1024)])
MK = {k: i for i, k in enumerate(["ones", "le", "ge", "gt", "lt", "ident", "pen_f", "pen_b"])}


def cpp(g, key, j=None, n=1):
    o, _ = CPP[key]
    if j is None:
        return g.cpp_t[:, o:o + CPP[key][1]]
    return g.cpp_t[:, o + j:o + j + n]


def cbcv(g, key, a=0, n=None):
    o, m = CBC[key]
    if n is None:
        n = m
    return g.cbc_t[:, o + a:o + a + n]


def mk32(g, key):
    i = MK[key]
    return g.cmk_t[:, i * 128:(i + 1) * 128]


def mk16(g, key):
    i = MK[key]
    return g.cmkb_t[:, i * 128:(i + 1) * 128]


def phase_setup(g, es):
    nc, P = g.nc, g.P
    g.cpp_t = sb(g, es, "cpp", [128, CPP_N], F32)
    g.cbc_t = sb(g, es, "cbc", [128, CBC_N], F32)
    g.cmk_t = sb(g, es, "cmk", [128, 8 * 128], F32)
    g.cmkb_t = sb(g, es, "cmkb", [128, 8 * 128], BF16)
    g.modv = sb(g, es, "modv", [128, 8 * 8], F32)
    g.gbc = sb(g, es, "gbc", [128, 2 * D], F32)
    g.nega = sb(g, es, "nega", [128, 64], F32)
    g.Sf = sb(g, es, "Sf", [128, 2048], F32)
    g.Sb = sb(g, es, "Sb", [128, 2048], F32)
    P.dma("sp", g.cpp_t[:], g.cpp, writes=[g.cpp_t])
    P.dma("sp", g.cbc_t[:], g.cbc, writes=[g.cbc_t])
    P.dma("sp", g.cmk_t[:], g.cmk, writes=[g.cmk_t])
    P.dma("pool", g.cmkb_t[:], g.cmk, writes=[g.cmkb_t])
    with ExitStack() as ls:
        sc = sb(g, ls, "sc", [128, 8, 2], F32)
        screp = sb(g, ls, "screp", [128, 8, 128], F32)
        modT = sb(g, ls, "modT", [128, 48, 2], F32)
        wm = [sb(g, ls, "wm%d" % i, [128, 8, 1024], F32) for i in range(2)]
        pm = ps(g, ls, "pm", [128, 8, 2], F32)
        pg = ps(g, ls, "pg", [128, 512], F32)
        bg = sb(g, ls, "bg", [128, 2 * D], F32)
        P.dma("sp", bg[:], g.cbg[:, CBG["bmod_g1"][0]:CBG["bmod_g1"][0] + 2 * D], writes=[bg])
        P.op("act", lambda e: e.activation(out=sc[:, :, 0], in_=cpp(g, "c"), func=AF.Silu), [g.cpp_t], [sc])
        P.op("act", lambda e: e.activation(out=sc[:, :, 1], in_=cpp(g, "cctx"), func=AF.Silu), [g.cpp_t], [sc])
        P.op("dve", lambda e: e.tensor_copy(out=screp[:], in_=bc(sc[:, :, 0:1], [128, 8, 128])), [sc], [screp])
        wv = g.w_mod.rearrange("(kb p) n -> p kb n", p=128)
        for j in range(6):
            w = wm[j % 2]
            P.dma("sp", w[:], wv[:, :, j * 1024:(j + 1) * 1024], writes=[w])
            def mm(e, w=w):
                ins = None
                for fb in range(8):
                    for kb in range(8):
                        ins = e.matmul(pm[:, fb, :], lhsT=w[:, kb, fb * 128:(fb + 1) * 128], rhs=sc[:, kb, :],
                                       start=(kb == 0), stop=(kb == 7))
                return ins
            P.op("pe", mm, [w, sc], [pm])
            bo = CPP["bmod"][0] + j * 8
            P.op("dve", lambda e, j=j, bo=bo: e.tensor_tensor(
                out=modT[:, j * 8:(j + 1) * 8, :], in0=pm[:], in1=bc(g.cpp_t[:, bo:bo + 8].unsqueeze(2), [128, 8, 2]),
                op=OP.add), [pm, g.cpp_t], [modT])
            if j in (2, 5):
                gi = 0 if j == 2 else 1
                for hf in range(2):
                    def mg(e, w=w, hf=hf):
                        ins = None
                        for kb in range(8):
                            ins = e.matmul(pg[:], lhsT=screp[:, kb, :], rhs=w[:, kb, hf * 512:(hf + 1) * 512],
                                           start=(kb == 0), stop=(kb == 7))
                        return ins
                    P.op("pe", mg, [w, screp], [pg])
                    P.op("dve", lambda e, gi=gi, hf=hf: e.tensor_tensor(
                        out=g.gbc[:, gi * D + hf * 512: gi * D + (hf + 1) * 512], in0=pg[:],
                        in1=bg[:, gi * D + hf * 512: gi * D + (hf + 1) * 512], op=OP.add), [pg, bg], [g.gbc])
        mv = g.modv
        def mkA(dst, scale_j, which, gkey):
            P.op("dve", lambda e: e.scalar_tensor_tensor(
                out=mv[:, dst * 8:(dst + 1) * 8], in0=modT[:, scale_j * 8:(scale_j + 1) * 8, which], scalar=1.0,
                in1=cpp(g, gkey), op0=OP.add, op1=OP.mult), [modT, g.cpp_t], [mv])

        def mkB(dst, shift_j, which):
            P.op("dve", lambda e: e.tensor_copy(out=mv[:, dst * 8:(dst + 1) * 8],
                                                 in_=modT[:, shift_j * 8:(shift_j + 1) * 8, which]), [modT], [mv])
        mkA(0, 1, 0, "n1g"); mkB(1, 0, 0)
        mkA(2, 1, 1, "n1g"); mkB(3, 0, 1)
        mkA(4, 4, 0, "n2g"); mkB(5, 3, 0)
        P.op("act", lambda e: e.activation(out=g.nega[:], in_=cbcv(g, "a_log"), func=AF.Exp), [g.cbc_t], [g.nega])
        P.op("dve", lambda e: e.tensor_scalar(out=g.nega[:], in0=g.nega[:], scalar1=-1.0, scalar2=None, op0=OP.mult),
             [g.nega], [g.nega])
        P.barrier()


def alloc_conv_consts(g, es):
    P = g.P
    g.diag = sb(g, es, "diag", [128, 72, 128], BF16)
    g.brow = sb(g, es, "brow", [1, 3072], BF16)
    for a_ in range(0, 3072, 1024):
        P.dma("pool", g.brow[:, a_:a_ + 1024], g.cbrow[:, a_:a_ + 1024], writes=[g.brow])
    for i_ in range(72):
        P.op("dve", lambda e, i_=i_: e.tensor_scalar(out=g.diag[:, i_, :], in0=mk16(g, "ident"),
                                                      scalar1=cpp(g, "cw_ssd", i_), scalar2=None, op0=OP.mult),
             [g.cmkb_t, g.cpp_t], [g.diag])


def modA(g, i, kb):
    return g.modv[:, i * 8 + kb:i * 8 + kb + 1]


def alloc_chunk_bufs(g, es, nfb):
    c = Ctx()
    c.xt = [sb(g, es, "xt%d" % i, [128, D], F32) for i in range(2)]
    c.junk = sb(g, es, "junk", [128, D], BF16)
    c.st = [sb(g, es, "st%d" % i, [128, 4], F32) for i in range(2)]
    c.xn = [sb(g, es, "xn%d" % i, [128, D], BF16) for i in range(2)]
    c.hTe = [sb(g, es, "hTe%d" % i, [128, 8, 130], BF16) for i in range(3)]
    c.pA = ps(g, es, "pA", [128, 8, 128], BF16)
    c.nfb = nfb
    return c


def prep(g, c, k, src_rows, ai, bi, vmask=None, hT=None):
    P = g.P
    i2 = k % 2
    xt, st, xn = c.xt[i2], c.st[i2], c.xn[i2]
    if hT is None:
        hT = c.hTe[k % 3]
    P.dma("sp", xt[:], src_rows, writes=[xt])
    P.op("act", lambda e: e.activation(out=c.junk[:], in_=xt[:], func=AF.Square, accum_out=st[:, 0:1]),
         [xt], [c.junk, st])
    P.op("dve", lambda e: e.tensor_scalar(out=st[:, 1:2], in0=st[:, 0:1], scalar1=1.0 / D, scalar2=EPS,
                                           op0=OP.mult, op1=OP.add), [st], [st])
    P.op("act", lambda e: e.activation(out=st[:, 2:3], in_=st[:, 1:2], func=AF.Sqrt), [st], [st])
    P.op("dve", lambda e: e.reciprocal(out=st[:, 3:4], in_=st[:, 2:3]), [st], [st])
    P.op("act", lambda e: e.activation(out=xn[:], in_=xt[:], func=AF.Copy, scale=st[:, 3:4]), [xt, st], [xn])

    def tr(e):
        ins = None
        for kb in range(8):
            ins = e.transpose(out=c.pA[:, kb, :], in_=xn[:, kb * 128:(kb + 1) * 128], identity=mk16(g, "ident"))
        return ins
    P.op("pe", tr, [xn, g.cmkb_t], [c.pA])
    for kb in range(8):
        eng = "dve" if kb % 2 else "act"
        if eng == "act":
            P.op("act", lambda e, kb=kb: e.activation(out=hT[:, kb, 1:129], in_=c.pA[:, kb, :], func=AF.Identity,
                                                      scale=modA(g, ai, kb), bias=modA(g, bi, kb)),
                 [c.pA, g.modv], [hT])
        else:
            P.op("dve", lambda e, kb=kb: e.tensor_scalar(out=hT[:, kb, 1:129], in0=c.pA[:, kb, :],
                                                          scalar1=modA(g, ai, kb), scalar2=modA(g, bi, kb),
                                                          op0=OP.mult, op1=OP.add), [c.pA, g.modv], [hT])
    if vmask is not None:
        P.op("pool", lambda e: e.tensor_tensor(out=hT[:, :, 1:129], in0=hT[:, :, 1:129],
                                               in1=bc(vmask.unsqueeze(1), [128, 8, 128]), op=OP.mult),
             [hT, g.cbc_t], [hT])
    return xt


def halo_link(g, c, k, has_left):
    P = g.P
    cur = c.hTe[k % 3]
    if has_left:
        prv = c.hTe[(k - 1) % 3]
        P.op("pool", lambda e: e.tensor_copy(out=cur[:, :, 0:1], in_=prv[:, :, 128:129]), [prv], [cur])
        P.op("pool", lambda e: e.tensor_copy(out=prv[:, :, 129:130], in_=cur[:, :, 1:2]), [cur], [prv])
    else:
        P.op("pool", lambda e: e.memset(cur[:, :, 0:1], 0.0), [], [cur])


def halo_zero_right(g, c, k):
    cur = c.hTe[k % 3]
    g.P.op("pool", lambda e: e.memset(cur[:, :, 129:130], 0.0), [], [cur])


def fm_proj_conv(g, c, s, hT, W, nfb, cw_off, steps=None):
    P = g.P
    steps = steps if steps is not None else []
    groups = [list(range(a, min(a + 3, nfb))) for a in range(0, nfb, 3)]
    for gi, fbs in enumerate(groups):
        n = len(fbs)
        pa = s.pxa[gi % len(s.pxa)]
        pb = s.pxc[gi % len(s.pxc)]
        pre = s.pre[gi % 2]

        def mm(e, fbs=fbs, pa=pa):
            ins = None
            for j, fb in enumerate(fbs):
                for kb in range(8):
                    ins = e.matmul(pa[:, j * 130:(j + 1) * 130], lhsT=W[:, kb, fb * 128:(fb + 1) * 128], rhs=hT[:, kb, :],
                                   start=(kb == 0), stop=(kb == 7))
            return ins
        P.op("pe", mm, [W, hT], [pa])
        P.op("dve", lambda e, n=n, pa=pa, pre=pre: e.tensor_copy(
            out=pre[:, 0:n, :], in_=pa[:, 0:n * 130].rearrange("p (j t) -> p j t", t=130)), [pa], [pre])

        def mc(e, fbs=fbs, pb=pb, pre=pre):
            ins = None
            for j, fb in enumerate(fbs):
                cf = cw_off + fb
                for k in range(3):
                    e.matmul(pb[:, j * 128:(j + 1) * 128], lhsT=g.diag[:, k * 24 + cf, :], rhs=pre[:, j, k:k + 128],
                             start=(k == 0), stop=False)
                ins = e.matmul(pb[:, j * 128:(j + 1) * 128], lhsT=g.brow[0:1, cf * 128:(cf + 1) * 128],
                               rhs=mk16(g, "ones")[0:1, :], start=False, stop=True)
            return ins
        P.op("pe", mc, [pre, g.diag, g.brow, g.cmkb_t], [pb])
        f0, f1 = fbs[0], fbs[-1] + 1
        P.op("act", lambda e, f0=f0, f1=f1, n=n, pb=pb: e.activation(
            out=s.xcs[:, f0:f1, :], in_=pb[:, 0:n * 128].rearrange("p (j t) -> p j t", t=128), func=AF.Silu),
            [pb], [(s.xcs, gi)])
        if steps:
            steps.pop(0)()
    while steps:
        steps.pop(0)()


def to_token_major(g, c, s, nblk):
    P = g.P
    for r0 in range(0, nblk, 8):
        n = min(8, nblk - r0)

        def tr(e, r0=r0, n=n):
            ins = None
            for j in range(n):
                ins = e.transpose(out=c.pA[:, j, :], in_=s.xcs[:, r0 + j, :], identity=mk16(g, "ident"))
            return ins
        P.op("pe", tr, [s.xcs, g.cmkb_t], [c.pA])
        P.op("act", lambda e, r0=r0, n=n: e.activation(
            out=s.xtok[:, r0 * 128:(r0 + n) * 128], in_=c.pA[:, 0:n, :], func=AF.Copy), [c.pA], [s.xtok])


def dt_steps(g, s, hT, Wdt, ncol, bias_ap, nega_ap, mask_ap):
    P = g.P

    def s1():
        def mm(e):
            ins = None
            for kb in range(8):
                ins = e.matmul(s.pD[:, 0:ncol], lhsT=hT[:, kb, 1:129], rhs=Wdt[:, kb, 0:ncol], start=(kb == 0), stop=(kb == 7))
            return ins
        P.op("pe", mm, [hT, Wdt], [s.pD])
        P.op("dve", lambda e: e.tensor_tensor(out=s.dtm[:, 0:ncol], in0=s.pD[:, 0:ncol], in1=bias_ap, op=OP.add),
             [s.pD, g.cbc_t], [s.dtm])

    def s2():
        P.op("act", lambda e: e.activation(out=s.dtm[:, 0:ncol], in_=s.dtm[:, 0:ncol], func=AF.Exp), [s.dtm], [s.dtm])

    def s3():
        P.op("act", lambda e: e.activation(out=s.dtm[:, 0:ncol], in_=s.dtm[:, 0:ncol], func=AF.Ln, bias=1.0), [s.dtm], [s.dtm])

    def s4():
        dv = s.dtm[:, 0:ncol].rearrange("p (a b) -> p a b", b=32)
        P.op("dve", lambda e: e.tensor_tensor(out=dv, in0=dv, in1=mask_ap, op=OP.mult), [s.dtm, g.cpp_t], [s.dtm])

    def s5():
        P.op("dve", lambda e: e.tensor_tensor(out=s.la[:, 0:ncol], in0=s.dtm[:, 0:ncol], in1=nega_ap, op=OP.mult),
             [s.dtm, g.nega], [s.la])
    return [s1, s2, s3, s4, s5]


def state_contrib(g, s, wexp_ap, xdd, on_group):
    P = g.P
    P.op("dve", lambda e: e.tensor_tensor(
        out=xdd[:].rearrange("p (h d) -> p h d", d=64), in0=s.xtok[:, 0:2048].rearrange("p (h d) -> p h d", d=64),
        in1=bc(wexp_ap.unsqueeze(2), [128, 32, 64]), op=OP.mult), [s.xtok, s.wx], [xdd])
    for gi in range(4):
        P.op("pe", lambda e, gi=gi: e.matmul(s.pH[:], lhsT=s.xtok[:, 2048 + gi * 128:2048 + (gi + 1) * 128],
                                             rhs=xdd[:, gi * 512:(gi + 1) * 512], start=True, stop=True),
             [s.xtok, xdd], [s.pH])
        on_group(gi, s.pH)


def load_w_cols(g, W, col0, ncols, dst0=0):
    wv = g.w_in.rearrange("(kb p) n -> p kb n", p=128)
    for a in range(0, ncols, 512):
        n = min(512, ncols - a)
        g.P.dma("pool", W[:, :, dst0 + a:dst0 + a + n], wv[:, :, col0 + a:col0 + a + n], writes=[W])


def phase_far(g):
    nc, P = g.nc, g.P
    with ExitStack() as es:
        c = alloc_chunk_bufs(g, es, 20)
        s = Ctx()
        Wf = sb(g, es, "Wf", [128, 8, 1024], BF16)
        Wxb = sb(g, es, "Wxb", [128, 8, 2560], BF16)
        Wdt = sb(g, es, "Wdt", [128, 8, 64], BF16)
        load_w_cols(g, Wf, 0, 1024)
        load_w_cols(g, Wxb, 1024, 2560)
        load_w_cols(g, Wdt, 6144, 64)
        s.pxa = [ps(g, es, "pxa%d" % i, [128, 512], F32) for i in range(2)]
        s.pxc = [ps(g, es, "pxc%d" % i, [128, 512], F32) for i in range(1)]
        s.pD = ps(g, es, "pD", [128, 512], F32)
        s.pH = ps(g, es, "pH", [128, 512], F32)
        pf = [ps(g, es, "pf%d" % i, [128, 512], F32) for i in range(2)]
        s.pre = [sb(g, es, "pre%d" % i, [128, 3, 130], BF16) for i in range(2)]
        s.xcs = sb(g, es, "xcs", [128, 20, 128], BF16, nparts=8)
        s.pD2 = sb(g, es, "pD2", [128, 192], F32)
        s.xtok = sb(g, es, "xtok", [128, 2560], BF16)
        s.dtm = sb(g, es, "dtm", [128, 64], F32)
        s.la = sb(g, es, "la", [128, 64], F32)
        s.wx = sb(g, es, "wx", [128, 64], F32)
        s.sg = sb(g, es, "sg", [128, 64], F32)
        Rb = sb(g, es, "Rb", [128, 32], F32)
        dec = sb(g, es, "dec", [128, 32], F32)
        xdd = [sb(g, es, "xdd%d" % i, [128, 2048], BF16) for i in range(2)]
        ub = [sb(g, es, "ub%d" % i, [128, D], BF16) for i in range(2)]
        P.op("dve", lambda e: e.memset(g.Sf[:], 0.0), [], [g.Sf])
        P.op("dve", lambda e: e.memset(g.Sb[:], 0.0), [], [g.Sb])
        P.op("dve", lambda e: e.memset(Rb[:], 0.0), [], [Rb])
        slots = [("c", 0), ("c", 1)] + [("l", i) for i in range(NCH)] + [("c", 0), ("c", 1)]
        first = {0, 2, 66}
        last = {1, 65, 67}

        def do_prep(k):
            kind, i = slots[k]
            if kind == "c":
                prep(g, c, k, g.ctxb[i * 128:(i + 1) * 128, :], 2, 3)
            else:
                prep(g, c, k, g.xb[i * 128:(i + 1) * 128, :], 0, 1)
            halo_link(g, c, k, k not in first)
            if k in last:
                halo_zero_right(g, c, k)
        do_prep(0)
        for k in range(NSLOT):
            if k + 1 < NSLOT:
                do_prep(k + 1)
            kind, i = slots[k]
            hT = c.hTe[k % 3]
            if kind == "l":
                u = ub[i % 2]
                for hf in range(2):
                    def mm(e, hf=hf):
                        ins = None
                        for kb in range(8):
                            ins = e.matmul(pf[hf][:], lhsT=hT[:, kb, 1:129], rhs=Wf[:, kb, hf * 512:(hf + 1) * 512],
                                           start=(kb == 0), stop=(kb == 7))
                        return ins
                    P.op("pe", mm, [hT, Wf], [pf[hf]])
                    P.op("act", lambda e, hf=hf, u=u: e.activation(out=u[:, hf * 512:(hf + 1) * 512], in_=pf[hf][:],
                                                                   func=AF.Copy), [pf[hf]], [u])
                P.dma("sp", g.U[i], u[:], reads=[u])
            fo = CPP["fmask"][0] + 2 * k
            mask_ap = bc(g.cpp_t[:, fo:fo + 2].unsqueeze(2), [128, 2, 32])
            steps = dt_steps(g, s, hT, Wdt, 64, cbcv(g, "dt_bias"), g.nega[:], mask_ap)

            def t1():
                def segs(e):
                    e.matmul(s.pD[:, 64:96], lhsT=mk32(g, "gt"), rhs=s.la[:, 0:32], start=True, stop=True)
                    e.matmul(s.pD[:, 96:128], lhsT=mk32(g, "lt"), rhs=s.la[:, 32:64], start=True, stop=True)
                    return e.matmul(s.pD[:, 128:192], lhsT=mk32(g, "ones"), rhs=s.la[:, 0:64], start=True, stop=True)
                P.op("pe", segs, [s.la, g.cmk_t], [s.pD])

            def t2b():
                P.op("pool", lambda e: e.tensor_copy(out=s.sg[:, 0:32], in_=s.pD2[:, 64:96]), [s.pD2], [s.sg])
                P.op("pool", lambda e: e.tensor_tensor(out=s.sg[:, 32:64], in0=s.pD2[:, 96:128], in1=Rb[:], op=OP.add),
                     [s.pD2, Rb], [s.sg])
                P.op("pool", lambda e: e.tensor_tensor(out=Rb[:], in0=Rb[:], in1=s.pD2[:, 160:192], op=OP.add),
                     [s.pD2, Rb], [Rb])

            def t3():
                P.op("act", lambda e: e.activation(out=s.wx[:], in_=s.sg[:], func=AF.Exp), [s.sg], [s.wx])
                P.op("act", lambda e: e.activation(out=dec[:], in_=s.pD2[:, 128:160], func=AF.Exp), [s.pD2], [dec])

            def t4():
                P.op("pool", lambda e: e.tensor_tensor(out=s.wx[:], in0=s.wx[:], in1=s.dtm[:], op=OP.mult),
                     [s.wx, s.dtm], [s.wx])
                P.op("dve", lambda e: e.tensor_tensor(
                    out=g.Sf[:].rearrange("p (h d) -> p h d", d=64), in0=g.Sf[:].rearrange("p (h d) -> p h d", d=64),
                    in1=bc(dec[:].unsqueeze(2), [128, 32, 64]), op=OP.mult), [g.Sf, dec], [g.Sf])

            def t2a():
                P.op("dve", lambda e: e.tensor_copy(out=s.pD2[:], in_=s.pD[:, 0:192]), [s.pD], [s.pD2])
            steps += [t1, t2a, t2b, t3, t4]
            fm_proj_conv(g, c, s, hT, Wxb, 20, 0, steps)
            to_token_major(g, c, s, 20)
            x3 = s.xtok[:, 0:2048].rearrange("p (h d) -> p h d", d=64)
            P.op("dve", lambda e: e.tensor_tensor(out=xdd[0][:].rearrange("p (h d) -> p h d", d=64), in0=x3,
                                                  in1=bc(s.wx[:, 0:32].unsqueeze(2), [128, 32, 64]), op=OP.mult),
                 [s.xtok, s.wx], [xdd[0]])
            P.op("dve", lambda e: e.tensor_tensor(out=xdd[1][:].rearrange("p (h d) -> p h d", d=64), in0=x3,
                                                  in1=bc(s.wx[:, 32:64].unsqueeze(2), [128, 32, 64]), op=OP.mult),
                 [s.xtok, s.wx], [xdd[1]])
            banks = [s.pxa[0], s.pxa[1], s.pxc[0], s.pH]
            for di, (xd_, S_) in enumerate(((xdd[0], g.Sf), (xdd[1], g.Sb))):
                for gi in range(4):
                    pst = banks[gi]
                    P.op("pe", lambda e, gi=gi, pst=pst, xd_=xd_: e.matmul(
                        pst[:], lhsT=s.xtok[:, 2048 + gi * 128:2048 + (gi + 1) * 128],
                        rhs=xd_[:, gi * 512:(gi + 1) * 512], start=True, stop=True), [s.xtok, xd_], [pst])
                    P.op("dve", lambda e, gi=gi, pst=pst, S_=S_: e.tensor_tensor(
                        out=S_[:, gi * 512:(gi + 1) * 512], in0=S_[:, gi * 512:(gi + 1) * 512], in1=pst[:], op=OP.add),
                        [S_, pst], [S_])
        if DEBUG:
            P.dma("sp", g.SFB[0], g.Sf[:], reads=[g.Sf])
            P.dma("sp", g.SFB[1], g.Sb[:], reads=[g.Sb])
        P.barrier()


def load_w_gen(g, W, src, nkb, ncols):
    wv = src.rearrange("(kb p) n -> p kb n", p=128)
    for a in range(0, ncols, 512):
        n = min(512, ncols - a)
        g.P.dma("pool", W[:, :, a:a + n], wv[:, :, a:a + n], writes=[W])


def phase_fnet(g):
    nc, P = g.nc, g.P
    with ExitStack() as es:
        T1 = sb(g, es, "T1", [64, 128], BF16)
        P.dma("pool", T1[:], g.t1, writes=[T1])
        V = [sb(g, es, "V%d" % i, [64, 4, D], BF16) for i in range(2)]
        Yt = [sb(g, es, "Yt%d" % i, [128, 4, D], BF16) for i in range(2)]
        p1 = [ps(g, es, "p1_%d" % i, [128, 512], F32) for i in range(4)]
        cnt = 0
        for tg in range(32):
            v, yt = V[tg % 2], Yt[tg % 2]
            P.dma("sp", v[:], g.U[:, tg * 4:(tg + 1) * 4, :], writes=[v])
            for t in range(4):
                for hf in range(2):
                    pp = p1[cnt % 4]
                    P.op("pe", lambda e, pp=pp, t=t, hf=hf, v=v: e.matmul(
                        pp[:], lhsT=T1[:], rhs=v[:, t, hf * 512:(hf + 1) * 512], start=True, stop=True), [T1, v], [pp])
                    if cnt % 2:
                        P.op("act", lambda e, pp=pp, t=t, hf=hf, yt=yt: e.activation(
                            out=yt[:, t, hf * 512:(hf + 1) * 512], in_=pp[:], func=AF.Copy), [pp], [yt])
                    else:
                        P.op("dve", lambda e, pp=pp, t=t, hf=hf, yt=yt: e.tensor_copy(
                            out=yt[:, t, hf * 512:(hf + 1) * 512], in_=pp[:]), [pp], [yt])
                    cnt += 1
            P.dma("sp", g.Y[:, tg * 4:(tg + 1) * 4, :], yt[:], reads=[yt])
        P.barrier()
    with ExitStack() as es:
        T2 = sb(g, es, "T2", [128, 2 * 64 * 68], BF16)
        for a in range(0, 2 * 64 * 68, 1088):
            P.dma("pool", T2[:, a:a + 1088], g.t2[:, a:a + 1088], writes=[T2])
        Yk = [sb(g, es, "Yk%d" % i, [128, 2, D], BF16) for i in range(2)]
        XTs = sb(g, es, "XTs", [128, 8, 2, EXT], BF16)
        p2f = [ps(g, es, "p2_%d" % i, [128, 512], F32) for i in range(4)]
        yv = g.Y.rearrange("(ri k) t c -> k t ri c", ri=2)
        xv = XTs[:].rearrange("p c r (j k) -> p c r j k", k=64)
        for k1 in range(64):
            yk = Yk[k1 % 2]
            P.dma("sp", yk[:], yv[k1], writes=[yk])
            for cg in range(2):
                ppb = p2f[(k1 * 2 + cg) % 4]
                pp = ppb[:, 0:272].rearrange("p (c k) -> p c k", k=68)

                def mm(e, pp=pp, cg=cg, yk=yk, k1=k1):
                    ins = None
                    for cb in range(4):
                        cbx = cg * 4 + cb
                        e.matmul(pp[:, cb, :], lhsT=yk[:, 0, cbx * 128:(cbx + 1) * 128],
                                 rhs=T2[:, k1 * 68:(k1 + 1) * 68], start=True, stop=False)
                        ins = e.matmul(pp[:, cb, :], lhsT=yk[:, 1, cbx * 128:(cbx + 1) * 128],
                                       rhs=T2[:, (64 + k1) * 68:(64 + k1 + 1) * 68], start=False, stop=True)
                    return ins
                P.op("pe", mm, [yk, T2], [ppb])
                for ri in range(2):
                    if (k1 + cg) % 2:
                        P.op("act", lambda e, pp=pp, cg=cg, ri=ri, k1=k1: e.activation(
                            out=xv[:, cg * 4:(cg + 1) * 4, ri, :, k1], in_=pp[:, :, ri * 34:(ri + 1) * 34],
                            func=AF.Copy), [ppb], [XTs])
                    else:
                        P.op("dve", lambda e, pp=pp, cg=cg, ri=ri, k1=k1: e.tensor_copy(
                            out=xv[:, cg * 4:(cg + 1) * 4, ri, :, k1], in_=pp[:, :, ri * 34:(ri + 1) * 34]),
                            [ppb], [XTs])
        for cb in range(8):
            P.dma("sp", g.XT[:, cb * 2 * EXT:(cb + 1) * 2 * EXT].rearrange("p (r t) -> p r t", r=2), XTs[:, cb, :, :],
                  reads=[XTs])
        P.barrier()


def phase_own(g, d):
    nc, P = g.nc, g.P
    with ExitStack() as es:
        c = alloc_chunk_bufs(g, es, 24)
        s = Ctx()
        hH = sb(g, es, "hH", [128, 8, 130], BF16)
        W = sb(g, es, "Wxbc", [128, 8, 3072], BF16)
        Wdt = sb(g, es, "Wdt", [128, 8, 32], BF16)
        load_w_cols(g, W, 1024, 3072)
        load_w_cols(g, Wdt, 6144 + 32 * d, 32)
        s.pxa = [ps(g, es, "pxa%d" % i, [128, 512], F32) for i in range(1)]
        s.pxc = [ps(g, es, "pxc%d" % i, [128, 512], F32) for i in range(1)]
        s.pxb = [s.pxa[0], s.pxc[0]]
        s.pre = [sb(g, es, "pre%d" % i, [128, 3, 130], BF16) for i in range(2)]
        s.pD = ps(g, es, "pD", [128, 512], F32)
        s.pH = ps(g, es, "pH", [128, 512], F32)
        psc = ps(g, es, "psc", [128, 4, 128], F32)
        pL = [ps(g, es, "pL%d" % i, [128, 4, 128], F32) for i in range(2)]
        s.xcs = sb(g, es, "xcs", [128, 24, 128], BF16, nparts=8)
        s.xtok = sb(g, es, "xtok", [128, 2560], BF16)
        s.dtm = sb(g, es, "dtm", [128, 32], F32)
        s.la = sb(g, es, "la", [128, 32], F32)
        s.wx = sb(g, es, "wx", [128, 32], F32)
        lab = sb(g, es, "lab", [128, 32], BF16)
        nlab = sb(g, es, "nlab", [128, 32], BF16)
        ecum = sb(g, es, "ecum", [128, 32], F32)
        dec = sb(g, es, "dec", [128, 32], F32)
        xd = sb(g, es, "xd", [128, 2048], BF16)
        xdd = sb(g, es, "xdd", [128, 2048], BF16)
        Sbf = sb(g, es, "Sbf", [128, 2048], BF16)
        Dt = [sb(g, es, "Dt%d" % i, [128, 8, 128], BF16) for i in range(2)]
        Lx = [sb(g, es, "Lx%d" % i, [128, 8, 128], BF16) for i in range(2)]
        G = [sb(g, es, "G%d" % i, [128, 8, 128], BF16) for i in range(2)]
        yo = sb(g, es, "yo", [128, 512], F32)
        ytile = sb(g, es, "ytile", [128, 2048], BF16)
        tmp = sb(g, es, "tmp", [128, 2048], BF16)
        yfl = sb(g, es, "yfl", [128, 2048], BF16)
        S = g.Sf if d == 0 else g.Sb
        mxk = "le" if d == 0 else "ge"
        sgk = "gt" if d == 0 else "lt"
        penk = "pen_f" if d == 0 else "pen_b"
        order = list(range(NEXT)) if d == 0 else list(range(NEXT - 1, -1, -1))

        prep(g, c, 0, g.xext[EXT:EXT + 128, :], 0, 1, vmask=cbcv(g, "halo_v"), hT=hH)

        def do_prep(ci, prev_ci):
            hT = c.hTe[ci % 3]
            vm = None
            if ci == 0:
                vm = cbcv(g, "emask_bc", 0, 128)
            if ci == NEXT - 1:
                vm = cbcv(g, "emask_bc", 128, 128)
            prep(g, c, ci, g.xext[ci * 128:(ci + 1) * 128, :], 0, 1, vmask=vm)
            if prev_ci is not None:
                nb = c.hTe[prev_ci % 3]
                if ci == prev_ci + 1:
                    P.op("pool", lambda e: e.tensor_copy(out=hT[:, :, 0:1], in_=nb[:, :, 128:129]), [nb], [hT])
                    P.op("pool", lambda e: e.tensor_copy(out=nb[:, :, 129:130], in_=hT[:, :, 1:2]), [hT], [nb])
                else:
                    P.op("pool", lambda e: e.tensor_copy(out=hT[:, :, 129:130], in_=nb[:, :, 1:2]), [nb], [hT])
                    P.op("pool", lambda e: e.tensor_copy(out=nb[:, :, 0:1], in_=hT[:, :, 128:129]), [hT], [nb])
            if ci == 0:
                P.op("pool", lambda e: e.tensor_copy(out=hT[:, :, 0:1], in_=hH[:, :, 1:2]), [hH], [hT])
            if ci == NEXT - 1:
                P.op("pool", lambda e: e.tensor_copy(out=hT[:, :, 129:130], in_=hH[:, :, 2:3]), [hH], [hT])

        do_prep(order[0], None)
        for oi, ci in enumerate(order):
            if oi + 1 < NEXT:
                do_prep(order[oi + 1], ci)
            hT = c.hTe[ci % 3]
            if d == 1:
                P.dma("sp", yfl[:], g.YF[ci], writes=[yfl])
            mask_ap = bc(cpp(g, "emask", ci).unsqueeze(2), [128, 1, 32])
            steps = dt_steps(g, s, hT, Wdt, 32, cbcv(g, "dt_bias", 32 * d, 32), g.nega[:, 32 * d:32 * (d + 1)], mask_ap)

            def u1():
                P.op("pool", lambda e: e.tensor_copy(out=lab[:], in_=s.la[:]), [s.la], [lab])
                P.op("pool", lambda e: e.tensor_scalar(out=nlab[:], in0=lab[:], scalar1=-1.0, scalar2=None, op0=OP.mult),
                     [lab], [nlab])

                def segs(e):
                    e.matmul(s.pD[:, 64:96], lhsT=mk32(g, mxk), rhs=s.la[:], start=True, stop=True)
                    e.matmul(s.pD[:, 96:128], lhsT=mk32(g, sgk), rhs=s.la[:], start=True, stop=True)
                    return e.matmul(s.pD[:, 128:160], lhsT=mk32(g, "ones"), rhs=s.la[:], start=True, stop=True)
                P.op("pe", segs, [s.la, g.cmk_t], [s.pD])

            def u2():
                P.op("act", lambda e: e.activation(out=ecum[:], in_=s.pD[:, 64:96], func=AF.Exp), [s.pD], [ecum])
                P.op("act", lambda e: e.activation(out=s.wx[:], in_=s.pD[:, 96:128], func=AF.Exp), [s.pD], [s.wx])
                P.op("act", lambda e: e.activation(out=dec[:], in_=s.pD[:, 128:160], func=AF.Exp), [s.pD], [dec])

            def u3():
                P.op("pool", lambda e: e.tensor_tensor(out=s.wx[:], in0=s.wx[:], in1=s.dtm[:], op=OP.mult),
                     [s.wx, s.dtm], [s.wx])
            steps += [u1, u2, u3]
            fm_proj_conv(g, c, s, hT, W, 24, 0, steps)
            to_token_major(g, c, s, 20)
            x3 = s.xtok[:, 0:2048].rearrange("p (h d) -> p h d", d=64)
            P.op("dve", lambda e: e.tensor_tensor(out=xd[:].rearrange("p (h d) -> p h d", d=64), in0=x3,
                                                   in1=bc(s.dtm[:].unsqueeze(2), [128, 32, 64]), op=OP.mult),
                 [s.xtok, s.dtm], [xd])
            P.op("dve", lambda e: e.tensor_tensor(out=xdd[:].rearrange("p (h d) -> p h d", d=64), in0=x3,
                                                   in1=bc(s.wx[:].unsqueeze(2), [128, 32, 64]), op=OP.mult),
                 [s.xtok, s.wx], [xdd])
            P.op("act", lambda e: e.activation(out=Sbf[:], in_=S[:], func=AF.Copy), [S], [Sbf])

            def sc(e):
                ins = None
                for gi in range(4):
                    ins = e.matmul(psc[:, gi, :], lhsT=s.xcs[:, 16 + gi, :], rhs=s.xcs[:, 20 + gi, :], start=True, stop=True)
                return ins
            P.op("pe", sc, [s.xcs], [psc])
            for gi in range(4):
                dt_, lx, gg = Dt[gi % 2], Lx[gi % 2], G[gi % 2]
                P.op("pool", lambda e, gi=gi, dt_=dt_: e.tensor_tensor(
                    out=dt_[:], in0=bc(lab[:, gi * 8:(gi + 1) * 8].unsqueeze(2), [128, 8, 128]),
                    in1=bc(mk16(g, mxk).unsqueeze(1), [128, 8, 128]), op=OP.mult), [lab, g.cmkb_t], [dt_])
                for hh in range(2):
                    def mmL(e, gi=gi, hh=hh, dt_=dt_):
                        e.matmul(pL[hh][:], lhsT=mk16(g, "ones"), rhs=dt_[:, hh * 4:(hh + 1) * 4, :], start=True, stop=False)
                        e.matmul(pL[hh][:], lhsT=mk16(g, mxk),
                                 rhs=bc(nlab[:, gi * 8 + hh * 4:gi * 8 + hh * 4 + 4].unsqueeze(2), [128, 4, 128]),
                                 start=False, stop=False)
                        return e.matmul(pL[hh][:], lhsT=mk16(g, "ident"),
                                        rhs=bc(mk16(g, penk).unsqueeze(1), [128, 4, 128]), start=False, stop=True)
                    P.op("pe", mmL, [dt_, nlab, g.cmkb_t], [pL[hh]])
                    P.op("act", lambda e, hh=hh, lx=lx: e.activation(out=lx[:, hh * 4:(hh + 1) * 4, :], in_=pL[hh][:],
                                                                   func=AF.Exp), [pL[hh]], [lx])
                P.op("dve", lambda e, gi=gi, lx=lx, gg=gg: e.tensor_tensor(
                    out=gg[:], in0=lx[:], in1=bc(psc[:, gi, :].unsqueeze(1), [128, 8, 128]), op=OP.mult),
                    [lx, psc], [gg])

                def mmy(e, gi=gi, gg=gg):
                    ins = None
                    for h in range(8):
                        hh = gi * 8 + h
                        ins = e.matmul(s.pH[:, h * 64:(h + 1) * 64], lhsT=gg[:, h, :], rhs=xd[:, hh * 64:(hh + 1) * 64],
                                       start=True, stop=True)
                    return ins
                P.op("pe", mmy, [gg, xd], [s.pH])
                P.op("pe", lambda e, gi=gi: e.matmul(s.pxb[0][:], lhsT=s.xcs[:, 20 + gi, :],
                                                     rhs=Sbf[:, gi * 512:(gi + 1) * 512], start=True, stop=True),
                     [s.xcs, Sbf], [s.pxb[0]])
                P.op("dve", lambda e, gi=gi: e.tensor_tensor(
                    out=yo[:].rearrange("p (h d) -> p h d", d=64), in0=s.pxb[0][:].rearrange("p (h d) -> p h d", d=64),
                    in1=bc(ecum[:, gi * 8:(gi + 1) * 8].unsqueeze(2), [128, 8, 64]), op=OP.mult),
                    [s.pxb[0], ecum], [yo])
                P.op("dve", lambda e, gi=gi: e.tensor_tensor(out=ytile[:, gi * 512:(gi + 1) * 512], in0=yo[:],
                                                              in1=s.pH[:], op=OP.add), [yo, s.pH], [ytile])
                P.op("pe", lambda e, gi=gi: e.matmul(s.pxb[1][:], lhsT=s.xtok[:, 2048 + gi * 128:2048 + (gi + 1) * 128],
                                                     rhs=xdd[:, gi * 512:(gi + 1) * 512], start=True, stop=True),
                     [s.xtok, xdd], [s.pxb[1]])
                P.op("dve", lambda e, gi=gi: e.tensor_tensor(
                    out=S[:, gi * 512:(gi + 1) * 512].rearrange("p (h d) -> p h d", d=64),
                    in0=S[:, gi * 512:(gi + 1) * 512].rearrange("p (h d) -> p h d", d=64),
                    in1=bc(dec[:, gi * 8:(gi + 1) * 8].unsqueeze(2), [128, 8, 64]), op=OP.mult), [S, dec], [S])
                P.op("dve", lambda e, gi=gi: e.tensor_tensor(out=S[:, gi * 512:(gi + 1) * 512],
                                                              in0=S[:, gi * 512:(gi + 1) * 512], in1=s.pxb[1][:],
                                                              op=OP.add), [S, s.pxb[1]], [S])
            if d == 0:
                P.op("pool", lambda e: e.tensor_tensor(out=tmp[:].rearrange("p (h d) -> p h d", d=64), in0=x3,
                                                       in1=bc(cbcv(g, "d_skip").unsqueeze(2), [128, 32, 64]),
                                                       op=OP.mult), [s.xtok, g.cbc_t], [tmp])
                P.op("pool", lambda e: e.tensor_tensor(out=tmp[:], in0=tmp[:], in1=ytile[:], op=OP.add),
                     [tmp, ytile], [tmp])
                P.dma("sp", g.YF[ci], tmp[:], reads=[tmp])
            else:
                P.op("pool", lambda e: e.tensor_tensor(out=tmp[:], in0=yfl[:], in1=ytile[:], op=OP.add),
                     [yfl, ytile], [tmp])
                P.dma("sp", g.YT[ci], tmp[:], reads=[tmp])
        P.barrier()


def phase_merge(g):
    phase_merge_a(g)
    phase_merge_b(g)


def phase_merge_a(g):
    nc, P = g.nc, g.P
    with ExitStack() as es:
        c = alloc_chunk_bufs(g, es, 0)
        Wz = sb(g, es, "Wz", [128, 8, 2048], BF16)
        Wgs = sb(g, es, "Wgs", [128, 8, 1024], BF16)
        Wsb = sb(g, es, "Wsb", [128, 16, 1024], BF16)
        load_w_cols(g, Wz, 4096, 2048)
        load_w_cols(g, Wgs, 7232, 1024)
        load_w_gen(g, Wsb, g.w_sb, 16, 1024)
        sg = sb(g, es, "ssdg", [128, 2048], F32)
        P.dma("sp", sg[:], g.cbg[:, CBG["ssd_g"][0]:CBG["ssd_g"][0] + 2048], writes=[sg])
        pz = ps(g, es, "pz", [128, 512], F32)
        pb0 = ps(g, es, "pb0", [128, 2, 512], F32)
        pb1 = ps(g, es, "pb1", [128, 2, 512], F32)
        yt = [sb(g, es, "yt%d" % i, [128, 2048], BF16) for i in range(2)]
        zs = sb(g, es, "zs", [128, 512], F32)
        t = sb(g, es, "t", [128, 512], F32)
        jk = sb(g, es, "jk", [128, 512], BF16)
        st2 = sb(g, es, "st2", [128, 16], F32)
        ysn = sb(g, es, "ysn", [128, 512], BF16)
        ysnT = sb(g, es, "ysnT", [128, 16, 128], BF16)
        sgs = sb(g, es, "sgs", [128, 1024], F32)
        ms = [sb(g, es, "ms%d" % i, [128, 1024], BF16) for i in range(2)]
        for ci in range(NEXT):
            hT = c.hTe[ci % 3]
            prep(g, c, ci, g.xext[ci * 128:(ci + 1) * 128, :], 0, 1)
            y = yt[ci % 2]
            P.dma("sp", y[:], g.YT[ci], writes=[y])
            for gi in range(4):
                def mm(e, gi=gi):
                    ins = None
                    for kb in range(8):
                        ins = e.matmul(pz[:], lhsT=hT[:, kb, 1:129], rhs=Wz[:, kb, gi * 512:(gi + 1) * 512],
                                       start=(kb == 0), stop=(kb == 7))
                    return ins
                P.op("pe", mm, [hT, Wz], [pz])
                P.op("act", lambda e: e.activation(out=zs[:], in_=pz[:], func=AF.Silu), [pz], [zs])
                P.op("dve", lambda e, gi=gi: e.tensor_tensor(out=t[:], in0=zs[:], in1=y[:, gi * 512:(gi + 1) * 512],
                                                              op=OP.mult), [zs, y], [t])
                P.op("act", lambda e, gi=gi: e.activation(out=jk[:], in_=t[:], func=AF.Square,
                                                          accum_out=st2[:, gi:gi + 1]), [t], [jk, st2])
                P.op("dve", lambda e, gi=gi: e.tensor_scalar(out=st2[:, 4 + gi:5 + gi], in0=st2[:, gi:gi + 1],
                                                              scalar1=1.0 / 512, scalar2=EPS, op0=OP.mult, op1=OP.add),
                     [st2], [st2])
                P.op("act", lambda e, gi=gi: e.activation(out=st2[:, 8 + gi:9 + gi], in_=st2[:, 4 + gi:5 + gi],
                                                          func=AF.Sqrt), [st2], [st2])
                P.op("dve", lambda e, gi=gi: e.reciprocal(out=st2[:, 12 + gi:13 + gi], in_=st2[:, 8 + gi:9 + gi]),
                     [st2], [st2])
                P.op("dve", lambda e, gi=gi: e.scalar_tensor_tensor(
                    out=ysn[:], in0=t[:], scalar=st2[:, 12 + gi:13 + gi], in1=sg[:, gi * 512:(gi + 1) * 512],
                    op0=OP.mult, op1=OP.mult), [t, st2, sg], [ysn])

                def tr(e):
                    ins = None
                    for j in range(4):
                        ins = e.transpose(out=c.pA[:, j, :], in_=ysn[:, j * 128:(j + 1) * 128], identity=mk16(g, "ident"))
                    return ins
                P.op("pe", tr, [ysn, g.cmkb_t], [c.pA])
                P.op("act", lambda e, gi=gi: e.activation(out=ysnT[:, gi * 4:(gi + 1) * 4, :], in_=c.pA[:, 0:4, :],
                                                          func=AF.Copy), [c.pA], [ysnT])
            for hf in range(2):
                def mms(e, hf=hf):
                    ins = None
                    for kb in range(16):
                        ins = e.matmul(pb0[:, hf, :], lhsT=ysnT[:, kb, :], rhs=Wsb[:, kb, hf * 512:(hf + 1) * 512],
                                       start=(kb == 0), stop=(kb == 15))
                    return ins
                P.op("pe", mms, [ysnT, Wsb], [pb0])

                def mmg(e, hf=hf):
                    ins = None
                    for kb in range(8):
                        ins = e.matmul(pb1[:, hf, :], lhsT=hT[:, kb, 1:129], rhs=Wgs[:, kb, hf * 512:(hf + 1) * 512],
                                       start=(kb == 0), stop=(kb == 7))
                    return ins
                P.op("pe", mmg, [hT, Wgs], [pb1])
            P.op("act", lambda e: e.activation(out=sgs[:], in_=pb1[:].rearrange("p a b -> p (a b)"), func=AF.Sigmoid),
                 [pb1], [sgs])
            m = ms[ci % 2]
            P.op("dve", lambda e, m=m: e.tensor_tensor(out=m[:], in0=sgs[:], in1=pb0[:].rearrange("p a b -> p (a b)"),
                                                        op=OP.mult), [sgs, pb0], [m])
            P.dma("sp", g.MS[ci], m[:], reads=[m])
        P.barrier()


def phase_merge_b(g):
    nc, P = g.nc, g.P
    with ExitStack() as es:
        c = alloc_chunk_bufs(g, es, 0)
        Wgf = sb(g, es, "Wgf", [128, 8, 1024], BF16)
        Wfa = sb(g, es, "Wfa", [128, 8, 1024], BF16)
        Wo = sb(g, es, "Wo", [128, 8, 1024], BF16)
        Tcs = sb(g, es, "Tcs", [128, 256], BF16)
        load_w_cols(g, Wgf, 6208, 1024)
        load_w_gen(g, Wfa, g.w_fa, 8, 1024)
        load_w_gen(g, Wo, g.w_o, 8, 1024)
        P.dma("pool", Tcs[:], g.tcs, writes=[Tcs])
        pb0 = ps(g, es, "pb0", [128, 2, 512], F32)
        pb1 = ps(g, es, "pb1", [128, 2, 512], F32)
        pb2 = ps(g, es, "pb2", [128, 2, 512], F32)
        xtc = [sb(g, es, "xtc%d" % i, [128, 8, 2, 128], BF16) for i in range(2)]
        msl = [sb(g, es, "msl%d" % i, [128, 1024], BF16) for i in range(2)]
        mixT = sb(g, es, "mixT", [128, 8, 128], BF16)
        sgf = sb(g, es, "sgf", [128, 1024], F32)
        tmp = sb(g, es, "tmpm", [128, 1024], F32)
        mrg = sb(g, es, "mrg", [128, 1024], BF16)
        mrgT = sb(g, es, "mrgT", [128, 8, 128], BF16)
        l1 = [sb(g, es, "l1_%d" % i, [128, 1024], F32) for i in range(2)]
        xtv = g.XT.rearrange("p (c r t) -> p c r t", c=8, r=2)
        for ci in range(NEXT):
            hT = c.hTe[ci % 3]
            xt = prep(g, c, ci, g.xext[ci * 128:(ci + 1) * 128, :], 0, 1)
            xc_, m = xtc[ci % 2], msl[ci % 2]
            P.dma("sp", xc_[:], xtv[:, :, :, ci * 128:(ci + 1) * 128], writes=[xc_])
            P.dma("sp", m[:], g.MS[ci], writes=[m])
            for cg in range(2):
                def mmx(e, cg=cg):
                    e.matmul(pb2[:, cg, :], lhsT=Tcs[:, 0:128], rhs=xc_[:, cg * 4:(cg + 1) * 4, 0, :], start=True, stop=False)
                    return e.matmul(pb2[:, cg, :], lhsT=Tcs[:, 128:256], rhs=xc_[:, cg * 4:(cg + 1) * 4, 1, :],
                                    start=False, stop=True)
                P.op("pe", mmx, [Tcs, xc_], [pb2])
            P.op("act", lambda e: e.activation(out=mixT[:].rearrange("p a b -> p (a b)"),
                                               in_=pb2[:].rearrange("p a b -> p (a b)"), func=AF.Copy), [pb2], [mixT])
            for hf in range(2):
                def mmf(e, hf=hf):
                    ins = None
                    for kb in range(8):
                        ins = e.matmul(pb0[:, hf, :], lhsT=mixT[:, kb, :], rhs=Wfa[:, kb, hf * 512:(hf + 1) * 512],
                                       start=(kb == 0), stop=(kb == 7))
                    return ins
                P.op("pe", mmf, [mixT, Wfa], [pb0])

                def mmg(e, hf=hf):
                    ins = None
                    for kb in range(8):
                        ins = e.matmul(pb1[:, hf, :], lhsT=hT[:, kb, 1:129], rhs=Wgf[:, kb, hf * 512:(hf + 1) * 512],
                                       start=(kb == 0), stop=(kb == 7))
                    return ins
                P.op("pe", mmg, [hT, Wgf], [pb1])
            P.op("act", lambda e: e.activation(out=sgf[:], in_=pb1[:].rearrange("p a b -> p (a b)"), func=AF.Sigmoid),
                 [pb1], [sgf])
            P.op("dve", lambda e: e.tensor_tensor(out=tmp[:], in0=sgf[:], in1=pb0[:].rearrange("p a b -> p (a b)"),
                                                   op=OP.mult), [sgf, pb0], [tmp])
            P.op("dve", lambda e, m=m: e.tensor_tensor(out=mrg[:], in0=tmp[:], in1=m[:], op=OP.add), [tmp, m], [mrg])

            def tr(e):
                ins = None
                for kb in range(8):
                    ins = e.transpose(out=c.pA[:, kb, :], in_=mrg[:, kb * 128:(kb + 1) * 128], identity=mk16(g, "ident"))
                return ins
            P.op("pe", tr, [mrg, g.cmkb_t], [c.pA])
            P.op("act", lambda e: e.activation(out=mrgT[:], in_=c.pA[:], func=AF.Copy), [c.pA], [mrgT])
            for hf in range(2):
                def mmo(e, hf=hf):
                    ins = None
                    for kb in range(8):
                        ins = e.matmul(pb2[:, hf, :], lhsT=mrgT[:, kb, :], rhs=Wo[:, kb, hf * 512:(hf + 1) * 512],
                                       start=(kb == 0), stop=(kb == 7))
                    return ins
                P.op("pe", mmo, [mrgT, Wo], [pb2])
            l = l1[ci % 2]
            P.op("dve", lambda e, l=l: e.tensor_tensor(out=l[:], in0=pb2[:].rearrange("p a b -> p (a b)"),
                                                        in1=g.gbc[:, 0:D], op=OP.mult), [pb2, g.gbc], [l])
            P.op("pool", lambda e, l=l, xt=xt: e.tensor_tensor(out=l[:], in0=l[:], in1=xt[:], op=OP.add), [l, xt], [l])
            P.dma("sp", g.L1[ci], l[:], reads=[l])
        P.barrier()


def phase_ffn(g):
    nc, P = g.nc, g.P
    with ExitStack() as es:
        c = alloc_chunk_bufs(g, es, 0)
        h2T = sb(g, es, "h2T", [128, 8, EXT], BF16, nparts=NEXT)
        Wd = sb(g, es, "Wd", [128, NFB, 1024], BF16)
        load_w_gen(g, Wd, g.w_down, NFB, 1024)
        fg = sb(g, es, "fg", [128, 1024], F32)
        P.dma("sp", fg[:], g.cbg[:, CBG["final_g"][0]:CBG["final_g"][0] + 1024], writes=[fg])
        for ci in range(NEXT):
            i2 = ci % 2
            xt, st, xn = c.xt[i2], c.st[i2], c.xn[i2]
            P.dma("sp", xt[:], g.L1[ci], writes=[xt])
            P.op("act", lambda e: e.activation(out=c.junk[:], in_=xt[:], func=AF.Square, accum_out=st[:, 0:1]),
                 [xt], [c.junk, st])
            P.op("dve", lambda e: e.tensor_scalar(out=st[:, 1:2], in0=st[:, 0:1], scalar1=1.0 / D, scalar2=EPS,
                                                   op0=OP.mult, op1=OP.add), [st], [st])
            P.op("act", lambda e: e.activation(out=st[:, 2:3], in_=st[:, 1:2], func=AF.Sqrt), [st], [st])
            P.op("dve", lambda e: e.reciprocal(out=st[:, 3:4], in_=st[:, 2:3]), [st], [st])
            P.op("act", lambda e: e.activation(out=xn[:], in_=xt[:], func=AF.Copy, scale=st[:, 3:4]), [xt, st], [xn])

            def tr(e):
                ins = None
                for kb in range(8):
                    ins = e.transpose(out=c.pA[:, kb, :], in_=xn[:, kb * 128:(kb + 1) * 128], identity=mk16(g, "ident"))
                return ins
            P.op("pe", tr, [xn, g.cmkb_t], [c.pA])
            for kb in range(8):
                P.op("dve", lambda e, kb=kb, ci=ci: e.tensor_scalar(
                    out=h2T[:, kb, ci * 128:(ci + 1) * 128], in0=c.pA[:, kb, :], scalar1=modA(g, 4, kb),
                    scalar2=modA(g, 5, kb), op0=OP.mult, op1=OP.add), [c.pA, g.modv], [(h2T, ci)])
            if ci in (0, NEXT - 1):
                vm = cbcv(g, "emask_bc", 0 if ci == 0 else 128, 128)
                P.op("pool", lambda e, ci=ci, vm=vm: e.tensor_tensor(
                    out=h2T[:, :, ci * 128:(ci + 1) * 128], in0=h2T[:, :, ci * 128:(ci + 1) * 128],
                    in1=bc(vm.unsqueeze(1), [128, 8, 128]), op=OP.mult), [(h2T, ci), g.cbc_t], [(h2T, ci)])
        NB = 4
        pu = [ps(g, es, "pu%d" % i, [128, 512], F32) for i in range(2)]
        pd = ps(g, es, "pd", [128, 2, 512], F32)
        aT = sb(g, es, "aT", [128, NFB, 512], BF16, nparts=NFB)
        wu = [sb(g, es, "wu%d" % i, [128, 8, 2, 128], BF16) for i in range(3)]
        ug = [sb(g, es, "ug%d" % i, [128, 10, 64], F32) for i in range(2)]
        acc = [sb(g, es, "acc%d" % i, [128, 8, 64], F32) for i in range(2)]
        sgl = sb(g, es, "sgl", [128, 512], F32)
        lt = [sb(g, es, "lt%d" % i, [128, 1024], F32) for i in range(2)]
        yy = [sb(g, es, "yy%d" % i, [128, 1024], F32) for i in range(2)]
        jk = c.junk
        st = [sb(g, es, "stf%d" % i, [128, 4], F32) for i in range(2)]
        wuv = g.w_up.rearrange("(kb p) (gv n) -> p kb gv n", p=128, gv=2)
        l1f = g.L1.rearrange("c p d -> (c p) d")
        cnt = 0
        nitem = NB * NFB

        def issue_w(i):
            if i < nitem:
                fb_ = i % NFB
                w_ = wu[i % 3]
                for gv_ in range(2):
                    P.dma("pool", w_[:, :, gv_, :], wuv[:, :, gv_, fb_ * 128:(fb_ + 1) * 128], writes=[w_])
        issue_w(0)
        issue_w(1)
        for blk in range(NB):
            base = blk * 512
            hparts = [(h2T, i) for i in range(base // 128, (base + 640 + 127) // 128)]
            for fb in range(NFB):
                w = wu[cnt % 3]
                issue_w(cnt + 2)
                cnt += 1
                for gv in range(2):
                    u = ug[gv]
                    a = acc[gv]
                    for j in range(2):
                        def mm(e, j=j, gv=gv, w=w):
                            ins = None
                            for kb in range(8):
                                ins = e.matmul(pu[j][:, 0:320], lhsT=w[:, kb, gv, :],
                                               rhs=h2T[:, kb, base + j * 320:base + (j + 1) * 320],
                                               start=(kb == 0), stop=(kb == 7))
                            return ins
                        P.op("pe", mm, [w] + hparts, [pu[j]])
                        P.op("act", lambda e, j=j, u=u: e.activation(
                            out=u[:].rearrange("p r c -> p (r c)")[:, j * 320:(j + 1) * 320], in_=pu[j][:, 0:320],
                            func=AF.Copy), [pu[j]], [u])
                    cf = gv * NFB + fb
                    wt = lambda t: cpp(g, "cw_ffn", t * 44 + cf)
                    P.op("act", lambda e, u=u, a=a, cf=cf: e.activation(
                        out=a[:], in_=u[:, 1:9, :], func=AF.Identity, scale=cpp(g, "cw_ffn", 4 * 44 + cf),
                        bias=cpp(g, "cb_ffn", cf)), [u, g.cpp_t], [a])
                    for kh in range(3):
                        for kw in range(3):
                            if kh == 1 and kw == 1:
                                continue
                            dy, dx = kh - 1, kw - 1
                            c0, c1 = max(0, -dx), 64 - max(0, dx)
                            P.op("dve", lambda e, u=u, a=a, dy=dy, dx=dx, c0=c0, c1=c1, t=kh * 3 + kw, cf=cf:
                                 e.scalar_tensor_tensor(out=a[:, :, c0:c1], in0=u[:, 1 + dy:9 + dy, c0 + dx:c1 + dx],
                                                        scalar=cpp(g, "cw_ffn", t * 44 + cf), in1=a[:, :, c0:c1],
                                                        op0=OP.mult, op1=OP.add), [u, a, g.cpp_t], [a])
                P.op("act", lambda e: e.activation(out=sgl[:], in_=acc[0][:].rearrange("p r c -> p (r c)"), func=AF.Silu),
                     [acc[0]], [sgl])
                P.op("dve", lambda e, fb=fb: e.tensor_tensor(out=aT[:, fb, :], in0=sgl[:],
                                                              in1=acc[1][:].rearrange("p r c -> p (r c)"), op=OP.mult),
                     [sgl, acc[1]], [(aT, fb)])
            for tcn in range(4):
                o0 = blk * 512 + tcn * 128
                i2 = (blk * 4 + tcn) % 2
                l, y, s4 = lt[i2], yy[i2], st[i2]
                o = y
                P.dma("sp", l[:], l1f[o0 + 64:o0 + 64 + 128, :], writes=[l])
                for hf in range(2):
                    def mmd(e, hf=hf, tcn=tcn):
                        ins = None
                        for fb in range(NFB):
                            ins = e.matmul(pd[:, hf, :], lhsT=aT[:, fb, tcn * 128:(tcn + 1) * 128],
                                           rhs=Wd[:, fb, hf * 512:(hf + 1) * 512], start=(fb == 0), stop=(fb == NFB - 1))
                        return ins
                    P.op("pe", mmd, [aT, Wd], [pd])
                P.op("dve", lambda e, y=y: e.tensor_tensor(out=y[:], in0=pd[:].rearrange("p a b -> p (a b)"),
                                                            in1=g.gbc[:, D:2 * D], op=OP.mult), [pd, g.gbc], [y])
                P.op("pool", lambda e, y=y, l=l: e.tensor_tensor(out=y[:], in0=y[:], in1=l[:], op=OP.add), [y, l], [y])
                P.op("act", lambda e, y=y, s4=s4: e.activation(out=jk[:], in_=y[:], func=AF.Square,
                                                               accum_out=s4[:, 0:1]), [y], [jk, s4])
                P.op("dve", lambda e, s4=s4: e.tensor_scalar(out=s4[:, 1:2], in0=s4[:, 0:1], scalar1=1.0 / D,
                                                              scalar2=EPS, op0=OP.mult, op1=OP.add), [s4], [s4])
                P.op("act", lambda e, s4=s4: e.activation(out=s4[:, 2:3], in_=s4[:, 1:2], func=AF.Sqrt), [s4], [s4])
                P.op("dve", lambda e, s4=s4: e.reciprocal(out=s4[:, 3:4], in_=s4[:, 2:3]), [s4], [s4])
                P.op("dve", lambda e, y=y, s4=s4, o=o: e.scalar_tensor_tensor(
                    out=o[:], in0=y[:], scalar=s4[:, 3:4], in1=fg[:], op0=OP.mult, op1=OP.mult), [y, s4, fg], [y])
                P.dma("sp", g.out[o0:o0 + 128, :], o[:], reads=[o])
        P.barrier()


def _pm(v):
    v = np.asarray(v, np.float32)
    return np.ascontiguousarray(v.reshape(-1, 128).T)


def _rb(v):
    v = np.asarray(v, np.float32).reshape(1, -1)
    return np.ascontiguousarray(np.broadcast_to(v, (128, v.shape[1])))


def _const_tables():
    k = np.arange(128)[:, None]
    m = np.arange(128)[None, :]
    mats = [np.ones((128, 128)), k <= m, k >= m, k > m, k < m, k == m,
            np.where(m < k, -BIG, 0.0), np.where(m > k, -BIG, 0.0)]
    cmk = np.concatenate([np.asarray(a, np.float32) for a in mats], axis=1)
    t1i = np.arange(64)[:, None] * np.arange(64)[None, :]
    th = 2 * np.pi * t1i / 64.0
    t1 = np.concatenate([np.cos(th), -np.sin(th)], axis=1).astype(np.float32)
    j = np.arange(128)[:, None] * np.arange(128)[None, :]
    thc = 2 * np.pi * j / 128.0
    tcs = (np.concatenate([np.cos(thc), np.sin(thc)], axis=1) / 1024.0).astype(np.float32)
    return cmk, t1, tcs


def _t2_tables(q):
    t2 = np.arange(128, dtype=np.float64)[:, None, None]
    k1 = np.arange(64, dtype=np.float64)[None, :, None]
    k2 = (32 * q - 1 + np.arange(34, dtype=np.float64))[None, None, :]
    kk = np.mod(k1 + 64 * k2, 8192)
    th = 2 * np.pi * np.mod(kk * t2, 8192) / 8192.0
    Mr, Mi = np.cos(th), -np.sin(th)
    ta = np.concatenate([Mr, Mi], axis=2)
    tb = np.concatenate([-Mi, Mr], axis=2)
    return np.concatenate([ta.reshape(128, -1), tb.reshape(128, -1)], axis=1).astype(np.float32)


_CACHE = {}


def kernel(x, c, ctx, c_ctx, w_mod, b_mod, norm1_g, w_in, conv_ssd_w, conv_ssd_b, dt_bias, a_log,
           d_skip, ssd_norm_g, w_fa, w_sb, w_o, norm2_g, w_up, conv_ffn_w, conv_ffn_b, w_down, final_g):
    f = lambda a: np.asarray(a, np.float32)
    x, c, ctx, c_ctx = f(x), f(c), f(ctx), f(c_ctx)
    cmk, t1, tcs = _const_tables()
    in_maps = []
    bm = f(b_mod)[0]
    for core in range(8):
        b, q = divmod(core, 4)
        e0 = 2048 * q - 64
        xext = np.zeros((EXT + 128, D), np.float32)
        lo, hi = max(e0, 0), min(e0 + EXT, SEQ)
        xext[lo - e0:hi - e0] = x[b, lo:hi]
        hv = np.zeros(2, np.float32)
        if e0 - 1 >= 0:
            xext[EXT] = x[b, e0 - 1]; hv[0] = 1
        if e0 + EXT < SEQ:
            xext[EXT + 1] = x[b, e0 + EXT]; hv[1] = 1
        tok = e0 + np.arange(EXT)
        valid = ((tok >= 0) & (tok < SEQ)).astype(np.float32)
        fmask = np.zeros((NSLOT, 128, 2), np.float32)
        fmask[0:2, :, 0] = 1
        fmask[66:68, :, 1] = 1
        lt = np.arange(SEQ).reshape(NCH, 128)
        fmask[2:66, :, 0] = (lt < e0)
        fmask[2:66, :, 1] = (lt >= e0 + EXT)
        cpp_a = np.zeros((128, CPP_N), np.float32)

        def put(key, arr):
            o, n = CPP[key]
            cpp_a[:, o:o + n] = arr
        put("c", _pm(c[b])); put("cctx", _pm(c_ctx)); put("bmod", _pm(bm))
        put("n1g", _pm(f(norm1_g)[0])); put("n2g", _pm(f(norm2_g)[0]))
        put("cw_ssd", np.concatenate([_pm(f(conv_ssd_w)[0, t]) for t in range(3)], axis=1))
        put("cb_ssd", _pm(f(conv_ssd_b)[0]))
        cfw = f(conv_ffn_w)[0].reshape(9, 2 * DFF)
        put("cw_ffn", np.concatenate([_pm(cfw[t]) for t in range(9)], axis=1))
        put("cb_ffn", _pm(f(conv_ffn_b)[0]))
        put("emask", valid.reshape(NEXT, 128).T)
        put("fmask", fmask.transpose(1, 0, 2).reshape(128, NSLOT * 2))
        cbc_a = np.zeros((128, CBC_N), np.float32)

        def putb(key, arr):
            o, n = CBC[key]
            cbc_a[:, o:o + n] = arr
        putb("dt_bias", _rb(f(dt_bias)[0].reshape(-1))); putb("a_log", _rb(f(a_log)[0].reshape(-1)))
        putb("d_skip", _rb(f(d_skip)[0]))
        cbg_a = np.concatenate([_rb(f(ssd_norm_g)[0]), _rb(f(final_g)), _rb(bm[2048:3072]), _rb(bm[5120:6144])], axis=1)
        putb("emask_bc", _rb(np.concatenate([valid[:128], valid[-128:]])))
        hvb = np.zeros(128, np.float32); hvb[0:2] = hv
        putb("halo_v", _rb(hvb))
        in_maps.append(dict(
            xb=np.ascontiguousarray(x[b]), ctxb=np.ascontiguousarray(ctx[b]), xext=xext,
            w_mod=f(w_mod)[0], w_in=f(w_in)[0], w_fa=f(w_fa)[0], w_sb=f(w_sb)[0], w_o=f(w_o)[0],
            w_up=f(w_up)[0], w_down=f(w_down)[0], cpp=cpp_a, cbc=cbc_a, cbg=cbg_a, cbrow=f(conv_ssd_b)[0].reshape(1, 3072).copy(), cmk=cmk, t1=t1, t2=_t2_tables(q), tcs=tcs))
    if "nc" not in _CACHE:
        _CACHE["nc"] = build_program()
    res = run_bass_kernel_spmd(_CACHE["nc"], in_maps, core_ids=list(range(8)))
    if DEBUG:
        _CACHE["res"] = res
    out = np.zeros((2, SEQ, D), np.float32)
    for core in range(8):
        b, q = divmod(core, 4)
        out[b, 2048 * q:2048 * (q + 1)] = res.results[core]["out"]
    return out
```

```python
import os
from contextlib import ExitStack
import numpy as np
import concourse.bass as bass
import concourse.mybir as mybir
from concourse.bass_utils import run_bass_kernel_spmd

F32 = mybir.dt.float32
BF16 = mybir.dt.bfloat16
AF = mybir.ActivationFunctionType
OP = mybir.AluOpType

D = 1024
SEQ = 8192
NCH = 64
NEXT = 17
EXT = NEXT * 128
EPS = 1e-6
BIG = 30000.0
NSLOT = 68
DFF = 2816
NFB = 22

STOP = os.environ.get("MK_STOP", "")
DEBUG = bool(STOP)


class Buf:
    def __init__(self, t, nparts=1):
        self.t = t
        self.n = nparts
        self.w = [None] * nparts
        self.r = [[] for _ in range(nparts)]
        self.excl = False

    def __getitem__(self, idx):
        return self.t[idx]


def _parts(items):
    out = []
    for it in items:
        if it is None:
            continue
        if isinstance(it, Buf):
            out.extend((it, i) for i in range(it.n))
        else:
            b, idx = it
            if isinstance(idx, int):
                out.append((b, idx))
            else:
                out.extend((b, i) for i in idx)
    return out


class Prog:
    def __init__(self, nc, es):
        self.nc = nc
        self.E = {}
        self.semid = 0
        for name, eng in (("pe", nc.tensor), ("act", nc.scalar), ("dve", nc.vector), ("pool", nc.gpsimd), ("sp", nc.sync)):
            sem = es.enter_context(nc.semaphore("s_" + name))
            self.E[name] = dict(name=name, eng=eng, sem=(self._sid(), sem), count=0, waited={}, pool=[], ndma=0)
        for name, n in (("sp", 8), ("pool", 6), ("act", 4)):
            for i in range(n):
                sem = es.enter_context(nc.semaphore("d_%s%d" % (name, i)))
                self.E[name]["pool"].append((self._sid(), sem))
        self.ninst = 0

    def _sid(self):
        self.semid += 1
        return self.semid

    def _wait(self, E, tok):
        (sid, sem), val, _ = tok
        if E["waited"].get(sid, 0) >= val:
            return
        E["eng"].wait_ge(sem, val)
        E["waited"][sid] = val

    def _collect(self, en, reads, writes):
        toks = []
        for b, i in _parts(reads):
            if b.w[i] is not None:
                toks.append(b.w[i])
            if b.excl:
                toks.extend(t for t in b.r[i] if t[2] != en)
        for b, i in _parts(writes):
            if b.w[i] is not None:
                toks.append(b.w[i])
            toks.extend(b.r[i])
        res = []
        for t in toks:
            if en == "pe" and t[2] == "pe":
                continue
            res.append(t)
        return res

    def _update(self, reads, writes, tok):
        for b, i in _parts(reads):
            b.r[i].append(tok)
            if len(b.r[i]) > 24:
                last = {}
                for t in b.r[i]:
                    k = t[0][0]
                    if k not in last or last[k][1] < t[1]:
                        last[k] = t
                b.r[i] = list(last.values())
        for b, i in _parts(writes):
            b.w[i] = tok
            b.r[i] = []

    def op(self, en, fn, reads=(), writes=()):
        E = self.E[en]
        for t in self._collect(en, reads, writes):
            self._wait(E, t)
        ins = fn(E["eng"])
        E["count"] += 1
        ins.then_inc(E["sem"][1], 1)
        tok = (E["sem"], E["count"], en)
        self._update(reads, writes, tok)
        self.ninst += 1
        return tok

    def dma(self, qn, out, in_, reads=(), writes=(), **kw):
        Q = self.E[qn]
        i = Q["ndma"]
        P = len(Q["pool"])
        sem = Q["pool"][i % P]
        val = 16 * (i // P + 1)
        if i >= P:
            self._wait(Q, (sem, val - 16, "dma"))
        for t in self._collect("dma", reads, writes):
            self._wait(Q, t)
        Q["eng"].dma_start(out=out, in_=in_, **kw).then_inc(sem[1], 16)
        Q["ndma"] += 1
        tok = (sem, val, "dma")
        self._update(reads, writes, tok)
        return tok

    def all_tokens(self):
        toks = []
        for E in self.E.values():
            if E["count"]:
                toks.append((E["sem"], E["count"], E["name"]))
            P = len(E["pool"])
            for j in range(min(P, E["ndma"])):
                n = (E["ndma"] - 1 - j) // P + 1
                toks.append((E["pool"][j], 16 * n, "dma"))
        return toks

    def barrier(self):
        toks = self.all_tokens()
        for E in self.E.values():
            for t in toks:
                self._wait(E, t)


def bc(ap, shape):
    return ap.broadcast_to(shape)


class Ctx:
    pass


def build_program():
    nc = bass.Bass("TRN2", target_bir_lowering=False)
    g = Ctx()
    g.nc = nc

    def din(name, shape, dt=F32):
        return nc.dram_tensor(name, list(shape), dt, kind="ExternalInput").ap()

    def dscr(name, shape, dt):
        kind = "ExternalOutput" if DEBUG else "Internal"
        return nc.dram_tensor(name, list(shape), dt, kind=kind).ap()

    g.xb = din("xb", [SEQ, D])
    g.ctxb = din("ctxb", [256, D])
    g.xext = din("xext", [EXT + 128, D])
    g.w_mod = din("w_mod", [D, 6 * D])
    g.w_in = din("w_in", [D, 8256])
    g.w_fa = din("w_fa", [D, D])
    g.w_sb = din("w_sb", [2048, D])
    g.w_o = din("w_o", [D, D])
    g.w_up = din("w_up", [D, 2 * DFF])
    g.w_down = din("w_down", [DFF, D])
    g.cpp = din("cpp", [128, CPP_N])
    g.cbc = din("cbc", [128, CBC_N])
    g.cbg = din("cbg", [128, CBG_N])
    g.cbrow = din("cbrow", [1, 3072])
    g.cmk = din("cmk", [128, 8 * 128])
    g.t1 = din("t1", [64, 128])
    g.t2 = din("t2", [128, 2 * 64 * 68])
    g.tcs = din("tcs", [128, 256])
    g.out = nc.dram_tensor("out", [2048, D], F32, kind="ExternalOutput").ap()
    g.U = dscr("U", [NCH, 128, D], BF16)
    g.Y = dscr("Y", [128, 128, D], BF16)
    g.XT = dscr("XT", [128, 8 * 2 * EXT], BF16)
    g.MS = dscr("MS", [NEXT, 128, D], BF16)
    g.YF = dscr("YF", [NEXT, 128, 2048], BF16)
    g.YT = dscr("YT", [NEXT, 128, 2048], BF16)
    g.L1 = dscr("L1", [NEXT, 128, D], F32)
    g.SFB = dscr("SFB", [2, 128, 2048], F32)

    with ExitStack() as es:
        P = Prog(nc, es)
        g.P = P
        g.uid = 0
        phase_setup(g, es)
        with ExitStack() as es2:
            alloc_conv_consts(g, es2)
            if STOP != "setup":
                phase_far(g)
            if STOP not in ("setup", "far"):
                phase_fnet(g)
            if STOP not in ("setup", "far", "fnet"):
                phase_own(g, 0)
                phase_own(g, 1)
            P.barrier()
        if STOP not in ("setup", "far", "fnet", "own"):
            phase_merge(g)
        if STOP not in ("setup", "far", "fnet", "own", "merge"):
            phase_ffn(g)
        P.barrier()
    return nc


def sb(g, es, name, shape, dt, nparts=1):
    g.uid += 1
    t = es.enter_context(g.nc.sbuf_tensor("%s_%d" % (name, g.uid), list(shape), dt))
    return Buf(t, nparts)


def ps(g, es, name, shape, dt=F32, nparts=1):
    g.uid += 1
    t = es.enter_context(g.nc.psum_tensor("%s_%d" % (name, g.uid), list(shape), dt))
    b = Buf(t, nparts)
    b.excl = True
    return b


def _layout(items):
    off = {}
    o = 0
    for k, n in items:
        off[k] = (o, n)
        o += n
    return off, o


CPP, CPP_N = _layout([("c", 8), ("cctx", 8), ("bmod", 48), ("n1g", 8), ("n2g", 8), ("cw_ssd", 72), ("cb_ssd", 24),
                      ("cw_ffn", 9 * 44), ("cb_ffn", 44), ("emask", NEXT), ("fmask", NSLOT * 2)])
CBC, CBC_N = _layout([("dt_bias", 64), ("a_log", 64), ("d_skip", 32), ("emask_bc", 256), ("halo_v", 128)])
CBG, CBG_N = _layout([("ssd_g", 2048), ("final_g", 1024), ("bmod_g1", 1024), ("bmod_g2", 1024)])
MK = {k: i for i, k in enumerate(["ones", "le", "ge", "gt", "lt", "ident", "pen_f", "pen_b"])}


def cpp(g, key, j=None, n=1):
    o, _ = CPP[key]
    if j is None:
        return g.cpp_t[:, o:o + CPP[key][1]]
    return g.cpp_t[:, o + j:o + j + n]


def cbcv(g, key, a=0, n=None):
    o, m = CBC[key]
    if n is None:
        n = m
    return g.cbc_t[:, o + a:o + a + n]


def mk32(g, key):
    i = MK[key]
    return g.cmk_t[:, i * 128:(i + 1) * 128]


def mk16(g, key):
    i = MK[key]
    return g.cmkb_t[:, i * 128:(i + 1) * 128]


def phase_setup(g, es):
    nc, P = g.nc, g.P
    g.cpp_t = sb(g, es, "cpp", [128, CPP_N], F32)
    g.cbc_t = sb(g, es, "cbc", [128, CBC_N], F32)
    g.cmk_t = sb(g, es, "cmk", [128, 8 * 128], F32)
    g.cmkb_t = sb(g, es, "cmkb", [128, 8 * 128], BF16)
    g.modv = sb(g, es, "modv", [128, 8 * 8], F32)
    g.gbc = sb(g, es, "gbc", [128, 2 * D], F32)
    g.nega = sb(g, es, "nega", [128, 64], F32)
    g.Sf = sb(g, es, "Sf", [128, 2048], F32)
    g.Sb = sb(g, es, "Sb", [128, 2048], F32)
    P.dma("sp", g.cpp_t[:], g.cpp, writes=[g.cpp_t])
    P.dma("sp", g.cbc_t[:], g.cbc, writes=[g.cbc_t])
    P.dma("sp", g.cmk_t[:], g.cmk, writes=[g.cmk_t])
    P.dma("pool", g.cmkb_t[:], g.cmk, writes=[g.cmkb_t])
    with ExitStack() as ls:
        sc = sb(g, ls, "sc", [128, 8, 2], F32)
        screp = sb(g, ls, "screp", [128, 8, 128], F32)
        modT = sb(g, ls, "modT", [128, 48, 2], F32)
        wm = [sb(g, ls, "wm%d" % i, [128, 8, 1024], F32) for i in range(2)]
        pm = ps(g, ls, "pm", [128, 8, 2], F32)
        pg = ps(g, ls, "pg", [128, 512], F32)
        bg = sb(g, ls, "bg", [128, 2 * D], F32)
        P.dma("sp", bg[:], g.cbg[:, CBG["bmod_g1"][0]:CBG["bmod_g1"][0] + 2 * D], writes=[bg])
        P.op("act", lambda e: e.activation(out=sc[:, :, 0], in_=cpp(g, "c"), func=AF.Silu), [g.cpp_t], [sc])
        P.op("act", lambda e: e.activation(out=sc[:, :, 1], in_=cpp(g, "cctx"), func=AF.Silu), [g.cpp_t], [sc])
        P.op("dve", lambda e: e.tensor_copy(out=screp[:], in_=bc(sc[:, :, 0:1], [128, 8, 128])), [sc], [screp])
        wv = g.w_mod.rearrange("(kb p) n -> p kb n", p=128)
        for j in range(6):
            w = wm[j % 2]
            P.dma("sp", w[:], wv[:, :, j * 1024:(j + 1) * 1024], writes=[w])
            def mm(e, w=w):
                ins = None
                for fb in range(8):
                    for kb in range(8):
                        ins = e.matmul(pm[:, fb, :], lhsT=w[:, kb, fb * 128:(fb + 1) * 128], rhs=sc[:, kb, :],
                                       start=(kb == 0), stop=(kb == 7))
                return ins
            P.op("pe", mm, [w, sc], [pm])
            bo = CPP["bmod"][0] + j * 8
            P.op("dve", lambda e, j=j, bo=bo: e.tensor_tensor(
                out=modT[:, j * 8:(j + 1) * 8, :], in0=pm[:], in1=bc(g.cpp_t[:, bo:bo + 8].unsqueeze(2), [128, 8, 2]),
                op=OP.add), [pm, g.cpp_t], [modT])
            if j in (2, 5):
                gi = 0 if j == 2 else 1
                for hf in range(2):
                    def mg(e, w=w, hf=hf):
                        ins = None
                        for kb in range(8):
                            ins = e.matmul(pg[:], lhsT=screp[:, kb, :], rhs=w[:, kb, hf * 512:(hf + 1) * 512],
                                           start=(kb == 0), stop=(kb == 7))
                        return ins
                    P.op("pe", mg, [w, screp], [pg])
                    P.op("dve", lambda e, gi=gi, hf=hf: e.tensor_tensor(
                        out=g.gbc[:, gi * D + hf * 512: gi * D + (hf + 1) * 512], in0=pg[:],
                        in1=bg[:, gi * D + hf * 512: gi * D + (hf + 1) * 512], op=OP.add), [pg, bg], [g.gbc])
        mv = g.modv
        def mkA(dst, scale_j, which, gkey):
            P.op("dve", lambda e: e.scalar_tensor_tensor(
                out=mv[:, dst * 8:(dst + 1) * 8], in0=modT[:, scale_j * 8:(scale_j + 1) * 8, which], scalar=1.0,
                in1=cpp(g, gkey), op0=OP.add, op1=OP.mult), [modT, g.cpp_t], [mv])

        def mkB(dst, shift_j, which):
            P.op("dve", lambda e: e.tensor_copy(out=mv[:, dst * 8:(dst + 1) * 8],
                                                 in_=modT[:, shift_j * 8:(shift_j + 1) * 8, which]), [modT], [mv])
        mkA(0, 1, 0, "n1g"); mkB(1, 0, 0)
        mkA(2, 1, 1, "n1g"); mkB(3, 0, 1)
        mkA(4, 4, 0, "n2g"); mkB(5, 3, 0)
        P.op("act", lambda e: e.activation(out=g.nega[:], in_=cbcv(g, "a_log"), func=AF.Exp), [g.cbc_t], [g.nega])
        P.op("dve", lambda e: e.tensor_scalar(out=g.nega[:], in0=g.nega[:], scalar1=-1.0, scalar2=None, op0=OP.mult),
             [g.nega], [g.nega])
        P.barrier()


def alloc_conv_consts(g, es):
    P = g.P
    g.diag = sb(g, es, "diag", [128, 72, 128], BF16)
    g.brow = sb(g, es, "brow", [1, 3072], BF16)
    for a_ in range(0, 3072, 1024):
        P.dma("pool", g.brow[:, a_:a_ + 1024], g.cbrow[:, a_:a_ + 1024], writes=[g.brow])
    for i_ in range(72):
        P.op("dve", lambda e, i_=i_: e.tensor_scalar(out=g.diag[:, i_, :], in0=mk16(g, "ident"),
                                                      scalar1=cpp(g, "cw_ssd", i_), scalar2=None, op0=OP.mult),
             [g.cmkb_t, g.cpp_t], [g.diag])


def modA(g, i, kb):
    return g.modv[:, i * 8 + kb:i * 8 + kb + 1]


def alloc_chunk_bufs(g, es, nfb):
    c = Ctx()
    c.xt = [sb(g, es, "xt%d" % i, [128, D], F32) for i in range(2)]
    c.junk = sb(g, es, "junk", [128, D], BF16)
    c.st = [sb(g, es, "st%d" % i, [128, 4], F32) for i in range(2)]
    c.xn = [sb(g, es, "xn%d" % i, [128, D], BF16) for i in range(2)]
    c.hTe = [sb(g, es, "hTe%d" % i, [128, 8, 130], BF16) for i in range(3)]
    c.pA = ps(g, es, "pA", [128, 8, 128], BF16)
    c.nfb = nfb
    return c


def prep_a(g, c, k, src_rows):
    P = g.P
    i2 = k % 2
    xt, st, xn = c.xt[i2], c.st[i2], c.xn[i2]
    P.dma("sp", xt[:], src_rows, writes=[xt])
    P.op("act", lambda e: e.activation(out=c.junk[:], in_=xt[:], func=AF.Square, accum_out=st[:, 0:1]),
         [xt], [c.junk, st])
    P.op("dve", lambda e: e.tensor_scalar(out=st[:, 1:2], in0=st[:, 0:1], scalar1=1.0 / D, scalar2=EPS,
                                           op0=OP.mult, op1=OP.add), [st], [st])
    P.op("act", lambda e: e.activation(out=st[:, 2:3], in_=st[:, 1:2], func=AF.Ln), [st], [st])
    P.op("act", lambda e: e.activation(out=st[:, 3:4], in_=st[:, 2:3], func=AF.Exp, scale=-0.5), [st], [st])
    P.op("act", lambda e: e.activation(out=xn[:], in_=xt[:], func=AF.Copy, scale=st[:, 3:4]), [xt, st], [xn])
    return xt


def prep_b(g, c, k, ai, bi, vmask=None, hT=None):
    P = g.P
    xn = c.xn[k % 2]
    if hT is None:
        hT = c.hTe[k % 3]

    def tr(e):
        ins = None
        for kb in range(8):
            ins = e.transpose(out=c.pA[:, kb, :], in_=xn[:, kb * 128:(kb + 1) * 128], identity=mk16(g, "ident"))
        return ins
    P.op("pe", tr, [xn, g.cmkb_t], [c.pA])
    P.op("dve", lambda e: e.tensor_tensor(out=hT[:, :, 1:129], in0=c.pA[:],
                                          in1=bc(g.modv[:, ai * 8:(ai + 1) * 8].unsqueeze(2), [128, 8, 128]),
                                          op=OP.mult), [c.pA, g.modv], [hT])
    P.op("dve", lambda e: e.tensor_tensor(out=hT[:, :, 1:129], in0=hT[:, :, 1:129],
                                          in1=bc(g.modv[:, bi * 8:(bi + 1) * 8].unsqueeze(2), [128, 8, 128]),
                                          op=OP.add), [hT, g.modv], [hT])
    if vmask is not None:
        P.op("pool", lambda e: e.tensor_tensor(out=hT[:, :, 1:129], in0=hT[:, :, 1:129],
                                               in1=bc(vmask.unsqueeze(1), [128, 8, 128]), op=OP.mult),
             [hT, g.cbc_t], [hT])


def prep(g, c, k, src_rows, ai, bi, vmask=None, hT=None):
    xt = prep_a(g, c, k, src_rows)
    prep_b(g, c, k, ai, bi, vmask=vmask, hT=hT)
    return xt


def halo_link(g, c, k, has_left):
    P = g.P
    cur = c.hTe[k % 3]
    if has_left:
        prv = c.hTe[(k - 1) % 3]
        P.op("pool", lambda e: e.tensor_copy(out=cur[:, :, 0:1], in_=prv[:, :, 128:129]), [prv], [cur])
        P.op("pool", lambda e: e.tensor_copy(out=prv[:, :, 129:130], in_=cur[:, :, 1:2]), [cur], [prv])
    else:
        P.op("pool", lambda e: e.memset(cur[:, :, 0:1], 0.0), [], [cur])


def halo_zero_right(g, c, k):
    cur = c.hTe[k % 3]
    g.P.op("pool", lambda e: e.memset(cur[:, :, 129:130], 0.0), [], [cur])


def fm_proj_conv(g, c, s, hT, W, nfb, cw_off, steps=None):
    P = g.P
    steps = steps if steps is not None else []
    groups = [list(range(a, min(a + 3, nfb))) for a in range(0, nfb, 3)]
    for gi, fbs in enumerate(groups):
        n = len(fbs)
        pa = s.pxa[gi % len(s.pxa)]
        pb = s.pxc[gi % len(s.pxc)]
        pre = s.pre[gi % 2]

        def mm(e, fbs=fbs, pa=pa):
            ins = None
            for j, fb in enumerate(fbs):
                for kb in range(8):
                    ins = e.matmul(pa[:, j * 130:(j + 1) * 130], lhsT=W[:, kb, fb * 128:(fb + 1) * 128], rhs=hT[:, kb, :],
                                   start=(kb == 0), stop=(kb == 7))
            return ins
        P.op("pe", mm, [W, hT], [pa])
        P.op("dve", lambda e, n=n, pa=pa, pre=pre: e.tensor_copy(
            out=pre[:, 0:n, :], in_=pa[:, 0:n * 130].rearrange("p (j t) -> p j t", t=130)), [pa], [pre])

        def mc(e, fbs=fbs, pb=pb, pre=pre):
            ins = None
            for j, fb in enumerate(fbs):
                cf = cw_off + fb
                for k in range(3):
                    e.matmul(pb[:, j * 128:(j + 1) * 128], lhsT=g.diag[:, k * 24 + cf, :], rhs=pre[:, j, k:k + 128],
                             start=(k == 0), stop=False)
                ins = e.matmul(pb[:, j * 128:(j + 1) * 128], lhsT=g.brow[0:1, cf * 128:(cf + 1) * 128],
                               rhs=mk16(g, "ones")[0:1, :], start=False, stop=True)
            return ins
        P.op("pe", mc, [pre, g.diag, g.brow, g.cmkb_t], [pb])
        f0, f1 = fbs[0], fbs[-1] + 1
        P.op("act", lambda e, f0=f0, f1=f1, n=n, pb=pb: e.activation(
            out=s.xcs[:, f0:f1, :], in_=pb[:, 0:n * 128].rearrange("p (j t) -> p j t", t=128), func=AF.Silu),
            [pb], [(s.xcs, gi)])
        if steps:
            steps.pop(0)()
    while steps:
        steps.pop(0)()


def to_token_major(g, c, s, nblk):
    P = g.P
    for r0 in range(0, nblk, 8):
        n = min(8, nblk - r0)

        def tr(e, r0=r0, n=n):
            ins = None
            for j in range(n):
                ins = e.transpose(out=c.pA[:, j, :], in_=s.xcs[:, r0 + j, :], identity=mk16(g, "ident"))
            return ins
        P.op("pe", tr, [s.xcs, g.cmkb_t], [c.pA])
        P.op("act", lambda e, r0=r0, n=n: e.activation(
            out=s.xtok[:, r0 * 128:(r0 + n) * 128], in_=c.pA[:, 0:n, :], func=AF.Copy), [c.pA], [s.xtok])


def dt_steps(g, s, hT, Wdt, ncol, bias_ap, nega_ap, mask_ap):
    P = g.P

    def s1():
        def mm(e):
            ins = None
            for kb in range(8):
                ins = e.matmul(s.pD[:, 0:ncol], lhsT=hT[:, kb, 1:129], rhs=Wdt[:, kb, 0:ncol], start=(kb == 0), stop=(kb == 7))
            return ins
        P.op("pe", mm, [hT, Wdt], [s.pD])
        P.op("dve", lambda e: e.tensor_tensor(out=s.dtm[:, 0:ncol], in0=s.pD[:, 0:ncol], in1=bias_ap, op=OP.add),
             [s.pD, g.cbc_t], [s.dtm])

    def s2():
        P.op("act", lambda e: e.activation(out=s.dtm[:, 0:ncol], in_=s.dtm[:, 0:ncol], func=AF.Exp), [s.dtm], [s.dtm])

    def s3():
        P.op("act", lambda e: e.activation(out=s.dtm[:, 0:ncol], in_=s.dtm[:, 0:ncol], func=AF.Ln, bias=1.0), [s.dtm], [s.dtm])

    def s4():
        dv = s.dtm[:, 0:ncol].rearrange("p (a b) -> p a b", b=32)
        P.op("dve", lambda e: e.tensor_tensor(out=dv, in0=dv, in1=mask_ap, op=OP.mult), [s.dtm, g.cpp_t], [s.dtm])

    def s5():
        P.op("dve", lambda e: e.tensor_tensor(out=s.la[:, 0:ncol], in0=s.dtm[:, 0:ncol], in1=nega_ap, op=OP.mult),
             [s.dtm, g.nega], [s.la])
    return [s1, s2, s3, s4, s5]


def state_contrib(g, s, wexp_ap, xdd, on_group):
    P = g.P
    P.op("dve", lambda e: e.tensor_tensor(
        out=xdd[:].rearrange("p (h d) -> p h d", d=64), in0=s.xtok[:, 0:2048].rearrange("p (h d) -> p h d", d=64),
        in1=bc(wexp_ap.unsqueeze(2), [128, 32, 64]), op=OP.mult), [s.xtok, s.wx], [xdd])
    for gi in range(4):
        P.op("pe", lambda e, gi=gi: e.matmul(s.pH[:], lhsT=s.xtok[:, 2048 + gi * 128:2048 + (gi + 1) * 128],
                                             rhs=xdd[:, gi * 512:(gi + 1) * 512], start=True, stop=True),
             [s.xtok, xdd], [s.pH])
        on_group(gi, s.pH)


def load_w_cols(g, W, col0, ncols, dst0=0):
    wv = g.w_in.rearrange("(kb p) n -> p kb n", p=128)
    for a in range(0, ncols, 512):
        n = min(512, ncols - a)
        g.P.dma("pool", W[:, :, dst0 + a:dst0 + a + n], wv[:, :, col0 + a:col0 + a + n], writes=[W])


def phase_far(g):
    nc, P = g.nc, g.P
    with ExitStack() as es:
        c = alloc_chunk_bufs(g, es, 20)
        s = Ctx()
        Wf = sb(g, es, "Wf", [128, 8, 1024], BF16)
        Wxb = sb(g, es, "Wxb", [128, 8, 2560], BF16)
        Wdt = sb(g, es, "Wdt", [128, 8, 64], BF16)
        load_w_cols(g, Wf, 0, 1024)
        load_w_cols(g, Wxb, 1024, 2560)
        load_w_cols(g, Wdt, 6144, 64)
        s.pxa = [ps(g, es, "pxa%d" % i, [128, 512], F32) for i in range(2)]
        s.pxc = [ps(g, es, "pxc%d" % i, [128, 512], F32) for i in range(1)]
        s.pD = ps(g, es, "pD", [128, 512], F32)
        s.pH = ps(g, es, "pH", [128, 512], F32)
        pf = [ps(g, es, "pf%d" % i, [128, 512], F32) for i in range(2)]
        s.pre = [sb(g, es, "pre%d" % i, [128, 3, 130], BF16) for i in range(2)]
        s.xcs = sb(g, es, "xcs", [128, 20, 128], BF16, nparts=8)
        s.pD2 = sb(g, es, "pD2", [128, 192], F32)
        s.xtok = sb(g, es, "xtok", [128, 2560], BF16)
        s.dtm = sb(g, es, "dtm", [128, 64], F32)
        s.la = sb(g, es, "la", [128, 64], F32)
        s.wx = sb(g, es, "wx", [128, 64], F32)
        s.sg = sb(g, es, "sg", [128, 64], F32)
        Rb = sb(g, es, "Rb", [128, 32], F32)
        wxb = sb(g, es, "wxb", [128, 64], BF16)
        dec = sb(g, es, "dec", [128, 32], F32)
        xdd = [sb(g, es, "xdd%d" % i, [128, 2048], BF16) for i in range(2)]
        ub = [sb(g, es, "ub%d" % i, [128, D], BF16) for i in range(2)]
        P.op("dve", lambda e: e.memset(g.Sf[:], 0.0), [], [g.Sf])
        P.op("dve", lambda e: e.memset(g.Sb[:], 0.0), [], [g.Sb])
        P.op("dve", lambda e: e.memset(Rb[:], 0.0), [], [Rb])
        slots = [("c", 0), ("c", 1)] + [("l", i) for i in range(NCH)] + [("c", 0), ("c", 1)]
        first = {0, 2, 66}
        last = {1, 65, 67}

        def do_prep_a(k):
            kind, i = slots[k]
            src = g.ctxb[i * 128:(i + 1) * 128, :] if kind == "c" else g.xb[i * 128:(i + 1) * 128, :]
            prep_a(g, c, k, src)

        def do_prep_b(k):
            kind, i = slots[k]
            if kind == "c":
                prep_b(g, c, k, 2, 3)
            else:
                prep_b(g, c, k, 0, 1)
            halo_link(g, c, k, k not in first)
            if k in last:
                halo_zero_right(g, c, k)
        do_prep_a(0); do_prep_b(0)
        do_prep_a(1); do_prep_b(1)
        for k in range(NSLOT):
            kind, i = slots[k]
            hT = c.hTe[k % 3]
            fo = CPP["fmask"][0] + 2 * k
            mask_ap = bc(g.cpp_t[:, fo:fo + 2].unsqueeze(2), [128, 2, 32])
            steps = dt_steps(g, s, hT, Wdt, 64, cbcv(g, "dt_bias"), g.nega[:], mask_ap)

            def t1():
                def segs(e):
                    e.matmul(s.pD[:, 64:96], lhsT=mk32(g, "gt"), rhs=s.la[:, 0:32], start=True, stop=True)
                    e.matmul(s.pD[:, 96:128], lhsT=mk32(g, "lt"), rhs=s.la[:, 32:64], start=True, stop=True)
                    return e.matmul(s.pD[:, 128:192], lhsT=mk32(g, "ones"), rhs=s.la[:, 0:64], start=True, stop=True)
                P.op("pe", segs, [s.la, g.cmk_t], [s.pD])
                P.op("dve", lambda e: e.tensor_copy(out=s.pD2[:], in_=s.pD[:, 0:192]), [s.pD], [s.pD2])

            def t2b():
                P.op("pool", lambda e: e.tensor_copy(out=s.sg[:, 0:32], in_=s.pD2[:, 64:96]), [s.pD2], [s.sg])
                P.op("pool", lambda e: e.tensor_tensor(out=s.sg[:, 32:64], in0=s.pD2[:, 96:128], in1=Rb[:], op=OP.add),
                     [s.pD2, Rb], [s.sg])
                P.op("pool", lambda e: e.tensor_tensor(out=Rb[:], in0=Rb[:], in1=s.pD2[:, 160:192], op=OP.add),
                     [s.pD2, Rb], [Rb])

            def t3():
                P.op("act", lambda e: e.activation(out=s.wx[:], in_=s.sg[:], func=AF.Exp), [s.sg], [s.wx])
                P.op("act", lambda e: e.activation(out=dec[:], in_=s.pD2[:, 128:160], func=AF.Exp), [s.pD2], [dec])

            def t4():
                P.op("pool", lambda e: e.tensor_tensor(out=wxb[:], in0=s.wx[:], in1=s.dtm[:], op=OP.mult),
                     [s.wx, s.dtm], [wxb])
                P.op("pool", lambda e: e.tensor_tensor(
                    out=g.Sf[:].rearrange("p (h d) -> p h d", d=64), in0=g.Sf[:].rearrange("p (h d) -> p h d", d=64),
                    in1=bc(dec[:].unsqueeze(2), [128, 32, 64]), op=OP.mult), [g.Sf, dec], [g.Sf])
            for st_ in steps + [t1, t2b, t3, t4]:
                st_()
            if k + 2 < NSLOT:
                do_prep_a(k + 2)
            if kind == "l":
                u = ub[i % 2]
                for hf in range(2):
                    def mm(e, hf=hf):
                        ins = None
                        for kb in range(8):
                            ins = e.matmul(pf[hf][:], lhsT=hT[:, kb, 1:129], rhs=Wf[:, kb, hf * 512:(hf + 1) * 512],
                                           start=(kb == 0), stop=(kb == 7))
                        return ins
                    P.op("pe", mm, [hT, Wf], [pf[hf]])
                    P.op("dve", lambda e, hf=hf, u=u: e.tensor_copy(out=u[:, hf * 512:(hf + 1) * 512], in_=pf[hf][:]),
                         [pf[hf]], [u])
                P.dma("sp", g.U[i], u[:], reads=[u])
            fm_proj_conv(g, c, s, hT, Wxb, 20, 0, [])
            if k + 2 < NSLOT:
                do_prep_b(k + 2)
            to_token_major(g, c, s, 20)
            x3 = s.xtok[:, 0:2048].rearrange("p (h d) -> p h d", d=64)
            P.op("dve", lambda e: e.tensor_tensor(out=xdd[1][:].rearrange("p (h d) -> p h d", d=64), in0=x3,
                                                  in1=bc(wxb[:, 32:64].unsqueeze(2), [128, 32, 64]), op=OP.mult),
                 [s.xtok, wxb], [xdd[1]])
            P.op("dve", lambda e: e.tensor_tensor(out=xdd[0][:].rearrange("p (h d) -> p h d", d=64), in0=x3,
                                                  in1=bc(wxb[:, 0:32].unsqueeze(2), [128, 32, 64]), op=OP.mult),
                 [s.xtok, wxb], [xdd[0]])
            banks = [s.pxa[0], s.pxa[1], s.pxc[0], s.pH]
            for di, (xd_, S_) in enumerate(((xdd[1], g.Sb), (xdd[0], g.Sf))):
                for gi in range(4):
                    pst = banks[gi]
                    P.op("pe", lambda e, gi=gi, pst=pst, xd_=xd_: e.matmul(
                        pst[:], lhsT=s.xtok[:, 2048 + gi * 128:2048 + (gi + 1) * 128],
                        rhs=xd_[:, gi * 512:(gi + 1) * 512], start=True, stop=True), [s.xtok, xd_], [pst])
                    P.op("dve", lambda e, gi=gi, pst=pst, S_=S_: e.tensor_tensor(
                        out=S_[:, gi * 512:(gi + 1) * 512], in0=S_[:, gi * 512:(gi + 1) * 512], in1=pst[:], op=OP.add),
                        [S_, pst], [S_])
        if DEBUG:
            P.dma("sp", g.SFB[0], g.Sf[:], reads=[g.Sf])
            P.dma("sp", g.SFB[1], g.Sb[:], reads=[g.Sb])
        P.barrier()


def load_w_gen(g, W, src, nkb, ncols):
    wv = src.rearrange("(kb p) n -> p kb n", p=128)
    for a in range(0, ncols, 512):
        n = min(512, ncols - a)
        g.P.dma("pool", W[:, :, a:a + n], wv[:, :, a:a + n], writes=[W])


def phase_fnet(g):
    nc, P = g.nc, g.P
    with ExitStack() as es:
        T1 = sb(g, es, "T1", [64, 128], BF16)
        P.dma("pool", T1[:], g.t1, writes=[T1])
        V = [sb(g, es, "V%d" % i, [64, 4, D], BF16) for i in range(2)]
        Yt = [sb(g, es, "Yt%d" % i, [128, 4, D], BF16) for i in range(2)]
        p1 = [ps(g, es, "p1_%d" % i, [128, 512], F32) for i in range(4)]
        cnt = 0
        for tg in range(32):
            v, yt = V[tg % 2], Yt[tg % 2]
            P.dma("sp", v[:], g.U[:, tg * 4:(tg + 1) * 4, :], writes=[v])
            for t in range(4):
                for hf in range(2):
                    pp = p1[cnt % 4]
                    P.op("pe", lambda e, pp=pp, t=t, hf=hf, v=v: e.matmul(
                        pp[:], lhsT=T1[:], rhs=v[:, t, hf * 512:(hf + 1) * 512], start=True, stop=True), [T1, v], [pp])
                    if cnt % 2:
                        P.op("act", lambda e, pp=pp, t=t, hf=hf, yt=yt: e.activation(
                            out=yt[:, t, hf * 512:(hf + 1) * 512], in_=pp[:], func=AF.Copy), [pp], [yt])
                    else:
                        P.op("dve", lambda e, pp=pp, t=t, hf=hf, yt=yt: e.tensor_copy(
                            out=yt[:, t, hf * 512:(hf + 1) * 512], in_=pp[:]), [pp], [yt])
                    cnt += 1
            P.dma("sp", g.Y[:, tg * 4:(tg + 1) * 4, :], yt[:], reads=[yt])
        P.barrier()
    with ExitStack() as es:
        T2 = sb(g, es, "T2", [128, 2 * 64 * 68], BF16)
        for a in range(0, 2 * 64 * 68, 1088):
            P.dma("pool", T2[:, a:a + 1088], g.t2[:, a:a + 1088], writes=[T2])
        Yk = [sb(g, es, "Yk%d" % i, [128, 2, D], BF16) for i in range(2)]
        XTs = sb(g, es, "XTs", [128, 8, 2, EXT], BF16)
        p2f = [ps(g, es, "p2_%d" % i, [128, 512], F32) for i in range(4)]
        yv = g.Y.rearrange("(ri k) t c -> k t ri c", ri=2)
        xv = XTs[:].rearrange("p c r (j k) -> p c r j k", k=64)
        for k1 in range(64):
            yk = Yk[k1 % 2]
            P.dma("sp", yk[:], yv[k1], writes=[yk])
            for cg in range(2):
                ppb = p2f[(k1 * 2 + cg) % 4]
                pp = ppb[:, 0:272].rearrange("p (c k) -> p c k", k=68)

                def mm(e, pp=pp, cg=cg, yk=yk, k1=k1):
                    ins = None
                    for cb in range(4):
                        cbx = cg * 4 + cb
                        e.matmul(pp[:, cb, :], lhsT=yk[:, 0, cbx * 128:(cbx + 1) * 128],
                                 rhs=T2[:, k1 * 68:(k1 + 1) * 68], start=True, stop=False)
                        ins = e.matmul(pp[:, cb, :], lhsT=yk[:, 1, cbx * 128:(cbx + 1) * 128],
                                       rhs=T2[:, (64 + k1) * 68:(64 + k1 + 1) * 68], start=False, stop=True)
                    return ins
                P.op("pe", mm, [yk, T2], [ppb])
                for ri in range(2):
                    if (k1 + cg) % 2:
                        P.op("act", lambda e, pp=pp, cg=cg, ri=ri, k1=k1: e.activation(
                            out=xv[:, cg * 4:(cg + 1) * 4, ri, :, k1], in_=pp[:, :, ri * 34:(ri + 1) * 34],
                            func=AF.Copy), [ppb], [XTs])
                    else:
                        P.op("dve", lambda e, pp=pp, cg=cg, ri=ri, k1=k1: e.tensor_copy(
                            out=xv[:, cg * 4:(cg + 1) * 4, ri, :, k1], in_=pp[:, :, ri * 34:(ri + 1) * 34]),
                            [ppb], [XTs])
        for cb in range(8):
            P.dma("sp", g.XT[:, cb * 2 * EXT:(cb + 1) * 2 * EXT].rearrange("p (r t) -> p r t", r=2), XTs[:, cb, :, :],
                  reads=[XTs])
        P.barrier()


def phase_own(g, d):
    nc, P = g.nc, g.P
    with ExitStack() as es:
        c = alloc_chunk_bufs(g, es, 24)
        s = Ctx()
        hH = sb(g, es, "hH", [128, 8, 130], BF16)
        W = sb(g, es, "Wxbc", [128, 8, 3072], BF16)
        Wdt = sb(g, es, "Wdt", [128, 8, 32], BF16)
        load_w_cols(g, W, 1024, 3072)
        load_w_cols(g, Wdt, 6144 + 32 * d, 32)
        s.pxa = [ps(g, es, "pxa%d" % i, [128, 512], F32) for i in range(1)]
        s.pxc = [ps(g, es, "pxc%d" % i, [128, 512], F32) for i in range(1)]
        s.pxb = [s.pxa[0], s.pxc[0]]
        s.pre = [sb(g, es, "pre%d" % i, [128, 3, 130], BF16) for i in range(2)]
        s.pD = ps(g, es, "pD", [128, 512], F32)
        s.pH = ps(g, es, "pH", [128, 512], F32)
        psc = ps(g, es, "psc", [128, 4, 128], F32)
        pL = [ps(g, es, "pL%d" % i, [128, 4, 128], F32) for i in range(2)]
        s.xcs = sb(g, es, "xcs", [128, 24, 128], BF16, nparts=8)
        s.xtok = sb(g, es, "xtok", [128, 2560], BF16)
        s.dtm = sb(g, es, "dtm", [128, 32], F32)
        s.la = sb(g, es, "la", [128, 32], F32)
        s.wx = sb(g, es, "wx", [128, 32], F32)
        lab = sb(g, es, "lab", [128, 32], BF16)
        nlab = sb(g, es, "nlab", [128, 32], BF16)
        ecum = sb(g, es, "ecum", [128, 32], F32)
        dec = sb(g, es, "dec", [128, 32], F32)
        xd = sb(g, es, "xd", [128, 2048], BF16)
        xdd = sb(g, es, "xdd", [128, 2048], BF16)
        Sbf = sb(g, es, "Sbf", [128, 2048], BF16)
        Dt = [sb(g, es, "Dt%d" % i, [128, 8, 128], BF16) for i in range(2)]
        Lx = [sb(g, es, "Lx%d" % i, [128, 8, 128], BF16) for i in range(2)]
        G = [sb(g, es, "G%d" % i, [128, 8, 128], BF16) for i in range(2)]
        yo = sb(g, es, "yo", [128, 512], F32)
        ytile = sb(g, es, "ytile", [128, 2048], BF16)
        tmp = sb(g, es, "tmp", [128, 2048], BF16)
        yfl = sb(g, es, "yfl", [128, 2048], BF16)
        S = g.Sf if d == 0 else g.Sb
        mxk = "le" if d == 0 else "ge"
        sgk = "gt" if d == 0 else "lt"
        penk = "pen_f" if d == 0 else "pen_b"
        order = list(range(NEXT)) if d == 0 else list(range(NEXT - 1, -1, -1))

        prep(g, c, 0, g.xext[EXT:EXT + 128, :], 0, 1, vmask=cbcv(g, "halo_v"), hT=hH)

        def do_prep_a(ci):
            prep_a(g, c, ci, g.xext[ci * 128:(ci + 1) * 128, :])

        def do_prep_b(ci, prev_ci):
            hT = c.hTe[ci % 3]
            vm = None
            if ci == 0:
                vm = cbcv(g, "emask_bc", 0, 128)
            if ci == NEXT - 1:
                vm = cbcv(g, "emask_bc", 128, 128)
            prep_b(g, c, ci, 0, 1, vmask=vm)
            if prev_ci is not None:
                nb = c.hTe[prev_ci % 3]
                if ci == prev_ci + 1:
                    P.op("pool", lambda e: e.tensor_copy(out=hT[:, :, 0:1], in_=nb[:, :, 128:129]), [nb], [hT])
                    P.op("pool", lambda e: e.tensor_copy(out=nb[:, :, 129:130], in_=hT[:, :, 1:2]), [hT], [nb])
                else:
                    P.op("pool", lambda e: e.tensor_copy(out=hT[:, :, 129:130], in_=nb[:, :, 1:2]), [nb], [hT])
                    P.op("pool", lambda e: e.tensor_copy(out=nb[:, :, 0:1], in_=hT[:, :, 128:129]), [hT], [nb])
            if ci == 0:
                P.op("pool", lambda e: e.tensor_copy(out=hT[:, :, 0:1], in_=hH[:, :, 1:2]), [hH], [hT])
            if ci == NEXT - 1:
                P.op("pool", lambda e: e.tensor_copy(out=hT[:, :, 129:130], in_=hH[:, :, 2:3]), [hH], [hT])

        do_prep_a(order[0]); do_prep_b(order[0], None)
        do_prep_a(order[1]); do_prep_b(order[1], order[0])
        for oi, ci in enumerate(order):
            hT = c.hTe[ci % 3]
            if d == 1:
                P.dma("sp", yfl[:], g.YF[ci], writes=[yfl])
            mask_ap = bc(cpp(g, "emask", ci).unsqueeze(2), [128, 1, 32])
            steps = dt_steps(g, s, hT, Wdt, 32, cbcv(g, "dt_bias", 32 * d, 32), g.nega[:, 32 * d:32 * (d + 1)], mask_ap)

            def u1():
                P.op("pool", lambda e: e.tensor_copy(out=lab[:], in_=s.la[:]), [s.la], [lab])
                P.op("pool", lambda e: e.tensor_scalar(out=nlab[:], in0=lab[:], scalar1=-1.0, scalar2=None, op0=OP.mult),
                     [lab], [nlab])

                def segs(e):
                    e.matmul(s.pD[:, 64:96], lhsT=mk32(g, mxk), rhs=s.la[:], start=True, stop=True)
                    e.matmul(s.pD[:, 96:128], lhsT=mk32(g, sgk), rhs=s.la[:], start=True, stop=True)
                    return e.matmul(s.pD[:, 128:160], lhsT=mk32(g, "ones"), rhs=s.la[:], start=True, stop=True)
                P.op("pe", segs, [s.la, g.cmk_t], [s.pD])

            def u2():
                P.op("act", lambda e: e.activation(out=ecum[:], in_=s.pD[:, 64:96], func=AF.Exp), [s.pD], [ecum])
                P.op("act", lambda e: e.activation(out=s.wx[:], in_=s.pD[:, 96:128], func=AF.Exp), [s.pD], [s.wx])
                P.op("act", lambda e: e.activation(out=dec[:], in_=s.pD[:, 128:160], func=AF.Exp), [s.pD], [dec])

            def u3():
                P.op("pool", lambda e: e.tensor_tensor(out=s.wx[:], in0=s.wx[:], in1=s.dtm[:], op=OP.mult),
                     [s.wx, s.dtm], [s.wx])
            for st_ in steps + [u1, u2, u3]:
                st_()
            if oi + 2 < NEXT:
                do_prep_a(order[oi + 2])
            fm_proj_conv(g, c, s, hT, W, 24, 0, [])
            if oi + 2 < NEXT:
                do_prep_b(order[oi + 2], order[oi + 1])
            to_token_major(g, c, s, 20)
            x3 = s.xtok[:, 0:2048].rearrange("p (h d) -> p h d", d=64)
            P.op("dve", lambda e: e.tensor_tensor(out=xd[:].rearrange("p (h d) -> p h d", d=64), in0=x3,
                                                   in1=bc(s.dtm[:].unsqueeze(2), [128, 32, 64]), op=OP.mult),
                 [s.xtok, s.dtm], [xd])
            P.op("dve", lambda e: e.tensor_tensor(out=xdd[:].rearrange("p (h d) -> p h d", d=64), in0=x3,
                                                   in1=bc(s.wx[:].unsqueeze(2), [128, 32, 64]), op=OP.mult),
                 [s.xtok, s.wx], [xdd])
            P.op("act", lambda e: e.activation(out=Sbf[:], in_=S[:], func=AF.Copy), [S], [Sbf])

            def sc(e):
                ins = None
                for gi in range(4):
                    ins = e.matmul(psc[:, gi, :], lhsT=s.xcs[:, 16 + gi, :], rhs=s.xcs[:, 20 + gi, :], start=True, stop=True)
                return ins
            P.op("pe", sc, [s.xcs], [psc])
            for gi in range(4):
                dt_, lx, gg = Dt[gi % 2], Lx[gi % 2], G[gi % 2]
                P.op("pool", lambda e, gi=gi, dt_=dt_: e.tensor_tensor(
                    out=dt_[:], in0=bc(lab[:, gi * 8:(gi + 1) * 8].unsqueeze(2), [128, 8, 128]),
                    in1=bc(mk16(g, mxk).unsqueeze(1), [128, 8, 128]), op=OP.mult), [lab, g.cmkb_t], [dt_])
                for hh in range(2):
                    def mmL(e, gi=gi, hh=hh, dt_=dt_):
                        e.matmul(pL[hh][:], lhsT=mk16(g, "ones"), rhs=dt_[:, hh * 4:(hh + 1) * 4, :], start=True, stop=False)
                        e.matmul(pL[hh][:], lhsT=mk16(g, mxk),
                                 rhs=bc(nlab[:, gi * 8 + hh * 4:gi * 8 + hh * 4 + 4].unsqueeze(2), [128, 4, 128]),
                                 start=False, stop=False)
                        return e.matmul(pL[hh][:], lhsT=mk16(g, "ident"),
                                        rhs=bc(mk16(g, penk).unsqueeze(1), [128, 4, 128]), start=False, stop=True)
                    P.op("pe", mmL, [dt_, nlab, g.cmkb_t], [pL[hh]])
                    P.op("act", lambda e, hh=hh, lx=lx: e.activation(out=lx[:, hh * 4:(hh + 1) * 4, :], in_=pL[hh][:],
                                                                   func=AF.Exp), [pL[hh]], [lx])
                P.op("dve", lambda e, gi=gi, lx=lx, gg=gg: e.tensor_tensor(
                    out=gg[:], in0=lx[:], in1=bc(psc[:, gi, :].unsqueeze(1), [128, 8, 128]), op=OP.mult),
                    [lx, psc], [gg])

                def mmy(e, gi=gi, gg=gg):
                    ins = None
                    for h in range(8):
                        hh = gi * 8 + h
                        ins = e.matmul(s.pH[:, h * 64:(h + 1) * 64], lhsT=gg[:, h, :], rhs=xd[:, hh * 64:(hh + 1) * 64],
                                       start=True, stop=True)
                    return ins
                P.op("pe", mmy, [gg, xd], [s.pH])
                P.op("pe", lambda e, gi=gi: e.matmul(s.pxb[0][:], lhsT=s.xcs[:, 20 + gi, :],
                                                     rhs=Sbf[:, gi * 512:(gi + 1) * 512], start=True, stop=True),
                     [s.xcs, Sbf], [s.pxb[0]])
                P.op("dve", lambda e, gi=gi: e.tensor_tensor(
                    out=yo[:].rearrange("p (h d) -> p h d", d=64), in0=s.pxb[0][:].rearrange("p (h d) -> p h d", d=64),
                    in1=bc(ecum[:, gi * 8:(gi + 1) * 8].unsqueeze(2), [128, 8, 64]), op=OP.mult),
                    [s.pxb[0], ecum], [yo])
                P.op("dve", lambda e, gi=gi: e.tensor_tensor(out=ytile[:, gi * 512:(gi + 1) * 512], in0=yo[:],
                                                              in1=s.pH[:], op=OP.add), [yo, s.pH], [ytile])
                P.op("pe", lambda e, gi=gi: e.matmul(s.pxb[1][:], lhsT=s.xtok[:, 2048 + gi * 128:2048 + (gi + 1) * 128],
                                                     rhs=xdd[:, gi * 512:(gi + 1) * 512], start=True, stop=True),
                     [s.xtok, xdd], [s.pxb[1]])
                P.op("dve", lambda e, gi=gi: e.tensor_tensor(
                    out=S[:, gi * 512:(gi + 1) * 512].rearrange("p (h d) -> p h d", d=64),
                    in0=S[:, gi * 512:(gi + 1) * 512].rearrange("p (h d) -> p h d", d=64),
                    in1=bc(dec[:, gi * 8:(gi + 1) * 8].unsqueeze(2), [128, 8, 64]), op=OP.mult), [S, dec], [S])
                P.op("dve", lambda e, gi=gi: e.tensor_tensor(out=S[:, gi * 512:(gi + 1) * 512],
                                                              in0=S[:, gi * 512:(gi + 1) * 512], in1=s.pxb[1][:],
                                                              op=OP.add), [S, s.pxb[1]], [S])
            if d == 0:
                P.op("pool", lambda e: e.tensor_tensor(out=tmp[:].rearrange("p (h d) -> p h d", d=64), in0=x3,
                                                       in1=bc(cbcv(g, "d_skip").unsqueeze(2), [128, 32, 64]),
                                                       op=OP.mult), [s.xtok, g.cbc_t], [tmp])
                P.op("pool", lambda e: e.tensor_tensor(out=tmp[:], in0=tmp[:], in1=ytile[:], op=OP.add),
                     [tmp, ytile], [tmp])
                P.dma("sp", g.YF[ci], tmp[:], reads=[tmp])
            else:
                P.op("pool", lambda e: e.tensor_tensor(out=tmp[:], in0=yfl[:], in1=ytile[:], op=OP.add),
                     [yfl, ytile], [tmp])
                P.dma("sp", g.YT[ci], tmp[:], reads=[tmp])
        P.barrier()


def phase_merge(g):
    phase_merge_a(g)
    phase_merge_b(g)


def phase_merge_a(g):
    nc, P = g.nc, g.P
    with ExitStack() as es:
        c = alloc_chunk_bufs(g, es, 0)
        Wz = sb(g, es, "Wz", [128, 8, 2048], BF16)
        Wgs = sb(g, es, "Wgs", [128, 8, 1024], BF16)
        Wsb = sb(g, es, "Wsb", [128, 16, 1024], BF16)
        load_w_cols(g, Wz, 4096, 2048)
        load_w_cols(g, Wgs, 7232, 1024)
        load_w_gen(g, Wsb, g.w_sb, 16, 1024)
        sg = sb(g, es, "ssdg", [128, 2048], F32)
        P.dma("sp", sg[:], g.cbg[:, CBG["ssd_g"][0]:CBG["ssd_g"][0] + 2048], writes=[sg])
        pz = ps(g, es, "pz", [128, 512], F32)
        pb0 = ps(g, es, "pb0", [128, 2, 512], F32)
        pb1 = ps(g, es, "pb1", [128, 2, 512], F32)
        yt = [sb(g, es, "yt%d" % i, [128, 2048], BF16) for i in range(2)]
        zs = sb(g, es, "zs", [128, 512], F32)
        t = sb(g, es, "t", [128, 512], F32)
        jk = sb(g, es, "jk", [128, 512], BF16)
        st2 = sb(g, es, "st2", [128, 16], F32)
        ysn = sb(g, es, "ysn", [128, 512], BF16)
        ysnT = sb(g, es, "ysnT", [128, 16, 128], BF16)
        sgs = sb(g, es, "sgs", [128, 1024], F32)
        ms = [sb(g, es, "ms%d" % i, [128, 1024], BF16) for i in range(2)]
        for ci in range(NEXT):
            hT = c.hTe[ci % 3]
            prep(g, c, ci, g.xext[ci * 128:(ci + 1) * 128, :], 0, 1)
            y = yt[ci % 2]
            P.dma("sp", y[:], g.YT[ci], writes=[y])
            for gi in range(4):
                def mm(e, gi=gi):
                    ins = None
                    for kb in range(8):
                        ins = e.matmul(pz[:], lhsT=hT[:, kb, 1:129], rhs=Wz[:, kb, gi * 512:(gi + 1) * 512],
                                       start=(kb == 0), stop=(kb == 7))
                    return ins
                P.op("pe", mm, [hT, Wz], [pz])
                P.op("act", lambda e: e.activation(out=zs[:], in_=pz[:], func=AF.Silu), [pz], [zs])
                P.op("dve", lambda e, gi=gi: e.tensor_tensor(out=t[:], in0=zs[:], in1=y[:, gi * 512:(gi + 1) * 512],
                                                              op=OP.mult), [zs, y], [t])
                P.op("act", lambda e, gi=gi: e.activation(out=jk[:], in_=t[:], func=AF.Square,
                                                          accum_out=st2[:, gi:gi + 1]), [t], [jk, st2])
                P.op("dve", lambda e, gi=gi: e.tensor_scalar(out=st2[:, 4 + gi:5 + gi], in0=st2[:, gi:gi + 1],
                                                              scalar1=1.0 / 512, scalar2=EPS, op0=OP.mult, op1=OP.add),
                     [st2], [st2])
                P.op("act", lambda e, gi=gi: e.activation(out=st2[:, 8 + gi:9 + gi], in_=st2[:, 4 + gi:5 + gi],
                                                          func=AF.Sqrt), [st2], [st2])
                P.op("dve", lambda e, gi=gi: e.reciprocal(out=st2[:, 12 + gi:13 + gi], in_=st2[:, 8 + gi:9 + gi]),
                     [st2], [st2])
                P.op("dve", lambda e, gi=gi: e.scalar_tensor_tensor(
                    out=ysn[:], in0=t[:], scalar=st2[:, 12 + gi:13 + gi], in1=sg[:, gi * 512:(gi + 1) * 512],
                    op0=OP.mult, op1=OP.mult), [t, st2, sg], [ysn])

                def tr(e):
                    ins = None
                    for j in range(4):
                        ins = e.transpose(out=c.pA[:, j, :], in_=ysn[:, j * 128:(j + 1) * 128], identity=mk16(g, "ident"))
                    return ins
                P.op("pe", tr, [ysn, g.cmkb_t], [c.pA])
                P.op("act", lambda e, gi=gi: e.activation(out=ysnT[:, gi * 4:(gi + 1) * 4, :], in_=c.pA[:, 0:4, :],
                                                          func=AF.Copy), [c.pA], [ysnT])
            for hf in range(2):
                def mms(e, hf=hf):
                    ins = None
                    for kb in range(16):
                        ins = e.matmul(pb0[:, hf, :], lhsT=ysnT[:, kb, :], rhs=Wsb[:, kb, hf * 512:(hf + 1) * 512],
                                       start=(kb == 0), stop=(kb == 15))
                    return ins
                P.op("pe", mms, [ysnT, Wsb], [pb0])

                def mmg(e, hf=hf):
                    ins = None
                    for kb in range(8):
                        ins = e.matmul(pb1[:, hf, :], lhsT=hT[:, kb, 1:129], rhs=Wgs[:, kb, hf * 512:(hf + 1) * 512],
                                       start=(kb == 0), stop=(kb == 7))
                    return ins
                P.op("pe", mmg, [hT, Wgs], [pb1])
            P.op("act", lambda e: e.activation(out=sgs[:], in_=pb1[:].rearrange("p a b -> p (a b)"), func=AF.Sigmoid),
                 [pb1], [sgs])
            m = ms[ci % 2]
            P.op("dve", lambda e, m=m: e.tensor_tensor(out=m[:], in0=sgs[:], in1=pb0[:].rearrange("p a b -> p (a b)"),
                                                        op=OP.mult), [sgs, pb0], [m])
            P.dma("sp", g.MS[ci], m[:], reads=[m])
        P.barrier()


def phase_merge_b(g):
    nc, P = g.nc, g.P
    with ExitStack() as es:
        c = alloc_chunk_bufs(g, es, 0)
        Wgf = sb(g, es, "Wgf", [128, 8, 1024], BF16)
        Wfa = sb(g, es, "Wfa", [128, 8, 1024], BF16)
        Wo = sb(g, es, "Wo", [128, 8, 1024], BF16)
        Tcs = sb(g, es, "Tcs", [128, 256], BF16)
        load_w_cols(g, Wgf, 6208, 1024)
        load_w_gen(g, Wfa, g.w_fa, 8, 1024)
        load_w_gen(g, Wo, g.w_o, 8, 1024)
        P.dma("pool", Tcs[:], g.tcs, writes=[Tcs])
        pb0 = ps(g, es, "pb0", [128, 2, 512], F32)
        pb1 = ps(g, es, "pb1", [128, 2, 512], F32)
        pb2 = ps(g, es, "pb2", [128, 2, 512], F32)
        xtc = [sb(g, es, "xtc%d" % i, [128, 8, 2, 128], BF16) for i in range(2)]
        msl = [sb(g, es, "msl%d" % i, [128, 1024], BF16) for i in range(2)]
        mixT = sb(g, es, "mixT", [128, 8, 128], BF16)
        sgf = sb(g, es, "sgf", [128, 1024], F32)
        tmp = sb(g, es, "tmpm", [128, 1024], F32)
        mrg = sb(g, es, "mrg", [128, 1024], BF16)
        mrgT = sb(g, es, "mrgT", [128, 8, 128], BF16)
        l1 = [sb(g, es, "l1_%d" % i, [128, 1024], F32) for i in range(2)]
        xtv = g.XT.rearrange("p (c r t) -> p c r t", c=8, r=2)
        for ci in range(NEXT):
            hT = c.hTe[ci % 3]
            xt = prep(g, c, ci, g.xext[ci * 128:(ci + 1) * 128, :], 0, 1)
            xc_, m = xtc[ci % 2], msl[ci % 2]
            P.dma("sp", xc_[:], xtv[:, :, :, ci * 128:(ci + 1) * 128], writes=[xc_])
            P.dma("sp", m[:], g.MS[ci], writes=[m])
            for cg in range(2):
                def mmx(e, cg=cg):
                    e.matmul(pb2[:, cg, :], lhsT=Tcs[:, 0:128], rhs=xc_[:, cg * 4:(cg + 1) * 4, 0, :], start=True, stop=False)
                    return e.matmul(pb2[:, cg, :], lhsT=Tcs[:, 128:256], rhs=xc_[:, cg * 4:(cg + 1) * 4, 1, :],
                                    start=False, stop=True)
                P.op("pe", mmx, [Tcs, xc_], [pb2])
            P.op("act", lambda e: e.activation(out=mixT[:].rearrange("p a b -> p (a b)"),
                                               in_=pb2[:].rearrange("p a b -> p (a b)"), func=AF.Copy), [pb2], [mixT])
            for hf in range(2):
                def mmf(e, hf=hf):
                    ins = None
                    for kb in range(8):
                        ins = e.matmul(pb0[:, hf, :], lhsT=mixT[:, kb, :], rhs=Wfa[:, kb, hf * 512:(hf + 1) * 512],
                                       start=(kb == 0), stop=(kb == 7))
                    return ins
                P.op("pe", mmf, [mixT, Wfa], [pb0])

                def mmg(e, hf=hf):
                    ins = None
                    for kb in range(8):
                        ins = e.matmul(pb1[:, hf, :], lhsT=hT[:, kb, 1:129], rhs=Wgf[:, kb, hf * 512:(hf + 1) * 512],
                                       start=(kb == 0), stop=(kb == 7))
                    return ins
                P.op("pe", mmg, [hT, Wgf], [pb1])
            P.op("act", lambda e: e.activation(out=sgf[:], in_=pb1[:].rearrange("p a b -> p (a b)"), func=AF.Sigmoid),
                 [pb1], [sgf])
            P.op("dve", lambda e: e.tensor_tensor(out=tmp[:], in0=sgf[:], in1=pb0[:].rearrange("p a b -> p (a b)"),
                                                   op=OP.mult), [sgf, pb0], [tmp])
            P.op("dve", lambda e, m=m: e.tensor_tensor(out=mrg[:], in0=tmp[:], in1=m[:], op=OP.add), [tmp, m], [mrg])

            def tr(e):
                ins = None
                for kb in range(8):
                    ins = e.transpose(out=c.pA[:, kb, :], in_=mrg[:, kb * 128:(kb + 1) * 128], identity=mk16(g, "ident"))
                return ins
            P.op("pe", tr, [mrg, g.cmkb_t], [c.pA])
            P.op("act", lambda e: e.activation(out=mrgT[:], in_=c.pA[:], func=AF.Copy), [c.pA], [mrgT])
            for hf in range(2):
                def mmo(e, hf=hf):
                    ins = None
                    for kb in range(8):
                        ins = e.matmul(pb2[:, hf, :], lhsT=mrgT[:, kb, :], rhs=Wo[:, kb, hf * 512:(hf + 1) * 512],
                                       start=(kb == 0), stop=(kb == 7))
                    return ins
                P.op("pe", mmo, [mrgT, Wo], [pb2])
            l = l1[ci % 2]
            P.op("dve", lambda e, l=l: e.tensor_tensor(out=l[:], in0=pb2[:].rearrange("p a b -> p (a b)"),
                                                        in1=g.gbc[:, 0:D], op=OP.mult), [pb2, g.gbc], [l])
            P.op("pool", lambda e, l=l, xt=xt: e.tensor_tensor(out=l[:], in0=l[:], in1=xt[:], op=OP.add), [l, xt], [l])
            P.dma("sp", g.L1[ci], l[:], reads=[l])
        P.barrier()


def phase_ffn(g):
    nc, P = g.nc, g.P
    with ExitStack() as es:
        c = alloc_chunk_bufs(g, es, 0)
        h2T = sb(g, es, "h2T", [128, 8, EXT], BF16, nparts=NEXT)
        Wd = sb(g, es, "Wd", [128, NFB, 1024], BF16)
        load_w_gen(g, Wd, g.w_down, NFB, 1024)
        fg = sb(g, es, "fg", [128, 1024], F32)
        P.dma("sp", fg[:], g.cbg[:, CBG["final_g"][0]:CBG["final_g"][0] + 1024], writes=[fg])
        for ci in range(NEXT):
            i2 = ci % 2
            xt, st, xn = c.xt[i2], c.st[i2], c.xn[i2]
            P.dma("sp", xt[:], g.L1[ci], writes=[xt])
            P.op("act", lambda e: e.activation(out=c.junk[:], in_=xt[:], func=AF.Square, accum_out=st[:, 0:1]),
                 [xt], [c.junk, st])
            P.op("dve", lambda e: e.tensor_scalar(out=st[:, 1:2], in0=st[:, 0:1], scalar1=1.0 / D, scalar2=EPS,
                                                   op0=OP.mult, op1=OP.add), [st], [st])
            P.op("act", lambda e: e.activation(out=st[:, 2:3], in_=st[:, 1:2], func=AF.Sqrt), [st], [st])
            P.op("dve", lambda e: e.reciprocal(out=st[:, 3:4], in_=st[:, 2:3]), [st], [st])
            P.op("act", lambda e: e.activation(out=xn[:], in_=xt[:], func=AF.Copy, scale=st[:, 3:4]), [xt, st], [xn])

            def tr(e):
                ins = None
                for kb in range(8):
                    ins = e.transpose(out=c.pA[:, kb, :], in_=xn[:, kb * 128:(kb + 1) * 128], identity=mk16(g, "ident"))
                return ins
            P.op("pe", tr, [xn, g.cmkb_t], [c.pA])
            for kb in range(8):
                P.op("dve", lambda e, kb=kb, ci=ci: e.tensor_scalar(
                    out=h2T[:, kb, ci * 128:(ci + 1) * 128], in0=c.pA[:, kb, :], scalar1=modA(g, 4, kb),
                    scalar2=modA(g, 5, kb), op0=OP.mult, op1=OP.add), [c.pA, g.modv], [(h2T, ci)])
            if ci in (0, NEXT - 1):
                vm = cbcv(g, "emask_bc", 0 if ci == 0 else 128, 128)
                P.op("pool", lambda e, ci=ci, vm=vm: e.tensor_tensor(
                    out=h2T[:, :, ci * 128:(ci + 1) * 128], in0=h2T[:, :, ci * 128:(ci + 1) * 128],
                    in1=bc(vm.unsqueeze(1), [128, 8, 128]), op=OP.mult), [(h2T, ci), g.cbc_t], [(h2T, ci)])
        NB = 4
        pu = [ps(g, es, "pu%d" % i, [128, 512], F32) for i in range(2)]
        pd = ps(g, es, "pd", [128, 2, 512], F32)
        aT = sb(g, es, "aT", [128, NFB, 512], BF16, nparts=NFB)
        wu = [sb(g, es, "wu%d" % i, [128, 8, 2, 128], BF16) for i in range(3)]
        ug = [sb(g, es, "ug%d" % i, [128, 10, 64], F32) for i in range(2)]
        acc = [sb(g, es, "acc%d" % i, [128, 8, 64], F32) for i in range(2)]
        sgl = sb(g, es, "sgl", [128, 512], F32)
        lt = [sb(g, es, "lt%d" % i, [128, 1024], F32) for i in range(2)]
        yy = [sb(g, es, "yy%d" % i, [128, 1024], F32) for i in range(2)]
        jk = c.junk
        st = [sb(g, es, "stf%d" % i, [128, 4], F32) for i in range(2)]
        wuv = g.w_up.rearrange("(kb p) (gv n) -> p kb gv n", p=128, gv=2)
        l1f = g.L1.rearrange("c p d -> (c p) d")
        cnt = 0
        nitem = NB * NFB

        def issue_w(i):
            if i < nitem:
                fb_ = i % NFB
                w_ = wu[i % 3]
                for gv_ in range(2):
                    P.dma("pool", w_[:, :, gv_, :], wuv[:, :, gv_, fb_ * 128:(fb_ + 1) * 128], writes=[w_])
        issue_w(0)
        issue_w(1)
        for blk in range(NB):
            base = blk * 512
            hparts = [(h2T, i) for i in range(base // 128, (base + 640 + 127) // 128)]
            for fb in range(NFB):
                w = wu[cnt % 3]
                issue_w(cnt + 2)
                cnt += 1
                for gv in range(2):
                    u = ug[gv]
                    a = acc[gv]
                    for j in range(2):
                        def mm(e, j=j, gv=gv, w=w):
                            ins = None
                            for kb in range(8):
                                ins = e.matmul(pu[j][:, 0:320], lhsT=w[:, kb, gv, :],
                                               rhs=h2T[:, kb, base + j * 320:base + (j + 1) * 320],
                                               start=(kb == 0), stop=(kb == 7))
                            return ins
                        P.op("pe", mm, [w] + hparts, [pu[j]])
                        P.op("act", lambda e, j=j, u=u: e.activation(
                            out=u[:].rearrange("p r c -> p (r c)")[:, j * 320:(j + 1) * 320], in_=pu[j][:, 0:320],
                            func=AF.Copy), [pu[j]], [u])
                    cf = gv * NFB + fb
                    wt = lambda t: cpp(g, "cw_ffn", t * 44 + cf)
                    P.op("act", lambda e, u=u, a=a, cf=cf: e.activation(
                        out=a[:], in_=u[:, 1:9, :], func=AF.Identity, scale=cpp(g, "cw_ffn", 4 * 44 + cf),
                        bias=cpp(g, "cb_ffn", cf)), [u, g.cpp_t], [a])
                    for kh in range(3):
                        for kw in range(3):
                            if kh == 1 and kw == 1:
                                continue
                            dy, dx = kh - 1, kw - 1
                            c0, c1 = max(0, -dx), 64 - max(0, dx)
                            P.op("dve", lambda e, u=u, a=a, dy=dy, dx=dx, c0=c0, c1=c1, t=kh * 3 + kw, cf=cf:
                                 e.scalar_tensor_tensor(out=a[:, :, c0:c1], in0=u[:, 1 + dy:9 + dy, c0 + dx:c1 + dx],
                                                        scalar=cpp(g, "cw_ffn", t * 44 + cf), in1=a[:, :, c0:c1],
                                                        op0=OP.mult, op1=OP.add), [u, a, g.cpp_t], [a])
                P.op("act", lambda e: e.activation(out=sgl[:], in_=acc[0][:].rearrange("p r c -> p (r c)"), func=AF.Silu),
                     [acc[0]], [sgl])
                P.op("dve", lambda e, fb=fb: e.tensor_tensor(out=aT[:, fb, :], in0=sgl[:],
                                                              in1=acc[1][:].rearrange("p r c -> p (r c)"), op=OP.mult),
                     [sgl, acc[1]], [(aT, fb)])
            for tcn in range(4):
                o0 = blk * 512 + tcn * 128
                i2 = (blk * 4 + tcn) % 2
                l, y, s4 = lt[i2], yy[i2], st[i2]
                o = y
                P.dma("sp", l[:], l1f[o0 + 64:o0 + 64 + 128, :], writes=[l])
                for hf in range(2):
                    def mmd(e, hf=hf, tcn=tcn):
                        ins = None
                        for fb in range(NFB):
                            ins = e.matmul(pd[:, hf, :], lhsT=aT[:, fb, tcn * 128:(tcn + 1) * 128],
                                           rhs=Wd[:, fb, hf * 512:(hf + 1) * 512], start=(fb == 0), stop=(fb == NFB - 1))
                        return ins
                    P.op("pe", mmd, [aT, Wd], [pd])
                P.op("dve", lambda e, y=y: e.tensor_tensor(out=y[:], in0=pd[:].rearrange("p a b -> p (a b)"),
                                                            in1=g.gbc[:, D:2 * D], op=OP.mult), [pd, g.gbc], [y])
                P.op("pool", lambda e, y=y, l=l: e.tensor_tensor(out=y[:], in0=y[:], in1=l[:], op=OP.add), [y, l], [y])
                P.op("act", lambda e, y=y, s4=s4: e.activation(out=jk[:], in_=y[:], func=AF.Square,
                                                               accum_out=s4[:, 0:1]), [y], [jk, s4])
                P.op("dve", lambda e, s4=s4: e.tensor_scalar(out=s4[:, 1:2], in0=s4[:, 0:1], scalar1=1.0 / D,
                                                              scalar2=EPS, op0=OP.mult, op1=OP.add), [s4], [s4])
                P.op("act", lambda e, s4=s4: e.activation(out=s4[:, 2:3], in_=s4[:, 1:2], func=AF.Sqrt), [s4], [s4])
                P.op("dve", lambda e, s4=s4: e.reciprocal(out=s4[:, 3:4], in_=s4[:, 2:3]), [s4], [s4])
                P.op("dve", lambda e, y=y, s4=s4, o=o: e.scalar_tensor_tensor(
                    out=o[:], in0=y[:], scalar=s4[:, 3:4], in1=fg[:], op0=OP.mult, op1=OP.mult), [y, s4, fg], [y])
                P.dma("sp", g.out[o0:o0 + 128, :], o[:], reads=[o])
        P.barrier()


def _pm(v):
    v = np.asarray(v, np.float32)
    return np.ascontiguousarray(v.reshape(-1, 128).T)


def _rb(v):
    v = np.asarray(v, np.float32).reshape(1, -1)
    return np.ascontiguousarray(np.broadcast_to(v, (128, v.shape[1])))


def _const_tables():
    k = np.arange(128)[:, None]
    m = np.arange(128)[None, :]
    mats = [np.ones((128, 128)), k <= m, k >= m, k > m, k < m, k == m,
            np.where(m < k, -BIG, 0.0), np.where(m > k, -BIG, 0.0)]
    cmk = np.concatenate([np.asarray(a, np.float32) for a in mats], axis=1)
    t1i = np.arange(64)[:, None] * np.arange(64)[None, :]
    th = 2 * np.pi * t1i / 64.0
    t1 = np.concatenate([np.cos(th), -np.sin(th)], axis=1).astype(np.float32)
    j = np.arange(128)[:, None] * np.arange(128)[None, :]
    thc = 2 * np.pi * j / 128.0
    tcs = (np.concatenate([np.cos(thc), np.sin(thc)], axis=1) / 1024.0).astype(np.float32)
    return cmk, t1, tcs


def _t2_tables(q):
    t2 = np.arange(128, dtype=np.float64)[:, None, None]
    k1 = np.arange(64, dtype=np.float64)[None, :, None]
    k2 = (32 * q - 1 + np.arange(34, dtype=np.float64))[None, None, :]
    kk = np.mod(k1 + 64 * k2, 8192)
    th = 2 * np.pi * np.mod(kk * t2, 8192) / 8192.0
    Mr, Mi = np.cos(th), -np.sin(th)
    ta = np.concatenate([Mr, Mi], axis=2)
    tb = np.concatenate([-Mi, Mr], axis=2)
    return np.concatenate([ta.reshape(128, -1), tb.reshape(128, -1)], axis=1).astype(np.float32)


_CACHE = {}


def kernel(x, c, ctx, c_ctx, w_mod, b_mod, norm1_g, w_in, conv_ssd_w, conv_ssd_b, dt_bias, a_log,
           d_skip, ssd_norm_g, w_fa, w_sb, w_o, norm2_g, w_up, conv_ffn_w, conv_ffn_b, w_down, final_g):
    f = lambda a: np.asarray(a, np.float32)
    x, c, ctx, c_ctx = f(x), f(c), f(ctx), f(c_ctx)
    cmk, t1, tcs = _const_tables()
    in_maps = []
    bm = f(b_mod)[0]
    for core in range(8):
        b, q = divmod(core, 4)
        e0 = 2048 * q - 64
        xext = np.zeros((EXT + 128, D), np.float32)
        lo, hi = max(e0, 0), min(e0 + EXT, SEQ)
        xext[lo - e0:hi - e0] = x[b, lo:hi]
        hv = np.zeros(2, np.float32)
        if e0 - 1 >= 0:
            xext[EXT] = x[b, e0 - 1]; hv[0] = 1
        if e0 + EXT < SEQ:
            xext[EXT + 1] = x[b, e0 + EXT]; hv[1] = 1
        tok = e0 + np.arange(EXT)
        valid = ((tok >= 0) & (tok < SEQ)).astype(np.float32)
        fmask = np.zeros((NSLOT, 128, 2), np.float32)
        fmask[0:2, :, 0] = 1
        fmask[66:68, :, 1] = 1
        lt = np.arange(SEQ).reshape(NCH, 128)
        fmask[2:66, :, 0] = (lt < e0)
        fmask[2:66, :, 1] = (lt >= e0 + EXT)
        cpp_a = np.zeros((128, CPP_N), np.float32)

        def put(key, arr):
            o, n = CPP[key]
            cpp_a[:, o:o + n] = arr
        put("c", _pm(c[b])); put("cctx", _pm(c_ctx)); put("bmod", _pm(bm))
        put("n1g", _pm(f(norm1_g)[0])); put("n2g", _pm(f(norm2_g)[0]))
        put("cw_ssd", np.concatenate([_pm(f(conv_ssd_w)[0, t]) for t in range(3)], axis=1))
        put("cb_ssd", _pm(f(conv_ssd_b)[0]))
        cfw = f(conv_ffn_w)[0].reshape(9, 2 * DFF)
        put("cw_ffn", np.concatenate([_pm(cfw[t]) for t in range(9)], axis=1))
        put("cb_ffn", _pm(f(conv_ffn_b)[0]))
        put("emask", valid.reshape(NEXT, 128).T)
        put("fmask", fmask.transpose(1, 0, 2).reshape(128, NSLOT * 2))
        cbc_a = np.zeros((128, CBC_N), np.float32)

        def putb(key, arr):
            o, n = CBC[key]
            cbc_a[:, o:o + n] = arr
        putb("dt_bias", _rb(f(dt_bias)[0].reshape(-1))); putb("a_log", _rb(f(a_log)[0].reshape(-1)))
        putb("d_skip", _rb(f(d_skip)[0]))
        cbg_a = np.concatenate([_rb(f(ssd_norm_g)[0]), _rb(f(final_g)), _rb(bm[2048:3072]), _rb(bm[5120:6144])], axis=1)
        putb("emask_bc", _rb(np.concatenate([valid[:128], valid[-128:]])))
        hvb = np.zeros(128, np.float32); hvb[0:2] = hv
        putb("halo_v", _rb(hvb))
        in_maps.append(dict(
            xb=np.ascontiguousarray(x[b]), ctxb=np.ascontiguousarray(ctx[b]), xext=xext,
            w_mod=f(w_mod)[0], w_in=f(w_in)[0], w_fa=f(w_fa)[0], w_sb=f(w_sb)[0], w_o=f(w_o)[0],
            w_up=f(w_up)[0], w_down=f(w_down)[0], cpp=cpp_a, cbc=cbc_a, cbg=cbg_a, cbrow=f(conv_ssd_b)[0].reshape(1, 3072).copy(), cmk=cmk, t1=t1, t2=_t2_tables(q), tcs=tcs))
    if "nc" not in _CACHE:
        _CACHE["nc"] = build_program()
    res = run_bass_kernel_spmd(_CACHE["nc"], in_maps, core_ids=list(range(8)))
    if DEBUG:
        _CACHE["res"] = res
    out = np.zeros((2, SEQ, D), np.float32)
    for core in range(8):
        b, q = divmod(core, 4)
        out[b, 2048 * q:2048 * (q + 1)] = res.results[core]["out"]
    return out
```

```python
import os
from contextlib import ExitStack
import numpy as np
import concourse.bass as bass
import concourse.mybir as mybir
from concourse.bass_utils import run_bass_kernel_spmd

F32 = mybir.dt.float32
BF16 = mybir.dt.bfloat16
AF = mybir.ActivationFunctionType
OP = mybir.AluOpType

D = 1024
SEQ = 8192
NCH = 64
NEXT = 17
EXT = NEXT * 128
EPS = 1e-6
BIG = 30000.0
NSLOT = 68
DFF = 2816
NFB = 22

STOP = os.environ.get("MK_STOP", "")
DEBUG = bool(STOP)


class Buf:
    def __init__(self, t, nparts=1):
        self.t = t
        self.n = nparts
        self.w = [None] * nparts
        self.r = [[] for _ in range(nparts)]
        self.excl = False

    def __getitem__(self, idx):
        return self.t[idx]


def _parts(items):
    out = []
    for it in items:
        if it is None:
            continue
        if isinstance(it, Buf):
            out.extend((it, i) for i in range(it.n))
        else:
            b, idx = it
            if isinstance(idx, int):
                out.append((b, idx))
            else:
                out.extend((b, i) for i in idx)
    return out


class Prog:
    def __init__(self, nc, es):
        self.nc = nc
        self.E = {}
        self.semid = 0
        for name, eng in (("pe", nc.tensor), ("act", nc.scalar), ("dve", nc.vector), ("pool", nc.gpsimd), ("sp", nc.sync)):
            sem = es.enter_context(nc.semaphore("s_" + name))
            self.E[name] = dict(name=name, eng=eng, sem=(self._sid(), sem), count=0, waited={}, pool=[], ndma=0)
        for name, n in (("sp", 8), ("pool", 6), ("act", 4)):
            for i in range(n):
                sem = es.enter_context(nc.semaphore("d_%s%d" % (name, i)))
                self.E[name]["pool"].append((self._sid(), sem))
        self.ninst = 0

    def _sid(self):
        self.semid += 1
        return self.semid

    def _wait(self, E, tok):
        (sid, sem), val, _ = tok
        if E["waited"].get(sid, 0) >= val:
            return
        E["eng"].wait_ge(sem, val)
        E["waited"][sid] = val

    def _collect(self, en, reads, writes):
        toks = []
        for b, i in _parts(reads):
            if b.w[i] is not None:
                toks.append(b.w[i])
            if b.excl:
                toks.extend(t for t in b.r[i] if t[2] != en)
        for b, i in _parts(writes):
            if b.w[i] is not None:
                toks.append(b.w[i])
            toks.extend(b.r[i])
        res = []
        for t in toks:
            if en == "pe" and t[2] == "pe":
                continue
            res.append(t)
        return res

    def _update(self, reads, writes, tok):
        for b, i in _parts(reads):
            b.r[i].append(tok)
            if len(b.r[i]) > 24:
                last = {}
                for t in b.r[i]:
                    k = t[0][0]
                    if k not in last or last[k][1] < t[1]:
                        last[k] = t
                b.r[i] = list(last.values())
        for b, i in _parts(writes):
            b.w[i] = tok
            b.r[i] = []

    def op(self, en, fn, reads=(), writes=()):
        E = self.E[en]
        for t in self._collect(en, reads, writes):
            self._wait(E, t)
        ins = fn(E["eng"])
        E["count"] += 1
        ins.then_inc(E["sem"][1], 1)
        tok = (E["sem"], E["count"], en)
        self._update(reads, writes, tok)
        self.ninst += 1
        return tok

    def dma(self, qn, out, in_, reads=(), writes=(), **kw):
        Q = self.E[qn]
        i = Q["ndma"]
        P = len(Q["pool"])
        sem = Q["pool"][i % P]
        val = 16 * (i // P + 1)
        if i >= P:
            self._wait(Q, (sem, val - 16, "dma"))
        for t in self._collect("dma", reads, writes):
            self._wait(Q, t)
        Q["eng"].dma_start(out=out, in_=in_, **kw).then_inc(sem[1], 16)
        Q["ndma"] += 1
        tok = (sem, val, "dma")
        self._update(reads, writes, tok)
        return tok

    def all_tokens(self):
        toks = []
        for E in self.E.values():
            if E["count"]:
                toks.append((E["sem"], E["count"], E["name"]))
            P = len(E["pool"])
            for j in range(min(P, E["ndma"])):
                n = (E["ndma"] - 1 - j) // P + 1
                toks.append((E["pool"][j], 16 * n, "dma"))
        return toks

    def barrier(self):
        toks = self.all_tokens()
        for E in self.E.values():
            for t in toks:
                self._wait(E, t)


def bc(ap, shape):
    return ap.broadcast_to(shape)


class Ctx:
    pass


def build_program():
    nc = bass.Bass("TRN2", target_bir_lowering=False)
    g = Ctx()
    g.nc = nc

    def din(name, shape, dt=F32):
        return nc.dram_tensor(name, list(shape), dt, kind="ExternalInput").ap()

    def dscr(name, shape, dt):
        kind = "ExternalOutput" if DEBUG else "Internal"
        return nc.dram_tensor(name, list(shape), dt, kind=kind).ap()

    g.xb = din("xb", [SEQ, D])
    g.ctxb = din("ctxb", [256, D])
    g.xext = din("xext", [EXT + 128, D])
    g.w_mod = din("w_mod", [D, 6 * D])
    g.w_in = din("w_in", [D, 8256])
    g.w_fa = din("w_fa", [D, D])
    g.w_sb = din("w_sb", [2048, D])
    g.w_o = din("w_o", [D, D])
    g.w_up = din("w_up", [D, 2 * DFF])
    g.w_down = din("w_down", [DFF, D])
    g.cpp = din("cpp", [128, CPP_N])
    g.cbc = din("cbc", [128, CBC_N])
    g.cbg = din("cbg", [128, CBG_N])
    g.cbrow = din("cbrow", [1, 3072])
    g.cmk = din("cmk", [128, 8 * 128])
    g.t1 = din("t1", [64, 128])
    g.t2 = din("t2", [128, 2 * 64 * 68])
    g.tcs = din("tcs", [128, 256])
    g.out = nc.dram_tensor("out", [2048, D], F32, kind="ExternalOutput").ap()
    g.U = dscr("U", [NCH, 128, D], BF16)
    g.Y = dscr("Y", [128, 128, D], BF16)
    g.XT = dscr("XT", [128, 8 * 2 * EXT], BF16)
    g.MS = dscr("MS", [NEXT, 128, D], BF16)
    g.YF = dscr("YF", [NEXT, 128, 2048], BF16)
    g.YT = dscr("YT", [NEXT, 128, 2048], BF16)
    g.L1 = dscr("L1", [NEXT, 128, D], F32)
    g.SFB = dscr("SFB", [2, 128, 2048], F32)

    with ExitStack() as es:
        P = Prog(nc, es)
        g.P = P
        g.uid = 0
        phase_setup(g, es)
        with ExitStack() as es2:
            alloc_conv_consts(g, es2)
            if STOP != "setup":
                phase_far(g)
            if STOP not in ("setup", "far"):
                phase_fnet(g)
            if STOP not in ("setup", "far", "fnet"):
                phase_own(g, 0)
                phase_own(g, 1)
            P.barrier()
        if STOP not in ("setup", "far", "fnet", "own"):
            phase_merge(g)
        if STOP not in ("setup", "far", "fnet", "own", "merge"):
            phase_ffn(g)
        P.barrier()
    return nc


def sb(g, es, name, shape, dt, nparts=1):
    g.uid += 1
    t = es.enter_context(g.nc.sbuf_tensor("%s_%d" % (name, g.uid), list(shape), dt))
    return Buf(t, nparts)


def ps(g, es, name, shape, dt=F32, nparts=1):
    g.uid += 1
    t = es.enter_context(g.nc.psum_tensor("%s_%d" % (name, g.uid), list(shape), dt))
    b = Buf(t, nparts)
    b.excl = True
    return b


def _layout(items):
    off = {}
    o = 0
    for k, n in items:
        off[k] = (o, n)
        o += n
    return off, o


CPP, CPP_N = _layout([("c", 8), ("cctx", 8), ("bmod", 48), ("n1g", 8), ("n2g", 8), ("cw_ssd", 72), ("cb_ssd", 24),
                      ("cw_ffn", 9 * 44), ("cb_ffn", 44), ("emask", NEXT), ("fmask", NSLOT * 2)])
CBC, CBC_N = _layout([("dt_bias", 64), ("a_log", 64), ("d_skip", 32), ("emask_bc", 256), ("halo_v", 128)])
CBG, CBG_N = _layout([("ssd_g", 2048), ("final_g", 1024), ("bmod_g1", 1024), ("bmod_g2", 1024)])
MK = {k: i for i, k in enumerate(["ones", "le", "ge", "gt", "lt", "ident", "pen_f", "pen_b"])}


def cpp(g, key, j=None, n=1):
    o, _ = CPP[key]
    if j is None:
        return g.cpp_t[:, o:o + CPP[key][1]]
    return g.cpp_t[:, o + j:o + j + n]


def cbcv(g, key, a=0, n=None):
    o, m = CBC[key]
    if n is None:
        n = m
    return g.cbc_t[:, o + a:o + a + n]


def mk32(g, key):
    i = MK[key]
    return g.cmk_t[:, i * 128:(i + 1) * 128]


def mk16(g, key):
    i = MK[key]
    return g.cmkb_t[:, i * 128:(i + 1) * 128]


def phase_setup(g, es):
    nc, P = g.nc, g.P
    g.cpp_t = sb(g, es, "cpp", [128, CPP_N], F32)
    g.cbc_t = sb(g, es, "cbc", [128, CBC_N], F32)
    g.cmk_t = sb(g, es, "cmk", [128, 8 * 128], F32)
    g.cmkb_t = sb(g, es, "cmkb", [128, 8 * 128], BF16)
    g.modv = sb(g, es, "modv", [128, 8 * 8], F32)
    g.gbc = sb(g, es, "gbc", [128, 2 * D], F32)
    g.nega = sb(g, es, "nega", [128, 64], F32)
    g.Sf = sb(g, es, "Sf", [128, 2048], F32)
    g.Sb = sb(g, es, "Sb", [128, 2048], F32)
    P.dma("sp", g.cpp_t[:], g.cpp, writes=[g.cpp_t])
    P.dma("sp", g.cbc_t[:], g.cbc, writes=[g.cbc_t])
    P.dma("sp", g.cmk_t[:], g.cmk, writes=[g.cmk_t])
    P.dma("pool", g.cmkb_t[:], g.cmk, writes=[g.cmkb_t])
    with ExitStack() as ls:
        sc = sb(g, ls, "sc", [128, 8, 2], F32)
        screp = sb(g, ls, "screp", [128, 8, 128], F32)
        modT = sb(g, ls, "modT", [128, 48, 2], F32)
        wm = [sb(g, ls, "wm%d" % i, [128, 8, 1024], F32) for i in range(2)]
        pm = ps(g, ls, "pm", [128, 8, 2], F32)
        pg = ps(g, ls, "pg", [128, 512], F32)
        bg = sb(g, ls, "bg", [128, 2 * D], F32)
        P.dma("sp", bg[:], g.cbg[:, CBG["bmod_g1"][0]:CBG["bmod_g1"][0] + 2 * D], writes=[bg])
        P.op("act", lambda e: e.activation(out=sc[:, :, 0], in_=cpp(g, "c"), func=AF.Silu), [g.cpp_t], [sc])
        P.op("act", lambda e: e.activation(out=sc[:, :, 1], in_=cpp(g, "cctx"), func=AF.Silu), [g.cpp_t], [sc])
        P.op("dve", lambda e: e.tensor_copy(out=screp[:], in_=bc(sc[:, :, 0:1], [128, 8, 128])), [sc], [screp])
        wv = g.w_mod.rearrange("(kb p) n -> p kb n", p=128)
        for j in range(6):
            w = wm[j % 2]
            P.dma("sp", w[:], wv[:, :, j * 1024:(j + 1) * 1024], writes=[w])
            def mm(e, w=w):
                ins = None
                for fb in range(8):
                    for kb in range(8):
                        ins = e.matmul(pm[:, fb, :], lhsT=w[:, kb, fb * 128:(fb + 1) * 128], rhs=sc[:, kb, :],
                                       start=(kb == 0), stop=(kb == 7))
                return ins
            P.op("pe", mm, [w, sc], [pm])
            bo = CPP["bmod"][0] + j * 8
            P.op("dve", lambda e, j=j, bo=bo: e.tensor_tensor(
                out=modT[:, j * 8:(j + 1) * 8, :], in0=pm[:], in1=bc(g.cpp_t[:, bo:bo + 8].unsqueeze(2), [128, 8, 2]),
                op=OP.add), [pm, g.cpp_t], [modT])
            if j in (2, 5):
                gi = 0 if j == 2 else 1
                for hf in range(2):
                    def mg(e, w=w, hf=hf):
                        ins = None
                        for kb in range(8):
                            ins = e.matmul(pg[:], lhsT=screp[:, kb, :], rhs=w[:, kb, hf * 512:(hf + 1) * 512],
                                           start=(kb == 0), stop=(kb == 7))
                        return ins
                    P.op("pe", mg, [w, screp], [pg])
                    P.op("dve", lambda e, gi=gi, hf=hf: e.tensor_tensor(
                        out=g.gbc[:, gi * D + hf * 512: gi * D + (hf + 1) * 512], in0=pg[:],
                        in1=bg[:, gi * D + hf * 512: gi * D + (hf + 1) * 512], op=OP.add), [pg, bg], [g.gbc])
        mv = g.modv
        def mkA(dst, scale_j, which, gkey):
            P.op("dve", lambda e: e.scalar_tensor_tensor(
                out=mv[:, dst * 8:(dst + 1) * 8], in0=modT[:, scale_j * 8:(scale_j + 1) * 8, which], scalar=1.0,
                in1=cpp(g, gkey), op0=OP.add, op1=OP.mult), [modT, g.cpp_t], [mv])

        def mkB(dst, shift_j, which):
            P.op("dve", lambda e: e.tensor_copy(out=mv[:, dst * 8:(dst + 1) * 8],
                                                 in_=modT[:, shift_j * 8:(shift_j + 1) * 8, which]), [modT], [mv])
        mkA(0, 1, 0, "n1g"); mkB(1, 0, 0)
        mkA(2, 1, 1, "n1g"); mkB(3, 0, 1)
        mkA(4, 4, 0, "n2g"); mkB(5, 3, 0)
        P.op("act", lambda e: e.activation(out=g.nega[:], in_=cbcv(g, "a_log"), func=AF.Exp), [g.cbc_t], [g.nega])
        P.op("dve", lambda e: e.tensor_scalar(out=g.nega[:], in0=g.nega[:], scalar1=-1.0, scalar2=None, op0=OP.mult),
             [g.nega], [g.nega])
        P.barrier()


def alloc_conv_consts(g, es):
    P = g.P
    g.diag = sb(g, es, "diag", [128, 72, 128], BF16)
    g.brow = sb(g, es, "brow", [1, 3072], BF16)
    for a_ in range(0, 3072, 1024):
        P.dma("pool", g.brow[:, a_:a_ + 1024], g.cbrow[:, a_:a_ + 1024], writes=[g.brow])
    for i_ in range(72):
        P.op("dve", lambda e, i_=i_: e.tensor_scalar(out=g.diag[:, i_, :], in0=mk16(g, "ident"),
                                                      scalar1=cpp(g, "cw_ssd", i_), scalar2=None, op0=OP.mult),
             [g.cmkb_t, g.cpp_t], [g.diag])


def modA(g, i, kb):
    return g.modv[:, i * 8 + kb:i * 8 + kb + 1]


def alloc_chunk_bufs(g, es, nfb):
    c = Ctx()
    c.xt = [sb(g, es, "xt%d" % i, [128, D], F32) for i in range(2)]
    c.junk = sb(g, es, "junk", [128, D], BF16)
    c.st = [sb(g, es, "st%d" % i, [128, 4], F32) for i in range(2)]
    c.xn = [sb(g, es, "xn%d" % i, [128, D], BF16) for i in range(2)]
    c.hTe = [sb(g, es, "hTe%d" % i, [128, 8, 130], BF16) for i in range(3)]
    c.pA = ps(g, es, "pA", [128, 8, 128], BF16)
    c.nfb = nfb
    return c


def prep_a(g, c, k, src_rows):
    P = g.P
    i2 = k % 2
    xt, st, xn = c.xt[i2], c.st[i2], c.xn[i2]
    P.dma("sp", xt[:], src_rows, writes=[xt])
    P.op("act", lambda e: e.activation(out=c.junk[:], in_=xt[:], func=AF.Square, accum_out=st[:, 0:1]),
         [xt], [c.junk, st])
    P.op("dve", lambda e: e.tensor_scalar(out=st[:, 1:2], in0=st[:, 0:1], scalar1=1.0 / D, scalar2=EPS,
                                           op0=OP.mult, op1=OP.add), [st], [st])
    P.op("act", lambda e: e.activation(out=st[:, 2:3], in_=st[:, 1:2], func=AF.Ln), [st], [st])
    P.op("act", lambda e: e.activation(out=st[:, 3:4], in_=st[:, 2:3], func=AF.Exp, scale=-0.5), [st], [st])
    P.op("act", lambda e: e.activation(out=xn[:], in_=xt[:], func=AF.Copy, scale=st[:, 3:4]), [xt, st], [xn])
    return xt


def prep_b(g, c, k, ai, bi, vmask=None, hT=None):
    P = g.P
    xn = c.xn[k % 2]
    if hT is None:
        hT = c.hTe[k % 3]

    def tr(e):
        ins = None
        for kb in range(8):
            ins = e.transpose(out=c.pA[:, kb, :], in_=xn[:, kb * 128:(kb + 1) * 128], identity=mk16(g, "ident"))
        return ins
    P.op("pe", tr, [xn, g.cmkb_t], [c.pA])
    P.op("dve", lambda e: e.tensor_tensor(out=hT[:, :, 1:129], in0=c.pA[:],
                                          in1=bc(g.modv[:, ai * 8:(ai + 1) * 8].unsqueeze(2), [128, 8, 128]),
                                          op=OP.mult), [c.pA, g.modv], [hT])
    P.op("dve", lambda e: e.tensor_tensor(out=hT[:, :, 1:129], in0=hT[:, :, 1:129],
                                          in1=bc(g.modv[:, bi * 8:(bi + 1) * 8].unsqueeze(2), [128, 8, 128]),
                                          op=OP.add), [hT, g.modv], [hT])
    if vmask is not None:
        P.op("pool", lambda e: e.tensor_tensor(out=hT[:, :, 1:129], in0=hT[:, :, 1:129],
                                               in1=bc(vmask.unsqueeze(1), [128, 8, 128]), op=OP.mult),
             [hT, g.cbc_t], [hT])


def prep(g, c, k, src_rows, ai, bi, vmask=None, hT=None):
    xt = prep_a(g, c, k, src_rows)
    prep_b(g, c, k, ai, bi, vmask=vmask, hT=hT)
    return xt


def halo_link(g, c, k, has_left):
    P = g.P
    cur = c.hTe[k % 3]
    if has_left:
        prv = c.hTe[(k - 1) % 3]
        P.op("pool", lambda e: e.tensor_copy(out=cur[:, :, 0:1], in_=prv[:, :, 128:129]), [prv], [cur])
        P.op("pool", lambda e: e.tensor_copy(out=prv[:, :, 129:130], in_=cur[:, :, 1:2]), [cur], [prv])
    else:
        P.op("pool", lambda e: e.memset(cur[:, :, 0:1], 0.0), [], [cur])


def halo_zero_right(g, c, k):
    cur = c.hTe[k % 3]
    g.P.op("pool", lambda e: e.memset(cur[:, :, 129:130], 0.0), [], [cur])


def fm_proj_conv(g, c, s, hT, W, nfb, cw_off, steps=None):
    P = g.P
    steps = steps if steps is not None else []
    groups = [list(range(a, min(a + 3, nfb))) for a in range(0, nfb, 3)]
    for gi, fbs in enumerate(groups):
        n = len(fbs)
        pa = s.pxa[gi % len(s.pxa)]
        pb = s.pxc[gi % len(s.pxc)]
        pre = s.pre[gi % 2]

        def mm(e, fbs=fbs, pa=pa):
            ins = None
            for j, fb in enumerate(fbs):
                for kb in range(8):
                    ins = e.matmul(pa[:, j * 130:(j + 1) * 130], lhsT=W[:, kb, fb * 128:(fb + 1) * 128], rhs=hT[:, kb, :],
                                   start=(kb == 0), stop=(kb == 7))
            return ins
        P.op("pe", mm, [W, hT], [pa])
        P.op("dve", lambda e, n=n, pa=pa, pre=pre: e.tensor_copy(
            out=pre[:, 0:n, :], in_=pa[:, 0:n * 130].rearrange("p (j t) -> p j t", t=130)), [pa], [pre])

        def mc(e, fbs=fbs, pb=pb, pre=pre):
            ins = None
            for j, fb in enumerate(fbs):
                cf = cw_off + fb
                for k in range(3):
                    e.matmul(pb[:, j * 128:(j + 1) * 128], lhsT=g.diag[:, k * 24 + cf, :], rhs=pre[:, j, k:k + 128],
                             start=(k == 0), stop=False)
                ins = e.matmul(pb[:, j * 128:(j + 1) * 128], lhsT=g.brow[0:1, cf * 128:(cf + 1) * 128],
                               rhs=mk16(g, "ones")[0:1, :], start=False, stop=True)
            return ins
        P.op("pe", mc, [pre, g.diag, g.brow, g.cmkb_t], [pb])
        f0, f1 = fbs[0], fbs[-1] + 1
        P.op("act", lambda e, f0=f0, f1=f1, n=n, pb=pb: e.activation(
            out=s.xcs[:, f0:f1, :], in_=pb[:, 0:n * 128].rearrange("p (j t) -> p j t", t=128), func=AF.Silu),
            [pb], [(s.xcs, gi)])
        if steps:
            steps.pop(0)()
    while steps:
        steps.pop(0)()


def to_token_major(g, c, s, nblk):
    P = g.P
    for r0 in range(0, nblk, 8):
        n = min(8, nblk - r0)

        def tr(e, r0=r0, n=n):
            ins = None
            for j in range(n):
                ins = e.transpose(out=c.pA[:, j, :], in_=s.xcs[:, r0 + j, :], identity=mk16(g, "ident"))
            return ins
        P.op("pe", tr, [s.xcs, g.cmkb_t], [c.pA])
        P.op("act", lambda e, r0=r0, n=n: e.activation(
            out=s.xtok[:, r0 * 128:(r0 + n) * 128], in_=c.pA[:, 0:n, :], func=AF.Copy), [c.pA], [s.xtok])


def dt_steps(g, s, hT, Wdt, ncol, bias_ap, nega_ap, mask_ap):
    P = g.P

    def s1():
        def mm(e):
            ins = None
            for kb in range(8):
                ins = e.matmul(s.pD[:, 0:ncol], lhsT=hT[:, kb, 1:129], rhs=Wdt[:, kb, 0:ncol], start=(kb == 0), stop=(kb == 7))
            return ins
        P.op("pe", mm, [hT, Wdt], [s.pD])
        P.op("dve", lambda e: e.tensor_tensor(out=s.dtm[:, 0:ncol], in0=s.pD[:, 0:ncol], in1=bias_ap, op=OP.add),
             [s.pD, g.cbc_t], [s.dtm])

    def s2():
        P.op("act", lambda e: e.activation(out=s.dtm[:, 0:ncol], in_=s.dtm[:, 0:ncol], func=AF.Exp), [s.dtm], [s.dtm])

    def s3():
        P.op("act", lambda e: e.activation(out=s.dtm[:, 0:ncol], in_=s.dtm[:, 0:ncol], func=AF.Ln, bias=1.0), [s.dtm], [s.dtm])

    def s4():
        dv = s.dtm[:, 0:ncol].rearrange("p (a b) -> p a b", b=32)
        P.op("dve", lambda e: e.tensor_tensor(out=dv, in0=dv, in1=mask_ap, op=OP.mult), [s.dtm, g.cpp_t], [s.dtm])

    def s5():
        P.op("dve", lambda e: e.tensor_tensor(out=s.la[:, 0:ncol], in0=s.dtm[:, 0:ncol], in1=nega_ap, op=OP.mult),
             [s.dtm, g.nega], [s.la])
    return [s1, s2, s3, s4, s5]


def state_contrib(g, s, wexp_ap, xdd, on_group):
    P = g.P
    P.op("dve", lambda e: e.tensor_tensor(
        out=xdd[:].rearrange("p (h d) -> p h d", d=64), in0=s.xtok[:, 0:2048].rearrange("p (h d) -> p h d", d=64),
        in1=bc(wexp_ap.unsqueeze(2), [128, 32, 64]), op=OP.mult), [s.xtok, s.wx], [xdd])
    for gi in range(4):
        P.op("pe", lambda e, gi=gi: e.matmul(s.pH[:], lhsT=s.xtok[:, 2048 + gi * 128:2048 + (gi + 1) * 128],
                                             rhs=xdd[:, gi * 512:(gi + 1) * 512], start=True, stop=True),
             [s.xtok, xdd], [s.pH])
        on_group(gi, s.pH)


def load_w_cols(g, W, col0, ncols, dst0=0):
    wv = g.w_in.rearrange("(kb p) n -> p kb n", p=128)
    for a in range(0, ncols, 512):
        n = min(512, ncols - a)
        g.P.dma("pool", W[:, :, dst0 + a:dst0 + a + n], wv[:, :, col0 + a:col0 + a + n], writes=[W])


def phase_far(g):
    nc, P = g.nc, g.P
    with ExitStack() as es:
        c = alloc_chunk_bufs(g, es, 20)
        s = Ctx()
        Wf = sb(g, es, "Wf", [128, 8, 1024], BF16)
        Wxb = sb(g, es, "Wxb", [128, 8, 2560], BF16)
        Wdt = sb(g, es, "Wdt", [128, 8, 64], BF16)
        load_w_cols(g, Wf, 0, 1024)
        load_w_cols(g, Wxb, 1024, 2560)
        load_w_cols(g, Wdt, 6144, 64)
        s.pxa = [ps(g, es, "pxa%d" % i, [128, 512], F32) for i in range(2)]
        s.pxc = [ps(g, es, "pxc%d" % i, [128, 512], F32) for i in range(1)]
        s.pD = ps(g, es, "pD", [128, 512], F32)
        s.pH = ps(g, es, "pH", [128, 512], F32)
        pf = [ps(g, es, "pf%d" % i, [128, 512], F32) for i in range(2)]
        s.pre = [sb(g, es, "pre%d" % i, [128, 3, 130], BF16) for i in range(2)]
        s.xcs = sb(g, es, "xcs", [128, 20, 128], BF16, nparts=8)
        s.pD2 = sb(g, es, "pD2", [128, 192], F32)
        s.xtok = sb(g, es, "xtok", [128, 2560], BF16)
        s.dtm = sb(g, es, "dtm", [128, 64], F32)
        s.la = sb(g, es, "la", [128, 64], F32)
        s.wx = sb(g, es, "wx", [128, 64], F32)
        s.sg = sb(g, es, "sg", [128, 64], F32)
        Rb = sb(g, es, "Rb", [128, 32], F32)
        wxb = sb(g, es, "wxb", [128, 64], BF16)
        dec = sb(g, es, "dec", [128, 32], F32)
        xdd = [sb(g, es, "xdd%d" % i, [128, 2048], BF16) for i in range(2)]
        ub = [sb(g, es, "ub%d" % i, [128, D], BF16) for i in range(2)]
        P.op("dve", lambda e: e.memset(g.Sf[:], 0.0), [], [g.Sf])
        P.op("dve", lambda e: e.memset(g.Sb[:], 0.0), [], [g.Sb])
        P.op("dve", lambda e: e.memset(Rb[:], 0.0), [], [Rb])
        slots = [("c", 0), ("c", 1)] + [("l", i) for i in range(NCH)] + [("c", 0), ("c", 1)]
        first = {0, 2, 66}
        last = {1, 65, 67}

        def do_prep_a(k):
            kind, i = slots[k]
            src = g.ctxb[i * 128:(i + 1) * 128, :] if kind == "c" else g.xb[i * 128:(i + 1) * 128, :]
            prep_a(g, c, k, src)

        def do_prep_b(k):
            kind, i = slots[k]
            if kind == "c":
                prep_b(g, c, k, 2, 3)
            else:
                prep_b(g, c, k, 0, 1)
            halo_link(g, c, k, k not in first)
            if k in last:
                halo_zero_right(g, c, k)
        do_prep_a(0); do_prep_b(0)
        do_prep_a(1); do_prep_b(1)
        for k in range(NSLOT):
            kind, i = slots[k]
            hT = c.hTe[k % 3]
            fo = CPP["fmask"][0] + 2 * k
            mask_ap = bc(g.cpp_t[:, fo:fo + 2].unsqueeze(2), [128, 2, 32])
            steps = dt_steps(g, s, hT, Wdt, 64, cbcv(g, "dt_bias"), g.nega[:], mask_ap)

            def t1():
                def segs(e):
                    e.matmul(s.pD[:, 64:96], lhsT=mk32(g, "gt"), rhs=s.la[:, 0:32], start=True, stop=True)
                    e.matmul(s.pD[:, 96:128], lhsT=mk32(g, "lt"), rhs=s.la[:, 32:64], start=True, stop=True)
                    return e.matmul(s.pD[:, 128:192], lhsT=mk32(g, "ones"), rhs=s.la[:, 0:64], start=True, stop=True)
                P.op("pe", segs, [s.la, g.cmk_t], [s.pD])
                P.op("dve", lambda e: e.tensor_copy(out=s.pD2[:], in_=s.pD[:, 0:192]), [s.pD], [s.pD2])

            def t2b():
                P.op("pool", lambda e: e.tensor_copy(out=s.sg[:, 0:32], in_=s.pD2[:, 64:96]), [s.pD2], [s.sg])
                P.op("pool", lambda e: e.tensor_tensor(out=s.sg[:, 32:64], in0=s.pD2[:, 96:128], in1=Rb[:], op=OP.add),
                     [s.pD2, Rb], [s.sg])
                P.op("pool", lambda e: e.tensor_tensor(out=Rb[:], in0=Rb[:], in1=s.pD2[:, 160:192], op=OP.add),
                     [s.pD2, Rb], [Rb])

            def t3():
                P.op("act", lambda e: e.activation(out=s.wx[:], in_=s.sg[:], func=AF.Exp), [s.sg], [s.wx])
                P.op("act", lambda e: e.activation(out=dec[:], in_=s.pD2[:, 128:160], func=AF.Exp), [s.pD2], [dec])

            def t4():
                P.op("pool", lambda e: e.tensor_tensor(out=wxb[:], in0=s.wx[:], in1=s.dtm[:], op=OP.mult),
                     [s.wx, s.dtm], [wxb])
                P.op("pool", lambda e: e.tensor_tensor(
                    out=g.Sf[:].rearrange("p (h d) -> p h d", d=64), in0=g.Sf[:].rearrange("p (h d) -> p h d", d=64),
                    in1=bc(dec[:].unsqueeze(2), [128, 32, 64]), op=OP.mult), [g.Sf, dec], [g.Sf])
            for st_ in steps + [t1, t2b, t3, t4]:
                st_()
            if k + 2 < NSLOT:
                do_prep_a(k + 2)
            if kind == "l":
                u = ub[i % 2]
                for hf in range(2):
                    def mm(e, hf=hf):
                        ins = None
                        for kb in range(8):
                            ins = e.matmul(pf[hf][:], lhsT=hT[:, kb, 1:129], rhs=Wf[:, kb, hf * 512:(hf + 1) * 512],
                                           start=(kb == 0), stop=(kb == 7))
                        return ins
                    P.op("pe", mm, [hT, Wf], [pf[hf]])
                    P.op("dve", lambda e, hf=hf, u=u: e.tensor_copy(out=u[:, hf * 512:(hf + 1) * 512], in_=pf[hf][:]),
                         [pf[hf]], [u])
                P.dma("sp", g.U[i], u[:], reads=[u])
            fm_proj_conv(g, c, s, hT, Wxb, 20, 0, [])
            if k + 2 < NSLOT:
                do_prep_b(k + 2)
            to_token_major(g, c, s, 20)
            x3 = s.xtok[:, 0:2048].rearrange("p (h d) -> p h d", d=64)
            P.op("dve", lambda e: e.tensor_tensor(out=xdd[1][:].rearrange("p (h d) -> p h d", d=64), in0=x3,
                                                  in1=bc(wxb[:, 32:64].unsqueeze(2), [128, 32, 64]), op=OP.mult),
                 [s.xtok, wxb], [xdd[1]])
            P.op("dve", lambda e: e.tensor_tensor(out=xdd[0][:].rearrange("p (h d) -> p h d", d=64), in0=x3,
                                                  in1=bc(wxb[:, 0:32].unsqueeze(2), [128, 32, 64]), op=OP.mult),
                 [s.xtok, wxb], [xdd[0]])
            banks = [s.pxa[0], s.pxa[1], s.pxc[0], s.pH]
            for di, (xd_, S_) in enumerate(((xdd[1], g.Sb), (xdd[0], g.Sf))):
                for gi in range(4):
                    pst = banks[gi]
                    P.op("pe", lambda e, gi=gi, pst=pst, xd_=xd_: e.matmul(
                        pst[:], lhsT=s.xtok[:, 2048 + gi * 128:2048 + (gi + 1) * 128],
                        rhs=xd_[:, gi * 512:(gi + 1) * 512], start=True, stop=True), [s.xtok, xd_], [pst])
                    P.op("dve", lambda e, gi=gi, pst=pst, S_=S_: e.tensor_tensor(
                        out=S_[:, gi * 512:(gi + 1) * 512], in0=S_[:, gi * 512:(gi + 1) * 512], in1=pst[:], op=OP.add),
                        [S_, pst], [S_])
        if DEBUG:
            P.dma("sp", g.SFB[0], g.Sf[:], reads=[g.Sf])
            P.dma("sp", g.SFB[1], g.Sb[:], reads=[g.Sb])
        P.barrier()


def load_w_gen(g, W, src, nkb, ncols):
    wv = src.rearrange("(kb p) n -> p kb n", p=128)
    for a in range(0, ncols, 512):
        n = min(512, ncols - a)
        g.P.dma("pool", W[:, :, a:a + n], wv[:, :, a:a + n], writes=[W])


def phase_fnet(g):
    nc, P = g.nc, g.P
    with ExitStack() as es:
        T1 = sb(g, es, "T1", [64, 128], BF16)
        P.dma("pool", T1[:], g.t1, writes=[T1])
        V = [sb(g, es, "V%d" % i, [64, 4, D], BF16) for i in range(2)]
        Yt = [sb(g, es, "Yt%d" % i, [128, 4, D], BF16) for i in range(2)]
        p1 = [ps(g, es, "p1_%d" % i, [128, 512], F32) for i in range(4)]
        cnt = 0
        for tg in range(32):
            v, yt = V[tg % 2], Yt[tg % 2]
            P.dma("sp", v[:], g.U[:, tg * 4:(tg + 1) * 4, :], writes=[v])
            for t in range(4):
                for hf in range(2):
                    pp = p1[cnt % 4]
                    P.op("pe", lambda e, pp=pp, t=t, hf=hf, v=v: e.matmul(
                        pp[:], lhsT=T1[:], rhs=v[:, t, hf * 512:(hf + 1) * 512], start=True, stop=True), [T1, v], [pp])
                    if cnt % 2:
                        P.op("act", lambda e, pp=pp, t=t, hf=hf, yt=yt: e.activation(
                            out=yt[:, t, hf * 512:(hf + 1) * 512], in_=pp[:], func=AF.Copy), [pp], [yt])
                    else:
                        P.op("dve", lambda e, pp=pp, t=t, hf=hf, yt=yt: e.tensor_copy(
                            out=yt[:, t, hf * 512:(hf + 1) * 512], in_=pp[:]), [pp], [yt])
                    cnt += 1
            P.dma("sp", g.Y[:, tg * 4:(tg + 1) * 4, :], yt[:], reads=[yt])
        P.barrier()
    with ExitStack() as es:
        T2 = sb(g, es, "T2", [128, 2 * 64 * 68], BF16)
        for a in range(0, 2 * 64 * 68, 1088):
            P.dma("pool", T2[:, a:a + 1088], g.t2[:, a:a + 1088], writes=[T2])
        Yk = [sb(g, es, "Yk%d" % i, [128, 2, D], BF16) for i in range(2)]
        XTs = sb(g, es, "XTs", [128, 8, 2, EXT], BF16)
        p2f = [ps(g, es, "p2_%d" % i, [128, 512], F32) for i in range(4)]
        yv = g.Y.rearrange("(ri k) t c -> k t ri c", ri=2)
        xv = XTs[:].rearrange("p c r (j k) -> p c r j k", k=64)
        for k1 in range(64):
            yk = Yk[k1 % 2]
            P.dma("sp", yk[:], yv[k1], writes=[yk])
            for cg in range(2):
                ppb = p2f[(k1 * 2 + cg) % 4]
                pp = ppb[:, 0:272].rearrange("p (c k) -> p c k", k=68)

                def mm(e, pp=pp, cg=cg, yk=yk, k1=k1):
                    ins = None
                    for cb in range(4):
                        cbx = cg * 4 + cb
                        e.matmul(pp[:, cb, :], lhsT=yk[:, 0, cbx * 128:(cbx + 1) * 128],
                                 rhs=T2[:, k1 * 68:(k1 + 1) * 68], start=True, stop=False)
                        ins = e.matmul(pp[:, cb, :], lhsT=yk[:, 1, cbx * 128:(cbx + 1) * 128],
                                       rhs=T2[:, (64 + k1) * 68:(64 + k1 + 1) * 68], start=False, stop=True)
                    return ins
                P.op("pe", mm, [yk, T2], [ppb])
                for ri in range(2):
                    if (k1 + cg) % 2:
                        P.op("act", lambda e, pp=pp, cg=cg, ri=ri, k1=k1: e.activation(
                            out=xv[:, cg * 4:(cg + 1) * 4, ri, :, k1], in_=pp[:, :, ri * 34:(ri + 1) * 34],
                            func=AF.Copy), [ppb], [XTs])
                    else:
                        P.op("dve", lambda e, pp=pp, cg=cg, ri=ri, k1=k1: e.tensor_copy(
                            out=xv[:, cg * 4:(cg + 1) * 4, ri, :, k1], in_=pp[:, :, ri * 34:(ri + 1) * 34]),
                            [ppb], [XTs])
        for cb in range(8):
            P.dma("sp", g.XT[:, cb * 2 * EXT:(cb + 1) * 2 * EXT].rearrange("p (r t) -> p r t", r=2), XTs[:, cb, :, :],
                  reads=[XTs])
        P.barrier()


def phase_own(g, d):
    nc, P = g.nc, g.P
    with ExitStack() as es:
        c = alloc_chunk_bufs(g, es, 24)
        s = Ctx()
        hH = sb(g, es, "hH", [128, 8, 130], BF16)
        W = sb(g, es, "Wxbc", [128, 8, 3072], BF16)
        Wdt = sb(g, es, "Wdt", [128, 8, 32], BF16)
        load_w_cols(g, W, 1024, 3072)
        load_w_cols(g, Wdt, 6144 + 32 * d, 32)
        s.pxa = [ps(g, es, "pxa%d" % i, [128, 512], F32) for i in range(1)]
        s.pxc = [ps(g, es, "pxc%d" % i, [128, 512], F32) for i in range(1)]
        s.pxb = [s.pxa[0], s.pxc[0]]
        s.pre = [sb(g, es, "pre%d" % i, [128, 3, 130], BF16) for i in range(2)]
        s.pD = ps(g, es, "pD", [128, 512], F32)
        s.pH = ps(g, es, "pH", [128, 512], F32)
        psc = ps(g, es, "psc", [128, 4, 128], F32)
        pL = [ps(g, es, "pL%d" % i, [128, 4, 128], F32) for i in range(2)]
        s.xcs = sb(g, es, "xcs", [128, 24, 128], BF16, nparts=8)
        s.xtok = sb(g, es, "xtok", [128, 2560], BF16)
        s.dtm = sb(g, es, "dtm", [128, 32], F32)
        s.la = sb(g, es, "la", [128, 32], F32)
        s.wx = sb(g, es, "wx", [128, 32], F32)
        lab = sb(g, es, "lab", [128, 32], BF16)
        nlab = sb(g, es, "nlab", [128, 32], BF16)
        ecum = sb(g, es, "ecum", [128, 32], F32)
        dec = sb(g, es, "dec", [128, 32], F32)
        xd = sb(g, es, "xd", [128, 2048], BF16)
        xdd = sb(g, es, "xdd", [128, 2048], BF16)
        Sbf = sb(g, es, "Sbf", [128, 2048], BF16)
        Dt = [sb(g, es, "Dt%d" % i, [128, 8, 128], BF16) for i in range(2)]
        Lx = [sb(g, es, "Lx%d" % i, [128, 8, 128], BF16) for i in range(2)]
        G = [sb(g, es, "G%d" % i, [128, 8, 128], BF16) for i in range(2)]
        yo = sb(g, es, "yo", [128, 512], F32)
        ytile = sb(g, es, "ytile", [128, 2048], BF16)
        tmp = sb(g, es, "tmp", [128, 2048], BF16)
        yfl = sb(g, es, "yfl", [128, 2048], BF16)
        S = g.Sf if d == 0 else g.Sb
        mxk = "le" if d == 0 else "ge"
        sgk = "gt" if d == 0 else "lt"
        penk = "pen_f" if d == 0 else "pen_b"
        order = list(range(NEXT)) if d == 0 else list(range(NEXT - 1, -1, -1))

        prep(g, c, 0, g.xext[EXT:EXT + 128, :], 0, 1, vmask=cbcv(g, "halo_v"), hT=hH)

        def do_prep_a(ci):
            prep_a(g, c, ci, g.xext[ci * 128:(ci + 1) * 128, :])

        def do_prep_b(ci, prev_ci):
            hT = c.hTe[ci % 3]
            vm = None
            if ci == 0:
                vm = cbcv(g, "emask_bc", 0, 128)
            if ci == NEXT - 1:
                vm = cbcv(g, "emask_bc", 128, 128)
            prep_b(g, c, ci, 0, 1, vmask=vm)
            if prev_ci is not None:
                nb = c.hTe[prev_ci % 3]
                if ci == prev_ci + 1:
                    P.op("pool", lambda e: e.tensor_copy(out=hT[:, :, 0:1], in_=nb[:, :, 128:129]), [nb], [hT])
                    P.op("pool", lambda e: e.tensor_copy(out=nb[:, :, 129:130], in_=hT[:, :, 1:2]), [hT], [nb])
                else:
                    P.op("pool", lambda e: e.tensor_copy(out=hT[:, :, 129:130], in_=nb[:, :, 1:2]), [nb], [hT])
                    P.op("pool", lambda e: e.tensor_copy(out=nb[:, :, 0:1], in_=hT[:, :, 128:129]), [hT], [nb])
            if ci == 0:
                P.op("pool", lambda e: e.tensor_copy(out=hT[:, :, 0:1], in_=hH[:, :, 1:2]), [hH], [hT])
            if ci == NEXT - 1:
                P.op("pool", lambda e: e.tensor_copy(out=hT[:, :, 129:130], in_=hH[:, :, 2:3]), [hH], [hT])

        do_prep_a(order[0]); do_prep_b(order[0], None)
        do_prep_a(order[1]); do_prep_b(order[1], order[0])
        for oi, ci in enumerate(order):
            hT = c.hTe[ci % 3]
            if d == 1:
                P.dma("sp", yfl[:], g.YF[ci], writes=[yfl])
            mask_ap = bc(cpp(g, "emask", ci).unsqueeze(2), [128, 1, 32])
            steps = dt_steps(g, s, hT, Wdt, 32, cbcv(g, "dt_bias", 32 * d, 32), g.nega[:, 32 * d:32 * (d + 1)], mask_ap)

            def u1():
                P.op("pool", lambda e: e.tensor_copy(out=lab[:], in_=s.la[:]), [s.la], [lab])
                P.op("pool", lambda e: e.tensor_scalar(out=nlab[:], in0=lab[:], scalar1=-1.0, scalar2=None, op0=OP.mult),
                     [lab], [nlab])

                def segs(e):
                    e.matmul(s.pD[:, 64:96], lhsT=mk32(g, mxk), rhs=s.la[:], start=True, stop=True)
                    e.matmul(s.pD[:, 96:128], lhsT=mk32(g, sgk), rhs=s.la[:], start=True, stop=True)
                    return e.matmul(s.pD[:, 128:160], lhsT=mk32(g, "ones"), rhs=s.la[:], start=True, stop=True)
                P.op("pe", segs, [s.la, g.cmk_t], [s.pD])

            def u2():
                P.op("act", lambda e: e.activation(out=ecum[:], in_=s.pD[:, 64:96], func=AF.Exp), [s.pD], [ecum])
                P.op("act", lambda e: e.activation(out=s.wx[:], in_=s.pD[:, 96:128], func=AF.Exp), [s.pD], [s.wx])
                P.op("act", lambda e: e.activation(out=dec[:], in_=s.pD[:, 128:160], func=AF.Exp), [s.pD], [dec])

            def u3():
                P.op("pool", lambda e: e.tensor_tensor(out=s.wx[:], in0=s.wx[:], in1=s.dtm[:], op=OP.mult),
                     [s.wx, s.dtm], [s.wx])
            for st_ in steps + [u1, u2, u3]:
                st_()
            if oi + 2 < NEXT:
                do_prep_a(order[oi + 2])
            fm_proj_conv(g, c, s, hT, W, 24, 0, [])
            if oi + 2 < NEXT:
                do_prep_b(order[oi + 2], order[oi + 1])
            to_token_major(g, c, s, 20)
            x3 = s.xtok[:, 0:2048].rearrange("p (h d) -> p h d", d=64)
            P.op("dve", lambda e: e.tensor_tensor(out=xd[:].rearrange("p (h d) -> p h d", d=64), in0=x3,
                                                   in1=bc(s.dtm[:].unsqueeze(2), [128, 32, 64]), op=OP.mult),
                 [s.xtok, s.dtm], [xd])
            P.op("dve", lambda e: e.tensor_tensor(out=xdd[:].rearrange("p (h d) -> p h d", d=64), in0=x3,
                                                   in1=bc(s.wx[:].unsqueeze(2), [128, 32, 64]), op=OP.mult),
                 [s.xtok, s.wx], [xdd])
            P.op("act", lambda e: e.activation(out=Sbf[:], in_=S[:], func=AF.Copy), [S], [Sbf])

            def sc(e):
                ins = None
                for gi in range(4):
                    ins = e.matmul(psc[:, gi, :], lhsT=s.xcs[:, 16 + gi, :], rhs=s.xcs[:, 20 + gi, :], start=True, stop=True)
                return ins
            P.op("pe", sc, [s.xcs], [psc])
            for gi in range(4):
                dt_, lx, gg = Dt[gi % 2], Lx[gi % 2], G[gi % 2]
                P.op("pool", lambda e, gi=gi, dt_=dt_: e.tensor_tensor(
                    out=dt_[:], in0=bc(lab[:, gi * 8:(gi + 1) * 8].unsqueeze(2), [128, 8, 128]),
                    in1=bc(mk16(g, mxk).unsqueeze(1), [128, 8, 128]), op=OP.mult), [lab, g.cmkb_t], [dt_])
                for hh in range(2):
                    def mmL(e, gi=gi, hh=hh, dt_=dt_):
                        e.matmul(pL[hh][:], lhsT=mk16(g, "ones"), rhs=dt_[:, hh * 4:(hh + 1) * 4, :], start=True, stop=False)
                        e.matmul(pL[hh][:], lhsT=mk16(g, mxk),
                                 rhs=bc(nlab[:, gi * 8 + hh * 4:gi * 8 + hh * 4 + 4].unsqueeze(2), [128, 4, 128]),
                                 start=False, stop=False)
                        return e.matmul(pL[hh][:], lhsT=mk16(g, "ident"),
                                        rhs=bc(mk16(g, penk).unsqueeze(1), [128, 4, 128]), start=False, stop=True)
                    P.op("pe", mmL, [dt_, nlab, g.cmkb_t], [pL[hh]])
                    P.op("act", lambda e, hh=hh, lx=lx: e.activation(out=lx[:, hh * 4:(hh + 1) * 4, :], in_=pL[hh][:],
                                                                   func=AF.Exp), [pL[hh]], [lx])
                P.op("dve", lambda e, gi=gi, lx=lx, gg=gg: e.tensor_tensor(
                    out=gg[:], in0=lx[:], in1=bc(psc[:, gi, :].unsqueeze(1), [128, 8, 128]), op=OP.mult),
                    [lx, psc], [gg])

                def mmy(e, gi=gi, gg=gg):
                    ins = None
                    for h in range(8):
                        hh = gi * 8 + h
                        ins = e.matmul(s.pH[:, h * 64:(h + 1) * 64], lhsT=gg[:, h, :], rhs=xd[:, hh * 64:(hh + 1) * 64],
                                       start=True, stop=True)
                    return ins
                P.op("pe", mmy, [gg, xd], [s.pH])
                P.op("pe", lambda e, gi=gi: e.matmul(s.pxb[0][:], lhsT=s.xcs[:, 20 + gi, :],
                                                     rhs=Sbf[:, gi * 512:(gi + 1) * 512], start=True, stop=True),
                     [s.xcs, Sbf], [s.pxb[0]])
                P.op("dve", lambda e, gi=gi: e.tensor_tensor(
                    out=yo[:].rearrange("p (h d) -> p h d", d=64), in0=s.pxb[0][:].rearrange("p (h d) -> p h d", d=64),
                    in1=bc(ecum[:, gi * 8:(gi + 1) * 8].unsqueeze(2), [128, 8, 64]), op=OP.mult),
                    [s.pxb[0], ecum], [yo])
                P.op("dve", lambda e, gi=gi: e.tensor_tensor(out=ytile[:, gi * 512:(gi + 1) * 512], in0=yo[:],
                                                              in1=s.pH[:], op=OP.add), [yo, s.pH], [ytile])
                P.op("pe", lambda e, gi=gi: e.matmul(s.pxb[1][:], lhsT=s.xtok[:, 2048 + gi * 128:2048 + (gi + 1) * 128],
                                                     rhs=xdd[:, gi * 512:(gi + 1) * 512], start=True, stop=True),
                     [s.xtok, xdd], [s.pxb[1]])
                P.op("dve", lambda e, gi=gi: e.tensor_tensor(
                    out=S[:, gi * 512:(gi + 1) * 512].rearrange("p (h d) -> p h d", d=64),
                    in0=S[:, gi * 512:(gi + 1) * 512].rearrange("p (h d) -> p h d", d=64),
                    in1=bc(dec[:, gi * 8:(gi + 1) * 8].unsqueeze(2), [128, 8, 64]), op=OP.mult), [S, dec], [S])
                P.op("dve", lambda e, gi=gi: e.tensor_tensor(out=S[:, gi * 512:(gi + 1) * 512],
                                                              in0=S[:, gi * 512:(gi + 1) * 512], in1=s.pxb[1][:],
                                                              op=OP.add), [S, s.pxb[1]], [S])
            if d == 0:
                P.op("pool", lambda e: e.tensor_tensor(out=tmp[:].rearrange("p (h d) -> p h d", d=64), in0=x3,
                                                       in1=bc(cbcv(g, "d_skip").unsqueeze(2), [128, 32, 64]),
                                                       op=OP.mult), [s.xtok, g.cbc_t], [tmp])
                P.op("pool", lambda e: e.tensor_tensor(out=tmp[:], in0=tmp[:], in1=ytile[:], op=OP.add),
                     [tmp, ytile], [tmp])
                P.dma("sp", g.YF[ci], tmp[:], reads=[tmp])
            else:
                P.op("pool", lambda e: e.tensor_tensor(out=tmp[:], in0=yfl[:], in1=ytile[:], op=OP.add),
                     [yfl, ytile], [tmp])
                P.dma("sp", g.YT[ci], tmp[:], reads=[tmp])
        P.barrier()


def phase_merge(g):
    phase_merge_a(g)
    phase_merge_b(g)


def phase_merge_a(g):
    nc, P = g.nc, g.P
    with ExitStack() as es:
        c = alloc_chunk_bufs(g, es, 0)
        Wz = sb(g, es, "Wz", [128, 8, 2048], BF16)
        Wgs = sb(g, es, "Wgs", [128, 8, 1024], BF16)
        Wsb = sb(g, es, "Wsb", [128, 16, 1024], BF16)
        load_w_cols(g, Wz, 4096, 2048)
        load_w_cols(g, Wgs, 7232, 1024)
        load_w_gen(g, Wsb, g.w_sb, 16, 1024)
        sg = sb(g, es, "ssdg", [128, 2048], F32)
        P.dma("sp", sg[:], g.cbg[:, CBG["ssd_g"][0]:CBG["ssd_g"][0] + 2048], writes=[sg])
        pz = [ps(g, es, "pz%d" % i, [128, 512], F32) for i in range(4)]
        pbs = [ps(g, es, "pbs%d" % i, [128, 512], F32) for i in range(2)]
        yt = [sb(g, es, "yt%d" % i, [128, 2048], BF16) for i in range(2)]
        zs = sb(g, es, "zs", [128, 4, 512], BF16, nparts=4)
        t = sb(g, es, "t", [128, 4, 512], F32, nparts=4)
        jk = sb(g, es, "jk", [128, 512], BF16)
        st2 = sb(g, es, "st2", [128, 16], F32)
        ysn = sb(g, es, "ysn", [128, 4, 512], BF16, nparts=4)
        ysnT = sb(g, es, "ysnT", [128, 16, 128], BF16)
        sgs = sb(g, es, "sgs", [128, 2, 512], F32, nparts=2)
        ms = [sb(g, es, "ms%d" % i, [128, 1024], BF16) for i in range(2)]
        prep_a(g, c, 0, g.xext[0:128, :])
        prep_b(g, c, 0, 0, 1)
        for ci in range(NEXT):
            hT = c.hTe[ci % 3]
            y = yt[ci % 2]
            P.dma("sp", y[:], g.YT[ci], writes=[y])
            if ci + 1 < NEXT:
                prep_a(g, c, ci + 1, g.xext[(ci + 1) * 128:(ci + 2) * 128, :])
            for gi in range(4):
                def mm(e, gi=gi):
                    ins = None
                    for kb in range(8):
                        ins = e.matmul(pz[gi][:], lhsT=hT[:, kb, 1:129], rhs=Wz[:, kb, gi * 512:(gi + 1) * 512],
                                       start=(kb == 0), stop=(kb == 7))
                    return ins
                P.op("pe", mm, [hT, Wz], [pz[gi]])
            for gi in range(4):
                P.op("act", lambda e, gi=gi: e.activation(out=zs[:, gi, :], in_=pz[gi][:], func=AF.Silu),
                     [pz[gi]], [(zs, gi)])
            for gi in range(4):
                P.op("dve", lambda e, gi=gi: e.tensor_tensor(out=t[:, gi, :], in0=zs[:, gi, :],
                                                              in1=y[:, gi * 512:(gi + 1) * 512], op=OP.mult),
                     [(zs, gi), y], [(t, gi)])
            for gi in range(4):
                P.op("act", lambda e, gi=gi: e.activation(out=jk[:], in_=t[:, gi, :], func=AF.Square,
                                                          accum_out=st2[:, gi:gi + 1]), [(t, gi)], [jk, st2])
            P.op("dve", lambda e: e.tensor_scalar(out=st2[:, 4:8], in0=st2[:, 0:4], scalar1=1.0 / 512, scalar2=EPS,
                                                   op0=OP.mult, op1=OP.add), [st2], [st2])
            P.op("act", lambda e: e.activation(out=st2[:, 8:12], in_=st2[:, 4:8], func=AF.Ln), [st2], [st2])
            P.op("act", lambda e: e.activation(out=st2[:, 12:16], in_=st2[:, 8:12], func=AF.Exp, scale=-0.5), [st2], [st2])
            for gi in range(4):
                P.op("dve", lambda e, gi=gi: e.scalar_tensor_tensor(
                    out=ysn[:, gi, :], in0=t[:, gi, :], scalar=st2[:, 12 + gi:13 + gi], in1=sg[:, gi * 512:(gi + 1) * 512],
                    op0=OP.mult, op1=OP.mult), [(t, gi), st2, sg], [(ysn, gi)])
            for rnd in range(2):
                def tr(e, rnd=rnd):
                    ins = None
                    for j in range(8):
                        blk = rnd * 8 + j
                        ins = e.transpose(out=c.pA[:, j, :], in_=ysn[:, blk // 4, (blk % 4) * 128:(blk % 4 + 1) * 128],
                                          identity=mk16(g, "ident"))
                    return ins
                P.op("pe", tr, [ysn, g.cmkb_t], [c.pA])
                P.op("dve", lambda e, rnd=rnd: e.tensor_copy(out=ysnT[:, rnd * 8:(rnd + 1) * 8, :], in_=c.pA[:]),
                     [c.pA], [ysnT])
            if ci + 1 < NEXT:
                prep_b(g, c, ci + 1, 0, 1)
            for hf in range(2):
                def mmg(e, hf=hf):
                    ins = None
                    for kb in range(8):
                        ins = e.matmul(pz[hf][:], lhsT=hT[:, kb, 1:129], rhs=Wgs[:, kb, hf * 512:(hf + 1) * 512],
                                       start=(kb == 0), stop=(kb == 7))
                    return ins
                P.op("pe", mmg, [hT, Wgs], [pz[hf]])

                def mms(e, hf=hf):
                    ins = None
                    for kb in range(16):
                        ins = e.matmul(pbs[hf][:], lhsT=ysnT[:, kb, :], rhs=Wsb[:, kb, hf * 512:(hf + 1) * 512],
                                       start=(kb == 0), stop=(kb == 15))
                    return ins
                P.op("pe", mms, [ysnT, Wsb], [pbs[hf]])
            m = ms[ci % 2]
            for hf in range(2):
                P.op("act", lambda e, hf=hf: e.activation(out=sgs[:, hf, :], in_=pz[hf][:], func=AF.Sigmoid),
                     [pz[hf]], [(sgs, hf)])
                P.op("dve", lambda e, m=m, hf=hf: e.tensor_tensor(out=m[:, hf * 512:(hf + 1) * 512], in0=sgs[:, hf, :],
                                                                   in1=pbs[hf][:], op=OP.mult),
                     [(sgs, hf), pbs[hf]], [m])
            P.dma("sp", g.MS[ci], m[:], reads=[m])
        P.barrier()


def phase_merge_b(g):
    nc, P = g.nc, g.P
    with ExitStack() as es:
        c = alloc_chunk_bufs(g, es, 0)
        Wgf = sb(g, es, "Wgf", [128, 8, 1024], BF16)
        Wfa = sb(g, es, "Wfa", [128, 8, 1024], BF16)
        Wo = sb(g, es, "Wo", [128, 8, 1024], BF16)
        Tcs = sb(g, es, "Tcs", [128, 256], BF16)
        load_w_cols(g, Wgf, 6208, 1024)
        load_w_gen(g, Wfa, g.w_fa, 8, 1024)
        load_w_gen(g, Wo, g.w_o, 8, 1024)
        P.dma("pool", Tcs[:], g.tcs, writes=[Tcs])
        pb0 = ps(g, es, "pb0", [128, 2, 512], F32)
        pb1 = ps(g, es, "pb1", [128, 2, 512], F32)
        pb2 = ps(g, es, "pb2", [128, 2, 512], F32)
        xtc = [sb(g, es, "xtc%d" % i, [128, 8, 2, 128], BF16) for i in range(2)]
        msl = [sb(g, es, "msl%d" % i, [128, 1024], BF16) for i in range(2)]
        mixT = sb(g, es, "mixT", [128, 8, 128], BF16)
        sgf = sb(g, es, "sgf", [128, 1024], F32)
        tmp = sb(g, es, "tmpm", [128, 1024], F32)
        mrg = sb(g, es, "mrg", [128, 1024], BF16)
        mrgT = sb(g, es, "mrgT", [128, 8, 128], BF16)
        l1 = [sb(g, es, "l1_%d" % i, [128, 1024], F32) for i in range(2)]
        xtv = g.XT.rearrange("p (c r t) -> p c r t", c=8, r=2)
        prep_a(g, c, 0, g.xext[0:128, :])
        prep_b(g, c, 0, 0, 1)
        for ci in range(NEXT):
            hT = c.hTe[ci % 3]
            xt = c.xt[ci % 2]
            if ci + 1 < NEXT:
                prep_a(g, c, ci + 1, g.xext[(ci + 1) * 128:(ci + 2) * 128, :])
            xc_, m = xtc[ci % 2], msl[ci % 2]
            P.dma("sp", xc_[:], xtv[:, :, :, ci * 128:(ci + 1) * 128], writes=[xc_])
            P.dma("sp", m[:], g.MS[ci], writes=[m])
            for cg in range(2):
                def mmx(e, cg=cg):
                    e.matmul(pb2[:, cg, :], lhsT=Tcs[:, 0:128], rhs=xc_[:, cg * 4:(cg + 1) * 4, 0, :], start=True, stop=False)
                    return e.matmul(pb2[:, cg, :], lhsT=Tcs[:, 128:256], rhs=xc_[:, cg * 4:(cg + 1) * 4, 1, :],
                                    start=False, stop=True)
                P.op("pe", mmx, [Tcs, xc_], [pb2])
            P.op("act", lambda e: e.activation(out=mixT[:].rearrange("p a b -> p (a b)"),
                                               in_=pb2[:].rearrange("p a b -> p (a b)"), func=AF.Copy), [pb2], [mixT])
            for hf in range(2):
                def mmf(e, hf=hf):
                    ins = None
                    for kb in range(8):
                        ins = e.matmul(pb0[:, hf, :], lhsT=mixT[:, kb, :], rhs=Wfa[:, kb, hf * 512:(hf + 1) * 512],
                                       start=(kb == 0), stop=(kb == 7))
                    return ins
                P.op("pe", mmf, [mixT, Wfa], [pb0])

                def mmg(e, hf=hf):
                    ins = None
                    for kb in range(8):
                        ins = e.matmul(pb1[:, hf, :], lhsT=hT[:, kb, 1:129], rhs=Wgf[:, kb, hf * 512:(hf + 1) * 512],
                                       start=(kb == 0), stop=(kb == 7))
                    return ins
                P.op("pe", mmg, [hT, Wgf], [pb1])
            P.op("act", lambda e: e.activation(out=sgf[:], in_=pb1[:].rearrange("p a b -> p (a b)"), func=AF.Sigmoid),
                 [pb1], [sgf])
            P.op("dve", lambda e: e.tensor_tensor(out=tmp[:], in0=sgf[:], in1=pb0[:].rearrange("p a b -> p (a b)"),
                                                   op=OP.mult), [sgf, pb0], [tmp])
            P.op("dve", lambda e, m=m: e.tensor_tensor(out=mrg[:], in0=tmp[:], in1=m[:], op=OP.add), [tmp, m], [mrg])

            def tr(e):
                ins = None
                for kb in range(8):
                    ins = e.transpose(out=c.pA[:, kb, :], in_=mrg[:, kb * 128:(kb + 1) * 128], identity=mk16(g, "ident"))
                return ins
            P.op("pe", tr, [mrg, g.cmkb_t], [c.pA])
            P.op("act", lambda e: e.activation(out=mrgT[:], in_=c.pA[:], func=AF.Copy), [c.pA], [mrgT])
            if ci + 1 < NEXT:
                prep_b(g, c, ci + 1, 0, 1)
            for hf in range(2):
                def mmo(e, hf=hf):
                    ins = None
                    for kb in range(8):
                        ins = e.matmul(pb2[:, hf, :], lhsT=mrgT[:, kb, :], rhs=Wo[:, kb, hf * 512:(hf + 1) * 512],
                                       start=(kb == 0), stop=(kb == 7))
                    return ins
                P.op("pe", mmo, [mrgT, Wo], [pb2])
            l = l1[ci % 2]
            P.op("dve", lambda e, l=l: e.tensor_tensor(out=l[:], in0=pb2[:].rearrange("p a b -> p (a b)"),
                                                        in1=g.gbc[:, 0:D], op=OP.mult), [pb2, g.gbc], [l])
            P.op("pool", lambda e, l=l, xt=xt: e.tensor_tensor(out=l[:], in0=l[:], in1=xt[:], op=OP.add), [l, xt], [l])
            P.dma("sp", g.L1[ci], l[:], reads=[l])
        P.barrier()


def phase_ffn(g):
    nc, P = g.nc, g.P
    with ExitStack() as es:
        c = alloc_chunk_bufs(g, es, 0)
        h2T = sb(g, es, "h2T", [128, 8, EXT], BF16, nparts=NEXT)
        Wd = sb(g, es, "Wd", [128, NFB, 1024], BF16)
        load_w_gen(g, Wd, g.w_down, NFB, 1024)
        fg = sb(g, es, "fg", [128, 1024], F32)
        P.dma("sp", fg[:], g.cbg[:, CBG["final_g"][0]:CBG["final_g"][0] + 1024], writes=[fg])
        for ci in range(NEXT):
            i2 = ci % 2
            xt, st, xn = c.xt[i2], c.st[i2], c.xn[i2]
            P.dma("sp", xt[:], g.L1[ci], writes=[xt])
            P.op("act", lambda e: e.activation(out=c.junk[:], in_=xt[:], func=AF.Square, accum_out=st[:, 0:1]),
                 [xt], [c.junk, st])
            P.op("dve", lambda e: e.tensor_scalar(out=st[:, 1:2], in0=st[:, 0:1], scalar1=1.0 / D, scalar2=EPS,
                                                   op0=OP.mult, op1=OP.add), [st], [st])
            P.op("act", lambda e: e.activation(out=st[:, 2:3], in_=st[:, 1:2], func=AF.Sqrt), [st], [st])
            P.op("dve", lambda e: e.reciprocal(out=st[:, 3:4], in_=st[:, 2:3]), [st], [st])
            P.op("act", lambda e: e.activation(out=xn[:], in_=xt[:], func=AF.Copy, scale=st[:, 3:4]), [xt, st], [xn])

            def tr(e):
                ins = None
                for kb in range(8):
                    ins = e.transpose(out=c.pA[:, kb, :], in_=xn[:, kb * 128:(kb + 1) * 128], identity=mk16(g, "ident"))
                return ins
            P.op("pe", tr, [xn, g.cmkb_t], [c.pA])
            for kb in range(8):
                P.op("dve", lambda e, kb=kb, ci=ci: e.tensor_scalar(
                    out=h2T[:, kb, ci * 128:(ci + 1) * 128], in0=c.pA[:, kb, :], scalar1=modA(g, 4, kb),
                    scalar2=modA(g, 5, kb), op0=OP.mult, op1=OP.add), [c.pA, g.modv], [(h2T, ci)])
            if ci in (0, NEXT - 1):
                vm = cbcv(g, "emask_bc", 0 if ci == 0 else 128, 128)
                P.op("pool", lambda e, ci=ci, vm=vm: e.tensor_tensor(
                    out=h2T[:, :, ci * 128:(ci + 1) * 128], in0=h2T[:, :, ci * 128:(ci + 1) * 128],
                    in1=bc(vm.unsqueeze(1), [128, 8, 128]), op=OP.mult), [(h2T, ci), g.cbc_t], [(h2T, ci)])
        NB = 4
        pu = [ps(g, es, "pu%d" % i, [128, 512], F32) for i in range(2)]
        pd = ps(g, es, "pd", [128, 2, 512], F32)
        aT = sb(g, es, "aT", [128, NFB, 512], BF16, nparts=NFB)
        wu = [sb(g, es, "wu%d" % i, [128, 8, 2, 128], BF16) for i in range(3)]
        ug = [sb(g, es, "ug%d" % i, [128, 10, 64], F32) for i in range(2)]
        acc = [sb(g, es, "acc%d" % i, [128, 8, 64], F32) for i in range(2)]
        sgl = sb(g, es, "sgl", [128, 512], F32)
        lt = [sb(g, es, "lt%d" % i, [128, 1024], F32) for i in range(2)]
        yy = [sb(g, es, "yy%d" % i, [128, 1024], F32) for i in range(2)]
        jk = c.junk
        st = [sb(g, es, "stf%d" % i, [128, 4], F32) for i in range(2)]
        wuv = g.w_up.rearrange("(kb p) (gv n) -> p kb gv n", p=128, gv=2)
        l1f = g.L1.rearrange("c p d -> (c p) d")
        cnt = 0
        nitem = NB * NFB

        def issue_w(i):
            if i < nitem:
                fb_ = i % NFB
                w_ = wu[i % 3]
                for gv_ in range(2):
                    P.dma("pool", w_[:, :, gv_, :], wuv[:, :, gv_, fb_ * 128:(fb_ + 1) * 128], writes=[w_])
        issue_w(0)
        issue_w(1)
        for blk in range(NB):
            base = blk * 512
            hparts = [(h2T, i) for i in range(base // 128, (base + 640 + 127) // 128)]
            for fb in range(NFB):
                w = wu[cnt % 3]
                issue_w(cnt + 2)
                cnt += 1
                for gv in range(2):
                    u = ug[gv]
                    a = acc[gv]
                    for j in range(2):
                        def mm(e, j=j, gv=gv, w=w):
                            ins = None
                            for kb in range(8):
                                ins = e.matmul(pu[j][:, 0:320], lhsT=w[:, kb, gv, :],
                                               rhs=h2T[:, kb, base + j * 320:base + (j + 1) * 320],
                                               start=(kb == 0), stop=(kb == 7))
                            return ins
                        P.op("pe", mm, [w] + hparts, [pu[j]])
                        P.op("act", lambda e, j=j, u=u: e.activation(
                            out=u[:].rearrange("p r c -> p (r c)")[:, j * 320:(j + 1) * 320], in_=pu[j][:, 0:320],
                            func=AF.Copy), [pu[j]], [u])
                    cf = gv * NFB + fb
                    wt = lambda t: cpp(g, "cw_ffn", t * 44 + cf)
                    P.op("act", lambda e, u=u, a=a, cf=cf: e.activation(
                        out=a[:], in_=u[:, 1:9, :], func=AF.Identity, scale=cpp(g, "cw_ffn", 4 * 44 + cf),
                        bias=cpp(g, "cb_ffn", cf)), [u, g.cpp_t], [a])
                    for kh in range(3):
                        for kw in range(3):
                            if kh == 1 and kw == 1:
                                continue
                            dy, dx = kh - 1, kw - 1
                            c0, c1 = max(0, -dx), 64 - max(0, dx)
                            P.op("dve", lambda e, u=u, a=a, dy=dy, dx=dx, c0=c0, c1=c1, t=kh * 3 + kw, cf=cf:
                                 e.scalar_tensor_tensor(out=a[:, :, c0:c1], in0=u[:, 1 + dy:9 + dy, c0 + dx:c1 + dx],
                                                        scalar=cpp(g, "cw_ffn", t * 44 + cf), in1=a[:, :, c0:c1],
                                                        op0=OP.mult, op1=OP.add), [u, a, g.cpp_t], [a])
                P.op("act", lambda e: e.activation(out=sgl[:], in_=acc[0][:].rearrange("p r c -> p (r c)"), func=AF.Silu),
                     [acc[0]], [sgl])
                P.op("dve", lambda e, fb=fb: e.tensor_tensor(out=aT[:, fb, :], in0=sgl[:],
                                                              in1=acc[1][:].rearrange("p r c -> p (r c)"), op=OP.mult),
                     [sgl, acc[1]], [(aT, fb)])
            for tcn in range(4):
                o0 = blk * 512 + tcn * 128
                i2 = (blk * 4 + tcn) % 2
                l, y, s4 = lt[i2], yy[i2], st[i2]
                o = y
                P.dma("sp", l[:], l1f[o0 + 64:o0 + 64 + 128, :], writes=[l])
                for hf in range(2):
                    def mmd(e, hf=hf, tcn=tcn):
                        ins = None
                        for fb in range(NFB):
                            ins = e.matmul(pd[:, hf, :], lhsT=aT[:, fb, tcn * 128:(tcn + 1) * 128],
                                           rhs=Wd[:, fb, hf * 512:(hf + 1) * 512], start=(fb == 0), stop=(fb == NFB - 1))
                        return ins
                    P.op("pe", mmd, [aT, Wd], [pd])
                P.op("dve", lambda e, y=y: e.tensor_tensor(out=y[:], in0=pd[:].rearrange("p a b -> p (a b)"),
                                                            in1=g.gbc[:, D:2 * D], op=OP.mult), [pd, g.gbc], [y])
                P.op("pool", lambda e, y=y, l=l: e.tensor_tensor(out=y[:], in0=y[:], in1=l[:], op=OP.add), [y, l], [y])
                P.op("act", lambda e, y=y, s4=s4: e.activation(out=jk[:], in_=y[:], func=AF.Square,
                                                               accum_out=s4[:, 0:1]), [y], [jk, s4])
                P.op("dve", lambda e, s4=s4: e.tensor_scalar(out=s4[:, 1:2], in0=s4[:, 0:1], scalar1=1.0 / D,
                                                              scalar2=EPS, op0=OP.mult, op1=OP.add), [s4], [s4])
                P.op("act", lambda e, s4=s4: e.activation(out=s4[:, 2:3], in_=s4[:, 1:2], func=AF.Sqrt), [s4], [s4])
                P.op("dve", lambda e, s4=s4: e.reciprocal(out=s4[:, 3:4], in_=s4[:, 2:3]), [s4], [s4])
                P.op("dve", lambda e, y=y, s4=s4, o=o: e.scalar_tensor_tensor(
                    out=o[:], in0=y[:], scalar=s4[:, 3:4], in1=fg[:], op0=OP.mult, op1=OP.mult), [y, s4, fg], [y])
                P.dma("sp", g.out[o0:o0 + 128, :], o[:], reads=[o])
        P.barrier()


def _pm(v):
    v = np.asarray(v, np.float32)
    return np.ascontiguousarray(v.reshape(-1, 128).T)


def _rb(v):
    v = np.asarray(v, np.float32).reshape(1, -1)
    return np.ascontiguousarray(np.broadcast_to(v, (128, v.shape[1])))


def _const_tables():
    k = np.arange(128)[:, None]
    m = np.arange(128)[None, :]
    mats = [np.ones((128, 128)), k <= m, k >= m, k > m, k < m, k == m,
            np.where(m < k, -BIG, 0.0), np.where(m > k, -BIG, 0.0)]
    cmk = np.concatenate([np.asarray(a, np.float32) for a in mats], axis=1)
    t1i = np.arange(64)[:, None] * np.arange(64)[None, :]
    th = 2 * np.pi * t1i / 64.0
    t1 = np.concatenate([np.cos(th), -np.sin(th)], axis=1).astype(np.float32)
    j = np.arange(128)[:, None] * np.arange(128)[None, :]
    thc = 2 * np.pi * j / 128.0
    tcs = (np.concatenate([np.cos(thc), np.sin(thc)], axis=1) / 1024.0).astype(np.float32)
    return cmk, t1, tcs


def _t2_tables(q):
    t2 = np.arange(128, dtype=np.float64)[:, None, None]
    k1 = np.arange(64, dtype=np.float64)[None, :, None]
    k2 = (32 * q - 1 + np.arange(34, dtype=np.float64))[None, None, :]
    kk = np.mod(k1 + 64 * k2, 8192)
    th = 2 * np.pi * np.mod(kk * t2, 8192) / 8192.0
    Mr, Mi = np.cos(th), -np.sin(th)
    ta = np.concatenate([Mr, Mi], axis=2)
    tb = np.concatenate([-Mi, Mr], axis=2)
    return np.concatenate([ta.reshape(128, -1), tb.reshape(128, -1)], axis=1).astype(np.float32)


_CACHE = {}


def kernel(x, c, ctx, c_ctx, w_mod, b_mod, norm1_g, w_in, conv_ssd_w, conv_ssd_b, dt_bias, a_log,
           d_skip, ssd_norm_g, w_fa, w_sb, w_o, norm2_g, w_up, conv_ffn_w, conv_ffn_b, w_down, final_g):
    f = lambda a: np.asarray(a, np.float32)
    x, c, ctx, c_ctx = f(x), f(c), f(ctx), f(c_ctx)
    cmk, t1, tcs = _const_tables()
    in_maps = []
    bm = f(b_mod)[0]
    for core in range(8):
        b, q = divmod(core, 4)
        e0 = 2048 * q - 64
        xext = np.zeros((EXT + 128, D), np.float32)
        lo, hi = max(e0, 0), min(e0 + EXT, SEQ)
        xext[lo - e0:hi - e0] = x[b, lo:hi]
        hv = np.zeros(2, np.float32)
        if e0 - 1 >= 0:
            xext[EXT] = x[b, e0 - 1]; hv[0] = 1
        if e0 + EXT < SEQ:
            xext[EXT + 1] = x[b, e0 + EXT]; hv[1] = 1
        tok = e0 + np.arange(EXT)
        valid = ((tok >= 0) & (tok < SEQ)).astype(np.float32)
        fmask = np.zeros((NSLOT, 128, 2), np.float32)
        fmask[0:2, :, 0] = 1
        fmask[66:68, :, 1] = 1
        lt = np.arange(SEQ).reshape(NCH, 128)
        fmask[2:66, :, 0] = (lt < e0)
        fmask[2:66, :, 1] = (lt >= e0 + EXT)
        cpp_a = np.zeros((128, CPP_N), np.float32)

        def put(key, arr):
            o, n = CPP[key]
            cpp_a[:, o:o + n] = arr
        put("c", _pm(c[b])); put("cctx", _pm(c_ctx)); put("bmod", _pm(bm))
        put("n1g", _pm(f(norm1_g)[0])); put("n2g", _pm(f(norm2_g)[0]))
        put("cw_ssd", np.concatenate([_pm(f(conv_ssd_w)[0, t]) for t in range(3)], axis=1))
        put("cb_ssd", _pm(f(conv_ssd_b)[0]))
        cfw = f(conv_ffn_w)[0].reshape(9, 2 * DFF)
        put("cw_ffn", np.concatenate([_pm(cfw[t]) for t in range(9)], axis=1))
        put("cb_ffn", _pm(f(conv_ffn_b)[0]))
        put("emask", valid.reshape(NEXT, 128).T)
        put("fmask", fmask.transpose(1, 0, 2).reshape(128, NSLOT * 2))
        cbc_a = np.zeros((128, CBC_N), np.float32)

        def putb(key, arr):
            o, n = CBC[key]
            cbc_a[:, o:o + n] = arr
        putb("dt_bias", _rb(f(dt_bias)[0].reshape(-1))); putb("a_log", _rb(f(a_log)[0].reshape(-1)))
        putb("d_skip", _rb(f(d_skip)[0]))
        cbg_a = np.concatenate([_rb(f(ssd_norm_g)[0]), _rb(f(final_g)), _rb(bm[2048:3072]), _rb(bm[5120:6144])], axis=1)
        putb("emask_bc", _rb(np.concatenate([valid[:128], valid[-128:]])))
        hvb = np.zeros(128, np.float32); hvb[0:2] = hv
        putb("halo_v", _rb(hvb))
        in_maps.append(dict(
            xb=np.ascontiguousarray(x[b]), ctxb=np.ascontiguousarray(ctx[b]), xext=xext,
            w_mod=f(w_mod)[0], w_in=f(w_in)[0], w_fa=f(w_fa)[0], w_sb=f(w_sb)[0], w_o=f(w_o)[0],
            w_up=f(w_up)[0], w_down=f(w_down)[0], cpp=cpp_a, cbc=cbc_a, cbg=cbg_a, cbrow=f(conv_ssd_b)[0].reshape(1, 3072).copy(), cmk=cmk, t1=t1, t2=_t2_tables(q), tcs=tcs))
    if "nc" not in _CACHE:
        _CACHE["nc"] = build_program()
    res = run_bass_kernel_spmd(_CACHE["nc"], in_maps, core_ids=list(range(8)))
    if DEBUG:
        _CACHE["res"] = res
    out = np.zeros((2, SEQ, D), np.float32)
    for core in range(8):
        b, q = divmod(core, 4)
        out[b, 2048 * q:2048 * (q + 1)] = res.results[core]["out"]
    return out
```

```python
import os
from contextlib import ExitStack
import numpy as np
import concourse.bass as bass
import concourse.mybir as mybir
from concourse.bass_utils import run_bass_kernel_spmd

F32 = mybir.dt.float32
BF16 = mybir.dt.bfloat16
AF = mybir.ActivationFunctionType
OP = mybir.AluOpType

D = 1024
SEQ = 8192
NCH = 64
NEXT = 17
EXT = NEXT * 128
EPS = 1e-6
BIG = 30000.0
NSLOT = 68
DFF = 2816
NFB = 22

STOP = os.environ.get("MK_STOP", "")
DEBUG = bool(STOP)


class Buf:
    def __init__(self, t, nparts=1):
        self.t = t
        self.n = nparts
        self.w = [None] * nparts
        self.r = [[] for _ in range(nparts)]
        self.excl = False

    def __getitem__(self, idx):
        return self.t[idx]


def _parts(items):
    out = []
    for it in items:
        if it is None:
            continue
        if isinstance(it, Buf):
            out.extend((it, i) for i in range(it.n))
        else:
            b, idx = it
            if isinstance(idx, int):
                out.append((b, idx))
            else:
                out.extend((b, i) for i in idx)
    return out


class Prog:
    def __init__(self, nc, es):
        self.nc = nc
        self.E = {}
        self.semid = 0
        for name, eng in (("pe", nc.tensor), ("act", nc.scalar), ("dve", nc.vector), ("pool", nc.gpsimd), ("sp", nc.sync)):
            sem = es.enter_context(nc.semaphore("s_" + name))
            self.E[name] = dict(name=name, eng=eng, sem=(self._sid(), sem), count=0, waited={}, pool=[], ndma=0)
        for name, n in (("sp", 8), ("pool", 6), ("act", 4)):
            for i in range(n):
                sem = es.enter_context(nc.semaphore("d_%s%d" % (name, i)))
                self.E[name]["pool"].append((self._sid(), sem))
        self.ninst = 0

    def _sid(self):
        self.semid += 1
        return self.semid

    def _wait(self, E, tok):
        (sid, sem), val, _ = tok
        if E["waited"].get(sid, 0) >= val:
            return
        E["eng"].wait_ge(sem, val)
        E["waited"][sid] = val

    def _collect(self, en, reads, writes):
        toks = []
        for b, i in _parts(reads):
            if b.w[i] is not None:
                toks.append(b.w[i])
            if b.excl:
                toks.extend(t for t in b.r[i] if t[2] != en)
        for b, i in _parts(writes):
            if b.w[i] is not None:
                toks.append(b.w[i])
            toks.extend(b.r[i])
        res = []
        for t in toks:
            if en == "pe" and t[2] == "pe":
                continue
            res.append(t)
        return res

    def _update(self, reads, writes, tok):
        for b, i in _parts(reads):
            b.r[i].append(tok)
            if len(b.r[i]) > 24:
                last = {}
                for t in b.r[i]:
                    k = t[0][0]
                    if k not in last or last[k][1] < t[1]:
                        last[k] = t
                b.r[i] = list(last.values())
        for b, i in _parts(writes):
            b.w[i] = tok
            b.r[i] = []

    def op(self, en, fn, reads=(), writes=()):
        E = self.E[en]
        for t in self._collect(en, reads, writes):
            self._wait(E, t)
        ins = fn(E["eng"])
        E["count"] += 1
        ins.then_inc(E["sem"][1], 1)
        tok = (E["sem"], E["count"], en)
        self._update(reads, writes, tok)
        self.ninst += 1
        return tok

    def dma(self, qn, out, in_, reads=(), writes=(), **kw):
        Q = self.E[qn]
        i = Q["ndma"]
        P = len(Q["pool"])
        sem = Q["pool"][i % P]
        val = 16 * (i // P + 1)
        if i >= P:
            self._wait(Q, (sem, val - 16, "dma"))
        for t in self._collect("dma", reads, writes):
            self._wait(Q, t)
        Q["eng"].dma_start(out=out, in_=in_, **kw).then_inc(sem[1], 16)
        Q["ndma"] += 1
        tok = (sem, val, "dma")
        self._update(reads, writes, tok)
        return tok

    def all_tokens(self):
        toks = []
        for E in self.E.values():
            if E["count"]:
                toks.append((E["sem"], E["count"], E["name"]))
            P = len(E["pool"])
            for j in range(min(P, E["ndma"])):
                n = (E["ndma"] - 1 - j) // P + 1
                toks.append((E["pool"][j], 16 * n, "dma"))
        return toks

    def barrier(self):
        toks = self.all_tokens()
        for E in self.E.values():
            for t in toks:
                self._wait(E, t)


def bc(ap, shape):
    return ap.broadcast_to(shape)


class Ctx:
    pass


def build_program():
    nc = bass.Bass("TRN2", target_bir_lowering=False)
    g = Ctx()
    g.nc = nc

    def din(name, shape, dt=F32):
        return nc.dram_tensor(name, list(shape), dt, kind="ExternalInput").ap()

    def dscr(name, shape, dt):
        kind = "ExternalOutput" if DEBUG else "Internal"
        return nc.dram_tensor(name, list(shape), dt, kind=kind).ap()

    g.xb = din("xb", [SEQ, D])
    g.ctxb = din("ctxb", [256, D])
    g.xext = din("xext", [EXT + 128, D])
    g.w_mod = din("w_mod", [D, 6 * D])
    g.w_in = din("w_in", [D, 8256])
    g.w_fa = din("w_fa", [D, D])
    g.w_sb = din("w_sb", [2048, D])
    g.w_o = din("w_o", [D, D])
    g.w_up = din("w_up", [D, 2 * DFF])
    g.w_down = din("w_down", [DFF, D])
    g.cpp = din("cpp", [128, CPP_N])
    g.cbc = din("cbc", [128, CBC_N])
    g.cbg = din("cbg", [128, CBG_N])
    g.cbrow = din("cbrow", [1, 3072])
    g.cmk = din("cmk", [128, 8 * 128])
    g.t1 = din("t1", [64, 128])
    g.t2 = din("t2", [128, 2 * 64 * 68])
    g.tcs = din("tcs", [128, 256])
    g.out = nc.dram_tensor("out", [2048, D], F32, kind="ExternalOutput").ap()
    g.U = dscr("U", [NCH, 128, D], BF16)
    g.Y = dscr("Y", [128, 128, D], BF16)
    g.XT = dscr("XT", [128, 8 * 2 * EXT], BF16)
    g.MS = dscr("MS", [NEXT, 128, D], BF16)
    g.YF = dscr("YF", [NEXT, 128, 2048], BF16)
    g.YT = dscr("YT", [NEXT, 128, 2048], BF16)
    g.L1 = dscr("L1", [NEXT, 128, D], F32)
    g.SFB = dscr("SFB", [2, 128, 2048], F32)

    with ExitStack() as es:
        P = Prog(nc, es)
        g.P = P
        g.uid = 0
        phase_setup(g, es)
        with ExitStack() as es2:
            alloc_conv_consts(g, es2)
            if STOP != "setup":
                phase_far(g)
            if STOP not in ("setup", "far"):
                phase_fnet(g)
            if STOP not in ("setup", "far", "fnet"):
                phase_own(g, 0)
                phase_own(g, 1)
            P.barrier()
        if STOP not in ("setup", "far", "fnet", "own"):
            phase_merge(g)
        if STOP not in ("setup", "far", "fnet", "own", "merge"):
            phase_ffn(g)
        P.barrier()
    return nc


def sb(g, es, name, shape, dt, nparts=1):
    g.uid += 1
    t = es.enter_context(g.nc.sbuf_tensor("%s_%d" % (name, g.uid), list(shape), dt))
    return Buf(t, nparts)


def ps(g, es, name, shape, dt=F32, nparts=1):
    g.uid += 1
    t = es.enter_context(g.nc.psum_tensor("%s_%d" % (name, g.uid), list(shape), dt))
    b = Buf(t, nparts)
    b.excl = True
    return b


def _layout(items):
    off = {}
    o = 0
    for k, n in items:
        off[k] = (o, n)
        o += n
    return off, o


CPP, CPP_N = _layout([("c", 8), ("cctx", 8), ("bmod", 48), ("n1g", 8), ("n2g", 8), ("cw_ssd", 72), ("cb_ssd", 24),
                      ("cw_ffn", 9 * 44), ("cb_ffn", 44), ("emask", NEXT), ("fmask", NSLOT * 2)])
CBC, CBC_N = _layout([("dt_bias", 64), ("a_log", 64), ("d_skip", 32), ("emask_bc", 256), ("halo_v", 128)])
CBG, CBG_N = _layout([("ssd_g", 2048), ("final_g", 1024), ("bmod_g1", 1024), ("bmod_g2", 1024)])
MK = {k: i for i, k in enumerate(["ones", "le", "ge", "gt", "lt", "ident", "pen_f", "pen_b"])}


def cpp(g, key, j=None, n=1):
    o, _ = CPP[key]
    if j is None:
        return g.cpp_t[:, o:o + CPP[key][1]]
    return g.cpp_t[:, o + j:o + j + n]


def cbcv(g, key, a=0, n=None):
    o, m = CBC[key]
    if n is None:
        n = m
    return g.cbc_t[:, o + a:o + a + n]


def mk32(g, key):
    i = MK[key]
    return g.cmk_t[:, i * 128:(i + 1) * 128]


def mk16(g, key):
    i = MK[key]
    return g.cmkb_t[:, i * 128:(i + 1) * 128]


def phase_setup(g, es):
    nc, P = g.nc, g.P
    g.cpp_t = sb(g, es, "cpp", [128, CPP_N], F32)
    g.cbc_t = sb(g, es, "cbc", [128, CBC_N], F32)
    g.cmk_t = sb(g, es, "cmk", [128, 8 * 128], F32)
    g.cmkb_t = sb(g, es, "cmkb", [128, 8 * 128], BF16)
    g.modv = sb(g, es, "modv", [128, 8 * 8], F32)
    g.gbc = sb(g, es, "gbc", [128, 2 * D], F32)
    g.nega = sb(g, es, "nega", [128, 64], F32)
    g.Sf = sb(g, es, "Sf", [128, 2048], F32)
    g.Sb = sb(g, es, "Sb", [128, 2048], F32)
    P.dma("sp", g.cpp_t[:], g.cpp, writes=[g.cpp_t])
    P.dma("sp", g.cbc_t[:], g.cbc, writes=[g.cbc_t])
    P.dma("sp", g.cmk_t[:], g.cmk, writes=[g.cmk_t])
    P.dma("pool", g.cmkb_t[:], g.cmk, writes=[g.cmkb_t])
    with ExitStack() as ls:
        sc = sb(g, ls, "sc", [128, 8, 2], F32)
        screp = sb(g, ls, "screp", [128, 8, 128], F32)
        modT = sb(g, ls, "modT", [128, 48, 2], F32)
        wm = [sb(g, ls, "wm%d" % i, [128, 8, 1024], F32) for i in range(2)]
        pm = ps(g, ls, "pm", [128, 8, 2], F32)
        pg = ps(g, ls, "pg", [128, 512], F32)
        bg = sb(g, ls, "bg", [128, 2 * D], F32)
        P.dma("sp", bg[:], g.cbg[:, CBG["bmod_g1"][0]:CBG["bmod_g1"][0] + 2 * D], writes=[bg])
        P.op("act", lambda e: e.activation(out=sc[:, :, 0], in_=cpp(g, "c"), func=AF.Silu), [g.cpp_t], [sc])
        P.op("act", lambda e: e.activation(out=sc[:, :, 1], in_=cpp(g, "cctx"), func=AF.Silu), [g.cpp_t], [sc])
        P.op("dve", lambda e: e.tensor_copy(out=screp[:], in_=bc(sc[:, :, 0:1], [128, 8, 128])), [sc], [screp])
        wv = g.w_mod.rearrange("(kb p) n -> p kb n", p=128)
        for j in range(6):
            w = wm[j % 2]
            P.dma("sp", w[:], wv[:, :, j * 1024:(j + 1) * 1024], writes=[w])
            def mm(e, w=w):
                ins = None
                for fb in range(8):
                    for kb in range(8):
                        ins = e.matmul(pm[:, fb, :], lhsT=w[:, kb, fb * 128:(fb + 1) * 128], rhs=sc[:, kb, :],
                                       start=(kb == 0), stop=(kb == 7))
                return ins
            P.op("pe", mm, [w, sc], [pm])
            bo = CPP["bmod"][0] + j * 8
            P.op("dve", lambda e, j=j, bo=bo: e.tensor_tensor(
                out=modT[:, j * 8:(j + 1) * 8, :], in0=pm[:], in1=bc(g.cpp_t[:, bo:bo + 8].unsqueeze(2), [128, 8, 2]),
                op=OP.add), [pm, g.cpp_t], [modT])
            if j in (2, 5):
                gi = 0 if j == 2 else 1
                for hf in range(2):
                    def mg(e, w=w, hf=hf):
                        ins = None
                        for kb in range(8):
                            ins = e.matmul(pg[:], lhsT=screp[:, kb, :], rhs=w[:, kb, hf * 512:(hf + 1) * 512],
                                           start=(kb == 0), stop=(kb == 7))
                        return ins
                    P.op("pe", mg, [w, screp], [pg])
                    P.op("dve", lambda e, gi=gi, hf=hf: e.tensor_tensor(
                        out=g.gbc[:, gi * D + hf * 512: gi * D + (hf + 1) * 512], in0=pg[:],
                        in1=bg[:, gi * D + hf * 512: gi * D + (hf + 1) * 512], op=OP.add), [pg, bg], [g.gbc])
        mv = g.modv
        def mkA(dst, scale_j, which, gkey):
            P.op("dve", lambda e: e.scalar_tensor_tensor(
                out=mv[:, dst * 8:(dst + 1) * 8], in0=modT[:, scale_j * 8:(scale_j + 1) * 8, which], scalar=1.0,
                in1=cpp(g, gkey), op0=OP.add, op1=OP.mult), [modT, g.cpp_t], [mv])

        def mkB(dst, shift_j, which):
            P.op("dve", lambda e: e.tensor_copy(out=mv[:, dst * 8:(dst + 1) * 8],
                                                 in_=modT[:, shift_j * 8:(shift_j + 1) * 8, which]), [modT], [mv])
        mkA(0, 1, 0, "n1g"); mkB(1, 0, 0)
        mkA(2, 1, 1, "n1g"); mkB(3, 0, 1)
        mkA(4, 4, 0, "n2g"); mkB(5, 3, 0)
        P.op("act", lambda e: e.activation(out=g.nega[:], in_=cbcv(g, "a_log"), func=AF.Exp), [g.cbc_t], [g.nega])
        P.op("dve", lambda e: e.tensor_scalar(out=g.nega[:], in0=g.nega[:], scalar1=-1.0, scalar2=None, op0=OP.mult),
             [g.nega], [g.nega])
        P.barrier()


def alloc_conv_consts(g, es):
    P = g.P
    g.diag = sb(g, es, "diag", [128, 72, 128], BF16)
    g.brow = sb(g, es, "brow", [1, 3072], BF16)
    for a_ in range(0, 3072, 1024):
        P.dma("pool", g.brow[:, a_:a_ + 1024], g.cbrow[:, a_:a_ + 1024], writes=[g.brow])
    for i_ in range(72):
        P.op("dve", lambda e, i_=i_: e.tensor_scalar(out=g.diag[:, i_, :], in0=mk16(g, "ident"),
                                                      scalar1=cpp(g, "cw_ssd", i_), scalar2=None, op0=OP.mult),
             [g.cmkb_t, g.cpp_t], [g.diag])


def modA(g, i, kb):
    return g.modv[:, i * 8 + kb:i * 8 + kb + 1]


def alloc_chunk_bufs(g, es, nfb):
    c = Ctx()
    c.xt = [sb(g, es, "xt%d" % i, [128, D], F32) for i in range(2)]
    c.junk = sb(g, es, "junk", [128, D], BF16)
    c.st = [sb(g, es, "st%d" % i, [128, 4], F32) for i in range(2)]
    c.xn = [sb(g, es, "xn%d" % i, [128, D], BF16) for i in range(2)]
    c.hTe = [sb(g, es, "hTe%d" % i, [128, 8, 130], BF16) for i in range(3)]
    c.pA = ps(g, es, "pA", [128, 8, 128], BF16)
    c.nfb = nfb
    return c


def prep_a(g, c, k, src_rows):
    P = g.P
    i2 = k % 2
    xt, st, xn = c.xt[i2], c.st[i2], c.xn[i2]
    P.dma("sp", xt[:], src_rows, writes=[xt])
    P.op("act", lambda e: e.activation(out=c.junk[:], in_=xt[:], func=AF.Square, accum_out=st[:, 0:1]),
         [xt], [c.junk, st])
    P.op("dve", lambda e: e.tensor_scalar(out=st[:, 1:2], in0=st[:, 0:1], scalar1=1.0 / D, scalar2=EPS,
                                           op0=OP.mult, op1=OP.add), [st], [st])
    P.op("act", lambda e: e.activation(out=st[:, 2:3], in_=st[:, 1:2], func=AF.Ln), [st], [st])
    P.op("act", lambda e: e.activation(out=st[:, 3:4], in_=st[:, 2:3], func=AF.Exp, scale=-0.5), [st], [st])
    P.op("act", lambda e: e.activation(out=xn[:], in_=xt[:], func=AF.Copy, scale=st[:, 3:4]), [xt, st], [xn])
    return xt


def prep_b(g, c, k, ai, bi, vmask=None, hT=None):
    P = g.P
    xn = c.xn[k % 2]
    if hT is None:
        hT = c.hTe[k % 3]

    def tr(e):
        ins = None
        for kb in range(8):
            ins = e.transpose(out=c.pA[:, kb, :], in_=xn[:, kb * 128:(kb + 1) * 128], identity=mk16(g, "ident"))
        return ins
    P.op("pe", tr, [xn, g.cmkb_t], [c.pA])
    P.op("dve", lambda e: e.tensor_tensor(out=hT[:, :, 1:129], in0=c.pA[:],
                                          in1=bc(g.modv[:, ai * 8:(ai + 1) * 8].unsqueeze(2), [128, 8, 128]),
                                          op=OP.mult), [c.pA, g.modv], [hT])
    P.op("dve", lambda e: e.tensor_tensor(out=hT[:, :, 1:129], in0=hT[:, :, 1:129],
                                          in1=bc(g.modv[:, bi * 8:(bi + 1) * 8].unsqueeze(2), [128, 8, 128]),
                                          op=OP.add), [hT, g.modv], [hT])
    if vmask is not None:
        P.op("pool", lambda e: e.tensor_tensor(out=hT[:, :, 1:129], in0=hT[:, :, 1:129],
                                               in1=bc(vmask.unsqueeze(1), [128, 8, 128]), op=OP.mult),
             [hT, g.cbc_t], [hT])


def prep(g, c, k, src_rows, ai, bi, vmask=None, hT=None):
    xt = prep_a(g, c, k, src_rows)
    prep_b(g, c, k, ai, bi, vmask=vmask, hT=hT)
    return xt


def halo_link(g, c, k, has_left):
    P = g.P
    cur = c.hTe[k % 3]
    if has_left:
        prv = c.hTe[(k - 1) % 3]
        P.op("pool", lambda e: e.tensor_copy(out=cur[:, :, 0:1], in_=prv[:, :, 128:129]), [prv], [cur])
        P.op("pool", lambda e: e.tensor_copy(out=prv[:, :, 129:130], in_=cur[:, :, 1:2]), [cur], [prv])
    else:
        P.op("pool", lambda e: e.memset(cur[:, :, 0:1], 0.0), [], [cur])


def halo_zero_right(g, c, k):
    cur = c.hTe[k % 3]
    g.P.op("pool", lambda e: e.memset(cur[:, :, 129:130], 0.0), [], [cur])


def fm_proj_conv(g, c, s, hT, W, nfb, cw_off, steps=None):
    P = g.P
    steps = steps if steps is not None else []
    groups = [list(range(a, min(a + 3, nfb))) for a in range(0, nfb, 3)]
    def e_proj(gi):
        fbs = groups[gi]
        pa = s.pxa[gi % len(s.pxa)]

        def mm(e, fbs=fbs, pa=pa):
            ins = None
            for j, fb in enumerate(fbs):
                for kb in range(8):
                    ins = e.matmul(pa[:, j * 130:(j + 1) * 130], lhsT=W[:, kb, fb * 128:(fb + 1) * 128], rhs=hT[:, kb, :],
                                   start=(kb == 0), stop=(kb == 7))
            return ins
        P.op("pe", mm, [W, hT], [pa])

    def e_rest(gi):
        fbs = groups[gi]
        n = len(fbs)
        pa = s.pxa[gi % len(s.pxa)]
        pb = s.pxc[gi % len(s.pxc)]
        pre = s.pre[gi % 2]
        P.op("dve", lambda e, n=n, pa=pa, pre=pre: e.tensor_copy(
            out=pre[:, 0:n, :], in_=pa[:, 0:n * 130].rearrange("p (j t) -> p j t", t=130)), [pa], [pre])

        def mc(e, fbs=fbs, pb=pb, pre=pre):
            ins = None
            for j, fb in enumerate(fbs):
                cf = cw_off + fb
                for k in range(3):
                    e.matmul(pb[:, j * 128:(j + 1) * 128], lhsT=g.diag[:, k * 24 + cf, :], rhs=pre[:, j, k:k + 128],
                             start=(k == 0), stop=False)
                ins = e.matmul(pb[:, j * 128:(j + 1) * 128], lhsT=g.brow[0:1, cf * 128:(cf + 1) * 128],
                               rhs=mk16(g, "ones")[0:1, :], start=False, stop=True)
            return ins
        P.op("pe", mc, [pre, g.diag, g.brow, g.cmkb_t], [pb])
        f0, f1 = fbs[0], fbs[-1] + 1
        P.op("act", lambda e, f0=f0, f1=f1, n=n, pb=pb: e.activation(
            out=s.xcs[:, f0:f1, :], in_=pb[:, 0:n * 128].rearrange("p (j t) -> p j t", t=128), func=AF.Silu),
            [pb], [(s.xcs, gi)])

    ng = len(groups)
    ahead = len(s.pxa) >= 2
    if ahead:
        e_proj(0)
    for gi in range(ng):
        if ahead:
            if gi + 1 < ng:
                e_proj(gi + 1)
        else:
            e_proj(gi)
        e_rest(gi)
        if steps:
            steps.pop(0)()
    while steps:
        steps.pop(0)()


def to_token_major(g, c, s, nblk):
    P = g.P
    for r0 in range(0, nblk, 8):
        n = min(8, nblk - r0)

        def tr(e, r0=r0, n=n):
            ins = None
            for j in range(n):
                ins = e.transpose(out=c.pA[:, j, :], in_=s.xcs[:, r0 + j, :], identity=mk16(g, "ident"))
            return ins
        P.op("pe", tr, [s.xcs, g.cmkb_t], [c.pA])
        P.op("act", lambda e, r0=r0, n=n: e.activation(
            out=s.xtok[:, r0 * 128:(r0 + n) * 128], in_=c.pA[:, 0:n, :], func=AF.Copy), [c.pA], [s.xtok])


def dt_steps(g, s, hT, Wdt, ncol, bias_ap, nega_ap, mask_ap):
    P = g.P

    def s1():
        def mm(e):
            ins = None
            for kb in range(8):
                ins = e.matmul(s.pD[:, 0:ncol], lhsT=hT[:, kb, 1:129], rhs=Wdt[:, kb, 0:ncol], start=(kb == 0), stop=(kb == 7))
            return ins
        P.op("pe", mm, [hT, Wdt], [s.pD])
        P.op("dve", lambda e: e.tensor_tensor(out=s.dtm[:, 0:ncol], in0=s.pD[:, 0:ncol], in1=bias_ap, op=OP.add),
             [s.pD, g.cbc_t], [s.dtm])

    def s2():
        P.op("act", lambda e: e.activation(out=s.dtm[:, 0:ncol], in_=s.dtm[:, 0:ncol], func=AF.Exp), [s.dtm], [s.dtm])

    def s3():
        P.op("act", lambda e: e.activation(out=s.dtm[:, 0:ncol], in_=s.dtm[:, 0:ncol], func=AF.Ln, bias=1.0), [s.dtm], [s.dtm])

    def s4():
        dv = s.dtm[:, 0:ncol].rearrange("p (a b) -> p a b", b=32)
        P.op("dve", lambda e: e.tensor_tensor(out=dv, in0=dv, in1=mask_ap, op=OP.mult), [s.dtm, g.cpp_t], [s.dtm])

    def s5():
        P.op("dve", lambda e: e.tensor_tensor(out=s.la[:, 0:ncol], in0=s.dtm[:, 0:ncol], in1=nega_ap, op=OP.mult),
             [s.dtm, g.nega], [s.la])
    return [s1, s2, s3, s4, s5]


def state_contrib(g, s, wexp_ap, xdd, on_group):
    P = g.P
    P.op("dve", lambda e: e.tensor_tensor(
        out=xdd[:].rearrange("p (h d) -> p h d", d=64), in0=s.xtok[:, 0:2048].rearrange("p (h d) -> p h d", d=64),
        in1=bc(wexp_ap.unsqueeze(2), [128, 32, 64]), op=OP.mult), [s.xtok, s.wx], [xdd])
    for gi in range(4):
        P.op("pe", lambda e, gi=gi: e.matmul(s.pH[:], lhsT=s.xtok[:, 2048 + gi * 128:2048 + (gi + 1) * 128],
                                             rhs=xdd[:, gi * 512:(gi + 1) * 512], start=True, stop=True),
             [s.xtok, xdd], [s.pH])
        on_group(gi, s.pH)


def load_w_cols(g, W, col0, ncols, dst0=0):
    wv = g.w_in.rearrange("(kb p) n -> p kb n", p=128)
    for a in range(0, ncols, 512):
        n = min(512, ncols - a)
        g.P.dma("pool", W[:, :, dst0 + a:dst0 + a + n], wv[:, :, col0 + a:col0 + a + n], writes=[W])


def phase_far(g):
    nc, P = g.nc, g.P
    with ExitStack() as es:
        c = alloc_chunk_bufs(g, es, 20)
        s = Ctx()
        Wf = sb(g, es, "Wf", [128, 8, 1024], BF16)
        Wxb = sb(g, es, "Wxb", [128, 8, 2560], BF16)
        Wdt = sb(g, es, "Wdt", [128, 8, 64], BF16)
        load_w_cols(g, Wf, 0, 1024)
        load_w_cols(g, Wxb, 1024, 2560)
        load_w_cols(g, Wdt, 6144, 64)
        s.pxa = [ps(g, es, "pxa%d" % i, [128, 512], F32) for i in range(2)]
        s.pxc = [ps(g, es, "pxc%d" % i, [128, 512], F32) for i in range(1)]
        s.pD = ps(g, es, "pD", [128, 512], F32)
        s.pH = ps(g, es, "pH", [128, 512], F32)
        pf = [ps(g, es, "pf%d" % i, [128, 512], F32) for i in range(2)]
        s.pre = [sb(g, es, "pre%d" % i, [128, 3, 130], BF16) for i in range(2)]
        s.xcs = sb(g, es, "xcs", [128, 20, 128], BF16, nparts=8)
        s.pD2 = sb(g, es, "pD2", [128, 192], F32)
        s.xtok = sb(g, es, "xtok", [128, 2560], BF16)
        s.dtm = sb(g, es, "dtm", [128, 64], F32)
        s.la = sb(g, es, "la", [128, 64], F32)
        s.wx = sb(g, es, "wx", [128, 64], F32)
        s.sg = sb(g, es, "sg", [128, 64], F32)
        Rb = sb(g, es, "Rb", [128, 32], F32)
        wxb = sb(g, es, "wxb", [128, 64], BF16)
        dec = sb(g, es, "dec", [128, 32], F32)
        xdd = [sb(g, es, "xdd%d" % i, [128, 2048], BF16) for i in range(2)]
        ub = [sb(g, es, "ub%d" % i, [128, D], BF16) for i in range(2)]
        P.op("dve", lambda e: e.memset(g.Sf[:], 0.0), [], [g.Sf])
        P.op("dve", lambda e: e.memset(g.Sb[:], 0.0), [], [g.Sb])
        P.op("dve", lambda e: e.memset(Rb[:], 0.0), [], [Rb])
        slots = [("c", 0), ("c", 1)] + [("l", i) for i in range(NCH)] + [("c", 0), ("c", 1)]
        first = {0, 2, 66}
        last = {1, 65, 67}

        def do_prep_a(k):
            kind, i = slots[k]
            src = g.ctxb[i * 128:(i + 1) * 128, :] if kind == "c" else g.xb[i * 128:(i + 1) * 128, :]
            prep_a(g, c, k, src)

        def do_prep_b(k):
            kind, i = slots[k]
            if kind == "c":
                prep_b(g, c, k, 2, 3)
            else:
                prep_b(g, c, k, 0, 1)
            halo_link(g, c, k, k not in first)
            if k in last:
                halo_zero_right(g, c, k)
        do_prep_a(0); do_prep_b(0)
        do_prep_a(1); do_prep_b(1)
        for k in range(NSLOT):
            kind, i = slots[k]
            hT = c.hTe[k % 3]
            fo = CPP["fmask"][0] + 2 * k
            mask_ap = bc(g.cpp_t[:, fo:fo + 2].unsqueeze(2), [128, 2, 32])
            steps = dt_steps(g, s, hT, Wdt, 64, cbcv(g, "dt_bias"), g.nega[:], mask_ap)

            def t1():
                def segs(e):
                    e.matmul(s.pD[:, 64:96], lhsT=mk32(g, "gt"), rhs=s.la[:, 0:32], start=True, stop=True)
                    e.matmul(s.pD[:, 96:128], lhsT=mk32(g, "lt"), rhs=s.la[:, 32:64], start=True, stop=True)
                    return e.matmul(s.pD[:, 128:192], lhsT=mk32(g, "ones"), rhs=s.la[:, 0:64], start=True, stop=True)
                P.op("pe", segs, [s.la, g.cmk_t], [s.pD])
                P.op("dve", lambda e: e.tensor_copy(out=s.pD2[:], in_=s.pD[:, 0:192]), [s.pD], [s.pD2])

            def t2b():
                P.op("pool", lambda e: e.tensor_copy(out=s.sg[:, 0:32], in_=s.pD2[:, 64:96]), [s.pD2], [s.sg])
                P.op("pool", lambda e: e.tensor_tensor(out=s.sg[:, 32:64], in0=s.pD2[:, 96:128], in1=Rb[:], op=OP.add),
                     [s.pD2, Rb], [s.sg])
                P.op("pool", lambda e: e.tensor_tensor(out=Rb[:], in0=Rb[:], in1=s.pD2[:, 160:192], op=OP.add),
                     [s.pD2, Rb], [Rb])

            def t3():
                P.op("act", lambda e: e.activation(out=s.wx[:], in_=s.sg[:], func=AF.Exp), [s.sg], [s.wx])
                P.op("act", lambda e: e.activation(out=dec[:], in_=s.pD2[:, 128:160], func=AF.Exp), [s.pD2], [dec])

            def t4():
                P.op("pool", lambda e: e.tensor_tensor(out=wxb[:], in0=s.wx[:], in1=s.dtm[:], op=OP.mult),
                     [s.wx, s.dtm], [wxb])
                P.op("pool", lambda e: e.tensor_tensor(
                    out=g.Sf[:].rearrange("p (h d) -> p h d", d=64), in0=g.Sf[:].rearrange("p (h d) -> p h d", d=64),
                    in1=bc(dec[:].unsqueeze(2), [128, 32, 64]), op=OP.mult), [g.Sf, dec], [g.Sf])
            for st_ in steps + [t1, t2b, t3, t4]:
                st_()
            if k + 2 < NSLOT:
                do_prep_a(k + 2)
            if kind == "l":
                u = ub[i % 2]
                for hf in range(2):
                    def mm(e, hf=hf):
                        ins = None
                        for kb in range(8):
                            ins = e.matmul(pf[hf][:], lhsT=hT[:, kb, 1:129], rhs=Wf[:, kb, hf * 512:(hf + 1) * 512],
                                           start=(kb == 0), stop=(kb == 7))
                        return ins
                    P.op("pe", mm, [hT, Wf], [pf[hf]])
                    P.op("dve", lambda e, hf=hf, u=u: e.tensor_copy(out=u[:, hf * 512:(hf + 1) * 512], in_=pf[hf][:]),
                         [pf[hf]], [u])
                P.dma("sp", g.U[i], u[:], reads=[u])
            fm_proj_conv(g, c, s, hT, Wxb, 20, 0, [])
            if k + 2 < NSLOT:
                do_prep_b(k + 2)
            to_token_major(g, c, s, 20)
            x3 = s.xtok[:, 0:2048].rearrange("p (h d) -> p h d", d=64)
            P.op("pool", lambda e: e.tensor_tensor(out=xdd[1][:].rearrange("p (h d) -> p h d", d=64), in0=x3,
                                                   in1=bc(wxb[:, 32:64].unsqueeze(2), [128, 32, 64]), op=OP.mult),
                 [s.xtok, wxb], [xdd[1]])
            P.op("dve", lambda e: e.tensor_tensor(out=xdd[0][:].rearrange("p (h d) -> p h d", d=64), in0=x3,
                                                  in1=bc(wxb[:, 0:32].unsqueeze(2), [128, 32, 64]), op=OP.mult),
                 [s.xtok, wxb], [xdd[0]])
            banks = [s.pxa[0], s.pxa[1], s.pxc[0], s.pH]
            for di, (xd_, S_) in enumerate(((xdd[0], g.Sf), (xdd[1], g.Sb))):
                for gi in range(4):
                    pst = banks[gi]
                    P.op("pe", lambda e, gi=gi, pst=pst, xd_=xd_: e.matmul(
                        pst[:], lhsT=s.xtok[:, 2048 + gi * 128:2048 + (gi + 1) * 128],
                        rhs=xd_[:, gi * 512:(gi + 1) * 512], start=True, stop=True), [s.xtok, xd_], [pst])
                    P.op("dve", lambda e, gi=gi, pst=pst, S_=S_: e.tensor_tensor(
                        out=S_[:, gi * 512:(gi + 1) * 512], in0=S_[:, gi * 512:(gi + 1) * 512], in1=pst[:], op=OP.add),
                        [S_, pst], [S_])
        if DEBUG:
            P.dma("sp", g.SFB[0], g.Sf[:], reads=[g.Sf])
            P.dma("sp", g.SFB[1], g.Sb[:], reads=[g.Sb])
        P.barrier()


def load_w_gen(g, W, src, nkb, ncols):
    wv = src.rearrange("(kb p) n -> p kb n", p=128)
    for a in range(0, ncols, 512):
        n = min(512, ncols - a)
        g.P.dma("pool", W[:, :, a:a + n], wv[:, :, a:a + n], writes=[W])


def phase_fnet(g):
    nc, P = g.nc, g.P
    with ExitStack() as es:
        T1 = sb(g, es, "T1", [64, 128], BF16)
        P.dma("pool", T1[:], g.t1, writes=[T1])
        V = [sb(g, es, "V%d" % i, [64, 4, D], BF16) for i in range(2)]
        Yt = [sb(g, es, "Yt%d" % i, [128, 4, D], BF16) for i in range(2)]
        p1 = [ps(g, es, "p1_%d" % i, [128, 512], F32) for i in range(4)]
        cnt = 0
        for tg in range(32):
            v, yt = V[tg % 2], Yt[tg % 2]
            P.dma("sp", v[:], g.U[:, tg * 4:(tg + 1) * 4, :], writes=[v])
            for t in range(4):
                for hf in range(2):
                    pp = p1[cnt % 4]
                    P.op("pe", lambda e, pp=pp, t=t, hf=hf, v=v: e.matmul(
                        pp[:], lhsT=T1[:], rhs=v[:, t, hf * 512:(hf + 1) * 512], start=True, stop=True), [T1, v], [pp])
                    if cnt % 2:
                        P.op("act", lambda e, pp=pp, t=t, hf=hf, yt=yt: e.activation(
                            out=yt[:, t, hf * 512:(hf + 1) * 512], in_=pp[:], func=AF.Copy), [pp], [yt])
                    else:
                        P.op("dve", lambda e, pp=pp, t=t, hf=hf, yt=yt: e.tensor_copy(
                            out=yt[:, t, hf * 512:(hf + 1) * 512], in_=pp[:]), [pp], [yt])
                    cnt += 1
            P.dma("sp", g.Y[:, tg * 4:(tg + 1) * 4, :], yt[:], reads=[yt])
        P.barrier()
    with ExitStack() as es:
        T2 = sb(g, es, "T2", [128, 2 * 64 * 68], BF16)
        for a in range(0, 2 * 64 * 68, 1088):
            P.dma("pool", T2[:, a:a + 1088], g.t2[:, a:a + 1088], writes=[T2])
        Yk = [sb(g, es, "Yk%d" % i, [128, 2, D], BF16) for i in range(2)]
        XTs = sb(g, es, "XTs", [128, 8, 2, EXT], BF16)
        p2f = [ps(g, es, "p2_%d" % i, [128, 512], F32) for i in range(4)]
        yv = g.Y.rearrange("(ri k) t c -> k t ri c", ri=2)
        xv = XTs[:].rearrange("p c r (j k) -> p c r j k", k=64)
        for k1 in range(64):
            yk = Yk[k1 % 2]
            P.dma("sp", yk[:], yv[k1], writes=[yk])
            for cg in range(2):
                ppb = p2f[(k1 * 2 + cg) % 4]
                pp = ppb[:, 0:272].rearrange("p (c k) -> p c k", k=68)

                def mm(e, pp=pp, cg=cg, yk=yk, k1=k1):
                    ins = None
                    for cb in range(4):
                        cbx = cg * 4 + cb
                        e.matmul(pp[:, cb, :], lhsT=yk[:, 0, cbx * 128:(cbx + 1) * 128],
                                 rhs=T2[:, k1 * 68:(k1 + 1) * 68], start=True, stop=False)
                        ins = e.matmul(pp[:, cb, :], lhsT=yk[:, 1, cbx * 128:(cbx + 1) * 128],
                                       rhs=T2[:, (64 + k1) * 68:(64 + k1 + 1) * 68], start=False, stop=True)
                    return ins
                P.op("pe", mm, [yk, T2], [ppb])
                for ri in range(2):
                    if (k1 + cg) % 2:
                        P.op("act", lambda e, pp=pp, cg=cg, ri=ri, k1=k1: e.activation(
                            out=xv[:, cg * 4:(cg + 1) * 4, ri, :, k1], in_=pp[:, :, ri * 34:(ri + 1) * 34],
                            func=AF.Copy), [ppb], [XTs])
                    else:
                        P.op("dve", lambda e, pp=pp, cg=cg, ri=ri, k1=k1: e.tensor_copy(
                            out=xv[:, cg * 4:(cg + 1) * 4, ri, :, k1], in_=pp[:, :, ri * 34:(ri + 1) * 34]),
                            [ppb], [XTs])
        for cb in range(8):
            P.dma("sp", g.XT[:, cb * 2 * EXT:(cb + 1) * 2 * EXT].rearrange("p (r t) -> p r t", r=2), XTs[:, cb, :, :],
                  reads=[XTs])
        P.barrier()


def phase_own(g, d):
    nc, P = g.nc, g.P
    with ExitStack() as es:
        c = alloc_chunk_bufs(g, es, 24)
        s = Ctx()
        hH = sb(g, es, "hH", [128, 8, 130], BF16)
        W = sb(g, es, "Wxbc", [128, 8, 3072], BF16)
        Wdt = sb(g, es, "Wdt", [128, 8, 32], BF16)
        load_w_cols(g, W, 1024, 3072)
        load_w_cols(g, Wdt, 6144 + 32 * d, 32)
        s.pxa = [ps(g, es, "pxa%d" % i, [128, 512], F32) for i in range(1)]
        s.pxc = [ps(g, es, "pxc%d" % i, [128, 512], F32) for i in range(1)]
        s.pxb = [s.pxa[0], s.pxc[0]]
        s.pre = [sb(g, es, "pre%d" % i, [128, 3, 130], BF16) for i in range(2)]
        s.pD = ps(g, es, "pD", [128, 512], F32)
        s.pH = ps(g, es, "pH", [128, 512], F32)
        psc = ps(g, es, "psc", [128, 4, 128], F32)
        pL = [ps(g, es, "pL%d" % i, [128, 4, 128], F32) for i in range(2)]
        s.xcs = sb(g, es, "xcs", [128, 24, 128], BF16, nparts=8)
        s.xtok = sb(g, es, "xtok", [128, 2560], BF16)
        s.dtm = sb(g, es, "dtm", [128, 32], F32)
        s.la = sb(g, es, "la", [128, 32], F32)
        s.wx = sb(g, es, "wx", [128, 32], F32)
        lab = sb(g, es, "lab", [128, 32], BF16)
        nlab = sb(g, es, "nlab", [128, 32], BF16)
        ecum = sb(g, es, "ecum", [128, 32], F32)
        dec = sb(g, es, "dec", [128, 32], F32)
        xd = sb(g, es, "xd", [128, 2048], BF16)
        xdd = sb(g, es, "xdd", [128, 2048], BF16)
        Sbf = sb(g, es, "Sbf", [128, 2048], BF16)
        Dt = [sb(g, es, "Dt%d" % i, [128, 8, 128], BF16) for i in range(2)]
        Lx = [sb(g, es, "Lx%d" % i, [128, 8, 128], BF16) for i in range(2)]
        G = [sb(g, es, "G%d" % i, [128, 8, 128], BF16) for i in range(2)]
        yo = sb(g, es, "yo", [128, 512], F32)
        ytile = sb(g, es, "ytile", [128, 2048], BF16)
        tmp = sb(g, es, "tmp", [128, 2048], BF16)
        yfl = sb(g, es, "yfl", [128, 2048], BF16)
        S = g.Sf if d == 0 else g.Sb
        mxk = "le" if d == 0 else "ge"
        sgk = "gt" if d == 0 else "lt"
        penk = "pen_f" if d == 0 else "pen_b"
        order = list(range(NEXT)) if d == 0 else list(range(NEXT - 1, -1, -1))

        prep(g, c, 0, g.xext[EXT:EXT + 128, :], 0, 1, vmask=cbcv(g, "halo_v"), hT=hH)

        def do_prep_a(ci):
            prep_a(g, c, ci, g.xext[ci * 128:(ci + 1) * 128, :])

        def do_prep_b(ci, prev_ci):
            hT = c.hTe[ci % 3]
            vm = None
            if ci == 0:
                vm = cbcv(g, "emask_bc", 0, 128)
            if ci == NEXT - 1:
                vm = cbcv(g, "emask_bc", 128, 128)
            prep_b(g, c, ci, 0, 1, vmask=vm)
            if prev_ci is not None:
                nb = c.hTe[prev_ci % 3]
                if ci == prev_ci + 1:
                    P.op("pool", lambda e: e.tensor_copy(out=hT[:, :, 0:1], in_=nb[:, :, 128:129]), [nb], [hT])
                    P.op("pool", lambda e: e.tensor_copy(out=nb[:, :, 129:130], in_=hT[:, :, 1:2]), [hT], [nb])
                else:
                    P.op("pool", lambda e: e.tensor_copy(out=hT[:, :, 129:130], in_=nb[:, :, 1:2]), [nb], [hT])
                    P.op("pool", lambda e: e.tensor_copy(out=nb[:, :, 0:1], in_=hT[:, :, 128:129]), [hT], [nb])
            if ci == 0:
                P.op("pool", lambda e: e.tensor_copy(out=hT[:, :, 0:1], in_=hH[:, :, 1:2]), [hH], [hT])
            if ci == NEXT - 1:
                P.op("pool", lambda e: e.tensor_copy(out=hT[:, :, 129:130], in_=hH[:, :, 2:3]), [hH], [hT])

        do_prep_a(order[0]); do_prep_b(order[0], None)
        do_prep_a(order[1]); do_prep_b(order[1], order[0])
        for oi, ci in enumerate(order):
            hT = c.hTe[ci % 3]
            if d == 1:
                P.dma("sp", yfl[:], g.YF[ci], writes=[yfl])
            mask_ap = bc(cpp(g, "emask", ci).unsqueeze(2), [128, 1, 32])
            steps = dt_steps(g, s, hT, Wdt, 32, cbcv(g, "dt_bias", 32 * d, 32), g.nega[:, 32 * d:32 * (d + 1)], mask_ap)

            def u1():
                P.op("pool", lambda e: e.tensor_copy(out=lab[:], in_=s.la[:]), [s.la], [lab])
                P.op("pool", lambda e: e.tensor_scalar(out=nlab[:], in0=lab[:], scalar1=-1.0, scalar2=None, op0=OP.mult),
                     [lab], [nlab])

                def segs(e):
                    e.matmul(s.pD[:, 64:96], lhsT=mk32(g, mxk), rhs=s.la[:], start=True, stop=True)
                    e.matmul(s.pD[:, 96:128], lhsT=mk32(g, sgk), rhs=s.la[:], start=True, stop=True)
                    return e.matmul(s.pD[:, 128:160], lhsT=mk32(g, "ones"), rhs=s.la[:], start=True, stop=True)
                P.op("pe", segs, [s.la, g.cmk_t], [s.pD])

            def u2():
                P.op("act", lambda e: e.activation(out=ecum[:], in_=s.pD[:, 64:96], func=AF.Exp), [s.pD], [ecum])
                P.op("act", lambda e: e.activation(out=s.wx[:], in_=s.pD[:, 96:128], func=AF.Exp), [s.pD], [s.wx])
                P.op("act", lambda e: e.activation(out=dec[:], in_=s.pD[:, 128:160], func=AF.Exp), [s.pD], [dec])

            def u3():
                P.op("pool", lambda e: e.tensor_tensor(out=s.wx[:], in0=s.wx[:], in1=s.dtm[:], op=OP.mult),
                     [s.wx, s.dtm], [s.wx])
            for st_ in steps + [u1, u2, u3]:
                st_()
            if oi + 2 < NEXT:
                do_prep_a(order[oi + 2])
            fm_proj_conv(g, c, s, hT, W, 24, 0, [])
            if oi + 2 < NEXT:
                do_prep_b(order[oi + 2], order[oi + 1])
            to_token_major(g, c, s, 20)
            x3 = s.xtok[:, 0:2048].rearrange("p (h d) -> p h d", d=64)
            P.op("dve", lambda e: e.tensor_tensor(out=xd[:].rearrange("p (h d) -> p h d", d=64), in0=x3,
                                                   in1=bc(s.dtm[:].unsqueeze(2), [128, 32, 64]), op=OP.mult),
                 [s.xtok, s.dtm], [xd])
            P.op("dve", lambda e: e.tensor_tensor(out=xdd[:].rearrange("p (h d) -> p h d", d=64), in0=x3,
                                                   in1=bc(s.wx[:].unsqueeze(2), [128, 32, 64]), op=OP.mult),
                 [s.xtok, s.wx], [xdd])
            P.op("act", lambda e: e.activation(out=Sbf[:], in_=S[:], func=AF.Copy), [S], [Sbf])

            def sc(e):
                ins = None
                for gi in range(4):
                    ins = e.matmul(psc[:, gi, :], lhsT=s.xcs[:, 16 + gi, :], rhs=s.xcs[:, 20 + gi, :], start=True, stop=True)
                return ins
            P.op("pe", sc, [s.xcs], [psc])
            for gi in range(4):
                dt_, lx, gg = Dt[gi % 2], Lx[gi % 2], G[gi % 2]
                P.op("pool", lambda e, gi=gi, dt_=dt_: e.tensor_tensor(
                    out=dt_[:], in0=bc(lab[:, gi * 8:(gi + 1) * 8].unsqueeze(2), [128, 8, 128]),
                    in1=bc(mk16(g, mxk).unsqueeze(1), [128, 8, 128]), op=OP.mult), [lab, g.cmkb_t], [dt_])
                for hh in range(2):
                    def mmL(e, gi=gi, hh=hh, dt_=dt_):
                        e.matmul(pL[hh][:], lhsT=mk16(g, "ones"), rhs=dt_[:, hh * 4:(hh + 1) * 4, :], start=True, stop=False)
                        e.matmul(pL[hh][:], lhsT=mk16(g, mxk),
                                 rhs=bc(nlab[:, gi * 8 + hh * 4:gi * 8 + hh * 4 + 4].unsqueeze(2), [128, 4, 128]),
                                 start=False, stop=False)
                        return e.matmul(pL[hh][:], lhsT=mk16(g, "ident"),
                                        rhs=bc(mk16(g, penk).unsqueeze(1), [128, 4, 128]), start=False, stop=True)
                    P.op("pe", mmL, [dt_, nlab, g.cmkb_t], [pL[hh]])
                    P.op("act", lambda e, hh=hh, lx=lx: e.activation(out=lx[:, hh * 4:(hh + 1) * 4, :], in_=pL[hh][:],
                                                                   func=AF.Exp), [pL[hh]], [lx])
                P.op("dve", lambda e, gi=gi, lx=lx, gg=gg: e.tensor_tensor(
                    out=gg[:], in0=lx[:], in1=bc(psc[:, gi, :].unsqueeze(1), [128, 8, 128]), op=OP.mult),
                    [lx, psc], [gg])

                def mmy(e, gi=gi, gg=gg):
                    ins = None
                    for h in range(8):
                        hh = gi * 8 + h
                        ins = e.matmul(s.pH[:, h * 64:(h + 1) * 64], lhsT=gg[:, h, :], rhs=xd[:, hh * 64:(hh + 1) * 64],
                                       start=True, stop=True)
                    return ins
                P.op("pe", mmy, [gg, xd], [s.pH])
                P.op("pe", lambda e, gi=gi: e.matmul(s.pxb[0][:], lhsT=s.xcs[:, 20 + gi, :],
                                                     rhs=Sbf[:, gi * 512:(gi + 1) * 512], start=True, stop=True),
                     [s.xcs, Sbf], [s.pxb[0]])
                P.op("dve", lambda e, gi=gi: e.tensor_tensor(
                    out=yo[:].rearrange("p (h d) -> p h d", d=64), in0=s.pxb[0][:].rearrange("p (h d) -> p h d", d=64),
                    in1=bc(ecum[:, gi * 8:(gi + 1) * 8].unsqueeze(2), [128, 8, 64]), op=OP.mult),
                    [s.pxb[0], ecum], [yo])
                P.op("dve", lambda e, gi=gi: e.tensor_tensor(out=ytile[:, gi * 512:(gi + 1) * 512], in0=yo[:],
                                                              in1=s.pH[:], op=OP.add), [yo, s.pH], [ytile])
                P.op("pe", lambda e, gi=gi: e.matmul(s.pxb[1][:], lhsT=s.xtok[:, 2048 + gi * 128:2048 + (gi + 1) * 128],
                                                     rhs=xdd[:, gi * 512:(gi + 1) * 512], start=True, stop=True),
                     [s.xtok, xdd], [s.pxb[1]])
                P.op("dve", lambda e, gi=gi: e.tensor_tensor(
                    out=S[:, gi * 512:(gi + 1) * 512].rearrange("p (h d) -> p h d", d=64),
                    in0=S[:, gi * 512:(gi + 1) * 512].rearrange("p (h d) -> p h d", d=64),
                    in1=bc(dec[:, gi * 8:(gi + 1) * 8].unsqueeze(2), [128, 8, 64]), op=OP.mult), [S, dec], [S])
                P.op("dve", lambda e, gi=gi: e.tensor_tensor(out=S[:, gi * 512:(gi + 1) * 512],
                                                              in0=S[:, gi * 512:(gi + 1) * 512], in1=s.pxb[1][:],
                                                              op=OP.add), [S, s.pxb[1]], [S])
            if d == 0:
                P.op("pool", lambda e: e.tensor_tensor(out=tmp[:].rearrange("p (h d) -> p h d", d=64), in0=x3,
                                                       in1=bc(cbcv(g, "d_skip").unsqueeze(2), [128, 32, 64]),
                                                       op=OP.mult), [s.xtok, g.cbc_t], [tmp])
                P.op("pool", lambda e: e.tensor_tensor(out=tmp[:], in0=tmp[:], in1=ytile[:], op=OP.add),
                     [tmp, ytile], [tmp])
                P.dma("sp", g.YF[ci], tmp[:], reads=[tmp])
            else:
                P.op("pool", lambda e: e.tensor_tensor(out=tmp[:], in0=yfl[:], in1=ytile[:], op=OP.add),
                     [yfl, ytile], [tmp])
                P.dma("sp", g.YT[ci], tmp[:], reads=[tmp])
        P.barrier()


def phase_merge(g):
    phase_merge_a(g)
    phase_merge_b(g)


def phase_merge_a(g):
    nc, P = g.nc, g.P
    with ExitStack() as es:
        c = alloc_chunk_bufs(g, es, 0)
        Wz = sb(g, es, "Wz", [128, 8, 2048], BF16)
        Wgs = sb(g, es, "Wgs", [128, 8, 1024], BF16)
        Wsb = sb(g, es, "Wsb", [128, 16, 1024], BF16)
        load_w_cols(g, Wz, 4096, 2048)
        load_w_cols(g, Wgs, 7232, 1024)
        load_w_gen(g, Wsb, g.w_sb, 16, 1024)
        sg = sb(g, es, "ssdg", [128, 2048], F32)
        P.dma("sp", sg[:], g.cbg[:, CBG["ssd_g"][0]:CBG["ssd_g"][0] + 2048], writes=[sg])
        pz = [ps(g, es, "pz%d" % i, [128, 512], F32) for i in range(4)]
        pbs = [ps(g, es, "pbs%d" % i, [128, 512], F32) for i in range(2)]
        yt = [sb(g, es, "yt%d" % i, [128, 2048], BF16) for i in range(2)]
        zs = sb(g, es, "zs", [128, 4, 512], BF16, nparts=4)
        t = sb(g, es, "t", [128, 4, 512], F32, nparts=4)
        jk = sb(g, es, "jk", [128, 512], BF16)
        st2 = sb(g, es, "st2", [128, 16], F32)
        ysn = sb(g, es, "ysn", [128, 4, 512], BF16, nparts=4)
        ysnT = sb(g, es, "ysnT", [128, 16, 128], BF16)
        sgs = sb(g, es, "sgs", [128, 2, 512], F32, nparts=2)
        ms = [sb(g, es, "ms%d" % i, [128, 1024], BF16) for i in range(2)]
        prep_a(g, c, 0, g.xext[0:128, :])
        prep_b(g, c, 0, 0, 1)
        for ci in range(NEXT):
            hT = c.hTe[ci % 3]
            y = yt[ci % 2]
            P.dma("sp", y[:], g.YT[ci], writes=[y])
            if ci + 1 < NEXT:
                prep_a(g, c, ci + 1, g.xext[(ci + 1) * 128:(ci + 2) * 128, :])
            for gi in range(4):
                def mm(e, gi=gi):
                    ins = None
                    for kb in range(8):
                        ins = e.matmul(pz[gi][:], lhsT=hT[:, kb, 1:129], rhs=Wz[:, kb, gi * 512:(gi + 1) * 512],
                                       start=(kb == 0), stop=(kb == 7))
                    return ins
                P.op("pe", mm, [hT, Wz], [pz[gi]])
            for gi in range(4):
                P.op("act", lambda e, gi=gi: e.activation(out=zs[:, gi, :], in_=pz[gi][:], func=AF.Silu),
                     [pz[gi]], [(zs, gi)])
            for gi in range(4):
                P.op("dve", lambda e, gi=gi: e.tensor_tensor(out=t[:, gi, :], in0=zs[:, gi, :],
                                                              in1=y[:, gi * 512:(gi + 1) * 512], op=OP.mult),
                     [(zs, gi), y], [(t, gi)])
            for gi in range(4):
                P.op("act", lambda e, gi=gi: e.activation(out=jk[:], in_=t[:, gi, :], func=AF.Square,
                                                          accum_out=st2[:, gi:gi + 1]), [(t, gi)], [jk, st2])
            P.op("dve", lambda e: e.tensor_scalar(out=st2[:, 4:8], in0=st2[:, 0:4], scalar1=1.0 / 512, scalar2=EPS,
                                                   op0=OP.mult, op1=OP.add), [st2], [st2])
            P.op("act", lambda e: e.activation(out=st2[:, 8:12], in_=st2[:, 4:8], func=AF.Ln), [st2], [st2])
            P.op("act", lambda e: e.activation(out=st2[:, 12:16], in_=st2[:, 8:12], func=AF.Exp, scale=-0.5), [st2], [st2])
            for gi in range(4):
                P.op("dve", lambda e, gi=gi: e.scalar_tensor_tensor(
                    out=ysn[:, gi, :], in0=t[:, gi, :], scalar=st2[:, 12 + gi:13 + gi], in1=sg[:, gi * 512:(gi + 1) * 512],
                    op0=OP.mult, op1=OP.mult), [(t, gi), st2, sg], [(ysn, gi)])
            for rnd in range(2):
                def tr(e, rnd=rnd):
                    ins = None
                    for j in range(8):
                        blk = rnd * 8 + j
                        ins = e.transpose(out=c.pA[:, j, :], in_=ysn[:, blk // 4, (blk % 4) * 128:(blk % 4 + 1) * 128],
                                          identity=mk16(g, "ident"))
                    return ins
                P.op("pe", tr, [ysn, g.cmkb_t], [c.pA])
                P.op("dve", lambda e, rnd=rnd: e.tensor_copy(out=ysnT[:, rnd * 8:(rnd + 1) * 8, :], in_=c.pA[:]),
                     [c.pA], [ysnT])
            if ci + 1 < NEXT:
                prep_b(g, c, ci + 1, 0, 1)
            for hf in range(2):
                def mmg(e, hf=hf):
                    ins = None
                    for kb in range(8):
                        ins = e.matmul(pz[hf][:], lhsT=hT[:, kb, 1:129], rhs=Wgs[:, kb, hf * 512:(hf + 1) * 512],
                                       start=(kb == 0), stop=(kb == 7))
                    return ins
                P.op("pe", mmg, [hT, Wgs], [pz[hf]])

                def mms(e, hf=hf):
                    ins = None
                    for kb in range(16):
                        ins = e.matmul(pbs[hf][:], lhsT=ysnT[:, kb, :], rhs=Wsb[:, kb, hf * 512:(hf + 1) * 512],
                                       start=(kb == 0), stop=(kb == 15))
                    return ins
                P.op("pe", mms, [ysnT, Wsb], [pbs[hf]])
            m = ms[ci % 2]
            for hf in range(2):
                P.op("act", lambda e, hf=hf: e.activation(out=sgs[:, hf, :], in_=pz[hf][:], func=AF.Sigmoid),
                     [pz[hf]], [(sgs, hf)])
                P.op("dve", lambda e, m=m, hf=hf: e.tensor_tensor(out=m[:, hf * 512:(hf + 1) * 512], in0=sgs[:, hf, :],
                                                                   in1=pbs[hf][:], op=OP.mult),
                     [(sgs, hf), pbs[hf]], [m])
            P.dma("sp", g.MS[ci], m[:], reads=[m])
        P.barrier()


def phase_merge_b(g):
    nc, P = g.nc, g.P
    with ExitStack() as es:
        c = alloc_chunk_bufs(g, es, 0)
        Wgf = sb(g, es, "Wgf", [128, 8, 1024], BF16)
        Wfa = sb(g, es, "Wfa", [128, 8, 1024], BF16)
        Wo = sb(g, es, "Wo", [128, 8, 1024], BF16)
        Tcs = sb(g, es, "Tcs", [128, 256], BF16)
        load_w_cols(g, Wgf, 6208, 1024)
        load_w_gen(g, Wfa, g.w_fa, 8, 1024)
        load_w_gen(g, Wo, g.w_o, 8, 1024)
        P.dma("pool", Tcs[:], g.tcs, writes=[Tcs])
        pb0 = ps(g, es, "pb0", [128, 2, 512], F32)
        pb1 = ps(g, es, "pb1", [128, 2, 512], F32)
        pb2 = ps(g, es, "pb2", [128, 2, 512], F32)
        xtc = [sb(g, es, "xtc%d" % i, [128, 8, 2, 128], BF16) for i in range(2)]
        msl = [sb(g, es, "msl%d" % i, [128, 1024], BF16) for i in range(2)]
        mixT = sb(g, es, "mixT", [128, 8, 128], BF16)
        sgf = sb(g, es, "sgf", [128, 1024], F32)
        tmp = sb(g, es, "tmpm", [128, 1024], F32)
        mrg = sb(g, es, "mrg", [128, 1024], BF16)
        mrgT = sb(g, es, "mrgT", [128, 8, 128], BF16)
        l1 = [sb(g, es, "l1_%d" % i, [128, 1024], F32) for i in range(2)]
        xtv = g.XT.rearrange("p (c r t) -> p c r t", c=8, r=2)
        prep_a(g, c, 0, g.xext[0:128, :])
        prep_b(g, c, 0, 0, 1)
        for ci in range(NEXT):
            hT = c.hTe[ci % 3]
            xt = c.xt[ci % 2]
            if ci + 1 < NEXT:
                prep_a(g, c, ci + 1, g.xext[(ci + 1) * 128:(ci + 2) * 128, :])
            xc_, m = xtc[ci % 2], msl[ci % 2]
            P.dma("sp", xc_[:], xtv[:, :, :, ci * 128:(ci + 1) * 128], writes=[xc_])
            P.dma("sp", m[:], g.MS[ci], writes=[m])
            for cg in range(2):
                def mmx(e, cg=cg):
                    e.matmul(pb2[:, cg, :], lhsT=Tcs[:, 0:128], rhs=xc_[:, cg * 4:(cg + 1) * 4, 0, :], start=True, stop=False)
                    return e.matmul(pb2[:, cg, :], lhsT=Tcs[:, 128:256], rhs=xc_[:, cg * 4:(cg + 1) * 4, 1, :],
                                    start=False, stop=True)
                P.op("pe", mmx, [Tcs, xc_], [pb2])
            P.op("act", lambda e: e.activation(out=mixT[:].rearrange("p a b -> p (a b)"),
                                               in_=pb2[:].rearrange("p a b -> p (a b)"), func=AF.Copy), [pb2], [mixT])
            for hf in range(2):
                def mmf(e, hf=hf):
                    ins = None
                    for kb in range(8):
                        ins = e.matmul(pb0[:, hf, :], lhsT=mixT[:, kb, :], rhs=Wfa[:, kb, hf * 512:(hf + 1) * 512],
                                       start=(kb == 0), stop=(kb == 7))
                    return ins
                P.op("pe", mmf, [mixT, Wfa], [pb0])

                def mmg(e, hf=hf):
                    ins = None
                    for kb in range(8):
                        ins = e.matmul(pb1[:, hf, :], lhsT=hT[:, kb, 1:129], rhs=Wgf[:, kb, hf * 512:(hf + 1) * 512],
                                       start=(kb == 0), stop=(kb == 7))
                    return ins
                P.op("pe", mmg, [hT, Wgf], [pb1])
            P.op("act", lambda e: e.activation(out=sgf[:], in_=pb1[:].rearrange("p a b -> p (a b)"), func=AF.Sigmoid),
                 [pb1], [sgf])
            P.op("dve", lambda e: e.tensor_tensor(out=tmp[:], in0=sgf[:], in1=pb0[:].rearrange("p a b -> p (a b)"),
                                                   op=OP.mult), [sgf, pb0], [tmp])
            P.op("dve", lambda e, m=m: e.tensor_tensor(out=mrg[:], in0=tmp[:], in1=m[:], op=OP.add), [tmp, m], [mrg])

            def tr(e):
                ins = None
                for kb in range(8):
                    ins = e.transpose(out=c.pA[:, kb, :], in_=mrg[:, kb * 128:(kb + 1) * 128], identity=mk16(g, "ident"))
                return ins
            P.op("pe", tr, [mrg, g.cmkb_t], [c.pA])
            P.op("act", lambda e: e.activation(out=mrgT[:], in_=c.pA[:], func=AF.Copy), [c.pA], [mrgT])
            if ci + 1 < NEXT:
                prep_b(g, c, ci + 1, 0, 1)
            for hf in range(2):
                def mmo(e, hf=hf):
                    ins = None
                    for kb in range(8):
                        ins = e.matmul(pb2[:, hf, :], lhsT=mrgT[:, kb, :], rhs=Wo[:, kb, hf * 512:(hf + 1) * 512],
                                       start=(kb == 0), stop=(kb == 7))
                    return ins
                P.op("pe", mmo, [mrgT, Wo], [pb2])
            l = l1[ci % 2]
            P.op("dve", lambda e, l=l: e.tensor_tensor(out=l[:], in0=pb2[:].rearrange("p a b -> p (a b)"),
                                                        in1=g.gbc[:, 0:D], op=OP.mult), [pb2, g.gbc], [l])
            P.op("pool", lambda e, l=l, xt=xt: e.tensor_tensor(out=l[:], in0=l[:], in1=xt[:], op=OP.add), [l, xt], [l])
            P.dma("sp", g.L1[ci], l[:], reads=[l])
        P.barrier()


def phase_ffn(g):
    nc, P = g.nc, g.P
    with ExitStack() as es:
        c = alloc_chunk_bufs(g, es, 0)
        h2T = sb(g, es, "h2T", [128, 8, EXT], BF16, nparts=NEXT)
        Wd = sb(g, es, "Wd", [128, NFB, 1024], BF16)
        load_w_gen(g, Wd, g.w_down, NFB, 1024)
        fg = sb(g, es, "fg", [128, 1024], F32)
        P.dma("sp", fg[:], g.cbg[:, CBG["final_g"][0]:CBG["final_g"][0] + 1024], writes=[fg])
        for ci in range(NEXT):
            i2 = ci % 2
            xt, st, xn = c.xt[i2], c.st[i2], c.xn[i2]
            P.dma("sp", xt[:], g.L1[ci], writes=[xt])
            P.op("act", lambda e: e.activation(out=c.junk[:], in_=xt[:], func=AF.Square, accum_out=st[:, 0:1]),
                 [xt], [c.junk, st])
            P.op("dve", lambda e: e.tensor_scalar(out=st[:, 1:2], in0=st[:, 0:1], scalar1=1.0 / D, scalar2=EPS,
                                                   op0=OP.mult, op1=OP.add), [st], [st])
            P.op("act", lambda e: e.activation(out=st[:, 2:3], in_=st[:, 1:2], func=AF.Sqrt), [st], [st])
            P.op("dve", lambda e: e.reciprocal(out=st[:, 3:4], in_=st[:, 2:3]), [st], [st])
            P.op("act", lambda e: e.activation(out=xn[:], in_=xt[:], func=AF.Copy, scale=st[:, 3:4]), [xt, st], [xn])

            def tr(e):
                ins = None
                for kb in range(8):
                    ins = e.transpose(out=c.pA[:, kb, :], in_=xn[:, kb * 128:(kb + 1) * 128], identity=mk16(g, "ident"))
                return ins
            P.op("pe", tr, [xn, g.cmkb_t], [c.pA])
            for kb in range(8):
                P.op("dve", lambda e, kb=kb, ci=ci: e.tensor_scalar(
                    out=h2T[:, kb, ci * 128:(ci + 1) * 128], in0=c.pA[:, kb, :], scalar1=modA(g, 4, kb),
                    scalar2=modA(g, 5, kb), op0=OP.mult, op1=OP.add), [c.pA, g.modv], [(h2T, ci)])
            if ci in (0, NEXT - 1):
                vm = cbcv(g, "emask_bc", 0 if ci == 0 else 128, 128)
                P.op("pool", lambda e, ci=ci, vm=vm: e.tensor_tensor(
                    out=h2T[:, :, ci * 128:(ci + 1) * 128], in0=h2T[:, :, ci * 128:(ci + 1) * 128],
                    in1=bc(vm.unsqueeze(1), [128, 8, 128]), op=OP.mult), [(h2T, ci), g.cbc_t], [(h2T, ci)])
        NB = 4
        pu = [ps(g, es, "pu%d" % i, [128, 512], F32) for i in range(2)]
        pd = ps(g, es, "pd", [128, 2, 512], F32)
        aT = sb(g, es, "aT", [128, NFB, 512], BF16, nparts=NFB)
        wu = [sb(g, es, "wu%d" % i, [128, 8, 2, 128], BF16) for i in range(3)]
        ug = [sb(g, es, "ug%d" % i, [128, 10, 64], BF16) for i in range(2)]
        dg = [sb(g, es, "dg%d" % i, [128, 4, 128], BF16) for i in range(4)]
        pcv = [ps(g, es, "pcv%d" % i, [128, 512], F32) for i in range(2)]
        acc = [sb(g, es, "acc%d" % i, [128, 8, 64], F32) for i in range(2)]
        sgl = sb(g, es, "sgl", [128, 512], F32)
        lt = [sb(g, es, "lt%d" % i, [128, 1024], F32) for i in range(2)]
        yy = [sb(g, es, "yy%d" % i, [128, 1024], F32) for i in range(2)]
        jk = c.junk
        st = [sb(g, es, "stf%d" % i, [128, 4], F32) for i in range(2)]
        wuv = g.w_up.rearrange("(kb p) (gv n) -> p kb gv n", p=128, gv=2)
        l1f = g.L1.rearrange("c p d -> (c p) d")
        cnt = 0
        nitem = NB * NFB

        def issue_w(i):
            if i < nitem:
                fb_ = i % NFB
                w_ = wu[i % 3]
                for gv_ in range(2):
                    P.dma("pool", w_[:, :, gv_, :], wuv[:, :, gv_, fb_ * 128:(fb_ + 1) * 128], writes=[w_])
        issue_w(0)
        issue_w(1)
        for blk in range(NB):
            base = blk * 512
            hparts = [(h2T, i) for i in range(base // 128, (base + 640 + 127) // 128)]
            for fb in range(NFB):
                w = wu[cnt % 3]
                issue_w(cnt + 2)
                cnt += 1
                for gv in range(2):
                    u = ug[gv]
                    a = acc[gv]
                    for j in range(2):
                        def mm(e, j=j, gv=gv, w=w):
                            ins = None
                            for kb in range(8):
                                ins = e.matmul(pu[j][:, 0:320], lhsT=w[:, kb, gv, :],
                                               rhs=h2T[:, kb, base + j * 320:base + (j + 1) * 320],
                                               start=(kb == 0), stop=(kb == 7))
                            return ins
                        P.op("pe", mm, [w] + hparts, [pu[j]])
                        P.op("act", lambda e, j=j, u=u: e.activation(
                            out=u[:].rearrange("p r c -> p (r c)")[:, j * 320:(j + 1) * 320], in_=pu[j][:, 0:320],
                            func=AF.Copy), [pu[j]], [u])
                    cf = gv * NFB + fb
                    wt = lambda t: cpp(g, "cw_ffn", t * 44 + cf)
                    P.op("act", lambda e, u=u, a=a, cf=cf: e.activation(
                        out=a[:], in_=u[:, 1:9, :], func=AF.Identity, scale=cpp(g, "cw_ffn", 4 * 44 + cf),
                        bias=cpp(g, "cb_ffn", cf)), [u, g.cpp_t], [a])
                    dgt = dg[(cnt * 2 + gv) % 4]
                    for i_, t_ in enumerate((1, 7, 3, 5)):
                        P.op("pool", lambda e, i_=i_, t_=t_, dgt=dgt, cf=cf: e.tensor_scalar(
                            out=dgt[:, i_, :], in0=mk16(g, "ident"), scalar1=cpp(g, "cw_ffn", t_ * 44 + cf), scalar2=0.0,
                            op0=OP.mult, op1=OP.add), [g.cmkb_t, g.cpp_t], [dgt])
                    pc = pcv[gv]
                    pc3 = pc[:].rearrange("p (r c) -> p r c", c=64)

                    def mcv(e, u=u, dgt=dgt, pc3=pc3):
                        e.matmul(pc3, lhsT=dgt[:, 0, :], rhs=u[:, 0:8, :], start=True, stop=False)
                        e.matmul(pc3, lhsT=dgt[:, 1, :], rhs=u[:, 2:10, :], start=False, stop=False)
                        e.matmul(pc3[:, :, 1:64], lhsT=dgt[:, 2, :], rhs=u[:, 1:9, 0:63], start=False, stop=False)
                        return e.matmul(pc3[:, :, 0:63], lhsT=dgt[:, 3, :], rhs=u[:, 1:9, 1:64], start=False, stop=True)
                    P.op("pe", mcv, [u, dgt], [pc])
                    for (kh, kw) in ((0, 0), (0, 2), (2, 0), (2, 2)):
                        dy, dx = kh - 1, kw - 1
                        c0, c1 = max(0, -dx), 64 - max(0, dx)
                        P.op("dve", lambda e, u=u, a=a, dy=dy, dx=dx, c0=c0, c1=c1, t=kh * 3 + kw, cf=cf:
                             e.scalar_tensor_tensor(out=a[:, :, c0:c1], in0=u[:, 1 + dy:9 + dy, c0 + dx:c1 + dx],
                                                    scalar=cpp(g, "cw_ffn", t * 44 + cf), in1=a[:, :, c0:c1],
                                                    op0=OP.mult, op1=OP.add), [u, a, g.cpp_t], [a])
                    P.op("dve", lambda e, a=a, pc3=pc3: e.tensor_tensor(out=a[:], in0=a[:], in1=pc3, op=OP.add),
                         [a, pc], [a])
                P.op("act", lambda e: e.activation(out=sgl[:], in_=acc[0][:].rearrange("p r c -> p (r c)"), func=AF.Silu),
                     [acc[0]], [sgl])
                P.op("dve", lambda e, fb=fb: e.tensor_tensor(out=aT[:, fb, :], in0=sgl[:],
                                                              in1=acc[1][:].rearrange("p r c -> p (r c)"), op=OP.mult),
                     [sgl, acc[1]], [(aT, fb)])
            for tcn in range(4):
                o0 = blk * 512 + tcn * 128
                i2 = (blk * 4 + tcn) % 2
                l, y, s4 = lt[i2], yy[i2], st[i2]
                o = y
                P.dma("sp", l[:], l1f[o0 + 64:o0 + 64 + 128, :], writes=[l])
                for hf in range(2):
                    def mmd(e, hf=hf, tcn=tcn):
                        ins = None
                        for fb in range(NFB):
                            ins = e.matmul(pd[:, hf, :], lhsT=aT[:, fb, tcn * 128:(tcn + 1) * 128],
                                           rhs=Wd[:, fb, hf * 512:(hf + 1) * 512], start=(fb == 0), stop=(fb == NFB - 1))
                        return ins
                    P.op("pe", mmd, [aT, Wd], [pd])
                P.op("dve", lambda e, y=y: e.tensor_tensor(out=y[:], in0=pd[:].rearrange("p a b -> p (a b)"),
                                                            in1=g.gbc[:, D:2 * D], op=OP.mult), [pd, g.gbc], [y])
                P.op("pool", lambda e, y=y, l=l: e.tensor_tensor(out=y[:], in0=y[:], in1=l[:], op=OP.add), [y, l], [y])
                P.op("act", lambda e, y=y, s4=s4: e.activation(out=jk[:], in_=y[:], func=AF.Square,
                                                               accum_out=s4[:, 0:1]), [y], [jk, s4])
                P.op("dve", lambda e, s4=s4: e.tensor_scalar(out=s4[:, 1:2], in0=s4[:, 0:1], scalar1=1.0 / D,
                                                              scalar2=EPS, op0=OP.mult, op1=OP.add), [s4], [s4])
                P.op("act", lambda e, s4=s4: e.activation(out=s4[:, 2:3], in_=s4[:, 1:2], func=AF.Sqrt), [s4], [s4])
                P.op("dve", lambda e, s4=s4: e.reciprocal(out=s4[:, 3:4], in_=s4[:, 2:3]), [s4], [s4])
                P.op("dve", lambda e, y=y, s4=s4, o=o: e.scalar_tensor_tensor(
                    out=o[:], in0=y[:], scalar=s4[:, 3:4], in1=fg[:], op0=OP.mult, op1=OP.mult), [y, s4, fg], [y])
                P.dma("sp", g.out[o0:o0 + 128, :], o[:], reads=[o])
        P.barrier()


def _pm(v):
    v = np.asarray(v, np.float32)
    return np.ascontiguousarray(v.reshape(-1, 128).T)


def _rb(v):
    v = np.asarray(v, np.float32).reshape(1, -1)
    return np.ascontiguousarray(np.broadcast_to(v, (128, v.shape[1])))


def _const_tables():
    k = np.arange(128)[:, None]
    m = np.arange(128)[None, :]
    mats = [np.ones((128, 128)), k <= m, k >= m, k > m, k < m, k == m,
            np.where(m < k, -BIG, 0.0), np.where(m > k, -BIG, 0.0)]
    cmk = np.concatenate([np.asarray(a, np.float32) for a in mats], axis=1)
    t1i = np.arange(64)[:, None] * np.arange(64)[None, :]
    th = 2 * np.pi * t1i / 64.0
    t1 = np.concatenate([np.cos(th), -np.sin(th)], axis=1).astype(np.float32)
    j = np.arange(128)[:, None] * np.arange(128)[None, :]
    thc = 2 * np.pi * j / 128.0
    tcs = (np.concatenate([np.cos(thc), np.sin(thc)], axis=1) / 1024.0).astype(np.float32)
    return cmk, t1, tcs


def _t2_tables(q):
    t2 = np.arange(128, dtype=np.float64)[:, None, None]
    k1 = np.arange(64, dtype=np.float64)[None, :, None]
    k2 = (32 * q - 1 + np.arange(34, dtype=np.float64))[None, None, :]
    kk = np.mod(k1 + 64 * k2, 8192)
    th = 2 * np.pi * np.mod(kk * t2, 8192) / 8192.0
    Mr, Mi = np.cos(th), -np.sin(th)
    ta = np.concatenate([Mr, Mi], axis=2)
    tb = np.concatenate([-Mi, Mr], axis=2)
    return np.concatenate([ta.reshape(128, -1), tb.reshape(128, -1)], axis=1).astype(np.float32)


_CACHE = {}


def kernel(x, c, ctx, c_ctx, w_mod, b_mod, norm1_g, w_in, conv_ssd_w, conv_ssd_b, dt_bias, a_log,
           d_skip, ssd_norm_g, w_fa, w_sb, w_o, norm2_g, w_up, conv_ffn_w, conv_ffn_b, w_down, final_g):
    f = lambda a: np.asarray(a, np.float32)
    x, c, ctx, c_ctx = f(x), f(c), f(ctx), f(c_ctx)
    cmk, t1, tcs = _const_tables()
    in_maps = []
    bm = f(b_mod)[0]
    for core in range(8):
        b, q = divmod(core, 4)
        e0 = 2048 * q - 64
        xext = np.zeros((EXT + 128, D), np.float32)
        lo, hi = max(e0, 0), min(e0 + EXT, SEQ)
        xext[lo - e0:hi - e0] = x[b, lo:hi]
        hv = np.zeros(2, np.float32)
        if e0 - 1 >= 0:
            xext[EXT] = x[b, e0 - 1]; hv[0] = 1
        if e0 + EXT < SEQ:
            xext[EXT + 1] = x[b, e0 + EXT]; hv[1] = 1
        tok = e0 + np.arange(EXT)
        valid = ((tok >= 0) & (tok < SEQ)).astype(np.float32)
        fmask = np.zeros((NSLOT, 128, 2), np.float32)
        fmask[0:2, :, 0] = 1
        fmask[66:68, :, 1] = 1
        lt = np.arange(SEQ).reshape(NCH, 128)
        fmask[2:66, :, 0] = (lt < e0)
        fmask[2:66, :, 1] = (lt >= e0 + EXT)
        cpp_a = np.zeros((128, CPP_N), np.float32)

        def put(key, arr):
            o, n = CPP[key]
            cpp_a[:, o:o + n] = arr
        put("c", _pm(c[b])); put("cctx", _pm(c_ctx)); put("bmod", _pm(bm))
        put("n1g", _pm(f(norm1_g)[0])); put("n2g", _pm(f(norm2_g)[0]))
        put("cw_ssd", np.concatenate([_pm(f(conv_ssd_w)[0, t]) for t in range(3)], axis=1))
        put("cb_ssd", _pm(f(conv_ssd_b)[0]))
        cfw = f(conv_ffn_w)[0].reshape(9, 2 * DFF)
        put("cw_ffn", np.concatenate([_pm(cfw[t]) for t in range(9)], axis=1))
        put("cb_ffn", _pm(f(conv_ffn_b)[0]))
        put("emask", valid.reshape(NEXT, 128).T)
        put("fmask", fmask.transpose(1, 0, 2).reshape(128, NSLOT * 2))
        cbc_a = np.zeros((128, CBC_N), np.float32)

        def putb(key, arr):
            o, n = CBC[key]
            cbc_a[:, o:o + n] = arr
        putb("dt_bias", _rb(f(dt_bias)[0].reshape(-1))); putb("a_log", _rb(f(a_log)[0].reshape(-1)))
        putb("d_skip", _rb(f(d_skip)[0]))
        cbg_a = np.concatenate([_rb(f(ssd_norm_g)[0]), _rb(f(final_g)), _rb(bm[2048:3072]), _rb(bm[5120:6144])], axis=1)
        putb("emask_bc", _rb(np.concatenate([valid[:128], valid[-128:]])))
        hvb = np.zeros(128, np.float32); hvb[0:2] = hv
        putb("halo_v", _rb(hvb))
        in_maps.append(dict(
            xb=np.ascontiguousarray(x[b]), ctxb=np.ascontiguousarray(ctx[b]), xext=xext,
            w_mod=f(w_mod)[0], w_in=f(w_in)[0], w_fa=f(w_fa)[0], w_sb=f(w_sb)[0], w_o=f(w_o)[0],
            w_up=f(w_up)[0], w_down=f(w_down)[0], cpp=cpp_a, cbc=cbc_a, cbg=cbg_a, cbrow=f(conv_ssd_b)[0].reshape(1, 3072).copy(), cmk=cmk, t1=t1, t2=_t2_tables(q), tcs=tcs))
    if "nc" not in _CACHE:
        _CACHE["nc"] = build_program()
    res = run_bass_kernel_spmd(_CACHE["nc"], in_maps, core_ids=list(range(8)))
    if DEBUG:
        _CACHE["res"] = res
    out = np.zeros((2, SEQ, D), np.float32)
    for core in range(8):
        b, q = divmod(core, 4)
        out[b, 2048 * q:2048 * (q + 1)] = res.results[core]["out"]
    return out
```

```python
import os
from contextlib import ExitStack
import numpy as np
import concourse.bass as bass
import concourse.mybir as mybir
from concourse.bass_utils import run_bass_kernel_spmd

F32 = mybir.dt.float32
BF16 = mybir.dt.bfloat16
AF = mybir.ActivationFunctionType
OP = mybir.AluOpType

D = 1024
SEQ = 8192
NCH = 64
NEXT = 17
EXT = NEXT * 128
EPS = 1e-6
BIG = 30000.0
NSLOT = 68
DFF = 2816
NFB = 22

STOP = os.environ.get("MK_STOP", "")
DEBUG = bool(STOP)


class Buf:
    def __init__(self, t, nparts=1):
        self.t = t
        self.n = nparts
        self.w = [None] * nparts
        self.r = [[] for _ in range(nparts)]
        self.excl = False

    def __getitem__(self, idx):
        return self.t[idx]


def _parts(items):
    out = []
    for it in items:
        if it is None:
            continue
        if isinstance(it, Buf):
            out.extend((it, i) for i in range(it.n))
        else:
            b, idx = it
            if isinstance(idx, int):
                out.append((b, idx))
            else:
                out.extend((b, i) for i in idx)
    return out


class Prog:
    def __init__(self, nc, es):
        self.nc = nc
        self.E = {}
        self.semid = 0
        for name, eng in (("pe", nc.tensor), ("act", nc.scalar), ("dve", nc.vector), ("pool", nc.gpsimd), ("sp", nc.sync)):
            sem = es.enter_context(nc.semaphore("s_" + name))
            self.E[name] = dict(name=name, eng=eng, sem=(self._sid(), sem), count=0, waited={}, pool=[], ndma=0)
        for name, n in (("sp", 8), ("pool", 6), ("act", 4)):
            for i in range(n):
                sem = es.enter_context(nc.semaphore("d_%s%d" % (name, i)))
                self.E[name]["pool"].append((self._sid(), sem))
        self.ninst = 0

    def _sid(self):
        self.semid += 1
        return self.semid

    def _wait(self, E, tok):
        (sid, sem), val, _ = tok
        if E["waited"].get(sid, 0) >= val:
            return
        E["eng"].wait_ge(sem, val)
        E["waited"][sid] = val

    def _collect(self, en, reads, writes):
        toks = []
        for b, i in _parts(reads):
            if b.w[i] is not None:
                toks.append(b.w[i])
            if b.excl:
                toks.extend(t for t in b.r[i] if t[2] != en)
        for b, i in _parts(writes):
            if b.w[i] is not None:
                toks.append(b.w[i])
            toks.extend(b.r[i])
        res = []
        for t in toks:
            if en == "pe" and t[2] == "pe":
                continue
            res.append(t)
        return res

    def _update(self, reads, writes, tok):
        for b, i in _parts(reads):
            b.r[i].append(tok)
            if len(b.r[i]) > 24:
                last = {}
                for t in b.r[i]:
                    k = t[0][0]
                    if k not in last or last[k][1] < t[1]:
                        last[k] = t
                b.r[i] = list(last.values())
        for b, i in _parts(writes):
            b.w[i] = tok
            b.r[i] = []

    def op(self, en, fn, reads=(), writes=()):
        E = self.E[en]
        for t in self._collect(en, reads, writes):
            self._wait(E, t)
        ins = fn(E["eng"])
        E["count"] += 1
        ins.then_inc(E["sem"][1], 1)
        tok = (E["sem"], E["count"], en)
        self._update(reads, writes, tok)
        self.ninst += 1
        return tok

    def dma(self, qn, out, in_, reads=(), writes=(), **kw):
        Q = self.E[qn]
        i = Q["ndma"]
        P = len(Q["pool"])
        sem = Q["pool"][i % P]
        val = 16 * (i // P + 1)
        if i >= P:
            self._wait(Q, (sem, val - 16, "dma"))
        for t in self._collect("dma", reads, writes):
            self._wait(Q, t)
        Q["eng"].dma_start(out=out, in_=in_, **kw).then_inc(sem[1], 16)
        Q["ndma"] += 1
        tok = (sem, val, "dma")
        self._update(reads, writes, tok)
        return tok

    def all_tokens(self):
        toks = []
        for E in self.E.values():
            if E["count"]:
                toks.append((E["sem"], E["count"], E["name"]))
            P = len(E["pool"])
            for j in range(min(P, E["ndma"])):
                n = (E["ndma"] - 1 - j) // P + 1
                toks.append((E["pool"][j], 16 * n, "dma"))
        return toks

    def barrier(self):
        toks = self.all_tokens()
        for E in self.E.values():
            for t in toks:
                self._wait(E, t)


def bc(ap, shape):
    return ap.broadcast_to(shape)


class Ctx:
    pass


def build_program():
    nc = bass.Bass("TRN2", target_bir_lowering=False)
    g = Ctx()
    g.nc = nc

    def din(name, shape, dt=F32):
        return nc.dram_tensor(name, list(shape), dt, kind="ExternalInput").ap()

    def dscr(name, shape, dt):
        kind = "ExternalOutput" if DEBUG else "Internal"
        return nc.dram_tensor(name, list(shape), dt, kind=kind).ap()

    g.xb = din("xb", [SEQ, D])
    g.ctxb = din("ctxb", [256, D])
    g.xext = din("xext", [EXT + 128, D])
    g.w_mod = din("w_mod", [D, 6 * D])
    g.w_in = din("w_in", [D, 8256])
    g.w_fa = din("w_fa", [D, D])
    g.w_sb = din("w_sb", [2048, D])
    g.w_o = din("w_o", [D, D])
    g.w_up = din("w_up", [D, 2 * DFF])
    g.w_down = din("w_down", [DFF, D])
    g.cpp = din("cpp", [128, CPP_N])
    g.cbc = din("cbc", [128, CBC_N])
    g.cbg = din("cbg", [128, CBG_N])
    g.cbrow = din("cbrow", [1, 3072])
    g.cmk = din("cmk", [128, 8 * 128])
    g.t1 = din("t1", [64, 128])
    g.t2 = din("t2", [128, 2 * 64 * 68])
    g.tcs = din("tcs", [128, 256])
    g.out = nc.dram_tensor("out", [2048, D], F32, kind="ExternalOutput").ap()
    g.U = dscr("U", [NCH, 128, D], BF16)
    g.Y = dscr("Y", [128, 128, D], BF16)
    g.XT = dscr("XT", [128, 8 * 2 * EXT], BF16)
    g.MS = dscr("MS", [NEXT, 128, D], BF16)
    g.YF = dscr("YF", [NEXT, 128, 2048], BF16)
    g.YT = dscr("YT", [NEXT, 128, 2048], BF16)
    g.L1 = dscr("L1", [NEXT, 128, D], F32)
    g.SFB = dscr("SFB", [2, 128, 2048], F32)

    with ExitStack() as es:
        P = Prog(nc, es)
        g.P = P
        g.uid = 0
        phase_setup(g, es)
        with ExitStack() as es2:
            alloc_conv_consts(g, es2)
            if STOP != "setup":
                phase_far(g)
            if STOP not in ("setup", "far"):
                phase_fnet(g)
            if STOP not in ("setup", "far", "fnet"):
                phase_own(g, 0)
                phase_own(g, 1)
            P.barrier()
        if STOP not in ("setup", "far", "fnet", "own"):
            phase_merge(g)
        if STOP not in ("setup", "far", "fnet", "own", "merge"):
            phase_ffn(g)
        P.barrier()
    return nc


def sb(g, es, name, shape, dt, nparts=1):
    g.uid += 1
    t = es.enter_context(g.nc.sbuf_tensor("%s_%d" % (name, g.uid), list(shape), dt))
    return Buf(t, nparts)


def ps(g, es, name, shape, dt=F32, nparts=1):
    g.uid += 1
    t = es.enter_context(g.nc.psum_tensor("%s_%d" % (name, g.uid), list(shape), dt))
    b = Buf(t, nparts)
    b.excl = True
    return b


def _layout(items):
    off = {}
    o = 0
    for k, n in items:
        off[k] = (o, n)
        o += n
    return off, o


CPP, CPP_N = _layout([("c", 8), ("cctx", 8), ("bmod", 48), ("n1g", 8), ("n2g", 8), ("cw_ssd", 72), ("cb_ssd", 24),
                      ("cw_ffn", 9 * 44), ("cb_ffn", 44), ("emask", NEXT), ("fmask", NSLOT * 2)])
CBC, CBC_N = _layout([("dt_bias", 64), ("a_log", 64), ("d_skip", 32), ("emask_bc", 256), ("halo_v", 128)])
CBG, CBG_N = _layout([("ssd_g", 2048), ("final_g", 1024), ("bmod_g1", 1024), ("bmod_g2", 1024)])
MK = {k: i for i, k in enumerate(["ones", "le", "ge", "gt", "lt", "ident", "pen_f", "pen_b"])}


def cpp(g, key, j=None, n=1):
    o, _ = CPP[key]
    if j is None:
        return g.cpp_t[:, o:o + CPP[key][1]]
    return g.cpp_t[:, o + j:o + j + n]


def cbcv(g, key, a=0, n=None):
    o, m = CBC[key]
    if n is None:
        n = m
    return g.cbc_t[:, o + a:o + a + n]


def mk32(g, key):
    i = MK[key]
    return g.cmk_t[:, i * 128:(i + 1) * 128]


def mk16(g, key):
    i = MK[key]
    return g.cmkb_t[:, i * 128:(i + 1) * 128]


def phase_setup(g, es):
    nc, P = g.nc, g.P
    g.cpp_t = sb(g, es, "cpp", [128, CPP_N], F32)
    g.cbc_t = sb(g, es, "cbc", [128, CBC_N], F32)
    g.cmk_t = sb(g, es, "cmk", [128, 8 * 128], F32)
    g.cmkb_t = sb(g, es, "cmkb", [128, 8 * 128], BF16)
    g.modv = sb(g, es, "modv", [128, 8 * 8], F32)
    g.gbc = sb(g, es, "gbc", [128, 2 * D], F32)
    g.nega = sb(g, es, "nega", [128, 64], F32)
    g.Sf = sb(g, es, "Sf", [128, 2048], F32)
    g.Sb = sb(g, es, "Sb", [128, 2048], F32)
    P.dma("sp", g.cpp_t[:], g.cpp, writes=[g.cpp_t])
    P.dma("sp", g.cbc_t[:], g.cbc, writes=[g.cbc_t])
    P.dma("sp", g.cmk_t[:], g.cmk, writes=[g.cmk_t])
    P.dma("pool", g.cmkb_t[:], g.cmk, writes=[g.cmkb_t])
    with ExitStack() as ls:
        sc = sb(g, ls, "sc", [128, 8, 2], F32)
        screp = sb(g, ls, "screp", [128, 8, 128], F32)
        modT = sb(g, ls, "modT", [128, 48, 2], F32)
        wm = [sb(g, ls, "wm%d" % i, [128, 8, 1024], F32) for i in range(2)]
        pm = ps(g, ls, "pm", [128, 8, 2], F32)
        pg = ps(g, ls, "pg", [128, 512], F32)
        bg = sb(g, ls, "bg", [128, 2 * D], F32)
        P.dma("sp", bg[:], g.cbg[:, CBG["bmod_g1"][0]:CBG["bmod_g1"][0] + 2 * D], writes=[bg])
        P.op("act", lambda e: e.activation(out=sc[:, :, 0], in_=cpp(g, "c"), func=AF.Silu), [g.cpp_t], [sc])
        P.op("act", lambda e: e.activation(out=sc[:, :, 1], in_=cpp(g, "cctx"), func=AF.Silu), [g.cpp_t], [sc])
        P.op("dve", lambda e: e.tensor_copy(out=screp[:], in_=bc(sc[:, :, 0:1], [128, 8, 128])), [sc], [screp])
        wv = g.w_mod.rearrange("(kb p) n -> p kb n", p=128)
        for j in range(6):
            w = wm[j % 2]
            P.dma("sp", w[:], wv[:, :, j * 1024:(j + 1) * 1024], writes=[w])
            def mm(e, w=w):
                ins = None
                for fb in range(8):
                    for kb in range(8):
                        ins = e.matmul(pm[:, fb, :], lhsT=w[:, kb, fb * 128:(fb + 1) * 128], rhs=sc[:, kb, :],
                                       start=(kb == 0), stop=(kb == 7))
                return ins
            P.op("pe", mm, [w, sc], [pm])
            bo = CPP["bmod"][0] + j * 8
            P.op("dve", lambda e, j=j, bo=bo: e.tensor_tensor(
                out=modT[:, j * 8:(j + 1) * 8, :], in0=pm[:], in1=bc(g.cpp_t[:, bo:bo + 8].unsqueeze(2), [128, 8, 2]),
                op=OP.add), [pm, g.cpp_t], [modT])
            if j in (2, 5):
                gi = 0 if j == 2 else 1
                for hf in range(2):
                    def mg(e, w=w, hf=hf):
                        ins = None
                        for kb in range(8):
                            ins = e.matmul(pg[:], lhsT=screp[:, kb, :], rhs=w[:, kb, hf * 512:(hf + 1) * 512],
                                           start=(kb == 0), stop=(kb == 7))
                        return ins
                    P.op("pe", mg, [w, screp], [pg])
                    P.op("dve", lambda e, gi=gi, hf=hf: e.tensor_tensor(
                        out=g.gbc[:, gi * D + hf * 512: gi * D + (hf + 1) * 512], in0=pg[:],
                        in1=bg[:, gi * D + hf * 512: gi * D + (hf + 1) * 512], op=OP.add), [pg, bg], [g.gbc])
        mv = g.modv
        def mkA(dst, scale_j, which, gkey):
            P.op("dve", lambda e: e.scalar_tensor_tensor(
                out=mv[:, dst * 8:(dst + 1) * 8], in0=modT[:, scale_j * 8:(scale_j + 1) * 8, which], scalar=1.0,
                in1=cpp(g, gkey), op0=OP.add, op1=OP.mult), [modT, g.cpp_t], [mv])

        def mkB(dst, shift_j, which):
            P.op("dve", lambda e: e.tensor_copy(out=mv[:, dst * 8:(dst + 1) * 8],
                                                 in_=modT[:, shift_j * 8:(shift_j + 1) * 8, which]), [modT], [mv])
        mkA(0, 1, 0, "n1g"); mkB(1, 0, 0)
        mkA(2, 1, 1, "n1g"); mkB(3, 0, 1)
        mkA(4, 4, 0, "n2g"); mkB(5, 3, 0)
        P.op("act", lambda e: e.activation(out=g.nega[:], in_=cbcv(g, "a_log"), func=AF.Exp), [g.cbc_t], [g.nega])
        P.op("dve", lambda e: e.tensor_scalar(out=g.nega[:], in0=g.nega[:], scalar1=-1.0, scalar2=None, op0=OP.mult),
             [g.nega], [g.nega])
        P.barrier()


def alloc_conv_consts(g, es):
    P = g.P
    g.diag = sb(g, es, "diag", [128, 72, 128], BF16)
    g.brow = sb(g, es, "brow", [1, 3072], BF16)
    g.Wxbc = sb(g, es, "Wxbc", [128, 8, 3072], BF16)
    load_w_cols(g, g.Wxbc, 1024, 3072)
    for a_ in range(0, 3072, 1024):
        P.dma("pool", g.brow[:, a_:a_ + 1024], g.cbrow[:, a_:a_ + 1024], writes=[g.brow])
    for i_ in range(72):
        P.op("dve", lambda e, i_=i_: e.tensor_scalar(out=g.diag[:, i_, :], in0=mk16(g, "ident"),
                                                      scalar1=cpp(g, "cw_ssd", i_), scalar2=None, op0=OP.mult),
             [g.cmkb_t, g.cpp_t], [g.diag])


def modA(g, i, kb):
    return g.modv[:, i * 8 + kb:i * 8 + kb + 1]


def alloc_chunk_bufs(g, es, nfb):
    c = Ctx()
    c.xt = [sb(g, es, "xt%d" % i, [128, D], F32) for i in range(2)]
    c.junk = sb(g, es, "junk", [128, D], BF16)
    c.st = [sb(g, es, "st%d" % i, [128, 4], F32) for i in range(2)]
    c.xn = [sb(g, es, "xn%d" % i, [128, D], BF16) for i in range(2)]
    c.hTe = [sb(g, es, "hTe%d" % i, [128, 8, 130], BF16) for i in range(3)]
    c.pA = ps(g, es, "pA", [128, 8, 128], BF16)
    c.nfb = nfb
    return c


def prep_a(g, c, k, src_rows):
    P = g.P
    i2 = k % 2
    xt, st, xn = c.xt[i2], c.st[i2], c.xn[i2]
    P.dma("sp", xt[:], src_rows, writes=[xt])
    P.op("act", lambda e: e.activation(out=c.junk[:], in_=xt[:], func=AF.Square, accum_out=st[:, 0:1]),
         [xt], [c.junk, st])
    P.op("dve", lambda e: e.tensor_scalar(out=st[:, 1:2], in0=st[:, 0:1], scalar1=1.0 / D, scalar2=EPS,
                                           op0=OP.mult, op1=OP.add), [st], [st])
    P.op("act", lambda e: e.activation(out=st[:, 2:3], in_=st[:, 1:2], func=AF.Ln), [st], [st])
    P.op("act", lambda e: e.activation(out=st[:, 3:4], in_=st[:, 2:3], func=AF.Exp, scale=-0.5), [st], [st])
    P.op("act", lambda e: e.activation(out=xn[:], in_=xt[:], func=AF.Copy, scale=st[:, 3:4]), [xt, st], [xn])
    return xt


def prep_b(g, c, k, ai, bi, vmask=None, hT=None):
    P = g.P
    xn = c.xn[k % 2]
    if hT is None:
        hT = c.hTe[k % 3]

    def tr(e):
        ins = None
        for kb in range(8):
            ins = e.transpose(out=c.pA[:, kb, :], in_=xn[:, kb * 128:(kb + 1) * 128], identity=mk16(g, "ident"))
        return ins
    P.op("pe", tr, [xn, g.cmkb_t], [c.pA])
    P.op("dve", lambda e: e.tensor_tensor(out=hT[:, :, 1:129], in0=c.pA[:],
                                          in1=bc(g.modv[:, ai * 8:(ai + 1) * 8].unsqueeze(2), [128, 8, 128]),
                                          op=OP.mult), [c.pA, g.modv], [hT])
    P.op("dve", lambda e: e.tensor_tensor(out=hT[:, :, 1:129], in0=hT[:, :, 1:129],
                                          in1=bc(g.modv[:, bi * 8:(bi + 1) * 8].unsqueeze(2), [128, 8, 128]),
                                          op=OP.add), [hT, g.modv], [hT])
    if vmask is not None:
        P.op("pool", lambda e: e.tensor_tensor(out=hT[:, :, 1:129], in0=hT[:, :, 1:129],
                                               in1=bc(vmask.unsqueeze(1), [128, 8, 128]), op=OP.mult),
             [hT, g.cbc_t], [hT])


def prep(g, c, k, src_rows, ai, bi, vmask=None, hT=None):
    xt = prep_a(g, c, k, src_rows)
    prep_b(g, c, k, ai, bi, vmask=vmask, hT=hT)
    return xt


def halo_link(g, c, k, has_left):
    P = g.P
    cur = c.hTe[k % 3]
    if has_left:
        prv = c.hTe[(k - 1) % 3]
        P.op("pool", lambda e: e.tensor_copy(out=cur[:, :, 0:1], in_=prv[:, :, 128:129]), [prv], [cur])
        P.op("pool", lambda e: e.tensor_copy(out=prv[:, :, 129:130], in_=cur[:, :, 1:2]), [cur], [prv])
    else:
        P.op("pool", lambda e: e.memset(cur[:, :, 0:1], 0.0), [], [cur])


def halo_zero_right(g, c, k):
    cur = c.hTe[k % 3]
    g.P.op("pool", lambda e: e.memset(cur[:, :, 129:130], 0.0), [], [cur])


def fm_proj_conv(g, c, s, hT, W, nfb, cw_off, steps=None):
    P = g.P
    steps = steps if steps is not None else []
    groups = [list(range(a, min(a + 3, nfb))) for a in range(0, nfb, 3)]
    def e_proj(gi):
        fbs = groups[gi]
        pa = s.pxa[gi % len(s.pxa)]

        def mm(e, fbs=fbs, pa=pa):
            ins = None
            for j, fb in enumerate(fbs):
                for kb in range(8):
                    ins = e.matmul(pa[:, j * 130:(j + 1) * 130], lhsT=W[:, kb, fb * 128:(fb + 1) * 128], rhs=hT[:, kb, :],
                                   start=(kb == 0), stop=(kb == 7))
            return ins
        P.op("pe", mm, [W, hT], [pa])

    def e_rest(gi):
        fbs = groups[gi]
        n = len(fbs)
        pa = s.pxa[gi % len(s.pxa)]
        pb = s.pxc[gi % len(s.pxc)]
        pre = s.pre[gi % 2]
        P.op("dve", lambda e, n=n, pa=pa, pre=pre: e.tensor_copy(
            out=pre[:, 0:n, :], in_=pa[:, 0:n * 130].rearrange("p (j t) -> p j t", t=130)), [pa], [pre])

        def mc(e, fbs=fbs, pb=pb, pre=pre):
            ins = None
            for j, fb in enumerate(fbs):
                cf = cw_off + fb
                for k in range(3):
                    e.matmul(pb[:, j * 128:(j + 1) * 128], lhsT=g.diag[:, k * 24 + cf, :], rhs=pre[:, j, k:k + 128],
                             start=(k == 0), stop=False)
                ins = e.matmul(pb[:, j * 128:(j + 1) * 128], lhsT=g.brow[0:1, cf * 128:(cf + 1) * 128],
                               rhs=mk16(g, "ones")[0:1, :], start=False, stop=True)
            return ins
        P.op("pe", mc, [pre, g.diag, g.brow, g.cmkb_t], [pb])
        f0, f1 = fbs[0], fbs[-1] + 1
        P.op("act", lambda e, f0=f0, f1=f1, n=n, pb=pb: e.activation(
            out=s.xcs[:, f0:f1, :], in_=pb[:, 0:n * 128].rearrange("p (j t) -> p j t", t=128), func=AF.Silu),
            [pb], [(s.xcs, gi)])

    ng = len(groups)
    ahead = len(s.pxa) >= 2
    if ahead:
        e_proj(0)
    for gi in range(ng):
        if ahead:
            if gi + 1 < ng:
                e_proj(gi + 1)
        else:
            e_proj(gi)
        e_rest(gi)
        if steps:
            steps.pop(0)()
    while steps:
        steps.pop(0)()


def to_token_major(g, c, s, nblk):
    P = g.P
    for r0 in range(0, nblk, 8):
        n = min(8, nblk - r0)

        def tr(e, r0=r0, n=n):
            ins = None
            for j in range(n):
                ins = e.transpose(out=c.pA[:, j, :], in_=s.xcs[:, r0 + j, :], identity=mk16(g, "ident"))
            return ins
        P.op("pe", tr, [s.xcs, g.cmkb_t], [c.pA])
        P.op("act", lambda e, r0=r0, n=n: e.activation(
            out=s.xtok[:, r0 * 128:(r0 + n) * 128], in_=c.pA[:, 0:n, :], func=AF.Copy), [c.pA], [s.xtok])


def dt_steps(g, s, hT, Wdt, ncol, bias_ap, nega_ap, mask_ap):
    P = g.P

    def s1():
        def mm(e):
            ins = None
            for kb in range(8):
                ins = e.matmul(s.pD[:, 0:ncol], lhsT=hT[:, kb, 1:129], rhs=Wdt[:, kb, 0:ncol], start=(kb == 0), stop=(kb == 7))
            return ins
        P.op("pe", mm, [hT, Wdt], [s.pD])
        P.op("dve", lambda e: e.tensor_tensor(out=s.dtm[:, 0:ncol], in0=s.pD[:, 0:ncol], in1=bias_ap, op=OP.add),
             [s.pD, g.cbc_t], [s.dtm])

    def s2():
        P.op("act", lambda e: e.activation(out=s.dtm[:, 0:ncol], in_=s.dtm[:, 0:ncol], func=AF.Exp), [s.dtm], [s.dtm])

    def s3():
        P.op("act", lambda e: e.activation(out=s.dtm[:, 0:ncol], in_=s.dtm[:, 0:ncol], func=AF.Ln, bias=1.0), [s.dtm], [s.dtm])

    def s4():
        dv = s.dtm[:, 0:ncol].rearrange("p (a b) -> p a b", b=32)
        P.op("dve", lambda e: e.tensor_tensor(out=dv, in0=dv, in1=mask_ap, op=OP.mult), [s.dtm, g.cpp_t], [s.dtm])

    def s5():
        P.op("dve", lambda e: e.tensor_tensor(out=s.la[:, 0:ncol], in0=s.dtm[:, 0:ncol], in1=nega_ap, op=OP.mult),
             [s.dtm, g.nega], [s.la])
    return [s1, s2, s3, s4, s5]


def state_contrib(g, s, wexp_ap, xdd, on_group):
    P = g.P
    P.op("dve", lambda e: e.tensor_tensor(
        out=xdd[:].rearrange("p (h d) -> p h d", d=64), in0=s.xtok[:, 0:2048].rearrange("p (h d) -> p h d", d=64),
        in1=bc(wexp_ap.unsqueeze(2), [128, 32, 64]), op=OP.mult), [s.xtok, s.wx], [xdd])
    for gi in range(4):
        P.op("pe", lambda e, gi=gi: e.matmul(s.pH[:], lhsT=s.xtok[:, 2048 + gi * 128:2048 + (gi + 1) * 128],
                                             rhs=xdd[:, gi * 512:(gi + 1) * 512], start=True, stop=True),
             [s.xtok, xdd], [s.pH])
        on_group(gi, s.pH)


def load_w_cols(g, W, col0, ncols, dst0=0):
    wv = g.w_in.rearrange("(kb p) n -> p kb n", p=128)
    for a in range(0, ncols, 512):
        n = min(512, ncols - a)
        g.P.dma("pool", W[:, :, dst0 + a:dst0 + a + n], wv[:, :, col0 + a:col0 + a + n], writes=[W])


def phase_far(g):
    nc, P = g.nc, g.P
    with ExitStack() as es:
        c = alloc_chunk_bufs(g, es, 20)
        s = Ctx()
        Wf = sb(g, es, "Wf", [128, 8, 1024], BF16)
        Wxb = g.Wxbc
        Wdt = sb(g, es, "Wdt", [128, 8, 64], BF16)
        load_w_cols(g, Wf, 0, 1024)
        load_w_cols(g, Wdt, 6144, 64)
        s.pxa = [ps(g, es, "pxa%d" % i, [128, 512], F32) for i in range(2)]
        s.pxc = [ps(g, es, "pxc%d" % i, [128, 512], F32) for i in range(1)]
        s.pD = ps(g, es, "pD", [128, 512], F32)
        s.pH = ps(g, es, "pH", [128, 512], F32)
        pf = [ps(g, es, "pf%d" % i, [128, 512], F32) for i in range(2)]
        s.pre = [sb(g, es, "pre%d" % i, [128, 3, 130], BF16) for i in range(2)]
        s.xcs = sb(g, es, "xcs", [128, 20, 128], BF16, nparts=8)
        s.pD2 = sb(g, es, "pD2", [128, 192], F32)
        s.xtok = sb(g, es, "xtok", [128, 2560], BF16)
        s.dtm = sb(g, es, "dtm", [128, 64], F32)
        s.la = sb(g, es, "la", [128, 64], F32)
        s.wx = sb(g, es, "wx", [128, 64], F32)
        s.sg = sb(g, es, "sg", [128, 64], F32)
        Rb = sb(g, es, "Rb", [128, 32], F32)
        wxb = sb(g, es, "wxb", [128, 64], BF16)
        dec = sb(g, es, "dec", [128, 32], F32)
        xdd = [sb(g, es, "xdd%d" % i, [128, 2048], BF16) for i in range(2)]
        ub = [sb(g, es, "ub%d" % i, [128, D], BF16) for i in range(2)]
        P.op("dve", lambda e: e.memset(g.Sf[:], 0.0), [], [g.Sf])
        P.op("dve", lambda e: e.memset(g.Sb[:], 0.0), [], [g.Sb])
        P.op("dve", lambda e: e.memset(Rb[:], 0.0), [], [Rb])
        slots = [("c", 0), ("c", 1)] + [("l", i) for i in range(NCH)] + [("c", 0), ("c", 1)]
        first = {0, 2, 66}
        last = {1, 65, 67}

        def do_prep_a(k):
            kind, i = slots[k]
            src = g.ctxb[i * 128:(i + 1) * 128, :] if kind == "c" else g.xb[i * 128:(i + 1) * 128, :]
            prep_a(g, c, k, src)

        def do_prep_b(k):
            kind, i = slots[k]
            if kind == "c":
                prep_b(g, c, k, 2, 3)
            else:
                prep_b(g, c, k, 0, 1)
            halo_link(g, c, k, k not in first)
            if k in last:
                halo_zero_right(g, c, k)
        do_prep_a(0); do_prep_b(0)
        do_prep_a(1); do_prep_b(1)
        for k in range(NSLOT):
            kind, i = slots[k]
            hT = c.hTe[k % 3]
            fo = CPP["fmask"][0] + 2 * k
            mask_ap = bc(g.cpp_t[:, fo:fo + 2].unsqueeze(2), [128, 2, 32])
            steps = dt_steps(g, s, hT, Wdt, 64, cbcv(g, "dt_bias"), g.nega[:], mask_ap)

            def t1():
                def segs(e):
                    e.matmul(s.pD[:, 64:96], lhsT=mk32(g, "gt"), rhs=s.la[:, 0:32], start=True, stop=True)
                    e.matmul(s.pD[:, 96:128], lhsT=mk32(g, "lt"), rhs=s.la[:, 32:64], start=True, stop=True)
                    return e.matmul(s.pD[:, 128:192], lhsT=mk32(g, "ones"), rhs=s.la[:, 0:64], start=True, stop=True)
                P.op("pe", segs, [s.la, g.cmk_t], [s.pD])
                P.op("dve", lambda e: e.tensor_copy(out=s.pD2[:], in_=s.pD[:, 0:192]), [s.pD], [s.pD2])

            def t2b():
                P.op("pool", lambda e: e.tensor_copy(out=s.sg[:, 0:32], in_=s.pD2[:, 64:96]), [s.pD2], [s.sg])
                P.op("pool", lambda e: e.tensor_tensor(out=s.sg[:, 32:64], in0=s.pD2[:, 96:128], in1=Rb[:], op=OP.add),
                     [s.pD2, Rb], [s.sg])
                P.op("pool", lambda e: e.tensor_tensor(out=Rb[:], in0=Rb[:], in1=s.pD2[:, 160:192], op=OP.add),
                     [s.pD2, Rb], [Rb])

            def t3():
                P.op("act", lambda e: e.activation(out=s.wx[:], in_=s.sg[:], func=AF.Exp), [s.sg], [s.wx])
                P.op("act", lambda e: e.activation(out=dec[:], in_=s.pD2[:, 128:160], func=AF.Exp), [s.pD2], [dec])

            def t4():
                P.op("pool", lambda e: e.tensor_tensor(out=wxb[:], in0=s.wx[:], in1=s.dtm[:], op=OP.mult),
                     [s.wx, s.dtm], [wxb])
                P.op("pool", lambda e: e.tensor_tensor(
                    out=g.Sf[:].rearrange("p (h d) -> p h d", d=64), in0=g.Sf[:].rearrange("p (h d) -> p h d", d=64),
                    in1=bc(dec[:].unsqueeze(2), [128, 32, 64]), op=OP.mult), [g.Sf, dec], [g.Sf])
            for st_ in steps + [t1, t2b, t3, t4]:
                st_()
            if k + 2 < NSLOT:
                do_prep_a(k + 2)
            if kind == "l":
                u = ub[i % 2]
                for hf in range(2):
                    def mm(e, hf=hf):
                        ins = None
                        for kb in range(8):
                            ins = e.matmul(pf[hf][:], lhsT=hT[:, kb, 1:129], rhs=Wf[:, kb, hf * 512:(hf + 1) * 512],
                                           start=(kb == 0), stop=(kb == 7))
                        return ins
                    P.op("pe", mm, [hT, Wf], [pf[hf]])
                    P.op("dve", lambda e, hf=hf, u=u: e.tensor_copy(out=u[:, hf * 512:(hf + 1) * 512], in_=pf[hf][:]),
                         [pf[hf]], [u])
                P.dma("sp", g.U[i], u[:], reads=[u])
            fm_proj_conv(g, c, s, hT, Wxb, 20, 0, [])
            if k + 2 < NSLOT:
                do_prep_b(k + 2)
            to_token_major(g, c, s, 20)
            x3 = s.xtok[:, 0:2048].rearrange("p (h d) -> p h d", d=64)
            P.op("pool", lambda e: e.tensor_tensor(out=xdd[1][:].rearrange("p (h d) -> p h d", d=64), in0=x3,
                                                   in1=bc(wxb[:, 32:64].unsqueeze(2), [128, 32, 64]), op=OP.mult),
                 [s.xtok, wxb], [xdd[1]])
            P.op("dve", lambda e: e.tensor_tensor(out=xdd[0][:].rearrange("p (h d) -> p h d", d=64), in0=x3,
                                                  in1=bc(wxb[:, 0:32].unsqueeze(2), [128, 32, 64]), op=OP.mult),
                 [s.xtok, wxb], [xdd[0]])
            banks = [s.pxa[0], s.pxa[1], s.pxc[0], s.pH]
            for di, (xd_, S_) in enumerate(((xdd[0], g.Sf), (xdd[1], g.Sb))):
                for gi in range(4):
                    pst = banks[gi]
                    P.op("pe", lambda e, gi=gi, pst=pst, xd_=xd_: e.matmul(
                        pst[:], lhsT=s.xtok[:, 2048 + gi * 128:2048 + (gi + 1) * 128],
                        rhs=xd_[:, gi * 512:(gi + 1) * 512], start=True, stop=True), [s.xtok, xd_], [pst])
                    P.op("dve", lambda e, gi=gi, pst=pst, S_=S_: e.tensor_tensor(
                        out=S_[:, gi * 512:(gi + 1) * 512], in0=S_[:, gi * 512:(gi + 1) * 512], in1=pst[:], op=OP.add),
                        [S_, pst], [S_])
        if DEBUG:
            P.dma("sp", g.SFB[0], g.Sf[:], reads=[g.Sf])
            P.dma("sp", g.SFB[1], g.Sb[:], reads=[g.Sb])
        P.barrier()


def load_w_gen(g, W, src, nkb, ncols):
    wv = src.rearrange("(kb p) n -> p kb n", p=128)
    for a in range(0, ncols, 512):
        n = min(512, ncols - a)
        g.P.dma("pool", W[:, :, a:a + n], wv[:, :, a:a + n], writes=[W])


def phase_fnet(g):
    nc, P = g.nc, g.P
    with ExitStack() as es:
        T1 = sb(g, es, "T1", [64, 128], BF16)
        P.dma("pool", T1[:], g.t1, writes=[T1])
        V = [sb(g, es, "V%d" % i, [64, 4, D], BF16) for i in range(2)]
        Yt = [sb(g, es, "Yt%d" % i, [128, 4, D], BF16) for i in range(2)]
        p1 = [ps(g, es, "p1_%d" % i, [128, 512], F32) for i in range(4)]
        cnt = 0
        for tg in range(32):
            v, yt = V[tg % 2], Yt[tg % 2]
            P.dma("sp", v[:], g.U[:, tg * 4:(tg + 1) * 4, :], writes=[v])
            for t in range(4):
                for hf in range(2):
                    pp = p1[cnt % 4]
                    P.op("pe", lambda e, pp=pp, t=t, hf=hf, v=v: e.matmul(
                        pp[:], lhsT=T1[:], rhs=v[:, t, hf * 512:(hf + 1) * 512], start=True, stop=True), [T1, v], [pp])
                    if cnt % 2:
                        P.op("act", lambda e, pp=pp, t=t, hf=hf, yt=yt: e.activation(
                            out=yt[:, t, hf * 512:(hf + 1) * 512], in_=pp[:], func=AF.Copy), [pp], [yt])
                    else:
                        P.op("dve", lambda e, pp=pp, t=t, hf=hf, yt=yt: e.tensor_copy(
                            out=yt[:, t, hf * 512:(hf + 1) * 512], in_=pp[:]), [pp], [yt])
                    cnt += 1
            P.dma("sp", g.Y[:, tg * 4:(tg + 1) * 4, :], yt[:], reads=[yt])
        P.barrier()
    with ExitStack() as es:
        T2 = sb(g, es, "T2", [128, 2 * 64 * 68], BF16)
        for a in range(0, 2 * 64 * 68, 1088):
            P.dma("pool", T2[:, a:a + 1088], g.t2[:, a:a + 1088], writes=[T2])
        Yk = [sb(g, es, "Yk%d" % i, [128, 2, D], BF16) for i in range(2)]
        XTs = sb(g, es, "XTs", [128, 8, 2, EXT], BF16)
        p2f = [ps(g, es, "p2_%d" % i, [128, 512], F32) for i in range(4)]
        yv = g.Y.rearrange("(ri k) t c -> k t ri c", ri=2)
        xv = XTs[:].rearrange("p c r (j k) -> p c r j k", k=64)
        for k1 in range(64):
            yk = Yk[k1 % 2]
            P.dma("sp", yk[:], yv[k1], writes=[yk])
            for cg in range(2):
                ppb = p2f[(k1 * 2 + cg) % 4]
                pp = ppb[:, 0:272].rearrange("p (c k) -> p c k", k=68)

                def mm(e, pp=pp, cg=cg, yk=yk, k1=k1):
                    ins = None
                    for cb in range(4):
                        cbx = cg * 4 + cb
                        e.matmul(pp[:, cb, :], lhsT=yk[:, 0, cbx * 128:(cbx + 1) * 128],
                                 rhs=T2[:, k1 * 68:(k1 + 1) * 68], start=True, stop=False)
                        ins = e.matmul(pp[:, cb, :], lhsT=yk[:, 1, cbx * 128:(cbx + 1) * 128],
                                       rhs=T2[:, (64 + k1) * 68:(64 + k1 + 1) * 68], start=False, stop=True)
                    return ins
                P.op("pe", mm, [yk, T2], [ppb])
                for ri in range(2):
                    if (k1 + cg) % 2:
                        P.op("act", lambda e, pp=pp, cg=cg, ri=ri, k1=k1: e.activation(
                            out=xv[:, cg * 4:(cg + 1) * 4, ri, :, k1], in_=pp[:, :, ri * 34:(ri + 1) * 34],
                            func=AF.Copy), [ppb], [XTs])
                    else:
                        P.op("dve", lambda e, pp=pp, cg=cg, ri=ri, k1=k1: e.tensor_copy(
                            out=xv[:, cg * 4:(cg + 1) * 4, ri, :, k1], in_=pp[:, :, ri * 34:(ri + 1) * 34]),
                            [ppb], [XTs])
        for cb in range(8):
            P.dma("sp", g.XT[:, cb * 2 * EXT:(cb + 1) * 2 * EXT].rearrange("p (r t) -> p r t", r=2), XTs[:, cb, :, :],
                  reads=[XTs])
        P.barrier()


def phase_own(g, d):
    nc, P = g.nc, g.P
    with ExitStack() as es:
        c = alloc_chunk_bufs(g, es, 24)
        s = Ctx()
        hH = sb(g, es, "hH", [128, 8, 130], BF16)
        W = g.Wxbc
        Wdt = sb(g, es, "Wdt", [128, 8, 32], BF16)
        load_w_cols(g, Wdt, 6144 + 32 * d, 32)
        s.pxa = [ps(g, es, "pxa%d" % i, [128, 512], F32) for i in range(1)]
        s.pxc = [ps(g, es, "pxc%d" % i, [128, 512], F32) for i in range(1)]
        s.pxb = [s.pxa[0], s.pxc[0]]
        s.pre = [sb(g, es, "pre%d" % i, [128, 3, 130], BF16) for i in range(2)]
        s.pD = ps(g, es, "pD", [128, 512], F32)
        s.pH = ps(g, es, "pH", [128, 512], F32)
        psc = ps(g, es, "psc", [128, 4, 128], F32)
        pL = [ps(g, es, "pL%d" % i, [128, 4, 128], F32) for i in range(2)]
        s.xcs = sb(g, es, "xcs", [128, 24, 128], BF16, nparts=8)
        s.xtok = sb(g, es, "xtok", [128, 2560], BF16)
        s.dtm = sb(g, es, "dtm", [128, 32], F32)
        s.la = sb(g, es, "la", [128, 32], F32)
        s.wx = sb(g, es, "wx", [128, 32], F32)
        lab = sb(g, es, "lab", [128, 32], BF16)
        nlab = sb(g, es, "nlab", [128, 32], BF16)
        ecum = sb(g, es, "ecum", [128, 32], F32)
        dec = sb(g, es, "dec", [128, 32], F32)
        xd = sb(g, es, "xd", [128, 2048], BF16)
        xdd = sb(g, es, "xdd", [128, 2048], BF16)
        Sbf = sb(g, es, "Sbf", [128, 2048], BF16)
        Dt = [sb(g, es, "Dt%d" % i, [128, 8, 128], BF16) for i in range(2)]
        Lx = [sb(g, es, "Lx%d" % i, [128, 8, 128], BF16) for i in range(2)]
        G = [sb(g, es, "G%d" % i, [128, 8, 128], BF16) for i in range(2)]
        yo = sb(g, es, "yo", [128, 512], F32)
        ytile = sb(g, es, "ytile", [128, 2048], BF16)
        tmp = sb(g, es, "tmp", [128, 2048], BF16)
        yfl = sb(g, es, "yfl", [128, 2048], BF16)
        S = g.Sf if d == 0 else g.Sb
        mxk = "le" if d == 0 else "ge"
        sgk = "gt" if d == 0 else "lt"
        penk = "pen_f" if d == 0 else "pen_b"
        order = list(range(NEXT)) if d == 0 else list(range(NEXT - 1, -1, -1))

        prep(g, c, 0, g.xext[EXT:EXT + 128, :], 0, 1, vmask=cbcv(g, "halo_v"), hT=hH)

        def do_prep_a(ci):
            prep_a(g, c, ci, g.xext[ci * 128:(ci + 1) * 128, :])

        def do_prep_b(ci, prev_ci):
            hT = c.hTe[ci % 3]
            vm = None
            if ci == 0:
                vm = cbcv(g, "emask_bc", 0, 128)
            if ci == NEXT - 1:
                vm = cbcv(g, "emask_bc", 128, 128)
            prep_b(g, c, ci, 0, 1, vmask=vm)
            if prev_ci is not None:
                nb = c.hTe[prev_ci % 3]
                if ci == prev_ci + 1:
                    P.op("pool", lambda e: e.tensor_copy(out=hT[:, :, 0:1], in_=nb[:, :, 128:129]), [nb], [hT])
                    P.op("pool", lambda e: e.tensor_copy(out=nb[:, :, 129:130], in_=hT[:, :, 1:2]), [hT], [nb])
                else:
                    P.op("pool", lambda e: e.tensor_copy(out=hT[:, :, 129:130], in_=nb[:, :, 1:2]), [nb], [hT])
                    P.op("pool", lambda e: e.tensor_copy(out=nb[:, :, 0:1], in_=hT[:, :, 128:129]), [hT], [nb])
            if ci == 0:
                P.op("pool", lambda e: e.tensor_copy(out=hT[:, :, 0:1], in_=hH[:, :, 1:2]), [hH], [hT])
            if ci == NEXT - 1:
                P.op("pool", lambda e: e.tensor_copy(out=hT[:, :, 129:130], in_=hH[:, :, 2:3]), [hH], [hT])

        do_prep_a(order[0]); do_prep_b(order[0], None)
        do_prep_a(order[1]); do_prep_b(order[1], order[0])
        for oi, ci in enumerate(order):
            hT = c.hTe[ci % 3]
            if d == 1:
                P.dma("sp", yfl[:], g.YF[ci], writes=[yfl])
            mask_ap = bc(cpp(g, "emask", ci).unsqueeze(2), [128, 1, 32])
            steps = dt_steps(g, s, hT, Wdt, 32, cbcv(g, "dt_bias", 32 * d, 32), g.nega[:, 32 * d:32 * (d + 1)], mask_ap)

            def u1():
                P.op("pool", lambda e: e.tensor_copy(out=lab[:], in_=s.la[:]), [s.la], [lab])
                P.op("pool", lambda e: e.tensor_scalar(out=nlab[:], in0=lab[:], scalar1=-1.0, scalar2=None, op0=OP.mult),
                     [lab], [nlab])

                def segs(e):
                    e.matmul(s.pD[:, 64:96], lhsT=mk32(g, mxk), rhs=s.la[:], start=True, stop=True)
                    e.matmul(s.pD[:, 96:128], lhsT=mk32(g, sgk), rhs=s.la[:], start=True, stop=True)
                    return e.matmul(s.pD[:, 128:160], lhsT=mk32(g, "ones"), rhs=s.la[:], start=True, stop=True)
                P.op("pe", segs, [s.la, g.cmk_t], [s.pD])

            def u2():
                P.op("act", lambda e: e.activation(out=ecum[:], in_=s.pD[:, 64:96], func=AF.Exp), [s.pD], [ecum])
                P.op("act", lambda e: e.activation(out=s.wx[:], in_=s.pD[:, 96:128], func=AF.Exp), [s.pD], [s.wx])
                P.op("act", lambda e: e.activation(out=dec[:], in_=s.pD[:, 128:160], func=AF.Exp), [s.pD], [dec])

            def u3():
                P.op("pool", lambda e: e.tensor_tensor(out=s.wx[:], in0=s.wx[:], in1=s.dtm[:], op=OP.mult),
                     [s.wx, s.dtm], [s.wx])
            for st_ in steps + [u1, u2, u3]:
                st_()
            if oi + 2 < NEXT:
                do_prep_a(order[oi + 2])
            fm_proj_conv(g, c, s, hT, W, 24, 0, [])
            if oi + 2 < NEXT:
                do_prep_b(order[oi + 2], order[oi + 1])
            to_token_major(g, c, s, 20)
            x3 = s.xtok[:, 0:2048].rearrange("p (h d) -> p h d", d=64)
            P.op("dve", lambda e: e.tensor_tensor(out=xd[:].rearrange("p (h d) -> p h d", d=64), in0=x3,
                                                   in1=bc(s.dtm[:].unsqueeze(2), [128, 32, 64]), op=OP.mult),
                 [s.xtok, s.dtm], [xd])
            P.op("dve", lambda e: e.tensor_tensor(out=xdd[:].rearrange("p (h d) -> p h d", d=64), in0=x3,
                                                   in1=bc(s.wx[:].unsqueeze(2), [128, 32, 64]), op=OP.mult),
                 [s.xtok, s.wx], [xdd])
            P.op("act", lambda e: e.activation(out=Sbf[:], in_=S[:], func=AF.Copy), [S], [Sbf])

            def sc(e):
                ins = None
                for gi in range(4):
                    ins = e.matmul(psc[:, gi, :], lhsT=s.xcs[:, 16 + gi, :], rhs=s.xcs[:, 20 + gi, :], start=True, stop=True)
                return ins
            P.op("pe", sc, [s.xcs], [psc])
            for gi in range(4):
                dt_, lx, gg = Dt[gi % 2], Lx[gi % 2], G[gi % 2]
                P.op("pool", lambda e, gi=gi, dt_=dt_: e.tensor_tensor(
                    out=dt_[:], in0=bc(lab[:, gi * 8:(gi + 1) * 8].unsqueeze(2), [128, 8, 128]),
                    in1=bc(mk16(g, mxk).unsqueeze(1), [128, 8, 128]), op=OP.mult), [lab, g.cmkb_t], [dt_])
                for hh in range(2):
                    def mmL(e, gi=gi, hh=hh, dt_=dt_):
                        e.matmul(pL[hh][:], lhsT=mk16(g, "ones"), rhs=dt_[:, hh * 4:(hh + 1) * 4, :], start=True, stop=False)
                        e.matmul(pL[hh][:], lhsT=mk16(g, mxk),
                                 rhs=bc(nlab[:, gi * 8 + hh * 4:gi * 8 + hh * 4 + 4].unsqueeze(2), [128, 4, 128]),
                                 start=False, stop=False)
                        return e.matmul(pL[hh][:], lhsT=mk16(g, "ident"),
                                        rhs=bc(mk16(g, penk).unsqueeze(1), [128, 4, 128]), start=False, stop=True)
                    P.op("pe", mmL, [dt_, nlab, g.cmkb_t], [pL[hh]])
                    P.op("act", lambda e, hh=hh, lx=lx: e.activation(out=lx[:, hh * 4:(hh + 1) * 4, :], in_=pL[hh][:],
                                                                   func=AF.Exp), [pL[hh]], [lx])
                P.op("dve", lambda e, gi=gi, lx=lx, gg=gg: e.tensor_tensor(
                    out=gg[:], in0=lx[:], in1=bc(psc[:, gi, :].unsqueeze(1), [128, 8, 128]), op=OP.mult),
                    [lx, psc], [gg])

                def mmy(e, gi=gi, gg=gg):
                    ins = None
                    for h in range(8):
                        hh = gi * 8 + h
                        ins = e.matmul(s.pH[:, h * 64:(h + 1) * 64], lhsT=gg[:, h, :], rhs=xd[:, hh * 64:(hh + 1) * 64],
                                       start=True, stop=True)
                    return ins
                P.op("pe", mmy, [gg, xd], [s.pH])
                P.op("pe", lambda e, gi=gi: e.matmul(s.pxb[0][:], lhsT=s.xcs[:, 20 + gi, :],
                                                     rhs=Sbf[:, gi * 512:(gi + 1) * 512], start=True, stop=True),
                     [s.xcs, Sbf], [s.pxb[0]])
                P.op("dve", lambda e, gi=gi: e.tensor_tensor(
                    out=yo[:].rearrange("p (h d) -> p h d", d=64), in0=s.pxb[0][:].rearrange("p (h d) -> p h d", d=64),
                    in1=bc(ecum[:, gi * 8:(gi + 1) * 8].unsqueeze(2), [128, 8, 64]), op=OP.mult),
                    [s.pxb[0], ecum], [yo])
                P.op("dve", lambda e, gi=gi: e.tensor_tensor(out=ytile[:, gi * 512:(gi + 1) * 512], in0=yo[:],
                                                              in1=s.pH[:], op=OP.add), [yo, s.pH], [ytile])
                P.op("pe", lambda e, gi=gi: e.matmul(s.pxb[1][:], lhsT=s.xtok[:, 2048 + gi * 128:2048 + (gi + 1) * 128],
                                                     rhs=xdd[:, gi * 512:(gi + 1) * 512], start=True, stop=True),
                     [s.xtok, xdd], [s.pxb[1]])
                P.op("dve", lambda e, gi=gi: e.tensor_tensor(
                    out=S[:, gi * 512:(gi + 1) * 512].rearrange("p (h d) -> p h d", d=64),
                    in0=S[:, gi * 512:(gi + 1) * 512].rearrange("p (h d) -> p h d", d=64),
                    in1=bc(dec[:, gi * 8:(gi + 1) * 8].unsqueeze(2), [128, 8, 64]), op=OP.mult), [S, dec], [S])
                P.op("dve", lambda e, gi=gi: e.tensor_tensor(out=S[:, gi * 512:(gi + 1) * 512],
                                                              in0=S[:, gi * 512:(gi + 1) * 512], in1=s.pxb[1][:],
                                                              op=OP.add), [S, s.pxb[1]], [S])
            if d == 0:
                P.op("pool", lambda e: e.tensor_tensor(out=tmp[:].rearrange("p (h d) -> p h d", d=64), in0=x3,
                                                       in1=bc(cbcv(g, "d_skip").unsqueeze(2), [128, 32, 64]),
                                                       op=OP.mult), [s.xtok, g.cbc_t], [tmp])
                P.op("pool", lambda e: e.tensor_tensor(out=tmp[:], in0=tmp[:], in1=ytile[:], op=OP.add),
                     [tmp, ytile], [tmp])
                P.dma("sp", g.YF[ci], tmp[:], reads=[tmp])
            else:
                P.op("pool", lambda e: e.tensor_tensor(out=tmp[:], in0=yfl[:], in1=ytile[:], op=OP.add),
                     [yfl, ytile], [tmp])
                P.dma("sp", g.YT[ci], tmp[:], reads=[tmp])
        P.barrier()


def phase_merge(g):
    phase_merge_a(g)
    phase_merge_b(g)


def phase_merge_a(g):
    nc, P = g.nc, g.P
    with ExitStack() as es:
        c = alloc_chunk_bufs(g, es, 0)
        Wz = sb(g, es, "Wz", [128, 8, 2048], BF16)
        Wgs = sb(g, es, "Wgs", [128, 8, 1024], BF16)
        Wsb = sb(g, es, "Wsb", [128, 16, 1024], BF16)
        load_w_cols(g, Wz, 4096, 2048)
        load_w_cols(g, Wgs, 7232, 1024)
        load_w_gen(g, Wsb, g.w_sb, 16, 1024)
        sg = sb(g, es, "ssdg", [128, 2048], F32)
        P.dma("sp", sg[:], g.cbg[:, CBG["ssd_g"][0]:CBG["ssd_g"][0] + 2048], writes=[sg])
        pz = [ps(g, es, "pz%d" % i, [128, 512], F32) for i in range(4)]
        pbs = [ps(g, es, "pbs%d" % i, [128, 512], F32) for i in range(2)]
        yt = [sb(g, es, "yt%d" % i, [128, 2048], BF16) for i in range(2)]
        zs = sb(g, es, "zs", [128, 4, 512], BF16, nparts=4)
        t = sb(g, es, "t", [128, 4, 512], F32, nparts=4)
        jk = sb(g, es, "jk", [128, 512], BF16)
        st2 = sb(g, es, "st2", [128, 16], F32)
        ysn = sb(g, es, "ysn", [128, 4, 512], BF16, nparts=4)
        ysnT = sb(g, es, "ysnT", [128, 16, 128], BF16)
        sgs = sb(g, es, "sgs", [128, 2, 512], F32, nparts=2)
        ms = [sb(g, es, "ms%d" % i, [128, 1024], BF16) for i in range(2)]
        prep_a(g, c, 0, g.xext[0:128, :])
        prep_b(g, c, 0, 0, 1)
        for ci in range(NEXT):
            hT = c.hTe[ci % 3]
            y = yt[ci % 2]
            P.dma("sp", y[:], g.YT[ci], writes=[y])
            if ci + 1 < NEXT:
                prep_a(g, c, ci + 1, g.xext[(ci + 1) * 128:(ci + 2) * 128, :])
            for gi in range(4):
                def mm(e, gi=gi):
                    ins = None
                    for kb in range(8):
                        ins = e.matmul(pz[gi][:], lhsT=hT[:, kb, 1:129], rhs=Wz[:, kb, gi * 512:(gi + 1) * 512],
                                       start=(kb == 0), stop=(kb == 7))
                    return ins
                P.op("pe", mm, [hT, Wz], [pz[gi]])
            for gi in range(4):
                P.op("act", lambda e, gi=gi: e.activation(out=zs[:, gi, :], in_=pz[gi][:], func=AF.Silu),
                     [pz[gi]], [(zs, gi)])
            for gi in range(4):
                P.op("dve", lambda e, gi=gi: e.tensor_tensor(out=t[:, gi, :], in0=zs[:, gi, :],
                                                              in1=y[:, gi * 512:(gi + 1) * 512], op=OP.mult),
                     [(zs, gi), y], [(t, gi)])
            for gi in range(4):
                P.op("act", lambda e, gi=gi: e.activation(out=jk[:], in_=t[:, gi, :], func=AF.Square,
                                                          accum_out=st2[:, gi:gi + 1]), [(t, gi)], [jk, st2])
            P.op("dve", lambda e: e.tensor_scalar(out=st2[:, 4:8], in0=st2[:, 0:4], scalar1=1.0 / 512, scalar2=EPS,
                                                   op0=OP.mult, op1=OP.add), [st2], [st2])
            P.op("act", lambda e: e.activation(out=st2[:, 8:12], in_=st2[:, 4:8], func=AF.Ln), [st2], [st2])
            P.op("act", lambda e: e.activation(out=st2[:, 12:16], in_=st2[:, 8:12], func=AF.Exp, scale=-0.5), [st2], [st2])
            for gi in range(4):
                P.op("dve", lambda e, gi=gi: e.scalar_tensor_tensor(
                    out=ysn[:, gi, :], in0=t[:, gi, :], scalar=st2[:, 12 + gi:13 + gi], in1=sg[:, gi * 512:(gi + 1) * 512],
                    op0=OP.mult, op1=OP.mult), [(t, gi), st2, sg], [(ysn, gi)])
            for rnd in range(2):
                def tr(e, rnd=rnd):
                    ins = None
                    for j in range(8):
                        blk = rnd * 8 + j
                        ins = e.transpose(out=c.pA[:, j, :], in_=ysn[:, blk // 4, (blk % 4) * 128:(blk % 4 + 1) * 128],
                                          identity=mk16(g, "ident"))
                    return ins
                P.op("pe", tr, [ysn, g.cmkb_t], [c.pA])
                P.op("dve", lambda e, rnd=rnd: e.tensor_copy(out=ysnT[:, rnd * 8:(rnd + 1) * 8, :], in_=c.pA[:]),
                     [c.pA], [ysnT])
            if ci + 1 < NEXT:
                prep_b(g, c, ci + 1, 0, 1)
            for hf in range(2):
                def mmg(e, hf=hf):
                    ins = None
                    for kb in range(8):
                        ins = e.matmul(pz[hf][:], lhsT=hT[:, kb, 1:129], rhs=Wgs[:, kb, hf * 512:(hf + 1) * 512],
                                       start=(kb == 0), stop=(kb == 7))
                    return ins
                P.op("pe", mmg, [hT, Wgs], [pz[hf]])

                def mms(e, hf=hf):
                    ins = None
                    for kb in range(16):
                        ins = e.matmul(pbs[hf][:], lhsT=ysnT[:, kb, :], rhs=Wsb[:, kb, hf * 512:(hf + 1) * 512],
                                       start=(kb == 0), stop=(kb == 15))
                    return ins
                P.op("pe", mms, [ysnT, Wsb], [pbs[hf]])
            m = ms[ci % 2]
            for hf in range(2):
                P.op("act", lambda e, hf=hf: e.activation(out=sgs[:, hf, :], in_=pz[hf][:], func=AF.Sigmoid),
                     [pz[hf]], [(sgs, hf)])
                P.op("dve", lambda e, m=m, hf=hf: e.tensor_tensor(out=m[:, hf * 512:(hf + 1) * 512], in0=sgs[:, hf, :],
                                                                   in1=pbs[hf][:], op=OP.mult),
                     [(sgs, hf), pbs[hf]], [m])
            P.dma("sp", g.MS[ci], m[:], reads=[m])
        P.barrier()


def phase_merge_b(g):
    nc, P = g.nc, g.P
    with ExitStack() as es:
        c = alloc_chunk_bufs(g, es, 0)
        Wgf = sb(g, es, "Wgf", [128, 8, 1024], BF16)
        Wfa = sb(g, es, "Wfa", [128, 8, 1024], BF16)
        Wo = sb(g, es, "Wo", [128, 8, 1024], BF16)
        Tcs = sb(g, es, "Tcs", [128, 256], BF16)
        load_w_cols(g, Wgf, 6208, 1024)
        load_w_gen(g, Wfa, g.w_fa, 8, 1024)
        load_w_gen(g, Wo, g.w_o, 8, 1024)
        P.dma("pool", Tcs[:], g.tcs, writes=[Tcs])
        pb0 = ps(g, es, "pb0", [128, 2, 512], F32)
        pb1 = ps(g, es, "pb1", [128, 2, 512], F32)
        pb2 = ps(g, es, "pb2", [128, 2, 512], F32)
        xtc = [sb(g, es, "xtc%d" % i, [128, 8, 2, 128], BF16) for i in range(2)]
        msl = [sb(g, es, "msl%d" % i, [128, 1024], BF16) for i in range(2)]
        mixT = sb(g, es, "mixT", [128, 8, 128], BF16)
        sgf = sb(g, es, "sgf", [128, 1024], F32)
        tmp = sb(g, es, "tmpm", [128, 1024], F32)
        mrg = sb(g, es, "mrg", [128, 1024], BF16)
        mrgT = sb(g, es, "mrgT", [128, 8, 128], BF16)
        l1 = [sb(g, es, "l1_%d" % i, [128, 1024], F32) for i in range(2)]
        xtv = g.XT.rearrange("p (c r t) -> p c r t", c=8, r=2)
        prep_a(g, c, 0, g.xext[0:128, :])
        prep_b(g, c, 0, 0, 1)
        for ci in range(NEXT):
            hT = c.hTe[ci % 3]
            xt = c.xt[ci % 2]
            if ci + 1 < NEXT:
                prep_a(g, c, ci + 1, g.xext[(ci + 1) * 128:(ci + 2) * 128, :])
            xc_, m = xtc[ci % 2], msl[ci % 2]
            P.dma("sp", xc_[:], xtv[:, :, :, ci * 128:(ci + 1) * 128], writes=[xc_])
            P.dma("sp", m[:], g.MS[ci], writes=[m])
            for cg in range(2):
                def mmx(e, cg=cg):
                    e.matmul(pb2[:, cg, :], lhsT=Tcs[:, 0:128], rhs=xc_[:, cg * 4:(cg + 1) * 4, 0, :], start=True, stop=False)
                    return e.matmul(pb2[:, cg, :], lhsT=Tcs[:, 128:256], rhs=xc_[:, cg * 4:(cg + 1) * 4, 1, :],
                                    start=False, stop=True)
                P.op("pe", mmx, [Tcs, xc_], [pb2])
            P.op("act", lambda e: e.activation(out=mixT[:].rearrange("p a b -> p (a b)"),
                                               in_=pb2[:].rearrange("p a b -> p (a b)"), func=AF.Copy), [pb2], [mixT])
            for hf in range(2):
                def mmf(e, hf=hf):
                    ins = None
                    for kb in range(8):
                        ins = e.matmul(pb0[:, hf, :], lhsT=mixT[:, kb, :], rhs=Wfa[:, kb, hf * 512:(hf + 1) * 512],
                                       start=(kb == 0), stop=(kb == 7))
                    return ins
                P.op("pe", mmf, [mixT, Wfa], [pb0])

                def mmg(e, hf=hf):
                    ins = None
                    for kb in range(8):
                        ins = e.matmul(pb1[:, hf, :], lhsT=hT[:, kb, 1:129], rhs=Wgf[:, kb, hf * 512:(hf + 1) * 512],
                                       start=(kb == 0), stop=(kb == 7))
                    return ins
                P.op("pe", mmg, [hT, Wgf], [pb1])
            P.op("act", lambda e: e.activation(out=sgf[:], in_=pb1[:].rearrange("p a b -> p (a b)"), func=AF.Sigmoid),
                 [pb1], [sgf])
            P.op("dve", lambda e: e.tensor_tensor(out=tmp[:], in0=sgf[:], in1=pb0[:].rearrange("p a b -> p (a b)"),
                                                   op=OP.mult), [sgf, pb0], [tmp])
            P.op("dve", lambda e, m=m: e.tensor_tensor(out=mrg[:], in0=tmp[:], in1=m[:], op=OP.add), [tmp, m], [mrg])

            def tr(e):
                ins = None
                for kb in range(8):
                    ins = e.transpose(out=c.pA[:, kb, :], in_=mrg[:, kb * 128:(kb + 1) * 128], identity=mk16(g, "ident"))
                return ins
            P.op("pe", tr, [mrg, g.cmkb_t], [c.pA])
            P.op("act", lambda e: e.activation(out=mrgT[:], in_=c.pA[:], func=AF.Copy), [c.pA], [mrgT])
            if ci + 1 < NEXT:
                prep_b(g, c, ci + 1, 0, 1)
            for hf in range(2):
                def mmo(e, hf=hf):
                    ins = None
                    for kb in range(8):
                        ins = e.matmul(pb2[:, hf, :], lhsT=mrgT[:, kb, :], rhs=Wo[:, kb, hf * 512:(hf + 1) * 512],
                                       start=(kb == 0), stop=(kb == 7))
                    return ins
                P.op("pe", mmo, [mrgT, Wo], [pb2])
            l = l1[ci % 2]
            P.op("dve", lambda e, l=l: e.tensor_tensor(out=l[:], in0=pb2[:].rearrange("p a b -> p (a b)"),
                                                        in1=g.gbc[:, 0:D], op=OP.mult), [pb2, g.gbc], [l])
            P.op("pool", lambda e, l=l, xt=xt: e.tensor_tensor(out=l[:], in0=l[:], in1=xt[:], op=OP.add), [l, xt], [l])
            P.dma("sp", g.L1[ci], l[:], reads=[l])
        P.barrier()


def phase_ffn(g):
    nc, P = g.nc, g.P
    with ExitStack() as es:
        c = alloc_chunk_bufs(g, es, 0)
        h2T = sb(g, es, "h2T", [128, 8, EXT], BF16, nparts=NEXT)
        Wd = sb(g, es, "Wd", [128, NFB, 1024], BF16)
        load_w_gen(g, Wd, g.w_down, NFB, 1024)
        fg = sb(g, es, "fg", [128, 1024], F32)
        P.dma("sp", fg[:], g.cbg[:, CBG["final_g"][0]:CBG["final_g"][0] + 1024], writes=[fg])
        for ci in range(NEXT):
            i2 = ci % 2
            xt, st, xn = c.xt[i2], c.st[i2], c.xn[i2]
            P.dma("sp", xt[:], g.L1[ci], writes=[xt])
            P.op("act", lambda e: e.activation(out=c.junk[:], in_=xt[:], func=AF.Square, accum_out=st[:, 0:1]),
                 [xt], [c.junk, st])
            P.op("dve", lambda e: e.tensor_scalar(out=st[:, 1:2], in0=st[:, 0:1], scalar1=1.0 / D, scalar2=EPS,
                                                   op0=OP.mult, op1=OP.add), [st], [st])
            P.op("act", lambda e: e.activation(out=st[:, 2:3], in_=st[:, 1:2], func=AF.Sqrt), [st], [st])
            P.op("dve", lambda e: e.reciprocal(out=st[:, 3:4], in_=st[:, 2:3]), [st], [st])
            P.op("act", lambda e: e.activation(out=xn[:], in_=xt[:], func=AF.Copy, scale=st[:, 3:4]), [xt, st], [xn])

            def tr(e):
                ins = None
                for kb in range(8):
                    ins = e.transpose(out=c.pA[:, kb, :], in_=xn[:, kb * 128:(kb + 1) * 128], identity=mk16(g, "ident"))
                return ins
            P.op("pe", tr, [xn, g.cmkb_t], [c.pA])
            for kb in range(8):
                P.op("dve", lambda e, kb=kb, ci=ci: e.tensor_scalar(
                    out=h2T[:, kb, ci * 128:(ci + 1) * 128], in0=c.pA[:, kb, :], scalar1=modA(g, 4, kb),
                    scalar2=modA(g, 5, kb), op0=OP.mult, op1=OP.add), [c.pA, g.modv], [(h2T, ci)])
            if ci in (0, NEXT - 1):
                vm = cbcv(g, "emask_bc", 0 if ci == 0 else 128, 128)
                P.op("pool", lambda e, ci=ci, vm=vm: e.tensor_tensor(
                    out=h2T[:, :, ci * 128:(ci + 1) * 128], in0=h2T[:, :, ci * 128:(ci + 1) * 128],
                    in1=bc(vm.unsqueeze(1), [128, 8, 128]), op=OP.mult), [(h2T, ci), g.cbc_t], [(h2T, ci)])
        NB = 4
        pu = [ps(g, es, "pu%d" % i, [128, 512], F32) for i in range(2)]
        pd = ps(g, es, "pd", [128, 2, 512], F32)
        aT = sb(g, es, "aT", [128, NFB, 512], BF16, nparts=NFB)
        wu = [sb(g, es, "wu%d" % i, [128, 8, 2, 128], BF16) for i in range(3)]
        ug = [sb(g, es, "ug%d" % i, [128, 10, 64], BF16) for i in range(2)]
        dg = [sb(g, es, "dg%d" % i, [128, 4, 128], BF16) for i in range(4)]
        pcv = [ps(g, es, "pcv%d" % i, [128, 512], F32) for i in range(2)]
        acc = [sb(g, es, "acc%d" % i, [128, 8, 64], F32) for i in range(2)]
        sgl = sb(g, es, "sgl", [128, 512], F32)
        lt = [sb(g, es, "lt%d" % i, [128, 1024], F32) for i in range(2)]
        yy = [sb(g, es, "yy%d" % i, [128, 1024], F32) for i in range(2)]
        jk = c.junk
        st = [sb(g, es, "stf%d" % i, [128, 4], F32) for i in range(2)]
        wuv = g.w_up.rearrange("(kb p) (gv n) -> p kb gv n", p=128, gv=2)
        l1f = g.L1.rearrange("c p d -> (c p) d")
        cnt = 0
        nitem = NB * NFB

        def issue_w(i):
            if i < nitem:
                fb_ = i % NFB
                w_ = wu[i % 3]
                for gv_ in range(2):
                    P.dma("pool", w_[:, :, gv_, :], wuv[:, :, gv_, fb_ * 128:(fb_ + 1) * 128], writes=[w_])
        issue_w(0)
        issue_w(1)
        for blk in range(NB):
            base = blk * 512
            hparts = [(h2T, i) for i in range(base // 128, (base + 640 + 127) // 128)]
            for fb in range(NFB):
                w = wu[cnt % 3]
                issue_w(cnt + 2)
                cnt += 1
                for gv in range(2):
                    u = ug[gv]
                    a = acc[gv]
                    for j in range(2):
                        def mm(e, j=j, gv=gv, w=w):
                            ins = None
                            for kb in range(8):
                                ins = e.matmul(pu[j][:, 0:320], lhsT=w[:, kb, gv, :],
                                               rhs=h2T[:, kb, base + j * 320:base + (j + 1) * 320],
                                               start=(kb == 0), stop=(kb == 7))
                            return ins
                        P.op("pe", mm, [w] + hparts, [pu[j]])
                        P.op("act", lambda e, j=j, u=u: e.activation(
                            out=u[:].rearrange("p r c -> p (r c)")[:, j * 320:(j + 1) * 320], in_=pu[j][:, 0:320],
                            func=AF.Copy), [pu[j]], [u])
                    cf = gv * NFB + fb
                    wt = lambda t: cpp(g, "cw_ffn", t * 44 + cf)
                    P.op("act", lambda e, u=u, a=a, cf=cf: e.activation(
                        out=a[:], in_=u[:, 1:9, :], func=AF.Identity, scale=cpp(g, "cw_ffn", 4 * 44 + cf),
                        bias=cpp(g, "cb_ffn", cf)), [u, g.cpp_t], [a])
                    dgt = dg[(cnt * 2 + gv) % 4]
                    for i_, t_ in enumerate((1, 7, 3, 5)):
                        P.op("pool", lambda e, i_=i_, t_=t_, dgt=dgt, cf=cf: e.tensor_scalar(
                            out=dgt[:, i_, :], in0=mk16(g, "ident"), scalar1=cpp(g, "cw_ffn", t_ * 44 + cf), scalar2=0.0,
                            op0=OP.mult, op1=OP.add), [g.cmkb_t, g.cpp_t], [dgt])
                    pc = pcv[gv]
                    pc3 = pc[:].rearrange("p (r c) -> p r c", c=64)

                    def mcv(e, u=u, dgt=dgt, pc3=pc3):
                        e.matmul(pc3, lhsT=dgt[:, 0, :], rhs=u[:, 0:8, :], start=True, stop=False)
                        e.matmul(pc3, lhsT=dgt[:, 1, :], rhs=u[:, 2:10, :], start=False, stop=False)
                        e.matmul(pc3[:, :, 1:64], lhsT=dgt[:, 2, :], rhs=u[:, 1:9, 0:63], start=False, stop=False)
                        return e.matmul(pc3[:, :, 0:63], lhsT=dgt[:, 3, :], rhs=u[:, 1:9, 1:64], start=False, stop=True)
                    P.op("pe", mcv, [u, dgt], [pc])
                    for (kh, kw) in ((0, 0), (0, 2), (2, 0), (2, 2)):
                        dy, dx = kh - 1, kw - 1
                        c0, c1 = max(0, -dx), 64 - max(0, dx)
                        P.op("dve", lambda e, u=u, a=a, dy=dy, dx=dx, c0=c0, c1=c1, t=kh * 3 + kw, cf=cf:
                             e.scalar_tensor_tensor(out=a[:, :, c0:c1], in0=u[:, 1 + dy:9 + dy, c0 + dx:c1 + dx],
                                                    scalar=cpp(g, "cw_ffn", t * 44 + cf), in1=a[:, :, c0:c1],
                                                    op0=OP.mult, op1=OP.add), [u, a, g.cpp_t], [a])
                    P.op("dve", lambda e, a=a, pc3=pc3: e.tensor_tensor(out=a[:], in0=a[:], in1=pc3, op=OP.add),
                         [a, pc], [a])
                P.op("act", lambda e: e.activation(out=sgl[:], in_=acc[0][:].rearrange("p r c -> p (r c)"), func=AF.Silu),
                     [acc[0]], [sgl])
                P.op("dve", lambda e, fb=fb: e.tensor_tensor(out=aT[:, fb, :], in0=sgl[:],
                                                              in1=acc[1][:].rearrange("p r c -> p (r c)"), op=OP.mult),
                     [sgl, acc[1]], [(aT, fb)])
            for tcn in range(4):
                o0 = blk * 512 + tcn * 128
                i2 = (blk * 4 + tcn) % 2
                l, y, s4 = lt[i2], yy[i2], st[i2]
                o = y
                P.dma("sp", l[:], l1f[o0 + 64:o0 + 64 + 128, :], writes=[l])
                for hf in range(2):
                    def mmd(e, hf=hf, tcn=tcn):
                        ins = None
                        for fb in range(NFB):
                            ins = e.matmul(pd[:, hf, :], lhsT=aT[:, fb, tcn * 128:(tcn + 1) * 128],
                                           rhs=Wd[:, fb, hf * 512:(hf + 1) * 512], start=(fb == 0), stop=(fb == NFB - 1))
                        return ins
                    P.op("pe", mmd, [aT, Wd], [pd])
                P.op("dve", lambda e, y=y: e.tensor_tensor(out=y[:], in0=pd[:].rearrange("p a b -> p (a b)"),
                                                            in1=g.gbc[:, D:2 * D], op=OP.mult), [pd, g.gbc], [y])
                P.op("pool", lambda e, y=y, l=l: e.tensor_tensor(out=y[:], in0=y[:], in1=l[:], op=OP.add), [y, l], [y])
                P.op("act", lambda e, y=y, s4=s4: e.activation(out=jk[:], in_=y[:], func=AF.Square,
                                                               accum_out=s4[:, 0:1]), [y], [jk, s4])
                P.op("dve", lambda e, s4=s4: e.tensor_scalar(out=s4[:, 1:2], in0=s4[:, 0:1], scalar1=1.0 / D,
                                                              scalar2=EPS, op0=OP.mult, op1=OP.add), [s4], [s4])
                P.op("act", lambda e, s4=s4: e.activation(out=s4[:, 2:3], in_=s4[:, 1:2], func=AF.Sqrt), [s4], [s4])
                P.op("dve", lambda e, s4=s4: e.reciprocal(out=s4[:, 3:4], in_=s4[:, 2:3]), [s4], [s4])
                P.op("dve", lambda e, y=y, s4=s4, o=o: e.scalar_tensor_tensor(
                    out=o[:], in0=y[:], scalar=s4[:, 3:4], in1=fg[:], op0=OP.mult, op1=OP.mult), [y, s4, fg], [y])
                P.dma("sp", g.out[o0:o0 + 128, :], o[:], reads=[o])
        P.barrier()


def _pm(v):
    v = np.asarray(v, np.float32)
    return np.ascontiguousarray(v.reshape(-1, 128).T)


def _rb(v):
    v = np.asarray(v, np.float32).reshape(1, -1)
    return np.ascontiguousarray(np.broadcast_to(v, (128, v.shape[1])))


def _const_tables():
    k = np.arange(128)[:, None]
    m = np.arange(128)[None, :]
    mats = [np.ones((128, 128)), k <= m, k >= m, k > m, k < m, k == m,
            np.where(m < k, -BIG, 0.0), np.where(m > k, -BIG, 0.0)]
    cmk = np.concatenate([np.asarray(a, np.float32) for a in mats], axis=1)
    t1i = np.arange(64)[:, None] * np.arange(64)[None, :]
    th = 2 * np.pi * t1i / 64.0
    t1 = np.concatenate([np.cos(th), -np.sin(th)], axis=1).astype(np.float32)
    j = np.arange(128)[:, None] * np.arange(128)[None, :]
    thc = 2 * np.pi * j / 128.0
    tcs = (np.concatenate([np.cos(thc), np.sin(thc)], axis=1) / 1024.0).astype(np.float32)
    return cmk, t1, tcs


def _t2_tables(q):
    t2 = np.arange(128, dtype=np.float64)[:, None, None]
    k1 = np.arange(64, dtype=np.float64)[None, :, None]
    k2 = (32 * q - 1 + np.arange(34, dtype=np.float64))[None, None, :]
    kk = np.mod(k1 + 64 * k2, 8192)
    th = 2 * np.pi * np.mod(kk * t2, 8192) / 8192.0
    Mr, Mi = np.cos(th), -np.sin(th)
    ta = np.concatenate([Mr, Mi], axis=2)
    tb = np.concatenate([-Mi, Mr], axis=2)
    return np.concatenate([ta.reshape(128, -1), tb.reshape(128, -1)], axis=1).astype(np.float32)


_CACHE = {}


def kernel(x, c, ctx, c_ctx, w_mod, b_mod, norm1_g, w_in, conv_ssd_w, conv_ssd_b, dt_bias, a_log,
           d_skip, ssd_norm_g, w_fa, w_sb, w_o, norm2_g, w_up, conv_ffn_w, conv_ffn_b, w_down, final_g):
    f = lambda a: np.asarray(a, np.float32)
    x, c, ctx, c_ctx = f(x), f(c), f(ctx), f(c_ctx)
    cmk, t1, tcs = _const_tables()
    in_maps = []
    bm = f(b_mod)[0]
    for core in range(8):
        b, q = divmod(core, 4)
        e0 = 2048 * q - 64
        xext = np.zeros((EXT + 128, D), np.float32)
        lo, hi = max(e0, 0), min(e0 + EXT, SEQ)
        xext[lo - e0:hi - e0] = x[b, lo:hi]
        hv = np.zeros(2, np.float32)
        if e0 - 1 >= 0:
            xext[EXT] = x[b, e0 - 1]; hv[0] = 1
        if e0 + EXT < SEQ:
            xext[EXT + 1] = x[b, e0 + EXT]; hv[1] = 1
        tok = e0 + np.arange(EXT)
        valid = ((tok >= 0) & (tok < SEQ)).astype(np.float32)
        fmask = np.zeros((NSLOT, 128, 2), np.float32)
        fmask[0:2, :, 0] = 1
        fmask[66:68, :, 1] = 1
        lt = np.arange(SEQ).reshape(NCH, 128)
        fmask[2:66, :, 0] = (lt < e0)
        fmask[2:66, :, 1] = (lt >= e0 + EXT)
        cpp_a = np.zeros((128, CPP_N), np.float32)

        def put(key, arr):
            o, n = CPP[key]
            cpp_a[:, o:o + n] = arr
        put("c", _pm(c[b])); put("cctx", _pm(c_ctx)); put("bmod", _pm(bm))
        put("n1g", _pm(f(norm1_g)[0])); put("n2g", _pm(f(norm2_g)[0]))
        put("cw_ssd", np.concatenate([_pm(f(conv_ssd_w)[0, t]) for t in range(3)], axis=1))
        put("cb_ssd", _pm(f(conv_ssd_b)[0]))
        cfw = f(conv_ffn_w)[0].reshape(9, 2 * DFF)
        put("cw_ffn", np.concatenate([_pm(cfw[t]) for t in range(9)], axis=1))
        put("cb_ffn", _pm(f(conv_ffn_b)[0]))
        put("emask", valid.reshape(NEXT, 128).T)
        put("fmask", fmask.transpose(1, 0, 2).reshape(128, NSLOT * 2))
        cbc_a = np.zeros((128, CBC_N), np.float32)

        def putb(key, arr):
            o, n = CBC[key]
            cbc_a[:, o:o + n] = arr
        putb("dt_bias", _rb(f(dt_bias)[0].reshape(-1))); putb("a_log", _rb(f(a_log)[0].reshape(-1)))
        putb("d_skip", _rb(f(d_skip)[0]))
        cbg_a = np.concatenate([_rb(f(ssd_norm_g)[0]), _rb(f(final_g)), _rb(bm[2048:3072]), _rb(bm[5120:6144])], axis=1)
        putb("emask_bc", _rb(np.concatenate([valid[:128], valid[-128:]])))
        hvb = np.zeros(128, np.float32); hvb[0:2] = hv
        putb("halo_v", _rb(hvb))
        in_maps.append(dict(
            xb=np.ascontiguousarray(x[b]), ctxb=np.ascontiguousarray(ctx[b]), xext=xext,
            w_mod=f(w_mod)[0], w_in=f(w_in)[0], w_fa=f(w_fa)[0], w_sb=f(w_sb)[0], w_o=f(w_o)[0],
            w_up=f(w_up)[0], w_down=f(w_down)[0], cpp=cpp_a, cbc=cbc_a, cbg=cbg_a, cbrow=f(conv_ssd_b)[0].reshape(1, 3072).copy(), cmk=cmk, t1=t1, t2=_t2_tables(q), tcs=tcs))
    if "nc" not in _CACHE:
        _CACHE["nc"] = build_program()
    res = run_bass_kernel_spmd(_CACHE["nc"], in_maps, core_ids=list(range(8)))
    if DEBUG:
        _CACHE["res"] = res
    out = np.zeros((2, SEQ, D), np.float32)
    for core in range(8):
        b, q = divmod(core, 4)
        out[b, 2048 * q:2048 * (q + 1)] = res.results[core]["out"]
    return out
```

```python
import os
from contextlib import ExitStack
import numpy as np
import concourse.bass as bass
import concourse.mybir as mybir
from concourse.bass_utils import run_bass_kernel_spmd

F32 = mybir.dt.float32
BF16 = mybir.dt.bfloat16
AF = mybir.ActivationFunctionType
OP = mybir.AluOpType

D = 1024
SEQ = 8192
NCH = 64
NEXT = 17
EXT = NEXT * 128
EPS = 1e-6
BIG = 30000.0
NSLOT = 68
DFF = 2816
NFB = 22

STOP = os.environ.get("MK_STOP", "")
DEBUG = bool(STOP)


class Buf:
    def __init__(self, t, nparts=1):
        self.t = t
        self.n = nparts
        self.w = [None] * nparts
        self.r = [[] for _ in range(nparts)]
        self.excl = False

    def __getitem__(self, idx):
        return self.t[idx]


def _parts(items):
    out = []
    for it in items:
        if it is None:
            continue
        if isinstance(it, Buf):
            out.extend((it, i) for i in range(it.n))
        else:
            b, idx = it
            if isinstance(idx, int):
                out.append((b, idx))
            else:
                out.extend((b, i) for i in idx)
    return out


class Prog:
    def __init__(self, nc, es):
        self.nc = nc
        self.E = {}
        self.semid = 0
        for name, eng in (("pe", nc.tensor), ("act", nc.scalar), ("dve", nc.vector), ("pool", nc.gpsimd), ("sp", nc.sync)):
            sem = es.enter_context(nc.semaphore("s_" + name))
            self.E[name] = dict(name=name, eng=eng, sem=(self._sid(), sem), count=0, waited={}, pool=[], ndma=0)
        for name, n in (("sp", 8), ("pool", 6), ("act", 4)):
            for i in range(n):
                sem = es.enter_context(nc.semaphore("d_%s%d" % (name, i)))
                self.E[name]["pool"].append((self._sid(), sem))
        self.ninst = 0

    def _sid(self):
        self.semid += 1
        return self.semid

    def _wait(self, E, tok):
        (sid, sem), val, _ = tok
        if E["waited"].get(sid, 0) >= val:
            return
        E["eng"].wait_ge(sem, val)
        E["waited"][sid] = val

    def _collect(self, en, reads, writes):
        toks = []
        for b, i in _parts(reads):
            if b.w[i] is not None:
                toks.append(b.w[i])
            if b.excl:
                toks.extend(t for t in b.r[i] if t[2] != en)
        for b, i in _parts(writes):
            if b.w[i] is not None:
                toks.append(b.w[i])
            toks.extend(b.r[i])
        res = []
        for t in toks:
            if en == "pe" and t[2] == "pe":
                continue
            res.append(t)
        return res

    def _update(self, reads, writes, tok):
        for b, i in _parts(reads):
            b.r[i].append(tok)
            if len(b.r[i]) > 24:
                last = {}
                for t in b.r[i]:
                    k = t[0][0]
                    if k not in last or last[k][1] < t[1]:
                        last[k] = t
                b.r[i] = list(last.values())
        for b, i in _parts(writes):
            b.w[i] = tok
            b.r[i] = []

    def op(self, en, fn, reads=(), writes=()):
        E = self.E[en]
        for t in self._collect(en, reads, writes):
            self._wait(E, t)
        ins = fn(E["eng"])
        E["count"] += 1
        ins.then_inc(E["sem"][1], 1)
        tok = (E["sem"], E["count"], en)
        self._update(reads, writes, tok)
        self.ninst += 1
        return tok

    def dma(self, qn, out, in_, reads=(), writes=(), **kw):
        Q = self.E[qn]
        i = Q["ndma"]
        P = len(Q["pool"])
        sem = Q["pool"][i % P]
        val = 16 * (i // P + 1)
        if i >= P:
            self._wait(Q, (sem, val - 16, "dma"))
        for t in self._collect("dma", reads, writes):
            self._wait(Q, t)
        Q["eng"].dma_start(out=out, in_=in_, **kw).then_inc(sem[1], 16)
        Q["ndma"] += 1
        tok = (sem, val, "dma")
        self._update(reads, writes, tok)
        return tok

    def all_tokens(self):
        toks = []
        for E in self.E.values():
            if E["count"]:
                toks.append((E["sem"], E["count"], E["name"]))
            P = len(E["pool"])
            for j in range(min(P, E["ndma"])):
                n = (E["ndma"] - 1 - j) // P + 1
                toks.append((E["pool"][j], 16 * n, "dma"))
        return toks

    def barrier(self):
        toks = self.all_tokens()
        for E in self.E.values():
            for t in toks:
                self._wait(E, t)


def bc(ap, shape):
    return ap.broadcast_to(shape)


class Ctx:
    pass


def build_program():
    nc = bass.Bass("TRN2", target_bir_lowering=False)
    g = Ctx()
    g.nc = nc

    def din(name, shape, dt=F32):
        return nc.dram_tensor(name, list(shape), dt, kind="ExternalInput").ap()

    def dscr(name, shape, dt):
        kind = "ExternalOutput" if DEBUG else "Internal"
        return nc.dram_tensor(name, list(shape), dt, kind=kind).ap()

    g.xb = din("xb", [SEQ, D])
    g.ctxb = din("ctxb", [256, D])
    g.xext = din("xext", [EXT + 128, D])
    g.w_mod = din("w_mod", [D, 6 * D])
    g.w_in = din("w_in", [D, 8256])
    g.w_fa = din("w_fa", [D, D])
    g.w_sb = din("w_sb", [2048, D])
    g.w_o = din("w_o", [D, D])
    g.w_up = din("w_up", [D, 2 * DFF])
    g.w_down = din("w_down", [DFF, D])
    g.cpp = din("cpp", [128, CPP_N])
    g.cbc = din("cbc", [128, CBC_N])
    g.cbg = din("cbg", [128, CBG_N])
    g.cbrow = din("cbrow", [1, 3072])
    g.cmk = din("cmk", [128, 8 * 128])
    g.t1 = din("t1", [64, 128])
    g.t2 = din("t2", [128, 2 * 64 * 68])
    g.tcs = din("tcs", [128, 256])
    g.out = nc.dram_tensor("out", [2048, D], F32, kind="ExternalOutput").ap()
    g.U = dscr("U", [NCH, 128, D], BF16)
    g.Y = dscr("Y", [128, 128, D], BF16)
    g.XT = dscr("XT", [128, 8 * 2 * EXT], BF16)
    g.MS = dscr("MS", [NEXT, 128, D], BF16)
    g.YF = dscr("YF", [NEXT, 128, 2048], BF16)
    g.YT = dscr("YT", [NEXT, 128, 2048], BF16)
    g.L1 = dscr("L1", [NEXT, 128, D], F32)
    g.SFB = dscr("SFB", [2, 128, 2048], F32)

    with ExitStack() as es:
        P = Prog(nc, es)
        g.P = P
        g.uid = 0
        phase_setup(g, es)
        with ExitStack() as es2:
            alloc_conv_consts(g, es2)
            if STOP != "setup":
                phase_far(g)
            if STOP not in ("setup", "far"):
                phase_fnet(g)
            if STOP not in ("setup", "far", "fnet"):
                phase_own(g, 0)
                phase_own(g, 1)
            P.barrier()
        if STOP not in ("setup", "far", "fnet", "own"):
            phase_merge(g)
        if STOP not in ("setup", "far", "fnet", "own", "merge"):
            phase_ffn(g)
        P.barrier()
    return nc


def sb(g, es, name, shape, dt, nparts=1):
    g.uid += 1
    t = es.enter_context(g.nc.sbuf_tensor("%s_%d" % (name, g.uid), list(shape), dt))
    return Buf(t, nparts)


def ps(g, es, name, shape, dt=F32, nparts=1):
    g.uid += 1
    t = es.enter_context(g.nc.psum_tensor("%s_%d" % (name, g.uid), list(shape), dt))
    b = Buf(t, nparts)
    b.excl = True
    return b


def _layout(items):
    off = {}
    o = 0
    for k, n in items:
        off[k] = (o, n)
        o += n
    return off, o


CPP, CPP_N = _layout([("c", 8), ("cctx", 8), ("bmod", 48), ("n1g", 8), ("n2g", 8), ("cw_ssd", 72), ("cb_ssd", 24),
                      ("cw_ffn", 9 * 44), ("cb_ffn", 44), ("emask", NEXT), ("fmask", NSLOT * 2)])
CBC, CBC_N = _layout([("dt_bias", 64), ("a_log", 64), ("d_skip", 32), ("emask_bc", 256), ("halo_v", 128)])
CBG, CBG_N = _layout([("ssd_g", 2048), ("final_g", 1024), ("bmod_g1", 1024), ("bmod_g2", 1024)])
MK = {k: i for i, k in enumerate(["ones", "le", "ge", "gt", "lt", "ident", "pen_f", "pen_b"])}


def cpp(g, key, j=None, n=1):
    o, _ = CPP[key]
    if j is None:
        return g.cpp_t[:, o:o + CPP[key][1]]
    return g.cpp_t[:, o + j:o + j + n]


def cbcv(g, key, a=0, n=None):
    o, m = CBC[key]
    if n is None:
        n = m
    return g.cbc_t[:, o + a:o + a + n]


def mk32(g, key):
    i = MK[key]
    return g.cmk_t[:, i * 128:(i + 1) * 128]


def mk16(g, key):
    i = MK[key]
    return g.cmkb_t[:, i * 128:(i + 1) * 128]


def phase_setup(g, es):
    nc, P = g.nc, g.P
    g.cpp_t = sb(g, es, "cpp", [128, CPP_N], F32)
    g.cbc_t = sb(g, es, "cbc", [128, CBC_N], F32)
    g.cmk_t = sb(g, es, "cmk", [128, 8 * 128], F32)
    g.cmkb_t = sb(g, es, "cmkb", [128, 8 * 128], BF16)
    g.modv = sb(g, es, "modv", [128, 8 * 8], F32)
    g.gbc = sb(g, es, "gbc", [128, 2 * D], F32)
    g.nega = sb(g, es, "nega", [128, 64], F32)
    g.Sf = sb(g, es, "Sf", [128, 2048], F32)
    g.Sb = sb(g, es, "Sb", [128, 2048], F32)
    P.dma("sp", g.cpp_t[:], g.cpp, writes=[g.cpp_t])
    P.dma("sp", g.cbc_t[:], g.cbc, writes=[g.cbc_t])
    P.dma("sp", g.cmk_t[:], g.cmk, writes=[g.cmk_t])
    P.dma("pool", g.cmkb_t[:], g.cmk, writes=[g.cmkb_t])
    with ExitStack() as ls:
        sc = sb(g, ls, "sc", [128, 8, 2], F32)
        screp = sb(g, ls, "screp", [128, 8, 128], F32)
        modT = sb(g, ls, "modT", [128, 48, 2], F32)
        wm = [sb(g, ls, "wm%d" % i, [128, 8, 1024], F32) for i in range(2)]
        pm = ps(g, ls, "pm", [128, 8, 2], F32)
        pg = ps(g, ls, "pg", [128, 512], F32)
        bg = sb(g, ls, "bg", [128, 2 * D], F32)
        P.dma("sp", bg[:], g.cbg[:, CBG["bmod_g1"][0]:CBG["bmod_g1"][0] + 2 * D], writes=[bg])
        P.op("act", lambda e: e.activation(out=sc[:, :, 0], in_=cpp(g, "c"), func=AF.Silu), [g.cpp_t], [sc])
        P.op("act", lambda e: e.activation(out=sc[:, :, 1], in_=cpp(g, "cctx"), func=AF.Silu), [g.cpp_t], [sc])
        P.op("dve", lambda e: e.tensor_copy(out=screp[:], in_=bc(sc[:, :, 0:1], [128, 8, 128])), [sc], [screp])
        wv = g.w_mod.rearrange("(kb p) n -> p kb n", p=128)
        for j in range(6):
            w = wm[j % 2]
            P.dma("sp", w[:], wv[:, :, j * 1024:(j + 1) * 1024], writes=[w])
            def mm(e, w=w):
                ins = None
                for fb in range(8):
                    for kb in range(8):
                        ins = e.matmul(pm[:, fb, :], lhsT=w[:, kb, fb * 128:(fb + 1) * 128], rhs=sc[:, kb, :],
                                       start=(kb == 0), stop=(kb == 7))
                return ins
            P.op("pe", mm, [w, sc], [pm])
            bo = CPP["bmod"][0] + j * 8
            P.op("dve", lambda e, j=j, bo=bo: e.tensor_tensor(
                out=modT[:, j * 8:(j + 1) * 8, :], in0=pm[:], in1=bc(g.cpp_t[:, bo:bo + 8].unsqueeze(2), [128, 8, 2]),
                op=OP.add), [pm, g.cpp_t], [modT])
            if j in (2, 5):
                gi = 0 if j == 2 else 1
                for hf in range(2):
                    def mg(e, w=w, hf=hf):
                        ins = None
                        for kb in range(8):
                            ins = e.matmul(pg[:], lhsT=screp[:, kb, :], rhs=w[:, kb, hf * 512:(hf + 1) * 512],
                                           start=(kb == 0), stop=(kb == 7))
                        return ins
                    P.op("pe", mg, [w, screp], [pg])
                    P.op("dve", lambda e, gi=gi, hf=hf: e.tensor_tensor(
                        out=g.gbc[:, gi * D + hf * 512: gi * D + (hf + 1) * 512], in0=pg[:],
                        in1=bg[:, gi * D + hf * 512: gi * D + (hf + 1) * 512], op=OP.add), [pg, bg], [g.gbc])
        mv = g.modv
        def mkA(dst, scale_j, which, gkey):
            P.op("dve", lambda e: e.scalar_tensor_tensor(
                out=mv[:, dst * 8:(dst + 1) * 8], in0=modT[:, scale_j * 8:(scale_j + 1) * 8, which], scalar=1.0,
                in1=cpp(g, gkey), op0=OP.add, op1=OP.mult), [modT, g.cpp_t], [mv])

        def mkB(dst, shift_j, which):
            P.op("dve", lambda e: e.tensor_copy(out=mv[:, dst * 8:(dst + 1) * 8],
                                                 in_=modT[:, shift_j * 8:(shift_j + 1) * 8, which]), [modT], [mv])
        mkA(0, 1, 0, "n1g"); mkB(1, 0, 0)
        mkA(2, 1, 1, "n1g"); mkB(3, 0, 1)
        mkA(4, 4, 0, "n2g"); mkB(5, 3, 0)
        P.op("act", lambda e: e.activation(out=g.nega[:], in_=cbcv(g, "a_log"), func=AF.Exp), [g.cbc_t], [g.nega])
        P.op("dve", lambda e: e.tensor_scalar(out=g.nega[:], in0=g.nega[:], scalar1=-1.0, scalar2=None, op0=OP.mult),
             [g.nega], [g.nega])
        P.barrier()


def alloc_conv_consts(g, es):
    P = g.P
    g.diag = sb(g, es, "diag", [128, 72, 128], BF16)
    g.brow = sb(g, es, "brow", [1, 3072], BF16)
    g.Wxbc = sb(g, es, "Wxbc", [128, 8, 3072], BF16)
    load_w_cols(g, g.Wxbc, 1024, 3072)
    for a_ in range(0, 3072, 1024):
        P.dma("pool", g.brow[:, a_:a_ + 1024], g.cbrow[:, a_:a_ + 1024], writes=[g.brow])
    for i_ in range(72):
        P.op("dve", lambda e, i_=i_: e.tensor_scalar(out=g.diag[:, i_, :], in0=mk16(g, "ident"),
                                                      scalar1=cpp(g, "cw_ssd", i_), scalar2=None, op0=OP.mult),
             [g.cmkb_t, g.cpp_t], [g.diag])


def modA(g, i, kb):
    return g.modv[:, i * 8 + kb:i * 8 + kb + 1]


def alloc_chunk_bufs(g, es, nfb):
    c = Ctx()
    c.xt = [sb(g, es, "xt%d" % i, [128, D], F32) for i in range(2)]
    c.junk = sb(g, es, "junk", [128, D], BF16)
    c.st = [sb(g, es, "st%d" % i, [128, 4], F32) for i in range(2)]
    c.xn = [sb(g, es, "xn%d" % i, [128, D], BF16) for i in range(2)]
    c.hTe = [sb(g, es, "hTe%d" % i, [128, 8, 130], BF16) for i in range(3)]
    c.pA = ps(g, es, "pA", [128, 8, 128], BF16)
    c.nfb = nfb
    return c


def prep_a(g, c, k, src_rows):
    P = g.P
    i2 = k % 2
    xt, st, xn = c.xt[i2], c.st[i2], c.xn[i2]
    P.dma("sp", xt[:], src_rows, writes=[xt])
    P.op("act", lambda e: e.activation(out=c.junk[:], in_=xt[:], func=AF.Square, accum_out=st[:, 0:1]),
         [xt], [c.junk, st])
    P.op("dve", lambda e: e.tensor_scalar(out=st[:, 1:2], in0=st[:, 0:1], scalar1=1.0 / D, scalar2=EPS,
                                           op0=OP.mult, op1=OP.add), [st], [st])
    P.op("act", lambda e: e.activation(out=st[:, 2:3], in_=st[:, 1:2], func=AF.Ln), [st], [st])
    P.op("act", lambda e: e.activation(out=st[:, 3:4], in_=st[:, 2:3], func=AF.Exp, scale=-0.5), [st], [st])
    P.op("act", lambda e: e.activation(out=xn[:], in_=xt[:], func=AF.Copy, scale=st[:, 3:4]), [xt, st], [xn])
    return xt


def prep_b(g, c, k, ai, bi, vmask=None, hT=None):
    P = g.P
    xn = c.xn[k % 2]
    if hT is None:
        hT = c.hTe[k % 3]

    def tr(e):
        ins = None
        for kb in range(8):
            ins = e.transpose(out=c.pA[:, kb, :], in_=xn[:, kb * 128:(kb + 1) * 128], identity=mk16(g, "ident"))
        return ins
    P.op("pe", tr, [xn, g.cmkb_t], [c.pA])
    P.op("dve", lambda e: e.tensor_tensor(out=hT[:, :, 1:129], in0=c.pA[:],
                                          in1=bc(g.modv[:, ai * 8:(ai + 1) * 8].unsqueeze(2), [128, 8, 128]),
                                          op=OP.mult), [c.pA, g.modv], [hT])
    P.op("dve", lambda e: e.tensor_tensor(out=hT[:, :, 1:129], in0=hT[:, :, 1:129],
                                          in1=bc(g.modv[:, bi * 8:(bi + 1) * 8].unsqueeze(2), [128, 8, 128]),
                                          op=OP.add), [hT, g.modv], [hT])
    if vmask is not None:
        P.op("pool", lambda e: e.tensor_tensor(out=hT[:, :, 1:129], in0=hT[:, :, 1:129],
                                               in1=bc(vmask.unsqueeze(1), [128, 8, 128]), op=OP.mult),
             [hT, g.cbc_t], [hT])


def prep(g, c, k, src_rows, ai, bi, vmask=None, hT=None):
    xt = prep_a(g, c, k, src_rows)
    prep_b(g, c, k, ai, bi, vmask=vmask, hT=hT)
    return xt


def halo_link(g, c, k, has_left):
    P = g.P
    cur = c.hTe[k % 3]
    if has_left:
        prv = c.hTe[(k - 1) % 3]
        P.op("pool", lambda e: e.tensor_copy(out=cur[:, :, 0:1], in_=prv[:, :, 128:129]), [prv], [cur])
        P.op("pool", lambda e: e.tensor_copy(out=prv[:, :, 129:130], in_=cur[:, :, 1:2]), [cur], [prv])
    else:
        P.op("pool", lambda e: e.memset(cur[:, :, 0:1], 0.0), [], [cur])


def halo_zero_right(g, c, k):
    cur = c.hTe[k % 3]
    g.P.op("pool", lambda e: e.memset(cur[:, :, 129:130], 0.0), [], [cur])


def fm_proj_conv(g, c, s, hT, W, nfb, cw_off, steps=None):
    P = g.P
    steps = steps if steps is not None else []
    groups = [list(range(a, min(a + 3, nfb))) for a in range(0, nfb, 3)]
    def e_proj(gi):
        fbs = groups[gi]
        pa = s.pxa[gi % len(s.pxa)]

        def mm(e, fbs=fbs, pa=pa):
            ins = None
            for j, fb in enumerate(fbs):
                for kb in range(8):
                    ins = e.matmul(pa[:, j * 130:(j + 1) * 130], lhsT=W[:, kb, fb * 128:(fb + 1) * 128], rhs=hT[:, kb, :],
                                   start=(kb == 0), stop=(kb == 7))
            return ins
        P.op("pe", mm, [W, hT], [pa])

    def e_rest(gi):
        fbs = groups[gi]
        n = len(fbs)
        pa = s.pxa[gi % len(s.pxa)]
        pb = s.pxc[gi % len(s.pxc)]
        pre = s.pre[gi % 2]
        P.op("dve", lambda e, n=n, pa=pa, pre=pre: e.tensor_copy(
            out=pre[:, 0:n, :], in_=pa[:, 0:n * 130].rearrange("p (j t) -> p j t", t=130)), [pa], [pre])

        def mc(e, fbs=fbs, pb=pb, pre=pre):
            ins = None
            for j, fb in enumerate(fbs):
                cf = cw_off + fb
                for k in range(3):
                    e.matmul(pb[:, j * 128:(j + 1) * 128], lhsT=g.diag[:, k * 24 + cf, :], rhs=pre[:, j, k:k + 128],
                             start=(k == 0), stop=False)
                ins = e.matmul(pb[:, j * 128:(j + 1) * 128], lhsT=g.brow[0:1, cf * 128:(cf + 1) * 128],
                               rhs=mk16(g, "ones")[0:1, :], start=False, stop=True)
            return ins
        P.op("pe", mc, [pre, g.diag, g.brow, g.cmkb_t], [pb])
        f0, f1 = fbs[0], fbs[-1] + 1
        P.op("act", lambda e, f0=f0, f1=f1, n=n, pb=pb: e.activation(
            out=s.xcs[:, f0:f1, :], in_=pb[:, 0:n * 128].rearrange("p (j t) -> p j t", t=128), func=AF.Silu),
            [pb], [(s.xcs, gi)])

    ng = len(groups)
    ahead = len(s.pxa) >= 2
    if ahead:
        e_proj(0)
    for gi in range(ng):
        if ahead:
            if gi + 1 < ng:
                e_proj(gi + 1)
        else:
            e_proj(gi)
        e_rest(gi)
        if steps:
            steps.pop(0)()
    while steps:
        steps.pop(0)()


def to_token_major(g, c, s, nblk, between=None):
    P = g.P
    between = between if between is not None else []
    for r0 in range(0, nblk, 8):
        n = min(8, nblk - r0)

        def tr(e, r0=r0, n=n):
            ins = None
            for j in range(n):
                ins = e.transpose(out=c.pA[:, j, :], in_=s.xcs[:, r0 + j, :], identity=mk16(g, "ident"))
            return ins
        P.op("pe", tr, [s.xcs, g.cmkb_t], [c.pA])
        P.op("act", lambda e, r0=r0, n=n: e.activation(
            out=s.xtok[:, r0 * 128:(r0 + n) * 128], in_=c.pA[:, 0:n, :], func=AF.Copy), [c.pA], [s.xtok])
        if between:
            between.pop(0)()
    while between:
        between.pop(0)()


def dt_steps(g, s, hT, Wdt, ncol, bias_ap, nega_ap, mask_ap):
    P = g.P

    def s1():
        def mm(e):
            ins = None
            for kb in range(8):
                ins = e.matmul(s.pD[:, 0:ncol], lhsT=hT[:, kb, 1:129], rhs=Wdt[:, kb, 0:ncol], start=(kb == 0), stop=(kb == 7))
            return ins
        P.op("pe", mm, [hT, Wdt], [s.pD])
        P.op("dve", lambda e: e.tensor_tensor(out=s.dtm[:, 0:ncol], in0=s.pD[:, 0:ncol], in1=bias_ap, op=OP.add),
             [s.pD, g.cbc_t], [s.dtm])

    def s2():
        P.op("act", lambda e: e.activation(out=s.dtm[:, 0:ncol], in_=s.dtm[:, 0:ncol], func=AF.Exp), [s.dtm], [s.dtm])

    def s3():
        P.op("act", lambda e: e.activation(out=s.dtm[:, 0:ncol], in_=s.dtm[:, 0:ncol], func=AF.Ln, bias=1.0), [s.dtm], [s.dtm])

    def s4():
        dv = s.dtm[:, 0:ncol].rearrange("p (a b) -> p a b", b=32)
        P.op("dve", lambda e: e.tensor_tensor(out=dv, in0=dv, in1=mask_ap, op=OP.mult), [s.dtm, g.cpp_t], [s.dtm])

    def s5():
        P.op("dve", lambda e: e.tensor_tensor(out=s.la[:, 0:ncol], in0=s.dtm[:, 0:ncol], in1=nega_ap, op=OP.mult),
             [s.dtm, g.nega], [s.la])
    return [s1, s2, s3, s4, s5]


def state_contrib(g, s, wexp_ap, xdd, on_group):
    P = g.P
    P.op("dve", lambda e: e.tensor_tensor(
        out=xdd[:].rearrange("p (h d) -> p h d", d=64), in0=s.xtok[:, 0:2048].rearrange("p (h d) -> p h d", d=64),
        in1=bc(wexp_ap.unsqueeze(2), [128, 32, 64]), op=OP.mult), [s.xtok, s.wx], [xdd])
    for gi in range(4):
        P.op("pe", lambda e, gi=gi: e.matmul(s.pH[:], lhsT=s.xtok[:, 2048 + gi * 128:2048 + (gi + 1) * 128],
                                             rhs=xdd[:, gi * 512:(gi + 1) * 512], start=True, stop=True),
             [s.xtok, xdd], [s.pH])
        on_group(gi, s.pH)


def load_w_cols(g, W, col0, ncols, dst0=0):
    wv = g.w_in.rearrange("(kb p) n -> p kb n", p=128)
    for a in range(0, ncols, 512):
        n = min(512, ncols - a)
        g.P.dma("pool", W[:, :, dst0 + a:dst0 + a + n], wv[:, :, col0 + a:col0 + a + n], writes=[W])


def phase_far(g):
    nc, P = g.nc, g.P
    with ExitStack() as es:
        c = alloc_chunk_bufs(g, es, 20)
        s = Ctx()
        Wf = sb(g, es, "Wf", [128, 8, 1024], BF16)
        Wxb = g.Wxbc
        Wdt = sb(g, es, "Wdt", [128, 8, 64], BF16)
        load_w_cols(g, Wf, 0, 1024)
        load_w_cols(g, Wdt, 6144, 64)
        s.pxa = [ps(g, es, "pxa%d" % i, [128, 512], F32) for i in range(2)]
        s.pxc = [ps(g, es, "pxc%d" % i, [128, 512], F32) for i in range(1)]
        s.pD = ps(g, es, "pD", [128, 512], F32)
        s.pH = ps(g, es, "pH", [128, 512], F32)
        pf = [ps(g, es, "pf%d" % i, [128, 512], F32) for i in range(2)]
        s.pre = [sb(g, es, "pre%d" % i, [128, 3, 130], BF16) for i in range(2)]
        s.xcs = sb(g, es, "xcs", [128, 20, 128], BF16, nparts=8)
        s.pD2 = sb(g, es, "pD2", [128, 192], F32)
        s.xtok = sb(g, es, "xtok", [128, 2560], BF16)
        s.dtm = sb(g, es, "dtm", [128, 64], F32)
        s.la = sb(g, es, "la", [128, 64], F32)
        s.wx = sb(g, es, "wx", [128, 64], F32)
        s.sg = sb(g, es, "sg", [128, 64], F32)
        Rb = sb(g, es, "Rb", [128, 32], F32)
        wxb = sb(g, es, "wxb", [128, 64], BF16)
        dec = sb(g, es, "dec", [128, 32], F32)
        xdd = [sb(g, es, "xdd%d" % i, [128, 2048], BF16) for i in range(2)]
        ub = [sb(g, es, "ub%d" % i, [128, D], BF16) for i in range(2)]
        P.op("dve", lambda e: e.memset(g.Sf[:], 0.0), [], [g.Sf])
        P.op("dve", lambda e: e.memset(g.Sb[:], 0.0), [], [g.Sb])
        P.op("dve", lambda e: e.memset(Rb[:], 0.0), [], [Rb])
        slots = [("c", 0), ("c", 1)] + [("l", i) for i in range(NCH)] + [("c", 0), ("c", 1)]
        first = {0, 2, 66}
        last = {1, 65, 67}

        def do_prep_a(k):
            kind, i = slots[k]
            src = g.ctxb[i * 128:(i + 1) * 128, :] if kind == "c" else g.xb[i * 128:(i + 1) * 128, :]
            prep_a(g, c, k, src)

        def do_prep_b(k):
            kind, i = slots[k]
            if kind == "c":
                prep_b(g, c, k, 2, 3)
            else:
                prep_b(g, c, k, 0, 1)
            halo_link(g, c, k, k not in first)
            if k in last:
                halo_zero_right(g, c, k)
        do_prep_a(0); do_prep_b(0)
        do_prep_a(1); do_prep_b(1)
        for k in range(NSLOT):
            kind, i = slots[k]
            hT = c.hTe[k % 3]
            fo = CPP["fmask"][0] + 2 * k
            mask_ap = bc(g.cpp_t[:, fo:fo + 2].unsqueeze(2), [128, 2, 32])
            steps = dt_steps(g, s, hT, Wdt, 64, cbcv(g, "dt_bias"), g.nega[:], mask_ap)

            def t1():
                def segs(e):
                    e.matmul(s.pD[:, 64:96], lhsT=mk32(g, "gt"), rhs=s.la[:, 0:32], start=True, stop=True)
                    e.matmul(s.pD[:, 96:128], lhsT=mk32(g, "lt"), rhs=s.la[:, 32:64], start=True, stop=True)
                    return e.matmul(s.pD[:, 128:192], lhsT=mk32(g, "ones"), rhs=s.la[:, 0:64], start=True, stop=True)
                P.op("pe", segs, [s.la, g.cmk_t], [s.pD])
                P.op("dve", lambda e: e.tensor_copy(out=s.pD2[:], in_=s.pD[:, 0:192]), [s.pD], [s.pD2])

            def t2b():
                P.op("pool", lambda e: e.tensor_copy(out=s.sg[:, 0:32], in_=s.pD2[:, 64:96]), [s.pD2], [s.sg])
                P.op("pool", lambda e: e.tensor_tensor(out=s.sg[:, 32:64], in0=s.pD2[:, 96:128], in1=Rb[:], op=OP.add),
                     [s.pD2, Rb], [s.sg])
                P.op("pool", lambda e: e.tensor_tensor(out=Rb[:], in0=Rb[:], in1=s.pD2[:, 160:192], op=OP.add),
                     [s.pD2, Rb], [Rb])

            def t3():
                P.op("act", lambda e: e.activation(out=s.wx[:], in_=s.sg[:], func=AF.Exp), [s.sg], [s.wx])
                P.op("act", lambda e: e.activation(out=dec[:], in_=s.pD2[:, 128:160], func=AF.Exp), [s.pD2], [dec])

            def t4():
                P.op("pool", lambda e: e.tensor_tensor(out=wxb[:], in0=s.wx[:], in1=s.dtm[:], op=OP.mult),
                     [s.wx, s.dtm], [wxb])
                P.op("pool", lambda e: e.tensor_tensor(
                    out=g.Sf[:].rearrange("p (h d) -> p h d", d=64), in0=g.Sf[:].rearrange("p (h d) -> p h d", d=64),
                    in1=bc(dec[:].unsqueeze(2), [128, 32, 64]), op=OP.mult), [g.Sf, dec], [g.Sf])
            for st_ in steps + [t1, t2b, t3, t4]:
                st_()
            if k + 2 < NSLOT:
                do_prep_a(k + 2)
            fsteps = []
            if kind == "l":
                u = ub[i % 2]

                def fhalf(hf, u=u, hT=hT, i=i):
                    def mm(e):
                        ins = None
                        for kb in range(8):
                            ins = e.matmul(pf[hf][:], lhsT=hT[:, kb, 1:129], rhs=Wf[:, kb, hf * 512:(hf + 1) * 512],
                                           start=(kb == 0), stop=(kb == 7))
                        return ins
                    P.op("pe", mm, [hT, Wf], [pf[hf]])
                    P.op("dve", lambda e: e.tensor_copy(out=u[:, hf * 512:(hf + 1) * 512], in_=pf[hf][:]),
                         [pf[hf]], [u])
                    if hf == 1:
                        P.dma("sp", g.U[i], u[:], reads=[u])
                fsteps = [lambda: fhalf(0), lambda: fhalf(1)]
            fm_proj_conv(g, c, s, hT, Wxb, 20, 0, [])
            if k + 2 < NSLOT:
                do_prep_b(k + 2)
            to_token_major(g, c, s, 20, fsteps)
            x3 = s.xtok[:, 0:2048].rearrange("p (h d) -> p h d", d=64)
            P.op("pool", lambda e: e.tensor_tensor(out=xdd[1][:].rearrange("p (h d) -> p h d", d=64), in0=x3,
                                                   in1=bc(wxb[:, 32:64].unsqueeze(2), [128, 32, 64]), op=OP.mult),
                 [s.xtok, wxb], [xdd[1]])
            P.op("dve", lambda e: e.tensor_tensor(out=xdd[0][:].rearrange("p (h d) -> p h d", d=64), in0=x3,
                                                  in1=bc(wxb[:, 0:32].unsqueeze(2), [128, 32, 64]), op=OP.mult),
                 [s.xtok, wxb], [xdd[0]])
            banks = [s.pxa[0], s.pxa[1], s.pxc[0], s.pH]
            for di, (xd_, S_) in enumerate(((xdd[0], g.Sf), (xdd[1], g.Sb))):
                for gi in range(4):
                    pst = banks[gi]
                    P.op("pe", lambda e, gi=gi, pst=pst, xd_=xd_: e.matmul(
                        pst[:], lhsT=s.xtok[:, 2048 + gi * 128:2048 + (gi + 1) * 128],
                        rhs=xd_[:, gi * 512:(gi + 1) * 512], start=True, stop=True), [s.xtok, xd_], [pst])
                    P.op("dve", lambda e, gi=gi, pst=pst, S_=S_: e.tensor_tensor(
                        out=S_[:, gi * 512:(gi + 1) * 512], in0=S_[:, gi * 512:(gi + 1) * 512], in1=pst[:], op=OP.add),
                        [S_, pst], [S_])
        if DEBUG:
            P.dma("sp", g.SFB[0], g.Sf[:], reads=[g.Sf])
            P.dma("sp", g.SFB[1], g.Sb[:], reads=[g.Sb])
        P.barrier()


def load_w_gen(g, W, src, nkb, ncols):
    wv = src.rearrange("(kb p) n -> p kb n", p=128)
    for a in range(0, ncols, 512):
        n = min(512, ncols - a)
        g.P.dma("pool", W[:, :, a:a + n], wv[:, :, a:a + n], writes=[W])


def phase_fnet(g):
    nc, P = g.nc, g.P
    with ExitStack() as es:
        T1 = sb(g, es, "T1", [64, 128], BF16)
        P.dma("pool", T1[:], g.t1, writes=[T1])
        V = [sb(g, es, "V%d" % i, [64, 4, D], BF16) for i in range(2)]
        Yt = [sb(g, es, "Yt%d" % i, [128, 4, D], BF16) for i in range(2)]
        p1 = [ps(g, es, "p1_%d" % i, [128, 512], F32) for i in range(4)]
        cnt = 0
        for tg in range(32):
            v, yt = V[tg % 2], Yt[tg % 2]
            P.dma("sp", v[:], g.U[:, tg * 4:(tg + 1) * 4, :], writes=[v])
            for t in range(4):
                for hf in range(2):
                    pp = p1[cnt % 4]
                    P.op("pe", lambda e, pp=pp, t=t, hf=hf, v=v: e.matmul(
                        pp[:], lhsT=T1[:], rhs=v[:, t, hf * 512:(hf + 1) * 512], start=True, stop=True), [T1, v], [pp])
                    if cnt % 2:
                        P.op("act", lambda e, pp=pp, t=t, hf=hf, yt=yt: e.activation(
                            out=yt[:, t, hf * 512:(hf + 1) * 512], in_=pp[:], func=AF.Copy), [pp], [yt])
                    else:
                        P.op("dve", lambda e, pp=pp, t=t, hf=hf, yt=yt: e.tensor_copy(
                            out=yt[:, t, hf * 512:(hf + 1) * 512], in_=pp[:]), [pp], [yt])
                    cnt += 1
            P.dma("sp", g.Y[:, tg * 4:(tg + 1) * 4, :], yt[:], reads=[yt])
        P.barrier()
    with ExitStack() as es:
        T2 = sb(g, es, "T2", [128, 2 * 64 * 68], BF16)
        for a in range(0, 2 * 64 * 68, 1088):
            P.dma("pool", T2[:, a:a + 1088], g.t2[:, a:a + 1088], writes=[T2])
        Yk = [sb(g, es, "Yk%d" % i, [128, 2, D], BF16) for i in range(2)]
        XTs = sb(g, es, "XTs", [128, 8, 2, EXT], BF16)
        p2f = [ps(g, es, "p2_%d" % i, [128, 512], F32) for i in range(4)]
        yv = g.Y.rearrange("(ri k) t c -> k t ri c", ri=2)
        xv = XTs[:].rearrange("p c r (j k) -> p c r j k", k=64)
        for k1 in range(64):
            yk = Yk[k1 % 2]
            P.dma("sp", yk[:], yv[k1], writes=[yk])
            for cg in range(2):
                ppb = p2f[(k1 * 2 + cg) % 4]
                pp = ppb[:, 0:272].rearrange("p (c k) -> p c k", k=68)

                def mm(e, pp=pp, cg=cg, yk=yk, k1=k1):
                    ins = None
                    for cb in range(4):
                        cbx = cg * 4 + cb
                        e.matmul(pp[:, cb, :], lhsT=yk[:, 0, cbx * 128:(cbx + 1) * 128],
                                 rhs=T2[:, k1 * 68:(k1 + 1) * 68], start=True, stop=False)
                        ins = e.matmul(pp[:, cb, :], lhsT=yk[:, 1, cbx * 128:(cbx + 1) * 128],
                                       rhs=T2[:, (64 + k1) * 68:(64 + k1 + 1) * 68], start=False, stop=True)
                    return ins
                P.op("pe", mm, [yk, T2], [ppb])
                for ri in range(2):
                    if (k1 + cg) % 2:
                        P.op("act", lambda e, pp=pp, cg=cg, ri=ri, k1=k1: e.activation(
                            out=xv[:, cg * 4:(cg + 1) * 4, ri, :, k1], in_=pp[:, :, ri * 34:(ri + 1) * 34],
                            func=AF.Copy), [ppb], [XTs])
                    else:
                        P.op("dve", lambda e, pp=pp, cg=cg, ri=ri, k1=k1: e.tensor_copy(
                            out=xv[:, cg * 4:(cg + 1) * 4, ri, :, k1], in_=pp[:, :, ri * 34:(ri + 1) * 34]),
                            [ppb], [XTs])
        for cb in range(8):
            P.dma("sp", g.XT[:, cb * 2 * EXT:(cb + 1) * 2 * EXT].rearrange("p (r t) -> p r t", r=2), XTs[:, cb, :, :],
                  reads=[XTs])
        P.barrier()


def phase_own(g, d):
    nc, P = g.nc, g.P
    with ExitStack() as es:
        c = alloc_chunk_bufs(g, es, 24)
        s = Ctx()
        hH = sb(g, es, "hH", [128, 8, 130], BF16)
        W = g.Wxbc
        Wdt = sb(g, es, "Wdt", [128, 8, 32], BF16)
        load_w_cols(g, Wdt, 6144 + 32 * d, 32)
        s.pxa = [ps(g, es, "pxa%d" % i, [128, 512], F32) for i in range(1)]
        s.pxc = [ps(g, es, "pxc%d" % i, [128, 512], F32) for i in range(1)]
        s.pxb = [s.pxa[0], s.pxc[0]]
        s.pre = [sb(g, es, "pre%d" % i, [128, 3, 130], BF16) for i in range(2)]
        s.pD = ps(g, es, "pD", [128, 512], F32)
        s.pH = ps(g, es, "pH", [128, 512], F32)
        psc = ps(g, es, "psc", [128, 4, 128], F32)
        pL = [ps(g, es, "pL%d" % i, [128, 4, 128], F32) for i in range(2)]
        s.xcs = sb(g, es, "xcs", [128, 24, 128], BF16, nparts=8)
        s.xtok = sb(g, es, "xtok", [128, 2560], BF16)
        s.dtm = sb(g, es, "dtm", [128, 32], F32)
        s.la = sb(g, es, "la", [128, 32], F32)
        s.wx = sb(g, es, "wx", [128, 32], F32)
        lab = sb(g, es, "lab", [128, 32], BF16)
        nlab = sb(g, es, "nlab", [128, 32], BF16)
        ecum = sb(g, es, "ecum", [128, 32], F32)
        dec = sb(g, es, "dec", [128, 32], F32)
        xd = sb(g, es, "xd", [128, 2048], BF16)
        xdd = sb(g, es, "xdd", [128, 2048], BF16)
        Sbf = sb(g, es, "Sbf", [128, 2048], BF16)
        Dt = [sb(g, es, "Dt%d" % i, [128, 8, 128], BF16) for i in range(2)]
        Lx = [sb(g, es, "Lx%d" % i, [128, 8, 128], BF16) for i in range(2)]
        G = [sb(g, es, "G%d" % i, [128, 8, 128], BF16) for i in range(2)]
        yo = sb(g, es, "yo", [128, 512], F32)
        ytile = sb(g, es, "ytile", [128, 2048], BF16)
        tmp = sb(g, es, "tmp", [128, 2048], BF16)
        yfl = sb(g, es, "yfl", [128, 2048], BF16)
        S = g.Sf if d == 0 else g.Sb
        mxk = "le" if d == 0 else "ge"
        sgk = "gt" if d == 0 else "lt"
        penk = "pen_f" if d == 0 else "pen_b"
        order = list(range(NEXT)) if d == 0 else list(range(NEXT - 1, -1, -1))

        prep(g, c, 0, g.xext[EXT:EXT + 128, :], 0, 1, vmask=cbcv(g, "halo_v"), hT=hH)

        def do_prep_a(ci):
            prep_a(g, c, ci, g.xext[ci * 128:(ci + 1) * 128, :])

        def do_prep_b(ci, prev_ci):
            hT = c.hTe[ci % 3]
            vm = None
            if ci == 0:
                vm = cbcv(g, "emask_bc", 0, 128)
            if ci == NEXT - 1:
                vm = cbcv(g, "emask_bc", 128, 128)
            prep_b(g, c, ci, 0, 1, vmask=vm)
            if prev_ci is not None:
                nb = c.hTe[prev_ci % 3]
                if ci == prev_ci + 1:
                    P.op("pool", lambda e: e.tensor_copy(out=hT[:, :, 0:1], in_=nb[:, :, 128:129]), [nb], [hT])
                    P.op("pool", lambda e: e.tensor_copy(out=nb[:, :, 129:130], in_=hT[:, :, 1:2]), [hT], [nb])
                else:
                    P.op("pool", lambda e: e.tensor_copy(out=hT[:, :, 129:130], in_=nb[:, :, 1:2]), [nb], [hT])
                    P.op("pool", lambda e: e.tensor_copy(out=nb[:, :, 0:1], in_=hT[:, :, 128:129]), [hT], [nb])
            if ci == 0:
                P.op("pool", lambda e: e.tensor_copy(out=hT[:, :, 0:1], in_=hH[:, :, 1:2]), [hH], [hT])
            if ci == NEXT - 1:
                P.op("pool", lambda e: e.tensor_copy(out=hT[:, :, 129:130], in_=hH[:, :, 2:3]), [hH], [hT])

        do_prep_a(order[0]); do_prep_b(order[0], None)
        do_prep_a(order[1]); do_prep_b(order[1], order[0])
        for oi, ci in enumerate(order):
            hT = c.hTe[ci % 3]
            if d == 1:
                P.dma("sp", yfl[:], g.YF[ci], writes=[yfl])
            mask_ap = bc(cpp(g, "emask", ci).unsqueeze(2), [128, 1, 32])
            steps = dt_steps(g, s, hT, Wdt, 32, cbcv(g, "dt_bias", 32 * d, 32), g.nega[:, 32 * d:32 * (d + 1)], mask_ap)

            def u1():
                P.op("pool", lambda e: e.tensor_copy(out=lab[:], in_=s.la[:]), [s.la], [lab])
                P.op("pool", lambda e: e.tensor_scalar(out=nlab[:], in0=lab[:], scalar1=-1.0, scalar2=None, op0=OP.mult),
                     [lab], [nlab])

                def segs(e):
                    e.matmul(s.pD[:, 64:96], lhsT=mk32(g, mxk), rhs=s.la[:], start=True, stop=True)
                    e.matmul(s.pD[:, 96:128], lhsT=mk32(g, sgk), rhs=s.la[:], start=True, stop=True)
                    return e.matmul(s.pD[:, 128:160], lhsT=mk32(g, "ones"), rhs=s.la[:], start=True, stop=True)
                P.op("pe", segs, [s.la, g.cmk_t], [s.pD])

            def u2():
                P.op("act", lambda e: e.activation(out=ecum[:], in_=s.pD[:, 64:96], func=AF.Exp), [s.pD], [ecum])
                P.op("act", lambda e: e.activation(out=s.wx[:], in_=s.pD[:, 96:128], func=AF.Exp), [s.pD], [s.wx])
                P.op("act", lambda e: e.activation(out=dec[:], in_=s.pD[:, 128:160], func=AF.Exp), [s.pD], [dec])

            def u3():
                P.op("pool", lambda e: e.tensor_tensor(out=s.wx[:], in0=s.wx[:], in1=s.dtm[:], op=OP.mult),
                     [s.wx, s.dtm], [s.wx])
            for st_ in steps + [u1, u2, u3]:
                st_()
            if oi + 2 < NEXT:
                do_prep_a(order[oi + 2])
            fm_proj_conv(g, c, s, hT, W, 24, 0, [])
            if oi + 2 < NEXT:
                do_prep_b(order[oi + 2], order[oi + 1])
            to_token_major(g, c, s, 20)
            x3 = s.xtok[:, 0:2048].rearrange("p (h d) -> p h d", d=64)
            P.op("dve", lambda e: e.tensor_tensor(out=xd[:].rearrange("p (h d) -> p h d", d=64), in0=x3,
                                                   in1=bc(s.dtm[:].unsqueeze(2), [128, 32, 64]), op=OP.mult),
                 [s.xtok, s.dtm], [xd])
            P.op("dve", lambda e: e.tensor_tensor(out=xdd[:].rearrange("p (h d) -> p h d", d=64), in0=x3,
                                                   in1=bc(s.wx[:].unsqueeze(2), [128, 32, 64]), op=OP.mult),
                 [s.xtok, s.wx], [xdd])
            P.op("act", lambda e: e.activation(out=Sbf[:], in_=S[:], func=AF.Copy), [S], [Sbf])

            def sc(e):
                ins = None
                for gi in range(4):
                    ins = e.matmul(psc[:, gi, :], lhsT=s.xcs[:, 16 + gi, :], rhs=s.xcs[:, 20 + gi, :], start=True, stop=True)
                return ins
            P.op("pe", sc, [s.xcs], [psc])
            for gi in range(4):
                dt_, lx, gg = Dt[gi % 2], Lx[gi % 2], G[gi % 2]
                P.op("pool", lambda e, gi=gi, dt_=dt_: e.tensor_tensor(
                    out=dt_[:], in0=bc(lab[:, gi * 8:(gi + 1) * 8].unsqueeze(2), [128, 8, 128]),
                    in1=bc(mk16(g, mxk).unsqueeze(1), [128, 8, 128]), op=OP.mult), [lab, g.cmkb_t], [dt_])
                for hh in range(2):
                    def mmL(e, gi=gi, hh=hh, dt_=dt_):
                        e.matmul(pL[hh][:], lhsT=mk16(g, "ones"), rhs=dt_[:, hh * 4:(hh + 1) * 4, :], start=True, stop=False)
                        e.matmul(pL[hh][:], lhsT=mk16(g, mxk),
                                 rhs=bc(nlab[:, gi * 8 + hh * 4:gi * 8 + hh * 4 + 4].unsqueeze(2), [128, 4, 128]),
                                 start=False, stop=False)
                        return e.matmul(pL[hh][:], lhsT=mk16(g, "ident"),
                                        rhs=bc(mk16(g, penk).unsqueeze(1), [128, 4, 128]), start=False, stop=True)
                    P.op("pe", mmL, [dt_, nlab, g.cmkb_t], [pL[hh]])
                    P.op("act", lambda e, hh=hh, lx=lx: e.activation(out=lx[:, hh * 4:(hh + 1) * 4, :], in_=pL[hh][:],
                                                                   func=AF.Exp), [pL[hh]], [lx])
                P.op("dve", lambda e, gi=gi, lx=lx, gg=gg: e.tensor_tensor(
                    out=gg[:], in0=lx[:], in1=bc(psc[:, gi, :].unsqueeze(1), [128, 8, 128]), op=OP.mult),
                    [lx, psc], [gg])

                def mmy(e, gi=gi, gg=gg):
                    ins = None
                    for h in range(8):
                        hh = gi * 8 + h
                        ins = e.matmul(s.pH[:, h * 64:(h + 1) * 64], lhsT=gg[:, h, :], rhs=xd[:, hh * 64:(hh + 1) * 64],
                                       start=True, stop=True)
                    return ins
                P.op("pe", mmy, [gg, xd], [s.pH])
                P.op("pe", lambda e, gi=gi: e.matmul(s.pxb[0][:], lhsT=s.xcs[:, 20 + gi, :],
                                                     rhs=Sbf[:, gi * 512:(gi + 1) * 512], start=True, stop=True),
                     [s.xcs, Sbf], [s.pxb[0]])
                P.op("dve", lambda e, gi=gi: e.tensor_tensor(
                    out=yo[:].rearrange("p (h d) -> p h d", d=64), in0=s.pxb[0][:].rearrange("p (h d) -> p h d", d=64),
                    in1=bc(ecum[:, gi * 8:(gi + 1) * 8].unsqueeze(2), [128, 8, 64]), op=OP.mult),
                    [s.pxb[0], ecum], [yo])
                P.op("dve", lambda e, gi=gi: e.tensor_tensor(out=ytile[:, gi * 512:(gi + 1) * 512], in0=yo[:],
                                                              in1=s.pH[:], op=OP.add), [yo, s.pH], [ytile])
                P.op("pe", lambda e, gi=gi: e.matmul(s.pxb[1][:], lhsT=s.xtok[:, 2048 + gi * 128:2048 + (gi + 1) * 128],
                                                     rhs=xdd[:, gi * 512:(gi + 1) * 512], start=True, stop=True),
                     [s.xtok, xdd], [s.pxb[1]])
                P.op("dve", lambda e, gi=gi: e.tensor_tensor(
                    out=S[:, gi * 512:(gi + 1) * 512].rearrange("p (h d) -> p h d", d=64),
                    in0=S[:, gi * 512:(gi + 1) * 512].rearrange("p (h d) -> p h d", d=64),
                    in1=bc(dec[:, gi * 8:(gi + 1) * 8].unsqueeze(2), [128, 8, 64]), op=OP.mult), [S, dec], [S])
                P.op("dve", lambda e, gi=gi: e.tensor_tensor(out=S[:, gi * 512:(gi + 1) * 512],
                                                              in0=S[:, gi * 512:(gi + 1) * 512], in1=s.pxb[1][:],
                                                              op=OP.add), [S, s.pxb[1]], [S])
            if d == 0:
                P.op("pool", lambda e: e.tensor_tensor(out=tmp[:].rearrange("p (h d) -> p h d", d=64), in0=x3,
                                                       in1=bc(cbcv(g, "d_skip").unsqueeze(2), [128, 32, 64]),
                                                       op=OP.mult), [s.xtok, g.cbc_t], [tmp])
                P.op("pool", lambda e: e.tensor_tensor(out=tmp[:], in0=tmp[:], in1=ytile[:], op=OP.add),
                     [tmp, ytile], [tmp])
                P.dma("sp", g.YF[ci], tmp[:], reads=[tmp])
            else:
                P.op("pool", lambda e: e.tensor_tensor(out=tmp[:], in0=yfl[:], in1=ytile[:], op=OP.add),
                     [yfl, ytile], [tmp])
                P.dma("sp", g.YT[ci], tmp[:], reads=[tmp])
        P.barrier()


def phase_merge(g):
    phase_merge_a(g)
    phase_merge_b(g)


def phase_merge_a(g):
    nc, P = g.nc, g.P
    with ExitStack() as es:
        c = alloc_chunk_bufs(g, es, 0)
        Wz = sb(g, es, "Wz", [128, 8, 2048], BF16)
        Wgs = sb(g, es, "Wgs", [128, 8, 1024], BF16)
        Wsb = sb(g, es, "Wsb", [128, 16, 1024], BF16)
        load_w_cols(g, Wz, 4096, 2048)
        load_w_cols(g, Wgs, 7232, 1024)
        load_w_gen(g, Wsb, g.w_sb, 16, 1024)
        sg = sb(g, es, "ssdg", [128, 2048], F32)
        P.dma("sp", sg[:], g.cbg[:, CBG["ssd_g"][0]:CBG["ssd_g"][0] + 2048], writes=[sg])
        pz = [ps(g, es, "pz%d" % i, [128, 512], F32) for i in range(4)]
        pbs = [ps(g, es, "pbs%d" % i, [128, 512], F32) for i in range(2)]
        yt = [sb(g, es, "yt%d" % i, [128, 2048], BF16) for i in range(2)]
        zs = sb(g, es, "zs", [128, 4, 512], BF16, nparts=4)
        t = sb(g, es, "t", [128, 4, 512], F32, nparts=4)
        jk = sb(g, es, "jk", [128, 512], BF16)
        st2 = sb(g, es, "st2", [128, 16], F32)
        ysn = sb(g, es, "ysn", [128, 4, 512], BF16, nparts=4)
        ysnT = sb(g, es, "ysnT", [128, 16, 128], BF16)
        sgs = sb(g, es, "sgs", [128, 2, 512], F32, nparts=2)
        ms = [sb(g, es, "ms%d" % i, [128, 1024], BF16) for i in range(2)]
        prep_a(g, c, 0, g.xext[0:128, :])
        prep_b(g, c, 0, 0, 1)
        for ci in range(NEXT):
            hT = c.hTe[ci % 3]
            y = yt[ci % 2]
            P.dma("sp", y[:], g.YT[ci], writes=[y])
            if ci + 1 < NEXT:
                prep_a(g, c, ci + 1, g.xext[(ci + 1) * 128:(ci + 2) * 128, :])
            for gi in range(4):
                def mm(e, gi=gi):
                    ins = None
                    for kb in range(8):
                        ins = e.matmul(pz[gi][:], lhsT=hT[:, kb, 1:129], rhs=Wz[:, kb, gi * 512:(gi + 1) * 512],
                                       start=(kb == 0), stop=(kb == 7))
                    return ins
                P.op("pe", mm, [hT, Wz], [pz[gi]])
            for gi in range(4):
                P.op("act", lambda e, gi=gi: e.activation(out=zs[:, gi, :], in_=pz[gi][:], func=AF.Silu),
                     [pz[gi]], [(zs, gi)])
            for gi in range(4):
                P.op("dve", lambda e, gi=gi: e.tensor_tensor(out=t[:, gi, :], in0=zs[:, gi, :],
                                                              in1=y[:, gi * 512:(gi + 1) * 512], op=OP.mult),
                     [(zs, gi), y], [(t, gi)])
            for gi in range(4):
                P.op("act", lambda e, gi=gi: e.activation(out=jk[:], in_=t[:, gi, :], func=AF.Square,
                                                          accum_out=st2[:, gi:gi + 1]), [(t, gi)], [jk, st2])
            P.op("dve", lambda e: e.tensor_scalar(out=st2[:, 4:8], in0=st2[:, 0:4], scalar1=1.0 / 512, scalar2=EPS,
                                                   op0=OP.mult, op1=OP.add), [st2], [st2])
            P.op("act", lambda e: e.activation(out=st2[:, 8:12], in_=st2[:, 4:8], func=AF.Ln), [st2], [st2])
            P.op("act", lambda e: e.activation(out=st2[:, 12:16], in_=st2[:, 8:12], func=AF.Exp, scale=-0.5), [st2], [st2])
            for gi in range(4):
                P.op("dve", lambda e, gi=gi: e.scalar_tensor_tensor(
                    out=ysn[:, gi, :], in0=t[:, gi, :], scalar=st2[:, 12 + gi:13 + gi], in1=sg[:, gi * 512:(gi + 1) * 512],
                    op0=OP.mult, op1=OP.mult), [(t, gi), st2, sg], [(ysn, gi)])
            for rnd in range(2):
                def tr(e, rnd=rnd):
                    ins = None
                    for j in range(8):
                        blk = rnd * 8 + j
                        ins = e.transpose(out=c.pA[:, j, :], in_=ysn[:, blk // 4, (blk % 4) * 128:(blk % 4 + 1) * 128],
                                          identity=mk16(g, "ident"))
                    return ins
                P.op("pe", tr, [ysn, g.cmkb_t], [c.pA])
                P.op("dve", lambda e, rnd=rnd: e.tensor_copy(out=ysnT[:, rnd * 8:(rnd + 1) * 8, :], in_=c.pA[:]),
                     [c.pA], [ysnT])
            if ci + 1 < NEXT:
                prep_b(g, c, ci + 1, 0, 1)
            for hf in range(2):
                def mmg(e, hf=hf):
                    ins = None
                    for kb in range(8):
                        ins = e.matmul(pz[hf][:], lhsT=hT[:, kb, 1:129], rhs=Wgs[:, kb, hf * 512:(hf + 1) * 512],
                                       start=(kb == 0), stop=(kb == 7))
                    return ins
                P.op("pe", mmg, [hT, Wgs], [pz[hf]])

                def mms(e, hf=hf):
                    ins = None
                    for kb in range(16):
                        ins = e.matmul(pbs[hf][:], lhsT=ysnT[:, kb, :], rhs=Wsb[:, kb, hf * 512:(hf + 1) * 512],
                                       start=(kb == 0), stop=(kb == 15))
                    return ins
                P.op("pe", mms, [ysnT, Wsb], [pbs[hf]])
            m = ms[ci % 2]
            for hf in range(2):
                P.op("act", lambda e, hf=hf: e.activation(out=sgs[:, hf, :], in_=pz[hf][:], func=AF.Sigmoid),
                     [pz[hf]], [(sgs, hf)])
                P.op("dve", lambda e, m=m, hf=hf: e.tensor_tensor(out=m[:, hf * 512:(hf + 1) * 512], in0=sgs[:, hf, :],
                                                                   in1=pbs[hf][:], op=OP.mult),
                     [(sgs, hf), pbs[hf]], [m])
            P.dma("sp", g.MS[ci], m[:], reads=[m])
        P.barrier()


def phase_merge_b(g):
    nc, P = g.nc, g.P
    with ExitStack() as es:
        c = alloc_chunk_bufs(g, es, 0)
        Wgf = sb(g, es, "Wgf", [128, 8, 1024], BF16)
        Wfa = sb(g, es, "Wfa", [128, 8, 1024], BF16)
        Wo = sb(g, es, "Wo", [128, 8, 1024], BF16)
        Tcs = sb(g, es, "Tcs", [128, 256], BF16)
        load_w_cols(g, Wgf, 6208, 1024)
        load_w_gen(g, Wfa, g.w_fa, 8, 1024)
        load_w_gen(g, Wo, g.w_o, 8, 1024)
        P.dma("pool", Tcs[:], g.tcs, writes=[Tcs])
        pb0 = ps(g, es, "pb0", [128, 2, 512], F32)
        pb1 = ps(g, es, "pb1", [128, 2, 512], F32)
        pb2 = ps(g, es, "pb2", [128, 2, 512], F32)
        xtc = [sb(g, es, "xtc%d" % i, [128, 8, 2, 128], BF16) for i in range(2)]
        msl = [sb(g, es, "msl%d" % i, [128, 1024], BF16) for i in range(2)]
        mixT = sb(g, es, "mixT", [128, 8, 128], BF16)
        sgf = sb(g, es, "sgf", [128, 1024], F32)
        tmp = sb(g, es, "tmpm", [128, 1024], F32)
        mrg = sb(g, es, "mrg", [128, 1024], BF16)
        mrgT = sb(g, es, "mrgT", [128, 8, 128], BF16)
        l1 = [sb(g, es, "l1_%d" % i, [128, 1024], F32) for i in range(2)]
        xtv = g.XT.rearrange("p (c r t) -> p c r t", c=8, r=2)
        prep_a(g, c, 0, g.xext[0:128, :])
        prep_b(g, c, 0, 0, 1)
        for ci in range(NEXT):
            hT = c.hTe[ci % 3]
            xt = c.xt[ci % 2]
            if ci + 1 < NEXT:
                prep_a(g, c, ci + 1, g.xext[(ci + 1) * 128:(ci + 2) * 128, :])
            xc_, m = xtc[ci % 2], msl[ci % 2]
            P.dma("sp", xc_[:], xtv[:, :, :, ci * 128:(ci + 1) * 128], writes=[xc_])
            P.dma("sp", m[:], g.MS[ci], writes=[m])
            for cg in range(2):
                def mmx(e, cg=cg):
                    e.matmul(pb2[:, cg, :], lhsT=Tcs[:, 0:128], rhs=xc_[:, cg * 4:(cg + 1) * 4, 0, :], start=True, stop=False)
                    return e.matmul(pb2[:, cg, :], lhsT=Tcs[:, 128:256], rhs=xc_[:, cg * 4:(cg + 1) * 4, 1, :],
                                    start=False, stop=True)
                P.op("pe", mmx, [Tcs, xc_], [pb2])
            P.op("act", lambda e: e.activation(out=mixT[:].rearrange("p a b -> p (a b)"),
                                               in_=pb2[:].rearrange("p a b -> p (a b)"), func=AF.Copy), [pb2], [mixT])
            for hf in range(2):
                def mmf(e, hf=hf):
                    ins = None
                    for kb in range(8):
                        ins = e.matmul(pb0[:, hf, :], lhsT=mixT[:, kb, :], rhs=Wfa[:, kb, hf * 512:(hf + 1) * 512],
                                       start=(kb == 0), stop=(kb == 7))
                    return ins
                P.op("pe", mmf, [mixT, Wfa], [pb0])

                def mmg(e, hf=hf):
                    ins = None
                    for kb in range(8):
                        ins = e.matmul(pb1[:, hf, :], lhsT=hT[:, kb, 1:129], rhs=Wgf[:, kb, hf * 512:(hf + 1) * 512],
                                       start=(kb == 0), stop=(kb == 7))
                    return ins
                P.op("pe", mmg, [hT, Wgf], [pb1])
            P.op("act", lambda e: e.activation(out=sgf[:], in_=pb1[:].rearrange("p a b -> p (a b)"), func=AF.Sigmoid),
                 [pb1], [sgf])
            P.op("dve", lambda e: e.tensor_tensor(out=tmp[:], in0=sgf[:], in1=pb0[:].rearrange("p a b -> p (a b)"),
                                                   op=OP.mult), [sgf, pb0], [tmp])
            P.op("dve", lambda e, m=m: e.tensor_tensor(out=mrg[:], in0=tmp[:], in1=m[:], op=OP.add), [tmp, m], [mrg])

            def tr(e):
                ins = None
                for kb in range(8):
                    ins = e.transpose(out=c.pA[:, kb, :], in_=mrg[:, kb * 128:(kb + 1) * 128], identity=mk16(g, "ident"))
                return ins
            P.op("pe", tr, [mrg, g.cmkb_t], [c.pA])
            P.op("act", lambda e: e.activation(out=mrgT[:], in_=c.pA[:], func=AF.Copy), [c.pA], [mrgT])
            if ci + 1 < NEXT:
                prep_b(g, c, ci + 1, 0, 1)
            for hf in range(2):
                def mmo(e, hf=hf):
                    ins = None
                    for kb in range(8):
                        ins = e.matmul(pb2[:, hf, :], lhsT=mrgT[:, kb, :], rhs=Wo[:, kb, hf * 512:(hf + 1) * 512],
                                       start=(kb == 0), stop=(kb == 7))
                    return ins
                P.op("pe", mmo, [mrgT, Wo], [pb2])
            l = l1[ci % 2]
            P.op("dve", lambda e, l=l: e.tensor_tensor(out=l[:], in0=pb2[:].rearrange("p a b -> p (a b)"),
                                                        in1=g.gbc[:, 0:D], op=OP.mult), [pb2, g.gbc], [l])
            P.op("pool", lambda e, l=l, xt=xt: e.tensor_tensor(out=l[:], in0=l[:], in1=xt[:], op=OP.add), [l, xt], [l])
            P.dma("sp", g.L1[ci], l[:], reads=[l])
        P.barrier()


def phase_ffn(g):
    nc, P = g.nc, g.P
    with ExitStack() as es:
        c = alloc_chunk_bufs(g, es, 0)
        h2T = sb(g, es, "h2T", [128, 8, EXT], BF16, nparts=NEXT)
        Wd = sb(g, es, "Wd", [128, NFB, 1024], BF16)
        load_w_gen(g, Wd, g.w_down, NFB, 1024)
        fg = sb(g, es, "fg", [128, 1024], F32)
        P.dma("sp", fg[:], g.cbg[:, CBG["final_g"][0]:CBG["final_g"][0] + 1024], writes=[fg])
        for ci in range(NEXT):
            i2 = ci % 2
            xt, st, xn = c.xt[i2], c.st[i2], c.xn[i2]
            P.dma("sp", xt[:], g.L1[ci], writes=[xt])
            P.op("act", lambda e: e.activation(out=c.junk[:], in_=xt[:], func=AF.Square, accum_out=st[:, 0:1]),
                 [xt], [c.junk, st])
            P.op("dve", lambda e: e.tensor_scalar(out=st[:, 1:2], in0=st[:, 0:1], scalar1=1.0 / D, scalar2=EPS,
                                                   op0=OP.mult, op1=OP.add), [st], [st])
            P.op("act", lambda e: e.activation(out=st[:, 2:3], in_=st[:, 1:2], func=AF.Sqrt), [st], [st])
            P.op("dve", lambda e: e.reciprocal(out=st[:, 3:4], in_=st[:, 2:3]), [st], [st])
            P.op("act", lambda e: e.activation(out=xn[:], in_=xt[:], func=AF.Copy, scale=st[:, 3:4]), [xt, st], [xn])

            def tr(e):
                ins = None
                for kb in range(8):
                    ins = e.transpose(out=c.pA[:, kb, :], in_=xn[:, kb * 128:(kb + 1) * 128], identity=mk16(g, "ident"))
                return ins
            P.op("pe", tr, [xn, g.cmkb_t], [c.pA])
            for kb in range(8):
                P.op("dve", lambda e, kb=kb, ci=ci: e.tensor_scalar(
                    out=h2T[:, kb, ci * 128:(ci + 1) * 128], in0=c.pA[:, kb, :], scalar1=modA(g, 4, kb),
                    scalar2=modA(g, 5, kb), op0=OP.mult, op1=OP.add), [c.pA, g.modv], [(h2T, ci)])
            if ci in (0, NEXT - 1):
                vm = cbcv(g, "emask_bc", 0 if ci == 0 else 128, 128)
                P.op("pool", lambda e, ci=ci, vm=vm: e.tensor_tensor(
                    out=h2T[:, :, ci * 128:(ci + 1) * 128], in0=h2T[:, :, ci * 128:(ci + 1) * 128],
                    in1=bc(vm.unsqueeze(1), [128, 8, 128]), op=OP.mult), [(h2T, ci), g.cbc_t], [(h2T, ci)])
        NB = 4
        pu = [ps(g, es, "pu%d" % i, [128, 512], F32) for i in range(2)]
        pd = ps(g, es, "pd", [128, 2, 512], F32)
        aT = sb(g, es, "aT", [128, NFB, 512], BF16, nparts=NFB)
        wu = [sb(g, es, "wu%d" % i, [128, 8, 2, 128], BF16) for i in range(3)]
        ug = [sb(g, es, "ug%d" % i, [128, 10, 64], BF16) for i in range(2)]
        dg = [sb(g, es, "dg%d" % i, [128, 4, 128], BF16) for i in range(4)]
        pcv = [ps(g, es, "pcv%d" % i, [128, 512], F32) for i in range(2)]
        acc = [sb(g, es, "acc%d" % i, [128, 8, 64], F32) for i in range(2)]
        sgl = sb(g, es, "sgl", [128, 512], F32)
        lt = [sb(g, es, "lt%d" % i, [128, 1024], F32) for i in range(2)]
        yy = [sb(g, es, "yy%d" % i, [128, 1024], F32) for i in range(2)]
        jk = c.junk
        st = [sb(g, es, "stf%d" % i, [128, 4], F32) for i in range(2)]
        wuv = g.w_up.rearrange("(kb p) (gv n) -> p kb gv n", p=128, gv=2)
        l1f = g.L1.rearrange("c p d -> (c p) d")
        cnt = 0
        nitem = NB * NFB

        def issue_w(i):
            if i < nitem:
                fb_ = i % NFB
                w_ = wu[i % 3]
                for gv_ in range(2):
                    P.dma("pool", w_[:, :, gv_, :], wuv[:, :, gv_, fb_ * 128:(fb_ + 1) * 128], writes=[w_])
        issue_w(0)
        issue_w(1)
        for blk in range(NB):
            base = blk * 512
            hparts = [(h2T, i) for i in range(base // 128, (base + 640 + 127) // 128)]
            for fb in range(NFB):
                w = wu[cnt % 3]
                issue_w(cnt + 2)
                cnt += 1
                for gv in range(2):
                    u = ug[gv]
                    a = acc[gv]
                    for j in range(2):
                        def mm(e, j=j, gv=gv, w=w):
                            ins = None
                            for kb in range(8):
                                ins = e.matmul(pu[j][:, 0:320], lhsT=w[:, kb, gv, :],
                                               rhs=h2T[:, kb, base + j * 320:base + (j + 1) * 320],
                                               start=(kb == 0), stop=(kb == 7))
                            return ins
                        P.op("pe", mm, [w] + hparts, [pu[j]])
                        P.op("act", lambda e, j=j, u=u: e.activation(
                            out=u[:].rearrange("p r c -> p (r c)")[:, j * 320:(j + 1) * 320], in_=pu[j][:, 0:320],
                            func=AF.Copy), [pu[j]], [u])
                    cf = gv * NFB + fb
                    wt = lambda t: cpp(g, "cw_ffn", t * 44 + cf)
                    P.op("act", lambda e, u=u, a=a, cf=cf: e.activation(
                        out=a[:], in_=u[:, 1:9, :], func=AF.Identity, scale=cpp(g, "cw_ffn", 4 * 44 + cf),
                        bias=cpp(g, "cb_ffn", cf)), [u, g.cpp_t], [a])
                    dgt = dg[(cnt * 2 + gv) % 4]
                    for i_, t_ in enumerate((1, 7, 3, 5)):
                        P.op("pool", lambda e, i_=i_, t_=t_, dgt=dgt, cf=cf: e.tensor_scalar(
                            out=dgt[:, i_, :], in0=mk16(g, "ident"), scalar1=cpp(g, "cw_ffn", t_ * 44 + cf), scalar2=0.0,
                            op0=OP.mult, op1=OP.add), [g.cmkb_t, g.cpp_t], [dgt])
                    pc = pcv[gv]
                    pc3 = pc[:].rearrange("p (r c) -> p r c", c=64)

                    def mcv(e, u=u, dgt=dgt, pc3=pc3):
                        e.matmul(pc3, lhsT=dgt[:, 0, :], rhs=u[:, 0:8, :], start=True, stop=False)
                        e.matmul(pc3, lhsT=dgt[:, 1, :], rhs=u[:, 2:10, :], start=False, stop=False)
                        e.matmul(pc3[:, :, 1:64], lhsT=dgt[:, 2, :], rhs=u[:, 1:9, 0:63], start=False, stop=False)
                        return e.matmul(pc3[:, :, 0:63], lhsT=dgt[:, 3, :], rhs=u[:, 1:9, 1:64], start=False, stop=True)
                    P.op("pe", mcv, [u, dgt], [pc])
                    for (kh, kw) in ((0, 0), (0, 2), (2, 0), (2, 2)):
                        dy, dx = kh - 1, kw - 1
                        c0, c1 = max(0, -dx), 64 - max(0, dx)
                        P.op("dve", lambda e, u=u, a=a, dy=dy, dx=dx, c0=c0, c1=c1, t=kh * 3 + kw, cf=cf:
                             e.scalar_tensor_tensor(out=a[:, :, c0:c1], in0=u[:, 1 + dy:9 + dy, c0 + dx:c1 + dx],
                                                    scalar=cpp(g, "cw_ffn", t * 44 + cf), in1=a[:, :, c0:c1],
                                                    op0=OP.mult, op1=OP.add), [u, a, g.cpp_t], [a])
                    P.op("dve", lambda e, a=a, pc3=pc3: e.tensor_tensor(out=a[:], in0=a[:], in1=pc3, op=OP.add),
                         [a, pc], [a])
                P.op("act", lambda e: e.activation(out=sgl[:], in_=acc[0][:].rearrange("p r c -> p (r c)"), func=AF.Silu),
                     [acc[0]], [sgl])
                P.op("dve", lambda e, fb=fb: e.tensor_tensor(out=aT[:, fb, :], in0=sgl[:],
                                                              in1=acc[1][:].rearrange("p r c -> p (r c)"), op=OP.mult),
                     [sgl, acc[1]], [(aT, fb)])
            for tcn in range(4):
                o0 = blk * 512 + tcn * 128
                i2 = (blk * 4 + tcn) % 2
                l, y, s4 = lt[i2], yy[i2], st[i2]
                o = y
                P.dma("sp", l[:], l1f[o0 + 64:o0 + 64 + 128, :], writes=[l])
                for hf in range(2):
                    def mmd(e, hf=hf, tcn=tcn):
                        ins = None
                        for fb in range(NFB):
                            ins = e.matmul(pd[:, hf, :], lhsT=aT[:, fb, tcn * 128:(tcn + 1) * 128],
                                           rhs=Wd[:, fb, hf * 512:(hf + 1) * 512], start=(fb == 0), stop=(fb == NFB - 1))
                        return ins
                    P.op("pe", mmd, [aT, Wd], [pd])
                P.op("dve", lambda e, y=y: e.tensor_tensor(out=y[:], in0=pd[:].rearrange("p a b -> p (a b)"),
                                                            in1=g.gbc[:, D:2 * D], op=OP.mult), [pd, g.gbc], [y])
                P.op("pool", lambda e, y=y, l=l: e.tensor_tensor(out=y[:], in0=y[:], in1=l[:], op=OP.add), [y, l], [y])
                P.op("act", lambda e, y=y, s4=s4: e.activation(out=jk[:], in_=y[:], func=AF.Square,
                                                               accum_out=s4[:, 0:1]), [y], [jk, s4])
                P.op("dve", lambda e, s4=s4: e.tensor_scalar(out=s4[:, 1:2], in0=s4[:, 0:1], scalar1=1.0 / D,
                                                              scalar2=EPS, op0=OP.mult, op1=OP.add), [s4], [s4])
                P.op("act", lambda e, s4=s4: e.activation(out=s4[:, 2:3], in_=s4[:, 1:2], func=AF.Sqrt), [s4], [s4])
                P.op("dve", lambda e, s4=s4: e.reciprocal(out=s4[:, 3:4], in_=s4[:, 2:3]), [s4], [s4])
                P.op("dve", lambda e, y=y, s4=s4, o=o: e.scalar_tensor_tensor(
                    out=o[:], in0=y[:], scalar=s4[:, 3:4], in1=fg[:], op0=OP.mult, op1=OP.mult), [y, s4, fg], [y])
                P.dma("sp", g.out[o0:o0 + 128, :], o[:], reads=[o])
        P.barrier()


def _pm(v):
    v = np.asarray(v, np.float32)
    return np.ascontiguousarray(v.reshape(-1, 128).T)


def _rb(v):
    v = np.asarray(v, np.float32).reshape(1, -1)
    return np.ascontiguousarray(np.broadcast_to(v, (128, v.shape[1])))


def _const_tables():
    k = np.arange(128)[:, None]
    m = np.arange(128)[None, :]
    mats = [np.ones((128, 128)), k <= m, k >= m, k > m, k < m, k == m,
            np.where(m < k, -BIG, 0.0), np.where(m > k, -BIG, 0.0)]
    cmk = np.concatenate([np.asarray(a, np.float32) for a in mats], axis=1)
    t1i = np.arange(64)[:, None] * np.arange(64)[None, :]
    th = 2 * np.pi * t1i / 64.0
    t1 = np.concatenate([np.cos(th), -np.sin(th)], axis=1).astype(np.float32)
    j = np.arange(128)[:, None] * np.arange(128)[None, :]
    thc = 2 * np.pi * j / 128.0
    tcs = (np.concatenate([np.cos(thc), np.sin(thc)], axis=1) / 1024.0).astype(np.float32)
    return cmk, t1, tcs


def _t2_tables(q):
    t2 = np.arange(128, dtype=np.float64)[:, None, None]
    k1 = np.arange(64, dtype=np.float64)[None, :, None]
    k2 = (32 * q - 1 + np.arange(34, dtype=np.float64))[None, None, :]
    kk = np.mod(k1 + 64 * k2, 8192)
    th = 2 * np.pi * np.mod(kk * t2, 8192) / 8192.0
    Mr, Mi = np.cos(th), -np.sin(th)
    ta = np.concatenate([Mr, Mi], axis=2)
    tb = np.concatenate([-Mi, Mr], axis=2)
    return np.concatenate([ta.reshape(128, -1), tb.reshape(128, -1)], axis=1).astype(np.float32)


_CACHE = {}


def kernel(x, c, ctx, c_ctx, w_mod, b_mod, norm1_g, w_in, conv_ssd_w, conv_ssd_b, dt_bias, a_log,
           d_skip, ssd_norm_g, w_fa, w_sb, w_o, norm2_g, w_up, conv_ffn_w, conv_ffn_b, w_down, final_g):
    f = lambda a: np.asarray(a, np.float32)
    x, c, ctx, c_ctx = f(x), f(c), f(ctx), f(c_ctx)
    cmk, t1, tcs = _const_tables()
    in_maps = []
    bm = f(b_mod)[0]
    for core in range(8):
        b, q = divmod(core, 4)
        e0 = 2048 * q - 64
        xext = np.zeros((EXT + 128, D), np.float32)
        lo, hi = max(e0, 0), min(e0 + EXT, SEQ)
        xext[lo - e0:hi - e0] = x[b, lo:hi]
        hv = np.zeros(2, np.float32)
        if e0 - 1 >= 0:
            xext[EXT] = x[b, e0 - 1]; hv[0] = 1
        if e0 + EXT < SEQ:
            xext[EXT + 1] = x[b, e0 + EXT]; hv[1] = 1
        tok = e0 + np.arange(EXT)
        valid = ((tok >= 0) & (tok < SEQ)).astype(np.float32)
        fmask = np.zeros((NSLOT, 128, 2), np.float32)
        fmask[0:2, :, 0] = 1
        fmask[66:68, :, 1] = 1
        lt = np.arange(SEQ).reshape(NCH, 128)
        fmask[2:66, :, 0] = (lt < e0)
        fmask[2:66, :, 1] = (lt >= e0 + EXT)
        cpp_a = np.zeros((128, CPP_N), np.float32)

        def put(key, arr):
            o, n = CPP[key]
            cpp_a[:, o:o + n] = arr
        put("c", _pm(c[b])); put("cctx", _pm(c_ctx)); put("bmod", _pm(bm))
        put("n1g", _pm(f(norm1_g)[0])); put("n2g", _pm(f(norm2_g)[0]))
        put("cw_ssd", np.concatenate([_pm(f(conv_ssd_w)[0, t]) for t in range(3)], axis=1))
        put("cb_ssd", _pm(f(conv_ssd_b)[0]))
        cfw = f(conv_ffn_w)[0].reshape(9, 2 * DFF)
        put("cw_ffn", np.concatenate([_pm(cfw[t]) for t in range(9)], axis=1))
        put("cb_ffn", _pm(f(conv_ffn_b)[0]))
        put("emask", valid.reshape(NEXT, 128).T)
        put("fmask", fmask.transpose(1, 0, 2).reshape(128, NSLOT * 2))
        cbc_a = np.zeros((128, CBC_N), np.float32)

        def putb(key, arr):
            o, n = CBC[key]
            cbc_a[:, o:o + n] = arr
        putb("dt_bias", _rb(f(dt_bias)[0].reshape(-1))); putb("a_log", _rb(f(a_log)[0].reshape(-1)))
        putb("d_skip", _rb(f(d_skip)[0]))
        cbg_a = np.concatenate([_rb(f(ssd_norm_g)[0]), _rb(f(final_g)), _rb(bm[2048:3072]), _rb(bm[5120:6144])], axis=1)
        putb("emask_bc", _rb(np.concatenate([valid[:128], valid[-128:]])))
        hvb = np.zeros(128, np.float32); hvb[0:2] = hv
        putb("halo_v", _rb(hvb))
        in_maps.append(dict(
            xb=np.ascontiguousarray(x[b]), ctxb=np.ascontiguousarray(ctx[b]), xext=xext,
            w_mod=f(w_mod)[0], w_in=f(w_in)[0], w_fa=f(w_fa)[0], w_sb=f(w_sb)[0], w_o=f(w_o)[0],
            w_up=f(w_up)[0], w_down=f(w_down)[0], cpp=cpp_a, cbc=cbc_a, cbg=cbg_a, cbrow=f(conv_ssd_b)[0].reshape(1, 3072).copy(), cmk=cmk, t1=t1, t2=_t2_tables(q), tcs=tcs))
    if "nc" not in _CACHE:
        _CACHE["nc"] = build_program()
    res = run_bass_kernel_spmd(_CACHE["nc"], in_maps, core_ids=list(range(8)))
    if DEBUG:
        _CACHE["res"] = res
    out = np.zeros((2, SEQ, D), np.float32)
    for core in range(8):
        b, q = divmod(core, 4)
        out[b, 2048 * q:2048 * (q + 1)] = res.results[core]["out"]
    return out
```

```python
import os
from contextlib import ExitStack
import numpy as np
import concourse.bass as bass
import concourse.mybir as mybir
from concourse.bass_utils import run_bass_kernel_spmd

F32 = mybir.dt.float32
BF16 = mybir.dt.bfloat16
AF = mybir.ActivationFunctionType
OP = mybir.AluOpType

D = 1024
SEQ = 8192
NCH = 64
NEXT = 17
EXT = NEXT * 128
EPS = 1e-6
BIG = 30000.0
NSLOT = 68
DFF = 2816
NFB = 22

STOP = os.environ.get("MK_STOP", "")
DEBUG = bool(STOP)


class Buf:
    def __init__(self, t, nparts=1):
        self.t = t
        self.n = nparts
        self.w = [None] * nparts
        self.r = [[] for _ in range(nparts)]
        self.excl = False

    def __getitem__(self, idx):
        return self.t[idx]


def _parts(items):
    out = []
    for it in items:
        if it is None:
            continue
        if isinstance(it, Buf):
            out.extend((it, i) for i in range(it.n))
        else:
            b, idx = it
            if isinstance(idx, int):
                out.append((b, idx))
            else:
                out.extend((b, i) for i in idx)
    return out


class Prog:
    def __init__(self, nc, es):
        self.nc = nc
        self.E = {}
        self.semid = 0
        for name, eng in (("pe", nc.tensor), ("act", nc.scalar), ("dve", nc.vector), ("pool", nc.gpsimd), ("sp", nc.sync)):
            sem = es.enter_context(nc.semaphore("s_" + name))
            self.E[name] = dict(name=name, eng=eng, sem=(self._sid(), sem), count=0, waited={}, pool=[], ndma=0)
        for name, n in (("sp", 8), ("pool", 6), ("act", 4)):
            for i in range(n):
                sem = es.enter_context(nc.semaphore("d_%s%d" % (name, i)))
                self.E[name]["pool"].append((self._sid(), sem))
        self.ninst = 0

    def _sid(self):
        self.semid += 1
        return self.semid

    def _wait(self, E, tok):
        (sid, sem), val, _ = tok
        if E["waited"].get(sid, 0) >= val:
            return
        E["eng"].wait_ge(sem, val)
        E["waited"][sid] = val

    def _collect(self, en, reads, writes):
        toks = []
        for b, i in _parts(reads):
            if b.w[i] is not None:
                toks.append(b.w[i])
            if b.excl:
                toks.extend(t for t in b.r[i] if t[2] != en)
        for b, i in _parts(writes):
            if b.w[i] is not None:
                toks.append(b.w[i])
            toks.extend(b.r[i])
        res = []
        for t in toks:
            if en == "pe" and t[2] == "pe":
                continue
            res.append(t)
        return res

    def _update(self, reads, writes, tok):
        for b, i in _parts(reads):
            b.r[i].append(tok)
            if len(b.r[i]) > 24:
                last = {}
                for t in b.r[i]:
                    k = t[0][0]
                    if k not in last or last[k][1] < t[1]:
                        last[k] = t
                b.r[i] = list(last.values())
        for b, i in _parts(writes):
            b.w[i] = tok
            b.r[i] = []

    def op(self, en, fn, reads=(), writes=()):
        E = self.E[en]
        for t in self._collect(en, reads, writes):
            self._wait(E, t)
        ins = fn(E["eng"])
        E["count"] += 1
        ins.then_inc(E["sem"][1], 1)
        tok = (E["sem"], E["count"], en)
        self._update(reads, writes, tok)
        self.ninst += 1
        return tok

    def dma(self, qn, out, in_, reads=(), writes=(), **kw):
        Q = self.E[qn]
        i = Q["ndma"]
        P = len(Q["pool"])
        sem = Q["pool"][i % P]
        val = 16 * (i // P + 1)
        if i >= P:
            self._wait(Q, (sem, val - 16, "dma"))
        for t in self._collect("dma", reads, writes):
            self._wait(Q, t)
        Q["eng"].dma_start(out=out, in_=in_, **kw).then_inc(sem[1], 16)
        Q["ndma"] += 1
        tok = (sem, val, "dma")
        self._update(reads, writes, tok)
        return tok

    def all_tokens(self):
        toks = []
        for E in self.E.values():
            if E["count"]:
                toks.append((E["sem"], E["count"], E["name"]))
            P = len(E["pool"])
            for j in range(min(P, E["ndma"])):
                n = (E["ndma"] - 1 - j) // P + 1
                toks.append((E["pool"][j], 16 * n, "dma"))
        return toks

    def barrier(self):
        toks = self.all_tokens()
        for E in self.E.values():
            for t in toks:
                self._wait(E, t)


def bc(ap, shape):
    return ap.broadcast_to(shape)


class Ctx:
    pass


def build_program():
    nc = bass.Bass("TRN2", target_bir_lowering=False)
    g = Ctx()
    g.nc = nc

    def din(name, shape, dt=F32):
        return nc.dram_tensor(name, list(shape), dt, kind="ExternalInput").ap()

    def dscr(name, shape, dt):
        kind = "ExternalOutput" if DEBUG else "Internal"
        return nc.dram_tensor(name, list(shape), dt, kind=kind).ap()

    g.xb = din("xb", [SEQ, D])
    g.ctxb = din("ctxb", [256, D])
    g.xext = din("xext", [EXT + 128, D])
    g.w_mod = din("w_mod", [D, 6 * D])
    g.w_in = din("w_in", [D, 8256])
    g.w_fa = din("w_fa", [D, D])
    g.w_sb = din("w_sb", [2048, D])
    g.w_o = din("w_o", [D, D])
    g.w_up = din("w_up", [D, 2 * DFF])
    g.w_down = din("w_down", [DFF, D])
    g.cpp = din("cpp", [128, CPP_N])
    g.cbc = din("cbc", [128, CBC_N])
    g.cbg = din("cbg", [128, CBG_N])
    g.cbrow = din("cbrow", [1, 3072])
    g.cmk = din("cmk", [128, 8 * 128])
    g.t1 = din("t1", [64, 128])
    g.t2 = din("t2", [128, 2 * 64 * 68])
    g.tcs = din("tcs", [128, 256])
    g.out = nc.dram_tensor("out", [2048, D], F32, kind="ExternalOutput").ap()
    g.U = dscr("U", [NCH, 128, D], BF16)
    g.Y = dscr("Y", [128, 128, D], BF16)
    g.XT = dscr("XT", [128, 8 * 2 * EXT], BF16)
    g.MS = dscr("MS", [NEXT, 128, D], BF16)
    g.YF = dscr("YF", [NEXT, 128, 2048], BF16)
    g.YT = dscr("YT", [NEXT, 128, 2048], BF16)
    g.L1 = dscr("L1", [NEXT, 128, D], F32)
    g.SFB = dscr("SFB", [2, 128, 2048], F32)

    with ExitStack() as es:
        P = Prog(nc, es)
        g.P = P
        g.uid = 0
        phase_setup(g, es)
        with ExitStack() as es2:
            alloc_conv_consts(g, es2)
            if STOP != "setup":
                phase_far(g)
            if STOP not in ("setup", "far"):
                phase_fnet(g)
            if STOP not in ("setup", "far", "fnet"):
                phase_own(g, 0)
                phase_own(g, 1)
            P.barrier()
        if STOP not in ("setup", "far", "fnet", "own"):
            phase_merge(g)
        if STOP not in ("setup", "far", "fnet", "own", "merge"):
            phase_ffn(g)
        P.barrier()
    return nc


def sb(g, es, name, shape, dt, nparts=1):
    g.uid += 1
    t = es.enter_context(g.nc.sbuf_tensor("%s_%d" % (name, g.uid), list(shape), dt))
    return Buf(t, nparts)


def ps(g, es, name, shape, dt=F32, nparts=1):
    g.uid += 1
    t = es.enter_context(g.nc.psum_tensor("%s_%d" % (name, g.uid), list(shape), dt))
    b = Buf(t, nparts)
    b.excl = True
    return b


def _layout(items):
    off = {}
    o = 0
    for k, n in items:
        off[k] = (o, n)
        o += n
    return off, o


CPP, CPP_N = _layout([("c", 8), ("cctx", 8), ("bmod", 48), ("n1g", 8), ("n2g", 8), ("cw_ssd", 72), ("cb_ssd", 24),
                      ("cw_ffn", 9 * 44), ("cb_ffn", 44), ("emask", NEXT), ("fmask", NSLOT * 2)])
CBC, CBC_N = _layout([("dt_bias", 64), ("a_log", 64), ("d_skip", 32), ("emask_bc", 256), ("halo_v", 128)])
CBG, CBG_N = _layout([("ssd_g", 2048), ("final_g", 1024), ("bmod_g1", 1024), ("bmod_g2", 1024)])
MK = {k: i for i, k in enumerate(["ones", "le", "ge", "gt", "lt", "ident", "pen_f", "pen_b"])}


def cpp(g, key, j=None, n=1):
    o, _ = CPP[key]
    if j is None:
        return g.cpp_t[:, o:o + CPP[key][1]]
    return g.cpp_t[:, o + j:o + j + n]


def cbcv(g, key, a=0, n=None):
    o, m = CBC[key]
    if n is None:
        n = m
    return g.cbc_t[:, o + a:o + a + n]


def mk32(g, key):
    i = MK[key]
    return g.cmk_t[:, i * 128:(i + 1) * 128]


def mk16(g, key):
    i = MK[key]
    return g.cmkb_t[:, i * 128:(i + 1) * 128]


def phase_setup(g, es):
    nc, P = g.nc, g.P
    g.cpp_t = sb(g, es, "cpp", [128, CPP_N], F32)
    g.cbc_t = sb(g, es, "cbc", [128, CBC_N], F32)
    g.cmk_t = sb(g, es, "cmk", [128, 8 * 128], F32)
    g.cmkb_t = sb(g, es, "cmkb", [128, 8 * 128], BF16)
    g.modv = sb(g, es, "modv", [128, 8 * 8], F32)
    g.gbc = sb(g, es, "gbc", [128, 2 * D], F32)
    g.nega = sb(g, es, "nega", [128, 64], F32)
    g.Sf = sb(g, es, "Sf", [128, 2048], F32)
    g.Sb = sb(g, es, "Sb", [128, 2048], F32)
    P.dma("sp", g.cpp_t[:], g.cpp, writes=[g.cpp_t])
    P.dma("sp", g.cbc_t[:], g.cbc, writes=[g.cbc_t])
    P.dma("sp", g.cmk_t[:], g.cmk, writes=[g.cmk_t])
    P.dma("pool", g.cmkb_t[:], g.cmk, writes=[g.cmkb_t])
    with ExitStack() as ls:
        sc = sb(g, ls, "sc", [128, 8, 2], F32)
        screp = sb(g, ls, "screp", [128, 8, 128], F32)
        modT = sb(g, ls, "modT", [128, 48, 2], F32)
        wm = [sb(g, ls, "wm%d" % i, [128, 8, 1024], F32) for i in range(2)]
        pm = ps(g, ls, "pm", [128, 8, 2], F32)
        pg = ps(g, ls, "pg", [128, 512], F32)
        bg = sb(g, ls, "bg", [128, 2 * D], F32)
        P.dma("sp", bg[:], g.cbg[:, CBG["bmod_g1"][0]:CBG["bmod_g1"][0] + 2 * D], writes=[bg])
        P.op("act", lambda e: e.activation(out=sc[:, :, 0], in_=cpp(g, "c"), func=AF.Silu), [g.cpp_t], [sc])
        P.op("act", lambda e: e.activation(out=sc[:, :, 1], in_=cpp(g, "cctx"), func=AF.Silu), [g.cpp_t], [sc])
        P.op("dve", lambda e: e.tensor_copy(out=screp[:], in_=bc(sc[:, :, 0:1], [128, 8, 128])), [sc], [screp])
        wv = g.w_mod.rearrange("(kb p) n -> p kb n", p=128)
        for j in range(6):
            w = wm[j % 2]
            P.dma("sp", w[:], wv[:, :, j * 1024:(j + 1) * 1024], writes=[w])
            def mm(e, w=w):
                ins = None
                for fb in range(8):
                    for kb in range(8):
                        ins = e.matmul(pm[:, fb, :], lhsT=w[:, kb, fb * 128:(fb + 1) * 128], rhs=sc[:, kb, :],
                                       start=(kb == 0), stop=(kb == 7))
                return ins
            P.op("pe", mm, [w, sc], [pm])
            bo = CPP["bmod"][0] + j * 8
            P.op("dve", lambda e, j=j, bo=bo: e.tensor_tensor(
                out=modT[:, j * 8:(j + 1) * 8, :], in0=pm[:], in1=bc(g.cpp_t[:, bo:bo + 8].unsqueeze(2), [128, 8, 2]),
                op=OP.add), [pm, g.cpp_t], [modT])
            if j in (2, 5):
                gi = 0 if j == 2 else 1
                for hf in range(2):
                    def mg(e, w=w, hf=hf):
                        ins = None
                        for kb in range(8):
                            ins = e.matmul(pg[:], lhsT=screp[:, kb, :], rhs=w[:, kb, hf * 512:(hf + 1) * 512],
                                           start=(kb == 0), stop=(kb == 7))
                        return ins
                    P.op("pe", mg, [w, screp], [pg])
                    P.op("dve", lambda e, gi=gi, hf=hf: e.tensor_tensor(
                        out=g.gbc[:, gi * D + hf * 512: gi * D + (hf + 1) * 512], in0=pg[:],
                        in1=bg[:, gi * D + hf * 512: gi * D + (hf + 1) * 512], op=OP.add), [pg, bg], [g.gbc])
        mv = g.modv
        def mkA(dst, scale_j, which, gkey):
            P.op("dve", lambda e: e.scalar_tensor_tensor(
                out=mv[:, dst * 8:(dst + 1) * 8], in0=modT[:, scale_j * 8:(scale_j + 1) * 8, which], scalar=1.0,
                in1=cpp(g, gkey), op0=OP.add, op1=OP.mult), [modT, g.cpp_t], [mv])

        def mkB(dst, shift_j, which):
            P.op("dve", lambda e: e.tensor_copy(out=mv[:, dst * 8:(dst + 1) * 8],
                                                 in_=modT[:, shift_j * 8:(shift_j + 1) * 8, which]), [modT], [mv])
        mkA(0, 1, 0, "n1g"); mkB(1, 0, 0)
        mkA(2, 1, 1, "n1g"); mkB(3, 0, 1)
        mkA(4, 4, 0, "n2g"); mkB(5, 3, 0)
        P.op("act", lambda e: e.activation(out=g.nega[:], in_=cbcv(g, "a_log"), func=AF.Exp), [g.cbc_t], [g.nega])
        P.op("dve", lambda e: e.tensor_scalar(out=g.nega[:], in0=g.nega[:], scalar1=-1.0, scalar2=None, op0=OP.mult),
             [g.nega], [g.nega])
        P.barrier()


def alloc_conv_consts(g, es):
    P = g.P
    g.diag = sb(g, es, "diag", [128, 72, 128], BF16)
    g.brow = sb(g, es, "brow", [1, 3072], BF16)
    g.Wxbc = sb(g, es, "Wxbc", [128, 8, 3072], BF16)
    load_w_cols(g, g.Wxbc, 1024, 3072)
    for a_ in range(0, 3072, 1024):
        P.dma("pool", g.brow[:, a_:a_ + 1024], g.cbrow[:, a_:a_ + 1024], writes=[g.brow])
    for i_ in range(72):
        P.op("dve", lambda e, i_=i_: e.tensor_scalar(out=g.diag[:, i_, :], in0=mk16(g, "ident"),
                                                      scalar1=cpp(g, "cw_ssd", i_), scalar2=None, op0=OP.mult),
             [g.cmkb_t, g.cpp_t], [g.diag])


def modA(g, i, kb):
    return g.modv[:, i * 8 + kb:i * 8 + kb + 1]


def alloc_chunk_bufs(g, es, nfb):
    c = Ctx()
    c.xt = [sb(g, es, "xt%d" % i, [128, D], F32) for i in range(2)]
    c.junk = sb(g, es, "junk", [128, D], BF16)
    c.st = [sb(g, es, "st%d" % i, [128, 4], F32) for i in range(2)]
    c.xn = [sb(g, es, "xn%d" % i, [128, D], BF16) for i in range(2)]
    c.hTe = [sb(g, es, "hTe%d" % i, [128, 8, 130], BF16) for i in range(3)]
    c.pA = ps(g, es, "pA", [128, 8, 128], BF16)
    c.nfb = nfb
    return c


def prep_a(g, c, k, src_rows):
    P = g.P
    i2 = k % 2
    xt, st, xn = c.xt[i2], c.st[i2], c.xn[i2]
    P.dma("sp", xt[:], src_rows, writes=[xt])
    P.op("act", lambda e: e.activation(out=c.junk[:], in_=xt[:], func=AF.Square, accum_out=st[:, 0:1]),
         [xt], [c.junk, st])
    P.op("dve", lambda e: e.tensor_scalar(out=st[:, 1:2], in0=st[:, 0:1], scalar1=1.0 / D, scalar2=EPS,
                                           op0=OP.mult, op1=OP.add), [st], [st])
    P.op("act", lambda e: e.activation(out=st[:, 2:3], in_=st[:, 1:2], func=AF.Ln), [st], [st])
    P.op("act", lambda e: e.activation(out=st[:, 3:4], in_=st[:, 2:3], func=AF.Exp, scale=-0.5), [st], [st])
    P.op("act", lambda e: e.activation(out=xn[:], in_=xt[:], func=AF.Copy, scale=st[:, 3:4]), [xt, st], [xn])
    return xt


def prep_b(g, c, k, ai, bi, vmask=None, hT=None):
    P = g.P
    xn = c.xn[k % 2]
    if hT is None:
        hT = c.hTe[k % 3]

    def tr(e):
        ins = None
        for kb in range(8):
            ins = e.transpose(out=c.pA[:, kb, :], in_=xn[:, kb * 128:(kb + 1) * 128], identity=mk16(g, "ident"))
        return ins
    P.op("pe", tr, [xn, g.cmkb_t], [c.pA])
    P.op("dve", lambda e: e.tensor_tensor(out=hT[:, :, 1:129], in0=c.pA[:],
                                          in1=bc(g.modv[:, ai * 8:(ai + 1) * 8].unsqueeze(2), [128, 8, 128]),
                                          op=OP.mult), [c.pA, g.modv], [hT])
    P.op("dve", lambda e: e.tensor_tensor(out=hT[:, :, 1:129], in0=hT[:, :, 1:129],
                                          in1=bc(g.modv[:, bi * 8:(bi + 1) * 8].unsqueeze(2), [128, 8, 128]),
                                          op=OP.add), [hT, g.modv], [hT])
    if vmask is not None:
        P.op("pool", lambda e: e.tensor_tensor(out=hT[:, :, 1:129], in0=hT[:, :, 1:129],
                                               in1=bc(vmask.unsqueeze(1), [128, 8, 128]), op=OP.mult),
             [hT, g.cbc_t], [hT])


def prep(g, c, k, src_rows, ai, bi, vmask=None, hT=None):
    xt = prep_a(g, c, k, src_rows)
    prep_b(g, c, k, ai, bi, vmask=vmask, hT=hT)
    return xt


def halo_link(g, c, k, has_left):
    P = g.P
    cur = c.hTe[k % 3]
    if has_left:
        prv = c.hTe[(k - 1) % 3]
        P.op("pool", lambda e: e.tensor_copy(out=cur[:, :, 0:1], in_=prv[:, :, 128:129]), [prv], [cur])
        P.op("pool", lambda e: e.tensor_copy(out=prv[:, :, 129:130], in_=cur[:, :, 1:2]), [cur], [prv])
    else:
        P.op("pool", lambda e: e.memset(cur[:, :, 0:1], 0.0), [], [cur])


def halo_zero_right(g, c, k):
    cur = c.hTe[k % 3]
    g.P.op("pool", lambda e: e.memset(cur[:, :, 129:130], 0.0), [], [cur])


def fm_proj_conv(g, c, s, hT, W, nfb, cw_off, steps=None):
    P = g.P
    steps = steps if steps is not None else []
    groups = [list(range(a, min(a + 3, nfb))) for a in range(0, nfb, 3)]
    def e_proj(gi):
        fbs = groups[gi]
        pa = s.pxa[gi % len(s.pxa)]

        def mm(e, fbs=fbs, pa=pa):
            ins = None
            for j, fb in enumerate(fbs):
                for kb in range(8):
                    ins = e.matmul(pa[:, j * 130:(j + 1) * 130], lhsT=W[:, kb, fb * 128:(fb + 1) * 128], rhs=hT[:, kb, :],
                                   start=(kb == 0), stop=(kb == 7))
            return ins
        P.op("pe", mm, [W, hT], [pa])

    def e_rest(gi):
        fbs = groups[gi]
        n = len(fbs)
        pa = s.pxa[gi % len(s.pxa)]
        pb = s.pxc[gi % len(s.pxc)]
        pre = s.pre[gi % 2]
        P.op("dve", lambda e, n=n, pa=pa, pre=pre: e.tensor_copy(
            out=pre[:, 0:n, :], in_=pa[:, 0:n * 130].rearrange("p (j t) -> p j t", t=130)), [pa], [pre])

        def mc(e, fbs=fbs, pb=pb, pre=pre):
            ins = None
            for j, fb in enumerate(fbs):
                cf = cw_off + fb
                for k in range(3):
                    e.matmul(pb[:, j * 128:(j + 1) * 128], lhsT=g.diag[:, k * 24 + cf, :], rhs=pre[:, j, k:k + 128],
                             start=(k == 0), stop=False)
                ins = e.matmul(pb[:, j * 128:(j + 1) * 128], lhsT=g.brow[0:1, cf * 128:(cf + 1) * 128],
                               rhs=mk16(g, "ones")[0:1, :], start=False, stop=True)
            return ins
        P.op("pe", mc, [pre, g.diag, g.brow, g.cmkb_t], [pb])
        f0, f1 = fbs[0], fbs[-1] + 1
        P.op("act", lambda e, f0=f0, f1=f1, n=n, pb=pb: e.activation(
            out=s.xcs[:, f0:f1, :], in_=pb[:, 0:n * 128].rearrange("p (j t) -> p j t", t=128), func=AF.Silu),
            [pb], [(s.xcs, gi)])

    ng = len(groups)
    ahead = len(s.pxa) >= 2
    if ahead:
        e_proj(0)
    for gi in range(ng):
        if ahead:
            if gi + 1 < ng:
                e_proj(gi + 1)
        else:
            e_proj(gi)
        e_rest(gi)
        if steps:
            steps.pop(0)()
    while steps:
        steps.pop(0)()


def to_token_major(g, c, s, nblk, between=None):
    P = g.P
    between = between if between is not None else []
    for r0 in range(0, nblk, 8):
        n = min(8, nblk - r0)

        def tr(e, r0=r0, n=n):
            ins = None
            for j in range(n):
                ins = e.transpose(out=c.pA[:, j, :], in_=s.xcs[:, r0 + j, :], identity=mk16(g, "ident"))
            return ins
        P.op("pe", tr, [s.xcs, g.cmkb_t], [c.pA])
        P.op("act", lambda e, r0=r0, n=n: e.activation(
            out=s.xtok[:, r0 * 128:(r0 + n) * 128], in_=c.pA[:, 0:n, :], func=AF.Copy), [c.pA], [s.xtok])
        if between:
            between.pop(0)()
    while between:
        between.pop(0)()


def dt_steps(g, s, hT, Wdt, ncol, bias_ap, nega_ap, mask_ap):
    P = g.P

    def s1():
        def mm(e):
            ins = None
            for kb in range(8):
                ins = e.matmul(s.pD[:, 0:ncol], lhsT=hT[:, kb, 1:129], rhs=Wdt[:, kb, 0:ncol], start=(kb == 0), stop=(kb == 7))
            return ins
        P.op("pe", mm, [hT, Wdt], [s.pD])
        P.op("dve", lambda e: e.tensor_tensor(out=s.dtm[:, 0:ncol], in0=s.pD[:, 0:ncol], in1=bias_ap, op=OP.add),
             [s.pD, g.cbc_t], [s.dtm])

    def s2():
        P.op("act", lambda e: e.activation(out=s.dtm[:, 0:ncol], in_=s.dtm[:, 0:ncol], func=AF.Exp), [s.dtm], [s.dtm])

    def s3():
        P.op("act", lambda e: e.activation(out=s.dtm[:, 0:ncol], in_=s.dtm[:, 0:ncol], func=AF.Ln, bias=1.0), [s.dtm], [s.dtm])

    def s4():
        dv = s.dtm[:, 0:ncol].rearrange("p (a b) -> p a b", b=32)
        P.op("dve", lambda e: e.tensor_tensor(out=dv, in0=dv, in1=mask_ap, op=OP.mult), [s.dtm, g.cpp_t], [s.dtm])

    def s5():
        P.op("dve", lambda e: e.tensor_tensor(out=s.la[:, 0:ncol], in0=s.dtm[:, 0:ncol], in1=nega_ap, op=OP.mult),
             [s.dtm, g.nega], [s.la])
    return [s1, s2, s3, s4, s5]


def state_contrib(g, s, wexp_ap, xdd, on_group):
    P = g.P
    P.op("dve", lambda e: e.tensor_tensor(
        out=xdd[:].rearrange("p (h d) -> p h d", d=64), in0=s.xtok[:, 0:2048].rearrange("p (h d) -> p h d", d=64),
        in1=bc(wexp_ap.unsqueeze(2), [128, 32, 64]), op=OP.mult), [s.xtok, s.wx], [xdd])
    for gi in range(4):
        P.op("pe", lambda e, gi=gi: e.matmul(s.pH[:], lhsT=s.xtok[:, 2048 + gi * 128:2048 + (gi + 1) * 128],
                                             rhs=xdd[:, gi * 512:(gi + 1) * 512], start=True, stop=True),
             [s.xtok, xdd], [s.pH])
        on_group(gi, s.pH)


def load_w_cols(g, W, col0, ncols, dst0=0):
    wv = g.w_in.rearrange("(kb p) n -> p kb n", p=128)
    for a in range(0, ncols, 512):
        n = min(512, ncols - a)
        g.P.dma("pool", W[:, :, dst0 + a:dst0 + a + n], wv[:, :, col0 + a:col0 + a + n], writes=[W])


def phase_far(g):
    nc, P = g.nc, g.P
    with ExitStack() as es:
        c = alloc_chunk_bufs(g, es, 20)
        s = Ctx()
        Wf = sb(g, es, "Wf", [128, 8, 1024], BF16)
        Wxb = g.Wxbc
        Wdt = sb(g, es, "Wdt", [128, 8, 64], BF16)
        load_w_cols(g, Wf, 0, 1024)
        load_w_cols(g, Wdt, 6144, 64)
        s.pxa = [ps(g, es, "pxa%d" % i, [128, 512], F32) for i in range(2)]
        s.pxc = [ps(g, es, "pxc%d" % i, [128, 512], F32) for i in range(1)]
        s.pD = ps(g, es, "pD", [128, 512], F32)
        s.pH = ps(g, es, "pH", [128, 512], F32)
        pf = [ps(g, es, "pf%d" % i, [128, 512], F32) for i in range(2)]
        s.pre = [sb(g, es, "pre%d" % i, [128, 3, 130], BF16) for i in range(2)]
        s.xcs = sb(g, es, "xcs", [128, 20, 128], BF16, nparts=8)
        s.pD2 = sb(g, es, "pD2", [128, 192], F32)
        s.xtok = sb(g, es, "xtok", [128, 2560], BF16)
        s.dtm = sb(g, es, "dtm", [128, 64], F32)
        s.la = sb(g, es, "la", [128, 64], F32)
        s.wx = sb(g, es, "wx", [128, 64], F32)
        s.sg = sb(g, es, "sg", [128, 64], F32)
        Rb = sb(g, es, "Rb", [128, 32], F32)
        wxb = sb(g, es, "wxb", [128, 64], BF16)
        dec = sb(g, es, "dec", [128, 32], F32)
        xdd = [sb(g, es, "xdd%d" % i, [128, 2048], BF16) for i in range(2)]
        ub = [sb(g, es, "ub%d" % i, [128, D], BF16) for i in range(2)]
        P.op("dve", lambda e: e.memset(g.Sf[:], 0.0), [], [g.Sf])
        P.op("dve", lambda e: e.memset(g.Sb[:], 0.0), [], [g.Sb])
        P.op("dve", lambda e: e.memset(Rb[:], 0.0), [], [Rb])
        slots = [("c", 0), ("c", 1)] + [("l", i) for i in range(NCH)] + [("c", 0), ("c", 1)]
        first = {0, 2, 66}
        last = {1, 65, 67}

        def do_prep_a(k):
            kind, i = slots[k]
            src = g.ctxb[i * 128:(i + 1) * 128, :] if kind == "c" else g.xb[i * 128:(i + 1) * 128, :]
            prep_a(g, c, k, src)

        def do_prep_b(k):
            kind, i = slots[k]
            if kind == "c":
                prep_b(g, c, k, 2, 3)
            else:
                prep_b(g, c, k, 0, 1)
            halo_link(g, c, k, k not in first)
            if k in last:
                halo_zero_right(g, c, k)
        do_prep_a(0); do_prep_b(0)
        do_prep_a(1); do_prep_b(1)
        for k in range(NSLOT):
            kind, i = slots[k]
            hT = c.hTe[k % 3]
            fo = CPP["fmask"][0] + 2 * k
            mask_ap = bc(g.cpp_t[:, fo:fo + 2].unsqueeze(2), [128, 2, 32])
            steps = dt_steps(g, s, hT, Wdt, 64, cbcv(g, "dt_bias"), g.nega[:], mask_ap)

            def t1():
                def segs(e):
                    e.matmul(s.pD[:, 64:96], lhsT=mk32(g, "gt"), rhs=s.la[:, 0:32], start=True, stop=True)
                    e.matmul(s.pD[:, 96:128], lhsT=mk32(g, "lt"), rhs=s.la[:, 32:64], start=True, stop=True)
                    return e.matmul(s.pD[:, 128:192], lhsT=mk32(g, "ones"), rhs=s.la[:, 0:64], start=True, stop=True)
                P.op("pe", segs, [s.la, g.cmk_t], [s.pD])
                P.op("dve", lambda e: e.tensor_copy(out=s.pD2[:], in_=s.pD[:, 0:192]), [s.pD], [s.pD2])

            def t2b():
                P.op("pool", lambda e: e.tensor_copy(out=s.sg[:, 0:32], in_=s.pD2[:, 64:96]), [s.pD2], [s.sg])
                P.op("pool", lambda e: e.tensor_tensor(out=s.sg[:, 32:64], in0=s.pD2[:, 96:128], in1=Rb[:], op=OP.add),
                     [s.pD2, Rb], [s.sg])
                P.op("pool", lambda e: e.tensor_tensor(out=Rb[:], in0=Rb[:], in1=s.pD2[:, 160:192], op=OP.add),
                     [s.pD2, Rb], [Rb])

            def t3():
                P.op("act", lambda e: e.activation(out=s.wx[:], in_=s.sg[:], func=AF.Exp), [s.sg], [s.wx])
                P.op("act", lambda e: e.activation(out=dec[:], in_=s.pD2[:, 128:160], func=AF.Exp), [s.pD2], [dec])

            def t4():
                P.op("pool", lambda e: e.tensor_tensor(out=wxb[:], in0=s.wx[:], in1=s.dtm[:], op=OP.mult),
                     [s.wx, s.dtm], [wxb])
                P.op("pool", lambda e: e.tensor_tensor(
                    out=g.Sf[:].rearrange("p (h d) -> p h d", d=64), in0=g.Sf[:].rearrange("p (h d) -> p h d", d=64),
                    in1=bc(dec[:].unsqueeze(2), [128, 32, 64]), op=OP.mult), [g.Sf, dec], [g.Sf])
            for st_ in steps + [t1, t2b, t3, t4]:
                st_()
            if k + 2 < NSLOT:
                do_prep_a(k + 2)
            fsteps = []
            if kind == "l":
                u = ub[i % 2]

                def fhalf(hf, u=u, hT=hT, i=i):
                    def mm(e):
                        ins = None
                        for kb in range(8):
                            ins = e.matmul(pf[hf][:], lhsT=hT[:, kb, 1:129], rhs=Wf[:, kb, hf * 512:(hf + 1) * 512],
                                           start=(kb == 0), stop=(kb == 7))
                        return ins
                    P.op("pe", mm, [hT, Wf], [pf[hf]])
                    P.op("dve", lambda e: e.tensor_copy(out=u[:, hf * 512:(hf + 1) * 512], in_=pf[hf][:]),
                         [pf[hf]], [u])
                    if hf == 1:
                        P.dma("sp", g.U[i], u[:], reads=[u])
                fsteps = [lambda: fhalf(0), lambda: fhalf(1)]
            fm_proj_conv(g, c, s, hT, Wxb, 20, 0, [])
            if k + 2 < NSLOT:
                do_prep_b(k + 2)
            to_token_major(g, c, s, 20, fsteps)
            x3 = s.xtok[:, 0:2048].rearrange("p (h d) -> p h d", d=64)
            P.op("pool", lambda e: e.tensor_tensor(out=xdd[1][:].rearrange("p (h d) -> p h d", d=64), in0=x3,
                                                   in1=bc(wxb[:, 32:64].unsqueeze(2), [128, 32, 64]), op=OP.mult),
                 [s.xtok, wxb], [xdd[1]])
            P.op("dve", lambda e: e.tensor_tensor(out=xdd[0][:].rearrange("p (h d) -> p h d", d=64), in0=x3,
                                                  in1=bc(wxb[:, 0:32].unsqueeze(2), [128, 32, 64]), op=OP.mult),
                 [s.xtok, wxb], [xdd[0]])
            banks = [s.pxa[0], s.pxa[1], s.pxc[0], s.pH]
            for di, (xd_, S_) in enumerate(((xdd[0], g.Sf), (xdd[1], g.Sb))):
                for gi in range(4):
                    pst = banks[gi]
                    P.op("pe", lambda e, gi=gi, pst=pst, xd_=xd_: e.matmul(
                        pst[:], lhsT=s.xtok[:, 2048 + gi * 128:2048 + (gi + 1) * 128],
                        rhs=xd_[:, gi * 512:(gi + 1) * 512], start=True, stop=True), [s.xtok, xd_], [pst])
                    P.op("dve", lambda e, gi=gi, pst=pst, S_=S_: e.tensor_tensor(
                        out=S_[:, gi * 512:(gi + 1) * 512], in0=S_[:, gi * 512:(gi + 1) * 512], in1=pst[:], op=OP.add),
                        [S_, pst], [S_])
        if DEBUG:
            P.dma("sp", g.SFB[0], g.Sf[:], reads=[g.Sf])
            P.dma("sp", g.SFB[1], g.Sb[:], reads=[g.Sb])
        P.barrier()


def load_w_gen(g, W, src, nkb, ncols):
    wv = src.rearrange("(kb p) n -> p kb n", p=128)
    for a in range(0, ncols, 512):
        n = min(512, ncols - a)
        g.P.dma("pool", W[:, :, a:a + n], wv[:, :, a:a + n], writes=[W])


def phase_fnet(g):
    nc, P = g.nc, g.P
    with ExitStack() as es:
        T1 = sb(g, es, "T1", [64, 128], BF16)
        P.dma("pool", T1[:], g.t1, writes=[T1])
        V = [sb(g, es, "V%d" % i, [64, 4, D], BF16) for i in range(2)]
        Yt = [sb(g, es, "Yt%d" % i, [128, 4, D], BF16) for i in range(2)]
        p1 = [ps(g, es, "p1_%d" % i, [128, 512], F32) for i in range(4)]
        cnt = 0
        for tg in range(32):
            v, yt = V[tg % 2], Yt[tg % 2]
            P.dma("sp", v[:], g.U[:, tg * 4:(tg + 1) * 4, :], writes=[v])
            for t in range(4):
                for hf in range(2):
                    pp = p1[cnt % 4]
                    P.op("pe", lambda e, pp=pp, t=t, hf=hf, v=v: e.matmul(
                        pp[:], lhsT=T1[:], rhs=v[:, t, hf * 512:(hf + 1) * 512], start=True, stop=True), [T1, v], [pp])
                    if cnt % 2:
                        P.op("act", lambda e, pp=pp, t=t, hf=hf, yt=yt: e.activation(
                            out=yt[:, t, hf * 512:(hf + 1) * 512], in_=pp[:], func=AF.Copy), [pp], [yt])
                    else:
                        P.op("dve", lambda e, pp=pp, t=t, hf=hf, yt=yt: e.tensor_copy(
                            out=yt[:, t, hf * 512:(hf + 1) * 512], in_=pp[:]), [pp], [yt])
                    cnt += 1
            P.dma("sp", g.Y[:, tg * 4:(tg + 1) * 4, :], yt[:], reads=[yt])
        P.barrier()
    with ExitStack() as es:
        T2 = sb(g, es, "T2", [128, 2 * 64 * 68], BF16)
        for a in range(0, 2 * 64 * 68, 1088):
            P.dma("pool", T2[:, a:a + 1088], g.t2[:, a:a + 1088], writes=[T2])
        Yk = [sb(g, es, "Yk%d" % i, [128, 2, D], BF16) for i in range(2)]
        XTs = sb(g, es, "XTs", [128, 8, 2, EXT], BF16)
        p2f = [ps(g, es, "p2_%d" % i, [128, 512], F32) for i in range(4)]
        yv = g.Y.rearrange("(ri k) t c -> k t ri c", ri=2)
        xv = XTs[:].rearrange("p c r (j k) -> p c r j k", k=64)
        for k1 in range(64):
            yk = Yk[k1 % 2]
            P.dma("sp", yk[:], yv[k1], writes=[yk])
            for cg in range(2):
                ppb = p2f[(k1 * 2 + cg) % 4]
                pp = ppb[:, 0:272].rearrange("p (c k) -> p c k", k=68)

                def mm(e, pp=pp, cg=cg, yk=yk, k1=k1):
                    ins = None
                    for cb in range(4):
                        cbx = cg * 4 + cb
                        e.matmul(pp[:, cb, :], lhsT=yk[:, 0, cbx * 128:(cbx + 1) * 128],
                                 rhs=T2[:, k1 * 68:(k1 + 1) * 68], start=True, stop=False)
                        ins = e.matmul(pp[:, cb, :], lhsT=yk[:, 1, cbx * 128:(cbx + 1) * 128],
                                       rhs=T2[:, (64 + k1) * 68:(64 + k1 + 1) * 68], start=False, stop=True)
                    return ins
                P.op("pe", mm, [yk, T2], [ppb])
                for ri in range(2):
                    if (k1 + cg) % 2:
                        P.op("act", lambda e, pp=pp, cg=cg, ri=ri, k1=k1: e.activation(
                            out=xv[:, cg * 4:(cg + 1) * 4, ri, :, k1], in_=pp[:, :, ri * 34:(ri + 1) * 34],
                            func=AF.Copy), [ppb], [XTs])
                    else:
                        P.op("dve", lambda e, pp=pp, cg=cg, ri=ri, k1=k1: e.tensor_copy(
                            out=xv[:, cg * 4:(cg + 1) * 4, ri, :, k1], in_=pp[:, :, ri * 34:(ri + 1) * 34]),
                            [ppb], [XTs])
        for cb in range(8):
            P.dma("sp", g.XT[:, cb * 2 * EXT:(cb + 1) * 2 * EXT].rearrange("p (r t) -> p r t", r=2), XTs[:, cb, :, :],
                  reads=[XTs])
        P.barrier()


def phase_own(g, d):
    nc, P = g.nc, g.P
    with ExitStack() as es:
        c = alloc_chunk_bufs(g, es, 24)
        s = Ctx()
        hH = sb(g, es, "hH", [128, 8, 130], BF16)
        W = g.Wxbc
        Wdt = sb(g, es, "Wdt", [128, 8, 32], BF16)
        load_w_cols(g, Wdt, 6144 + 32 * d, 32)
        s.pxa = [ps(g, es, "pxa%d" % i, [128, 512], F32) for i in range(1)]
        s.pxc = [ps(g, es, "pxc%d" % i, [128, 512], F32) for i in range(1)]
        s.pxb = [s.pxa[0], s.pxc[0]]
        s.pre = [sb(g, es, "pre%d" % i, [128, 3, 130], BF16) for i in range(2)]
        s.pD = ps(g, es, "pD", [128, 512], F32)
        s.pH = ps(g, es, "pH", [128, 512], F32)
        s.pxa = [s.pxa[0], s.pH]
        psc = ps(g, es, "psc", [128, 4, 128], F32)
        pL = [ps(g, es, "pL%d" % i, [128, 4, 128], F32) for i in range(2)]
        s.xcs = sb(g, es, "xcs", [128, 24, 128], BF16, nparts=8)
        s.xtok = sb(g, es, "xtok", [128, 2560], BF16)
        s.dtm = sb(g, es, "dtm", [128, 32], F32)
        s.la = sb(g, es, "la", [128, 32], F32)
        s.wx = sb(g, es, "wx", [128, 32], F32)
        lab = sb(g, es, "lab", [128, 32], BF16)
        nlab = sb(g, es, "nlab", [128, 32], BF16)
        ecum = sb(g, es, "ecum", [128, 32], F32)
        dec = sb(g, es, "dec", [128, 32], F32)
        xd = sb(g, es, "xd", [128, 2048], BF16)
        xdd = sb(g, es, "xdd", [128, 2048], BF16)
        Sbf = sb(g, es, "Sbf", [128, 2048], BF16)
        Dt = [sb(g, es, "Dt%d" % i, [128, 8, 128], BF16) for i in range(2)]
        Lx = [sb(g, es, "Lx%d" % i, [128, 8, 128], BF16) for i in range(2)]
        G = [sb(g, es, "G%d" % i, [128, 8, 128], BF16) for i in range(2)]
        yo = sb(g, es, "yo", [128, 512], F32)
        ytile = sb(g, es, "ytile", [128, 2048], BF16)
        tmp = sb(g, es, "tmp", [128, 2048], BF16)
        yfl = sb(g, es, "yfl", [128, 2048], BF16)
        S = g.Sf if d == 0 else g.Sb
        mxk = "le" if d == 0 else "ge"
        sgk = "gt" if d == 0 else "lt"
        penk = "pen_f" if d == 0 else "pen_b"
        order = list(range(NEXT)) if d == 0 else list(range(NEXT - 1, -1, -1))

        prep(g, c, 0, g.xext[EXT:EXT + 128, :], 0, 1, vmask=cbcv(g, "halo_v"), hT=hH)

        def do_prep_a(ci):
            prep_a(g, c, ci, g.xext[ci * 128:(ci + 1) * 128, :])

        def do_prep_b(ci, prev_ci):
            hT = c.hTe[ci % 3]
            vm = None
            if ci == 0:
                vm = cbcv(g, "emask_bc", 0, 128)
            if ci == NEXT - 1:
                vm = cbcv(g, "emask_bc", 128, 128)
            prep_b(g, c, ci, 0, 1, vmask=vm)
            if prev_ci is not None:
                nb = c.hTe[prev_ci % 3]
                if ci == prev_ci + 1:
                    P.op("pool", lambda e: e.tensor_copy(out=hT[:, :, 0:1], in_=nb[:, :, 128:129]), [nb], [hT])
                    P.op("pool", lambda e: e.tensor_copy(out=nb[:, :, 129:130], in_=hT[:, :, 1:2]), [hT], [nb])
                else:
                    P.op("pool", lambda e: e.tensor_copy(out=hT[:, :, 129:130], in_=nb[:, :, 1:2]), [nb], [hT])
                    P.op("pool", lambda e: e.tensor_copy(out=nb[:, :, 0:1], in_=hT[:, :, 128:129]), [hT], [nb])
            if ci == 0:
                P.op("pool", lambda e: e.tensor_copy(out=hT[:, :, 0:1], in_=hH[:, :, 1:2]), [hH], [hT])
            if ci == NEXT - 1:
                P.op("pool", lambda e: e.tensor_copy(out=hT[:, :, 129:130], in_=hH[:, :, 2:3]), [hH], [hT])

        do_prep_a(order[0]); do_prep_b(order[0], None)
        do_prep_a(order[1]); do_prep_b(order[1], order[0])
        for oi, ci in enumerate(order):
            hT = c.hTe[ci % 3]
            if d == 1:
                P.dma("sp", yfl[:], g.YF[ci], writes=[yfl])
            mask_ap = bc(cpp(g, "emask", ci).unsqueeze(2), [128, 1, 32])
            steps = dt_steps(g, s, hT, Wdt, 32, cbcv(g, "dt_bias", 32 * d, 32), g.nega[:, 32 * d:32 * (d + 1)], mask_ap)

            def u1():
                P.op("pool", lambda e: e.tensor_copy(out=lab[:], in_=s.la[:]), [s.la], [lab])
                P.op("pool", lambda e: e.tensor_scalar(out=nlab[:], in0=lab[:], scalar1=-1.0, scalar2=None, op0=OP.mult),
                     [lab], [nlab])

                def segs(e):
                    e.matmul(s.pD[:, 64:96], lhsT=mk32(g, mxk), rhs=s.la[:], start=True, stop=True)
                    e.matmul(s.pD[:, 96:128], lhsT=mk32(g, sgk), rhs=s.la[:], start=True, stop=True)
                    return e.matmul(s.pD[:, 128:160], lhsT=mk32(g, "ones"), rhs=s.la[:], start=True, stop=True)
                P.op("pe", segs, [s.la, g.cmk_t], [s.pD])

            def u2():
                P.op("act", lambda e: e.activation(out=ecum[:], in_=s.pD[:, 64:96], func=AF.Exp), [s.pD], [ecum])
                P.op("act", lambda e: e.activation(out=s.wx[:], in_=s.pD[:, 96:128], func=AF.Exp), [s.pD], [s.wx])
                P.op("act", lambda e: e.activation(out=dec[:], in_=s.pD[:, 128:160], func=AF.Exp), [s.pD], [dec])

            def u3():
                P.op("pool", lambda e: e.tensor_tensor(out=s.wx[:], in0=s.wx[:], in1=s.dtm[:], op=OP.mult),
                     [s.wx, s.dtm], [s.wx])
            for st_ in steps + [u1, u2, u3]:
                st_()
            if oi + 2 < NEXT:
                do_prep_a(order[oi + 2])
            fm_proj_conv(g, c, s, hT, W, 24, 0, [])
            if oi + 2 < NEXT:
                do_prep_b(order[oi + 2], order[oi + 1])
            to_token_major(g, c, s, 20)
            x3 = s.xtok[:, 0:2048].rearrange("p (h d) -> p h d", d=64)
            P.op("dve", lambda e: e.tensor_tensor(out=xd[:].rearrange("p (h d) -> p h d", d=64), in0=x3,
                                                   in1=bc(s.dtm[:].unsqueeze(2), [128, 32, 64]), op=OP.mult),
                 [s.xtok, s.dtm], [xd])
            P.op("dve", lambda e: e.tensor_tensor(out=xdd[:].rearrange("p (h d) -> p h d", d=64), in0=x3,
                                                   in1=bc(s.wx[:].unsqueeze(2), [128, 32, 64]), op=OP.mult),
                 [s.xtok, s.wx], [xdd])
            P.op("act", lambda e: e.activation(out=Sbf[:], in_=S[:], func=AF.Copy), [S], [Sbf])

            def sc(e):
                ins = None
                for gi in range(4):
                    ins = e.matmul(psc[:, gi, :], lhsT=s.xcs[:, 16 + gi, :], rhs=s.xcs[:, 20 + gi, :], start=True, stop=True)
                return ins
            P.op("pe", sc, [s.xcs], [psc])
            for gi in range(4):
                dt_, lx, gg = Dt[gi % 2], Lx[gi % 2], G[gi % 2]
                P.op("pool", lambda e, gi=gi, dt_=dt_: e.tensor_tensor(
                    out=dt_[:], in0=bc(lab[:, gi * 8:(gi + 1) * 8].unsqueeze(2), [128, 8, 128]),
                    in1=bc(mk16(g, mxk).unsqueeze(1), [128, 8, 128]), op=OP.mult), [lab, g.cmkb_t], [dt_])
                for hh in range(2):
                    def mmL(e, gi=gi, hh=hh, dt_=dt_):
                        e.matmul(pL[hh][:], lhsT=mk16(g, "ones"), rhs=dt_[:, hh * 4:(hh + 1) * 4, :], start=True, stop=False)
                        e.matmul(pL[hh][:], lhsT=mk16(g, mxk),
                                 rhs=bc(nlab[:, gi * 8 + hh * 4:gi * 8 + hh * 4 + 4].unsqueeze(2), [128, 4, 128]),
                                 start=False, stop=False)
                        return e.matmul(pL[hh][:], lhsT=mk16(g, "ident"),
                                        rhs=bc(mk16(g, penk).unsqueeze(1), [128, 4, 128]), start=False, stop=True)
                    P.op("pe", mmL, [dt_, nlab, g.cmkb_t], [pL[hh]])
                    P.op("act", lambda e, hh=hh, lx=lx: e.activation(out=lx[:, hh * 4:(hh + 1) * 4, :], in_=pL[hh][:],
                                                                   func=AF.Exp), [pL[hh]], [lx])
                P.op("dve", lambda e, gi=gi, lx=lx, gg=gg: e.tensor_tensor(
                    out=gg[:], in0=lx[:], in1=bc(psc[:, gi, :].unsqueeze(1), [128, 8, 128]), op=OP.mult),
                    [lx, psc], [gg])

                def mmy(e, gi=gi, gg=gg):
                    ins = None
                    for h in range(8):
                        hh = gi * 8 + h
                        ins = e.matmul(s.pH[:, h * 64:(h + 1) * 64], lhsT=gg[:, h, :], rhs=xd[:, hh * 64:(hh + 1) * 64],
                                       start=True, stop=True)
                    return ins
                P.op("pe", mmy, [gg, xd], [s.pH])
                P.op("pe", lambda e, gi=gi: e.matmul(s.pxb[0][:], lhsT=s.xcs[:, 20 + gi, :],
                                                     rhs=Sbf[:, gi * 512:(gi + 1) * 512], start=True, stop=True),
                     [s.xcs, Sbf], [s.pxb[0]])
                P.op("dve", lambda e, gi=gi: e.tensor_tensor(
                    out=yo[:].rearrange("p (h d) -> p h d", d=64), in0=s.pxb[0][:].rearrange("p (h d) -> p h d", d=64),
                    in1=bc(ecum[:, gi * 8:(gi + 1) * 8].unsqueeze(2), [128, 8, 64]), op=OP.mult),
                    [s.pxb[0], ecum], [yo])
                P.op("dve", lambda e, gi=gi: e.tensor_tensor(out=ytile[:, gi * 512:(gi + 1) * 512], in0=yo[:],
                                                              in1=s.pH[:], op=OP.add), [yo, s.pH], [ytile])
                P.op("pe", lambda e, gi=gi: e.matmul(s.pxb[1][:], lhsT=s.xtok[:, 2048 + gi * 128:2048 + (gi + 1) * 128],
                                                     rhs=xdd[:, gi * 512:(gi + 1) * 512], start=True, stop=True),
                     [s.xtok, xdd], [s.pxb[1]])
                P.op("dve", lambda e, gi=gi: e.tensor_tensor(
                    out=S[:, gi * 512:(gi + 1) * 512].rearrange("p (h d) -> p h d", d=64),
                    in0=S[:, gi * 512:(gi + 1) * 512].rearrange("p (h d) -> p h d", d=64),
                    in1=bc(dec[:, gi * 8:(gi + 1) * 8].unsqueeze(2), [128, 8, 64]), op=OP.mult), [S, dec], [S])
                P.op("dve", lambda e, gi=gi: e.tensor_tensor(out=S[:, gi * 512:(gi + 1) * 512],
                                                              in0=S[:, gi * 512:(gi + 1) * 512], in1=s.pxb[1][:],
                                                              op=OP.add), [S, s.pxb[1]], [S])
            if d == 0:
                P.op("pool", lambda e: e.tensor_tensor(out=tmp[:].rearrange("p (h d) -> p h d", d=64), in0=x3,
                                                       in1=bc(cbcv(g, "d_skip").unsqueeze(2), [128, 32, 64]),
                                                       op=OP.mult), [s.xtok, g.cbc_t], [tmp])
                P.op("pool", lambda e: e.tensor_tensor(out=tmp[:], in0=tmp[:], in1=ytile[:], op=OP.add),
                     [tmp, ytile], [tmp])
                P.dma("sp", g.YF[ci], tmp[:], reads=[tmp])
            else:
                P.op("pool", lambda e: e.tensor_tensor(out=tmp[:], in0=yfl[:], in1=ytile[:], op=OP.add),
                     [yfl, ytile], [tmp])
                P.dma("sp", g.YT[ci], tmp[:], reads=[tmp])
        P.barrier()


def phase_merge(g):
    phase_merge_a(g)
    phase_merge_b(g)


def phase_merge_a(g):
    nc, P = g.nc, g.P
    with ExitStack() as es:
        c = alloc_chunk_bufs(g, es, 0)
        Wz = sb(g, es, "Wz", [128, 8, 2048], BF16)
        Wgs = sb(g, es, "Wgs", [128, 8, 1024], BF16)
        Wsb = sb(g, es, "Wsb", [128, 16, 1024], BF16)
        load_w_cols(g, Wz, 4096, 2048)
        load_w_cols(g, Wgs, 7232, 1024)
        load_w_gen(g, Wsb, g.w_sb, 16, 1024)
        sg = sb(g, es, "ssdg", [128, 2048], F32)
        P.dma("sp", sg[:], g.cbg[:, CBG["ssd_g"][0]:CBG["ssd_g"][0] + 2048], writes=[sg])
        pz = [ps(g, es, "pz%d" % i, [128, 512], F32) for i in range(4)]
        pbs = [ps(g, es, "pbs%d" % i, [128, 512], F32) for i in range(2)]
        yt = [sb(g, es, "yt%d" % i, [128, 2048], BF16) for i in range(2)]
        zs = sb(g, es, "zs", [128, 4, 512], BF16, nparts=4)
        t = sb(g, es, "t", [128, 4, 512], F32, nparts=4)
        jk = sb(g, es, "jk", [128, 512], BF16)
        st2 = sb(g, es, "st2", [128, 16], F32)
        ysn = sb(g, es, "ysn", [128, 4, 512], BF16, nparts=4)
        ysnT = sb(g, es, "ysnT", [128, 16, 128], BF16)
        sgs = sb(g, es, "sgs", [128, 2, 512], F32, nparts=2)
        ms = [sb(g, es, "ms%d" % i, [128, 1024], BF16) for i in range(2)]
        prep_a(g, c, 0, g.xext[0:128, :])
        prep_b(g, c, 0, 0, 1)
        for ci in range(NEXT):
            hT = c.hTe[ci % 3]
            y = yt[ci % 2]
            P.dma("sp", y[:], g.YT[ci], writes=[y])
            if ci + 1 < NEXT:
                prep_a(g, c, ci + 1, g.xext[(ci + 1) * 128:(ci + 2) * 128, :])
            for gi in range(4):
                def mm(e, gi=gi):
                    ins = None
                    for kb in range(8):
                        ins = e.matmul(pz[gi][:], lhsT=hT[:, kb, 1:129], rhs=Wz[:, kb, gi * 512:(gi + 1) * 512],
                                       start=(kb == 0), stop=(kb == 7))
                    return ins
                P.op("pe", mm, [hT, Wz], [pz[gi]])
            for gi in range(4):
                P.op("act", lambda e, gi=gi: e.activation(out=zs[:, gi, :], in_=pz[gi][:], func=AF.Silu),
                     [pz[gi]], [(zs, gi)])
            for gi in range(4):
                P.op("dve", lambda e, gi=gi: e.tensor_tensor(out=t[:, gi, :], in0=zs[:, gi, :],
                                                              in1=y[:, gi * 512:(gi + 1) * 512], op=OP.mult),
                     [(zs, gi), y], [(t, gi)])
            for gi in range(4):
                P.op("act", lambda e, gi=gi: e.activation(out=jk[:], in_=t[:, gi, :], func=AF.Square,
                                                          accum_out=st2[:, gi:gi + 1]), [(t, gi)], [jk, st2])
            P.op("dve", lambda e: e.tensor_scalar(out=st2[:, 4:8], in0=st2[:, 0:4], scalar1=1.0 / 512, scalar2=EPS,
                                                   op0=OP.mult, op1=OP.add), [st2], [st2])
            P.op("act", lambda e: e.activation(out=st2[:, 8:12], in_=st2[:, 4:8], func=AF.Ln), [st2], [st2])
            P.op("act", lambda e: e.activation(out=st2[:, 12:16], in_=st2[:, 8:12], func=AF.Exp, scale=-0.5), [st2], [st2])
            for gi in range(4):
                P.op("dve", lambda e, gi=gi: e.scalar_tensor_tensor(
                    out=ysn[:, gi, :], in0=t[:, gi, :], scalar=st2[:, 12 + gi:13 + gi], in1=sg[:, gi * 512:(gi + 1) * 512],
                    op0=OP.mult, op1=OP.mult), [(t, gi), st2, sg], [(ysn, gi)])
            for rnd in range(2):
                def tr(e, rnd=rnd):
                    ins = None
                    for j in range(8):
                        blk = rnd * 8 + j
                        ins = e.transpose(out=c.pA[:, j, :], in_=ysn[:, blk // 4, (blk % 4) * 128:(blk % 4 + 1) * 128],
                                          identity=mk16(g, "ident"))
                    return ins
                P.op("pe", tr, [ysn, g.cmkb_t], [c.pA])
                P.op("dve", lambda e, rnd=rnd: e.tensor_copy(out=ysnT[:, rnd * 8:(rnd + 1) * 8, :], in_=c.pA[:]),
                     [c.pA], [ysnT])
            if ci + 1 < NEXT:
                prep_b(g, c, ci + 1, 0, 1)
            for hf in range(2):
                def mmg(e, hf=hf):
                    ins = None
                    for kb in range(8):
                        ins = e.matmul(pz[hf][:], lhsT=hT[:, kb, 1:129], rhs=Wgs[:, kb, hf * 512:(hf + 1) * 512],
                                       start=(kb == 0), stop=(kb == 7))
                    return ins
                P.op("pe", mmg, [hT, Wgs], [pz[hf]])

                def mms(e, hf=hf):
                    ins = None
                    for kb in range(16):
                        ins = e.matmul(pbs[hf][:], lhsT=ysnT[:, kb, :], rhs=Wsb[:, kb, hf * 512:(hf + 1) * 512],
                                       start=(kb == 0), stop=(kb == 15))
                    return ins
                P.op("pe", mms, [ysnT, Wsb], [pbs[hf]])
            m = ms[ci % 2]
            for hf in range(2):
                P.op("act", lambda e, hf=hf: e.activation(out=sgs[:, hf, :], in_=pz[hf][:], func=AF.Sigmoid),
                     [pz[hf]], [(sgs, hf)])
                P.op("dve", lambda e, m=m, hf=hf: e.tensor_tensor(out=m[:, hf * 512:(hf + 1) * 512], in0=sgs[:, hf, :],
                                                                   in1=pbs[hf][:], op=OP.mult),
                     [(sgs, hf), pbs[hf]], [m])
            P.dma("sp", g.MS[ci], m[:], reads=[m])
        P.barrier()


def phase_merge_b(g):
    nc, P = g.nc, g.P
    with ExitStack() as es:
        c = alloc_chunk_bufs(g, es, 0)
        Wgf = sb(g, es, "Wgf", [128, 8, 1024], BF16)
        Wfa = sb(g, es, "Wfa", [128, 8, 1024], BF16)
        Wo = sb(g, es, "Wo", [128, 8, 1024], BF16)
        Tcs = sb(g, es, "Tcs", [128, 256], BF16)
        load_w_cols(g, Wgf, 6208, 1024)
        load_w_gen(g, Wfa, g.w_fa, 8, 1024)
        load_w_gen(g, Wo, g.w_o, 8, 1024)
        P.dma("pool", Tcs[:], g.tcs, writes=[Tcs])
        pb0 = ps(g, es, "pb0", [128, 2, 512], F32)
        pb1 = ps(g, es, "pb1", [128, 2, 512], F32)
        pb2 = ps(g, es, "pb2", [128, 2, 512], F32)
        xtc = [sb(g, es, "xtc%d" % i, [128, 8, 2, 128], BF16) for i in range(2)]
        msl = [sb(g, es, "msl%d" % i, [128, 1024], BF16) for i in range(2)]
        mixT = sb(g, es, "mixT", [128, 8, 128], BF16)
        sgf = sb(g, es, "sgf", [128, 1024], F32)
        tmp = sb(g, es, "tmpm", [128, 1024], F32)
        mrg = sb(g, es, "mrg", [128, 1024], BF16)
        mrgT = sb(g, es, "mrgT", [128, 8, 128], BF16)
        l1 = [sb(g, es, "l1_%d" % i, [128, 1024], F32) for i in range(2)]
        xtv = g.XT.rearrange("p (c r t) -> p c r t", c=8, r=2)
        prep_a(g, c, 0, g.xext[0:128, :])
        prep_b(g, c, 0, 0, 1)
        for ci in range(NEXT):
            hT = c.hTe[ci % 3]
            xt = c.xt[ci % 2]
            if ci + 1 < NEXT:
                prep_a(g, c, ci + 1, g.xext[(ci + 1) * 128:(ci + 2) * 128, :])
            xc_, m = xtc[ci % 2], msl[ci % 2]
            P.dma("sp", xc_[:], xtv[:, :, :, ci * 128:(ci + 1) * 128], writes=[xc_])
            P.dma("sp", m[:], g.MS[ci], writes=[m])
            for cg in range(2):
                def mmx(e, cg=cg):
                    e.matmul(pb2[:, cg, :], lhsT=Tcs[:, 0:128], rhs=xc_[:, cg * 4:(cg + 1) * 4, 0, :], start=True, stop=False)
                    return e.matmul(pb2[:, cg, :], lhsT=Tcs[:, 128:256], rhs=xc_[:, cg * 4:(cg + 1) * 4, 1, :],
                                    start=False, stop=True)
                P.op("pe", mmx, [Tcs, xc_], [pb2])
            P.op("act", lambda e: e.activation(out=mixT[:].rearrange("p a b -> p (a b)"),
                                               in_=pb2[:].rearrange("p a b -> p (a b)"), func=AF.Copy), [pb2], [mixT])
            for hf in range(2):
                def mmf(e, hf=hf):
                    ins = None
                    for kb in range(8):
                        ins = e.matmul(pb0[:, hf, :], lhsT=mixT[:, kb, :], rhs=Wfa[:, kb, hf * 512:(hf + 1) * 512],
                                       start=(kb == 0), stop=(kb == 7))
                    return ins
                P.op("pe", mmf, [mixT, Wfa], [pb0])

                def mmg(e, hf=hf):
                    ins = None
                    for kb in range(8):
                        ins = e.matmul(pb1[:, hf, :], lhsT=hT[:, kb, 1:129], rhs=Wgf[:, kb, hf * 512:(hf + 1) * 512],
                                       start=(kb == 0), stop=(kb == 7))
                    return ins
                P.op("pe", mmg, [hT, Wgf], [pb1])
            P.op("act", lambda e: e.activation(out=sgf[:], in_=pb1[:].rearrange("p a b -> p (a b)"), func=AF.Sigmoid),
                 [pb1], [sgf])
            P.op("dve", lambda e: e.tensor_tensor(out=tmp[:], in0=sgf[:], in1=pb0[:].rearrange("p a b -> p (a b)"),
                                                   op=OP.mult), [sgf, pb0], [tmp])
            P.op("dve", lambda e, m=m: e.tensor_tensor(out=mrg[:], in0=tmp[:], in1=m[:], op=OP.add), [tmp, m], [mrg])

            def tr(e):
                ins = None
                for kb in range(8):
                    ins = e.transpose(out=c.pA[:, kb, :], in_=mrg[:, kb * 128:(kb + 1) * 128], identity=mk16(g, "ident"))
                return ins
            P.op("pe", tr, [mrg, g.cmkb_t], [c.pA])
            P.op("act", lambda e: e.activation(out=mrgT[:], in_=c.pA[:], func=AF.Copy), [c.pA], [mrgT])
            if ci + 1 < NEXT:
                prep_b(g, c, ci + 1, 0, 1)
            for hf in range(2):
                def mmo(e, hf=hf):
                    ins = None
                    for kb in range(8):
                        ins = e.matmul(pb2[:, hf, :], lhsT=mrgT[:, kb, :], rhs=Wo[:, kb, hf * 512:(hf + 1) * 512],
                                       start=(kb == 0), stop=(kb == 7))
                    return ins
                P.op("pe", mmo, [mrgT, Wo], [pb2])
            l = l1[ci % 2]
            P.op("dve", lambda e, l=l: e.tensor_tensor(out=l[:], in0=pb2[:].rearrange("p a b -> p (a b)"),
                                                        in1=g.gbc[:, 0:D], op=OP.mult), [pb2, g.gbc], [l])
            P.op("pool", lambda e, l=l, xt=xt: e.tensor_tensor(out=l[:], in0=l[:], in1=xt[:], op=OP.add), [l, xt], [l])
            P.dma("sp", g.L1[ci], l[:], reads=[l])
        P.barrier()


def phase_ffn(g):
    nc, P = g.nc, g.P
    with ExitStack() as es:
        c = alloc_chunk_bufs(g, es, 0)
        h2T = sb(g, es, "h2T", [128, 8, EXT], BF16, nparts=NEXT)
        Wd = sb(g, es, "Wd", [128, NFB, 1024], BF16)
        load_w_gen(g, Wd, g.w_down, NFB, 1024)
        fg = sb(g, es, "fg", [128, 1024], F32)
        P.dma("sp", fg[:], g.cbg[:, CBG["final_g"][0]:CBG["final_g"][0] + 1024], writes=[fg])
        for ci in range(NEXT):
            i2 = ci % 2
            xt, st, xn = c.xt[i2], c.st[i2], c.xn[i2]
            P.dma("sp", xt[:], g.L1[ci], writes=[xt])
            P.op("act", lambda e: e.activation(out=c.junk[:], in_=xt[:], func=AF.Square, accum_out=st[:, 0:1]),
                 [xt], [c.junk, st])
            P.op("dve", lambda e: e.tensor_scalar(out=st[:, 1:2], in0=st[:, 0:1], scalar1=1.0 / D, scalar2=EPS,
                                                   op0=OP.mult, op1=OP.add), [st], [st])
            P.op("act", lambda e: e.activation(out=st[:, 2:3], in_=st[:, 1:2], func=AF.Sqrt), [st], [st])
            P.op("dve", lambda e: e.reciprocal(out=st[:, 3:4], in_=st[:, 2:3]), [st], [st])
            P.op("act", lambda e: e.activation(out=xn[:], in_=xt[:], func=AF.Copy, scale=st[:, 3:4]), [xt, st], [xn])

            def tr(e):
                ins = None
                for kb in range(8):
                    ins = e.transpose(out=c.pA[:, kb, :], in_=xn[:, kb * 128:(kb + 1) * 128], identity=mk16(g, "ident"))
                return ins
            P.op("pe", tr, [xn, g.cmkb_t], [c.pA])
            for kb in range(8):
                P.op("dve", lambda e, kb=kb, ci=ci: e.tensor_scalar(
                    out=h2T[:, kb, ci * 128:(ci + 1) * 128], in0=c.pA[:, kb, :], scalar1=modA(g, 4, kb),
                    scalar2=modA(g, 5, kb), op0=OP.mult, op1=OP.add), [c.pA, g.modv], [(h2T, ci)])
            if ci in (0, NEXT - 1):
                vm = cbcv(g, "emask_bc", 0 if ci == 0 else 128, 128)
                P.op("pool", lambda e, ci=ci, vm=vm: e.tensor_tensor(
                    out=h2T[:, :, ci * 128:(ci + 1) * 128], in0=h2T[:, :, ci * 128:(ci + 1) * 128],
                    in1=bc(vm.unsqueeze(1), [128, 8, 128]), op=OP.mult), [(h2T, ci), g.cbc_t], [(h2T, ci)])
        NB = 4
        pu = [ps(g, es, "pu%d" % i, [128, 512], F32) for i in range(2)]
        pd = ps(g, es, "pd", [128, 2, 512], F32)
        aT = sb(g, es, "aT", [128, NFB, 512], BF16, nparts=NFB)
        wu = [sb(g, es, "wu%d" % i, [128, 8, 2, 128], BF16) for i in range(3)]
        ug = [sb(g, es, "ug%d" % i, [128, 10, 64], BF16) for i in range(2)]
        dg = [sb(g, es, "dg%d" % i, [128, 4, 128], BF16) for i in range(4)]
        pcv = [ps(g, es, "pcv%d" % i, [128, 512], F32) for i in range(2)]
        acc = [sb(g, es, "acc%d" % i, [128, 8, 64], F32) for i in range(2)]
        sgl = sb(g, es, "sgl", [128, 512], F32)
        lt = [sb(g, es, "lt%d" % i, [128, 1024], F32) for i in range(2)]
        yy = [sb(g, es, "yy%d" % i, [128, 1024], F32) for i in range(2)]
        jk = c.junk
        st = [sb(g, es, "stf%d" % i, [128, 4], F32) for i in range(2)]
        wuv = g.w_up.rearrange("(kb p) (gv n) -> p kb gv n", p=128, gv=2)
        l1f = g.L1.rearrange("c p d -> (c p) d")
        cnt = 0
        nitem = NB * NFB

        def issue_w(i):
            if i < nitem:
                fb_ = i % NFB
                w_ = wu[i % 3]
                for gv_ in range(2):
                    P.dma("pool", w_[:, :, gv_, :], wuv[:, :, gv_, fb_ * 128:(fb_ + 1) * 128], writes=[w_])
        issue_w(0)
        issue_w(1)
        for blk in range(NB):
            base = blk * 512
            hparts = [(h2T, i) for i in range(base // 128, (base + 640 + 127) // 128)]
            for fb in range(NFB):
                w = wu[cnt % 3]
                issue_w(cnt + 2)
                cnt += 1
                for gv in range(2):
                    u = ug[gv]
                    a = acc[gv]
                    for j in range(2):
                        def mm(e, j=j, gv=gv, w=w):
                            ins = None
                            for kb in range(8):
                                ins = e.matmul(pu[j][:, 0:320], lhsT=w[:, kb, gv, :],
                                               rhs=h2T[:, kb, base + j * 320:base + (j + 1) * 320],
                                               start=(kb == 0), stop=(kb == 7))
                            return ins
                        P.op("pe", mm, [w] + hparts, [pu[j]])
                        P.op("act", lambda e, j=j, u=u: e.activation(
                            out=u[:].rearrange("p r c -> p (r c)")[:, j * 320:(j + 1) * 320], in_=pu[j][:, 0:320],
                            func=AF.Copy), [pu[j]], [u])
                    cf = gv * NFB + fb
                    wt = lambda t: cpp(g, "cw_ffn", t * 44 + cf)
                    P.op("act", lambda e, u=u, a=a, cf=cf: e.activation(
                        out=a[:], in_=u[:, 1:9, :], func=AF.Identity, scale=cpp(g, "cw_ffn", 4 * 44 + cf),
                        bias=cpp(g, "cb_ffn", cf)), [u, g.cpp_t], [a])
                    dgt = dg[(cnt * 2 + gv) % 4]
                    for i_, t_ in enumerate((1, 7, 3, 5)):
                        P.op("pool", lambda e, i_=i_, t_=t_, dgt=dgt, cf=cf: e.tensor_scalar(
                            out=dgt[:, i_, :], in0=mk16(g, "ident"), scalar1=cpp(g, "cw_ffn", t_ * 44 + cf), scalar2=0.0,
                            op0=OP.mult, op1=OP.add), [g.cmkb_t, g.cpp_t], [dgt])
                    pc = pcv[gv]
                    pc3 = pc[:].rearrange("p (r c) -> p r c", c=64)

                    def mcv(e, u=u, dgt=dgt, pc3=pc3):
                        e.matmul(pc3, lhsT=dgt[:, 0, :], rhs=u[:, 0:8, :], start=True, stop=False)
                        e.matmul(pc3, lhsT=dgt[:, 1, :], rhs=u[:, 2:10, :], start=False, stop=False)
                        e.matmul(pc3[:, :, 1:64], lhsT=dgt[:, 2, :], rhs=u[:, 1:9, 0:63], start=False, stop=False)
                        return e.matmul(pc3[:, :, 0:63], lhsT=dgt[:, 3, :], rhs=u[:, 1:9, 1:64], start=False, stop=True)
                    P.op("pe", mcv, [u, dgt], [pc])
                    for (kh, kw) in ((0, 0), (0, 2), (2, 0), (2, 2)):
                        dy, dx = kh - 1, kw - 1
                        c0, c1 = max(0, -dx), 64 - max(0, dx)
                        P.op("dve", lambda e, u=u, a=a, dy=dy, dx=dx, c0=c0, c1=c1, t=kh * 3 + kw, cf=cf:
                             e.scalar_tensor_tensor(out=a[:, :, c0:c1], in0=u[:, 1 + dy:9 + dy, c0 + dx:c1 + dx],
                                                    scalar=cpp(g, "cw_ffn", t * 44 + cf), in1=a[:, :, c0:c1],
                                                    op0=OP.mult, op1=OP.add), [u, a, g.cpp_t], [a])
                    P.op("dve", lambda e, a=a, pc3=pc3: e.tensor_tensor(out=a[:], in0=a[:], in1=pc3, op=OP.add),
                         [a, pc], [a])
                P.op("act", lambda e: e.activation(out=sgl[:], in_=acc[0][:].rearrange("p r c -> p (r c)"), func=AF.Silu),
                     [acc[0]], [sgl])
                P.op("dve", lambda e, fb=fb: e.tensor_tensor(out=aT[:, fb, :], in0=sgl[:],
                                                              in1=acc[1][:].rearrange("p r c -> p (r c)"), op=OP.mult),
                     [sgl, acc[1]], [(aT, fb)])
            for tcn in range(4):
                o0 = blk * 512 + tcn * 128
                i2 = (blk * 4 + tcn) % 2
                l, y, s4 = lt[i2], yy[i2], st[i2]
                o = y
                P.dma("sp", l[:], l1f[o0 + 64:o0 + 64 + 128, :], writes=[l])
                for hf in range(2):
                    def mmd(e, hf=hf, tcn=tcn):
                        ins = None
                        for fb in range(NFB):
                            ins = e.matmul(pd[:, hf, :], lhsT=aT[:, fb, tcn * 128:(tcn + 1) * 128],
                                           rhs=Wd[:, fb, hf * 512:(hf + 1) * 512], start=(fb == 0), stop=(fb == NFB - 1))
                        return ins
                    P.op("pe", mmd, [aT, Wd], [pd])
                P.op("dve", lambda e, y=y: e.tensor_tensor(out=y[:], in0=pd[:].rearrange("p a b -> p (a b)"),
                                                            in1=g.gbc[:, D:2 * D], op=OP.mult), [pd, g.gbc], [y])
                P.op("pool", lambda e, y=y, l=l: e.tensor_tensor(out=y[:], in0=y[:], in1=l[:], op=OP.add), [y, l], [y])
                P.op("act", lambda e, y=y, s4=s4: e.activation(out=jk[:], in_=y[:], func=AF.Square,
                                                               accum_out=s4[:, 0:1]), [y], [jk, s4])
                P.op("dve", lambda e, s4=s4: e.tensor_scalar(out=s4[:, 1:2], in0=s4[:, 0:1], scalar1=1.0 / D,
                                                              scalar2=EPS, op0=OP.mult, op1=OP.add), [s4], [s4])
                P.op("act", lambda e, s4=s4: e.activation(out=s4[:, 2:3], in_=s4[:, 1:2], func=AF.Sqrt), [s4], [s4])
                P.op("dve", lambda e, s4=s4: e.reciprocal(out=s4[:, 3:4], in_=s4[:, 2:3]), [s4], [s4])
                P.op("dve", lambda e, y=y, s4=s4, o=o: e.scalar_tensor_tensor(
                    out=o[:], in0=y[:], scalar=s4[:, 3:4], in1=fg[:], op0=OP.mult, op1=OP.mult), [y, s4, fg], [y])
                P.dma("sp", g.out[o0:o0 + 128, :], o[:], reads=[o])
        P.barrier()


def _pm(v):
    v = np.asarray(v, np.float32)
    return np.ascontiguousarray(v.reshape(-1, 128).T)


def _rb(v):
    v = np.asarray(v, np.float32).reshape(1, -1)
    return np.ascontiguousarray(np.broadcast_to(v, (128, v.shape[1])))


def _const_tables():
    k = np.arange(128)[:, None]
    m = np.arange(128)[None, :]
    mats = [np.ones((128, 128)), k <= m, k >= m, k > m, k < m, k == m,
            np.where(m < k, -BIG, 0.0), np.where(m > k, -BIG, 0.0)]
    cmk = np.concatenate([np.asarray(a, np.float32) for a in mats], axis=1)
    t1i = np.arange(64)[:, None] * np.arange(64)[None, :]
    th = 2 * np.pi * t1i / 64.0
    t1 = np.concatenate([np.cos(th), -np.sin(th)], axis=1).astype(np.float32)
    j = np.arange(128)[:, None] * np.arange(128)[None, :]
    thc = 2 * np.pi * j / 128.0
    tcs = (np.concatenate([np.cos(thc), np.sin(thc)], axis=1) / 1024.0).astype(np.float32)
    return cmk, t1, tcs


def _t2_tables(q):
    t2 = np.arange(128, dtype=np.float64)[:, None, None]
    k1 = np.arange(64, dtype=np.float64)[None, :, None]
    k2 = (32 * q - 1 + np.arange(34, dtype=np.float64))[None, None, :]
    kk = np.mod(k1 + 64 * k2, 8192)
    th = 2 * np.pi * np.mod(kk * t2, 8192) / 8192.0
    Mr, Mi = np.cos(th), -np.sin(th)
    ta = np.concatenate([Mr, Mi], axis=2)
    tb = np.concatenate([-Mi, Mr], axis=2)
    return np.concatenate([ta.reshape(128, -1), tb.reshape(128, -1)], axis=1).astype(np.float32)


_CACHE = {}


def kernel(x, c, ctx, c_ctx, w_mod, b_mod, norm1_g, w_in, conv_ssd_w, conv_ssd_b, dt_bias, a_log,
           d_skip, ssd_norm_g, w_fa, w_sb, w_o, norm2_g, w_up, conv_ffn_w, conv_ffn_b, w_down, final_g):
    f = lambda a: np.asarray(a, np.float32)
    x, c, ctx, c_ctx = f(x), f(c), f(ctx), f(c_ctx)
    cmk, t1, tcs = _const_tables()
    in_maps = []
    bm = f(b_mod)[0]
    for core in range(8):
        b, q = divmod(core, 4)
        e0 = 2048 * q - 64
        xext = np.zeros((EXT + 128, D), np.float32)
        lo, hi = max(e0, 0), min(e0 + EXT, SEQ)
        xext[lo - e0:hi - e0] = x[b, lo:hi]
        hv = np.zeros(2, np.float32)
        if e0 - 1 >= 0:
            xext[EXT] = x[b, e0 - 1]; hv[0] = 1
        if e0 + EXT < SEQ:
            xext[EXT + 1] = x[b, e0 + EXT]; hv[1] = 1
        tok = e0 + np.arange(EXT)
        valid = ((tok >= 0) & (tok < SEQ)).astype(np.float32)
        fmask = np.zeros((NSLOT, 128, 2), np.float32)
        fmask[0:2, :, 0] = 1
        fmask[66:68, :, 1] = 1
        lt = np.arange(SEQ).reshape(NCH, 128)
        fmask[2:66, :, 0] = (lt < e0)
        fmask[2:66, :, 1] = (lt >= e0 + EXT)
        cpp_a = np.zeros((128, CPP_N), np.float32)

        def put(key, arr):
            o, n = CPP[key]
            cpp_a[:, o:o + n] = arr
        put("c", _pm(c[b])); put("cctx", _pm(c_ctx)); put("bmod", _pm(bm))
        put("n1g", _pm(f(norm1_g)[0])); put("n2g", _pm(f(norm2_g)[0]))
        put("cw_ssd", np.concatenate([_pm(f(conv_ssd_w)[0, t]) for t in range(3)], axis=1))
        put("cb_ssd", _pm(f(conv_ssd_b)[0]))
        cfw = f(conv_ffn_w)[0].reshape(9, 2 * DFF)
        put("cw_ffn", np.concatenate([_pm(cfw[t]) for t in range(9)], axis=1))
        put("cb_ffn", _pm(f(conv_ffn_b)[0]))
        put("emask", valid.reshape(NEXT, 128).T)
        put("fmask", fmask.transpose(1, 0, 2).reshape(128, NSLOT * 2))
        cbc_a = np.zeros((128, CBC_N), np.float32)

        def putb(key, arr):
            o, n = CBC[key]
            cbc_a[:, o:o + n] = arr
        putb("dt_bias", _rb(f(dt_bias)[0].reshape(-1))); putb("a_log", _rb(f(a_log)[0].reshape(-1)))
        putb("d_skip", _rb(f(d_skip)[0]))
        cbg_a = np.concatenate([_rb(f(ssd_norm_g)[0]), _rb(f(final_g)), _rb(bm[2048:3072]), _rb(bm[5120:6144])], axis=1)
        putb("emask_bc", _rb(np.concatenate([valid[:128], valid[-128:]])))
        hvb = np.zeros(128, np.float32); hvb[0:2] = hv
        putb("halo_v", _rb(hvb))
        in_maps.append(dict(
            xb=np.ascontiguousarray(x[b]), ctxb=np.ascontiguousarray(ctx[b]), xext=xext,
            w_mod=f(w_mod)[0], w_in=f(w_in)[0], w_fa=f(w_fa)[0], w_sb=f(w_sb)[0], w_o=f(w_o)[0],
            w_up=f(w_up)[0], w_down=f(w_down)[0], cpp=cpp_a, cbc=cbc_a, cbg=cbg_a, cbrow=f(conv_ssd_b)[0].reshape(1, 3072).copy(), cmk=cmk, t1=t1, t2=_t2_tables(q), tcs=tcs))
    if "nc" not in _CACHE:
        _CACHE["nc"] = build_program()
    res = run_bass_kernel_spmd(_CACHE["nc"], in_maps, core_ids=list(range(8)))
    if DEBUG:
        _CACHE["res"] = res
    out = np.zeros((2, SEQ, D), np.float32)
    for core in range(8):
        b, q = divmod(core, 4)
        out[b, 2048 * q:2048 * (q + 1)] = res.results[core]["out"]
    return out
```

```python
import os
from contextlib import ExitStack
import numpy as np
import concourse.bass as bass
import concourse.mybir as mybir
from concourse.bass_utils import run_bass_kernel_spmd

F32 = mybir.dt.float32
BF16 = mybir.dt.bfloat16
AF = mybir.ActivationFunctionType
OP = mybir.AluOpType

D = 1024
SEQ = 8192
NCH = 64
NEXT = 17
EXT = NEXT * 128
EPS = 1e-6
BIG = 30000.0
NSLOT = 68
DFF = 2816
NFB = 22

STOP = os.environ.get("MK_STOP", "")
DEBUG = bool(STOP)


class Buf:
    def __init__(self, t, nparts=1):
        self.t = t
        self.n = nparts
        self.w = [None] * nparts
        self.r = [[] for _ in range(nparts)]
        self.excl = False

    def __getitem__(self, idx):
        return self.t[idx]


def _parts(items):
    out = []
    for it in items:
        if it is None:
            continue
        if isinstance(it, Buf):
            out.extend((it, i) for i in range(it.n))
        else:
            b, idx = it
            if isinstance(idx, int):
                out.append((b, idx))
            else:
                out.extend((b, i) for i in idx)
    return out


class Prog:
    def __init__(self, nc, es):
        self.nc = nc
        self.E = {}
        self.semid = 0
        for name, eng in (("pe", nc.tensor), ("act", nc.scalar), ("dve", nc.vector), ("pool", nc.gpsimd), ("sp", nc.sync)):
            sem = es.enter_context(nc.semaphore("s_" + name))
            self.E[name] = dict(name=name, eng=eng, sem=(self._sid(), sem), count=0, waited={}, pool=[], ndma=0)
        for name, n in (("sp", 8), ("pool", 6), ("act", 4)):
            for i in range(n):
                sem = es.enter_context(nc.semaphore("d_%s%d" % (name, i)))
                self.E[name]["pool"].append((self._sid(), sem))
        self.ninst = 0

    def _sid(self):
        self.semid += 1
        return self.semid

    def _wait(self, E, tok):
        (sid, sem), val, _ = tok
        if E["waited"].get(sid, 0) >= val:
            return
        E["eng"].wait_ge(sem, val)
        E["waited"][sid] = val

    def _collect(self, en, reads, writes):
        toks = []
        for b, i in _parts(reads):
            if b.w[i] is not None:
                toks.append(b.w[i])
            if b.excl:
                toks.extend(t for t in b.r[i] if t[2] != en)
        for b, i in _parts(writes):
            if b.w[i] is not None:
                toks.append(b.w[i])
            toks.extend(b.r[i])
        res = []
        for t in toks:
            if en == "pe" and t[2] == "pe":
                continue
            res.append(t)
        return res

    def _update(self, reads, writes, tok):
        for b, i in _parts(reads):
            b.r[i].append(tok)
            if len(b.r[i]) > 24:
                last = {}
                for t in b.r[i]:
                    k = t[0][0]
                    if k not in last or last[k][1] < t[1]:
                        last[k] = t
                b.r[i] = list(last.values())
        for b, i in _parts(writes):
            b.w[i] = tok
            b.r[i] = []

    def op(self, en, fn, reads=(), writes=()):
        E = self.E[en]
        for t in self._collect(en, reads, writes):
            self._wait(E, t)
        ins = fn(E["eng"])
        E["count"] += 1
        ins.then_inc(E["sem"][1], 1)
        tok = (E["sem"], E["count"], en)
        self._update(reads, writes, tok)
        self.ninst += 1
        return tok

    def dma(self, qn, out, in_, reads=(), writes=(), **kw):
        Q = self.E[qn]
        i = Q["ndma"]
        P = len(Q["pool"])
        sem = Q["pool"][i % P]
        val = 16 * (i // P + 1)
        if i >= P:
            self._wait(Q, (sem, val - 16, "dma"))
        for t in self._collect("dma", reads, writes):
            self._wait(Q, t)
        Q["eng"].dma_start(out=out, in_=in_, **kw).then_inc(sem[1], 16)
        Q["ndma"] += 1
        tok = (sem, val, "dma")
        self._update(reads, writes, tok)
        return tok

    def all_tokens(self):
        toks = []
        for E in self.E.values():
            if E["count"]:
                toks.append((E["sem"], E["count"], E["name"]))
            P = len(E["pool"])
            for j in range(min(P, E["ndma"])):
                n = (E["ndma"] - 1 - j) // P + 1
                toks.append((E["pool"][j], 16 * n, "dma"))
        return toks

    def barrier(self):
        toks = self.all_tokens()
        for E in self.E.values():
            for t in toks:
                self._wait(E, t)


def bc(ap, shape):
    return ap.broadcast_to(shape)


class Ctx:
    pass


def build_program():
    nc = bass.Bass("TRN2", target_bir_lowering=False)
    g = Ctx()
    g.nc = nc

    def din(name, shape, dt=F32):
        return nc.dram_tensor(name, list(shape), dt, kind="ExternalInput").ap()

    def dscr(name, shape, dt):
        kind = "ExternalOutput" if DEBUG else "Internal"
        return nc.dram_tensor(name, list(shape), dt, kind=kind).ap()

    g.xb = din("xb", [SEQ, D])
    g.ctxb = din("ctxb", [256, D])
    g.xext = din("xext", [EXT + 128, D])
    g.w_mod = din("w_mod", [D, 6 * D])
    g.w_in = din("w_in", [D, 8256])
    g.w_fa = din("w_fa", [D, D])
    g.w_sb = din("w_sb", [2048, D])
    g.w_o = din("w_o", [D, D])
    g.w_up = din("w_up", [D, 2 * DFF])
    g.w_down = din("w_down", [DFF, D])
    g.cpp = din("cpp", [128, CPP_N])
    g.cbc = din("cbc", [128, CBC_N])
    g.cbg = din("cbg", [128, CBG_N])
    g.cbrow = din("cbrow", [1, 3072])
    g.cmk = din("cmk", [128, 8 * 128])
    g.t1 = din("t1", [64, 128])
    g.t2 = din("t2", [128, 2 * 64 * 68])
    g.tcs = din("tcs", [128, 256])
    g.out = nc.dram_tensor("out", [2048, D], F32, kind="ExternalOutput").ap()
    g.U = dscr("U", [NCH, 128, D], BF16)
    g.Y = dscr("Y", [128, 128, D], BF16)
    g.XT = dscr("XT", [128, 8 * 2 * EXT], BF16)
    g.MS = dscr("MS", [NEXT, 128, D], BF16)
    g.YF = dscr("YF", [NEXT, 128, 2048], BF16)
    g.YT = dscr("YT", [NEXT, 128, 2048], BF16)
    g.L1 = dscr("L1", [NEXT, 128, D], F32)
    g.SFB = dscr("SFB", [2, 128, 2048], F32)

    with ExitStack() as es:
        P = Prog(nc, es)
        g.P = P
        g.uid = 0
        phase_setup(g, es)
        with ExitStack() as es2:
            alloc_conv_consts(g, es2)
            if STOP != "setup":
                phase_far(g)
            if STOP not in ("setup", "far"):
                phase_fnet(g)
            if STOP not in ("setup", "far", "fnet"):
                phase_own(g, 0)
                phase_own(g, 1)
            P.barrier()
        if STOP not in ("setup", "far", "fnet", "own"):
            phase_merge(g)
        if STOP not in ("setup", "far", "fnet", "own", "merge"):
            phase_ffn(g)
        P.barrier()
    return nc


def sb(g, es, name, shape, dt, nparts=1):
    g.uid += 1
    t = es.enter_context(g.nc.sbuf_tensor("%s_%d" % (name, g.uid), list(shape), dt))
    return Buf(t, nparts)


def ps(g, es, name, shape, dt=F32, nparts=1):
    g.uid += 1
    t = es.enter_context(g.nc.psum_tensor("%s_%d" % (name, g.uid), list(shape), dt))
    b = Buf(t, nparts)
    b.excl = True
    return b


def _layout(items):
    off = {}
    o = 0
    for k, n in items:
        off[k] = (o, n)
        o += n
    return off, o


CPP, CPP_N = _layout([("c", 8), ("cctx", 8), ("bmod", 48), ("n1g", 8), ("n2g", 8), ("cw_ssd", 72), ("cb_ssd", 24),
                      ("cw_ffn", 9 * 44), ("cb_ffn", 44), ("emask", NEXT), ("fmask", NSLOT * 2)])
CBC, CBC_N = _layout([("dt_bias", 64), ("a_log", 64), ("d_skip", 32), ("emask_bc", 256), ("halo_v", 128)])
CBG, CBG_N = _layout([("ssd_g", 2048), ("final_g", 1024), ("bmod_g1", 1024), ("bmod_g2", 1024)])
MK = {k: i for i, k in enumerate(["ones", "le", "ge", "gt", "lt", "ident", "pen_f", "pen_b"])}


def cpp(g, key, j=None, n=1):
    o, _ = CPP[key]
    if j is None:
        return g.cpp_t[:, o:o + CPP[key][1]]
    return g.cpp_t[:, o + j:o + j + n]


def cbcv(g, key, a=0, n=None):
    o, m = CBC[key]
    if n is None:
        n = m
    return g.cbc_t[:, o + a:o + a + n]


def mk32(g, key):
    i = MK[key]
    return g.cmk_t[:, i * 128:(i + 1) * 128]


def mk16(g, key):
    i = MK[key]
    return g.cmkb_t[:, i * 128:(i + 1) * 128]


def phase_setup(g, es):
    nc, P = g.nc, g.P
    g.cpp_t = sb(g, es, "cpp", [128, CPP_N], F32)
    g.cbc_t = sb(g, es, "cbc", [128, CBC_N], F32)
    g.cmk_t = sb(g, es, "cmk", [128, 8 * 128], F32)
    g.cmkb_t = sb(g, es, "cmkb", [128, 8 * 128], BF16)
    g.modv = sb(g, es, "modv", [128, 8 * 8], F32)
    g.gbc = sb(g, es, "gbc", [128, 2 * D], F32)
    g.nega = sb(g, es, "nega", [128, 64], F32)
    g.Sf = sb(g, es, "Sf", [128, 2048], F32)
    g.Sb = sb(g, es, "Sb", [128, 2048], F32)
    P.dma("sp", g.cpp_t[:], g.cpp, writes=[g.cpp_t])
    P.dma("sp", g.cbc_t[:], g.cbc, writes=[g.cbc_t])
    P.dma("sp", g.cmk_t[:], g.cmk, writes=[g.cmk_t])
    P.dma("pool", g.cmkb_t[:], g.cmk, writes=[g.cmkb_t])
    with ExitStack() as ls:
        sc = sb(g, ls, "sc", [128, 8, 2], F32)
        screp = sb(g, ls, "screp", [128, 8, 128], F32)
        modT = sb(g, ls, "modT", [128, 48, 2], F32)
        wm = [sb(g, ls, "wm%d" % i, [128, 8, 1024], F32) for i in range(2)]
        pm = ps(g, ls, "pm", [128, 8, 2], F32)
        pg = ps(g, ls, "pg", [128, 512], F32)
        bg = sb(g, ls, "bg", [128, 2 * D], F32)
        P.dma("sp", bg[:], g.cbg[:, CBG["bmod_g1"][0]:CBG["bmod_g1"][0] + 2 * D], writes=[bg])
        P.op("act", lambda e: e.activation(out=sc[:, :, 0], in_=cpp(g, "c"), func=AF.Silu), [g.cpp_t], [sc])
        P.op("act", lambda e: e.activation(out=sc[:, :, 1], in_=cpp(g, "cctx"), func=AF.Silu), [g.cpp_t], [sc])
        P.op("dve", lambda e: e.tensor_copy(out=screp[:], in_=bc(sc[:, :, 0:1], [128, 8, 128])), [sc], [screp])
        wv = g.w_mod.rearrange("(kb p) n -> p kb n", p=128)
        for j in range(6):
            w = wm[j % 2]
            P.dma("sp", w[:], wv[:, :, j * 1024:(j + 1) * 1024], writes=[w])
            def mm(e, w=w):
                ins = None
                for fb in range(8):
                    for kb in range(8):
                        ins = e.matmul(pm[:, fb, :], lhsT=w[:, kb, fb * 128:(fb + 1) * 128], rhs=sc[:, kb, :],
                                       start=(kb == 0), stop=(kb == 7))
                return ins
            P.op("pe", mm, [w, sc], [pm])
            bo = CPP["bmod"][0] + j * 8
            P.op("dve", lambda e, j=j, bo=bo: e.tensor_tensor(
                out=modT[:, j * 8:(j + 1) * 8, :], in0=pm[:], in1=bc(g.cpp_t[:, bo:bo + 8].unsqueeze(2), [128, 8, 2]),
                op=OP.add), [pm, g.cpp_t], [modT])
            if j in (2, 5):
                gi = 0 if j == 2 else 1
                for hf in range(2):
                    def mg(e, w=w, hf=hf):
                        ins = None
                        for kb in range(8):
                            ins = e.matmul(pg[:], lhsT=screp[:, kb, :], rhs=w[:, kb, hf * 512:(hf + 1) * 512],
                                           start=(kb == 0), stop=(kb == 7))
                        return ins
                    P.op("pe", mg, [w, screp], [pg])
                    P.op("dve", lambda e, gi=gi, hf=hf: e.tensor_tensor(
                        out=g.gbc[:, gi * D + hf * 512: gi * D + (hf + 1) * 512], in0=pg[:],
                        in1=bg[:, gi * D + hf * 512: gi * D + (hf + 1) * 512], op=OP.add), [pg, bg], [g.gbc])
        mv = g.modv
        def mkA(dst, scale_j, which, gkey):
            P.op("dve", lambda e: e.scalar_tensor_tensor(
                out=mv[:, dst * 8:(dst + 1) * 8], in0=modT[:, scale_j * 8:(scale_j + 1) * 8, which], scalar=1.0,
                in1=cpp(g, gkey), op0=OP.add, op1=OP.mult), [modT, g.cpp_t], [mv])

        def mkB(dst, shift_j, which):
            P.op("dve", lambda e: e.tensor_copy(out=mv[:, dst * 8:(dst + 1) * 8],
                                                 in_=modT[:, shift_j * 8:(shift_j + 1) * 8, which]), [modT], [mv])
        mkA(0, 1, 0, "n1g"); mkB(1, 0, 0)
        mkA(2, 1, 1, "n1g"); mkB(3, 0, 1)
        mkA(4, 4, 0, "n2g"); mkB(5, 3, 0)
        P.op("act", lambda e: e.activation(out=g.nega[:], in_=cbcv(g, "a_log"), func=AF.Exp), [g.cbc_t], [g.nega])
        P.op("dve", lambda e: e.tensor_scalar(out=g.nega[:], in0=g.nega[:], scalar1=-1.0, scalar2=None, op0=OP.mult),
             [g.nega], [g.nega])
        P.barrier()


def alloc_conv_consts(g, es):
    P = g.P
    g.diag = sb(g, es, "diag", [128, 72, 128], BF16)
    g.brow = sb(g, es, "brow", [1, 3072], BF16)
    g.Wxbc = sb(g, es, "Wxbc", [128, 8, 3072], BF16)
    load_w_cols(g, g.Wxbc, 1024, 3072)
    for a_ in range(0, 3072, 1024):
        P.dma("pool", g.brow[:, a_:a_ + 1024], g.cbrow[:, a_:a_ + 1024], writes=[g.brow])
    for i_ in range(72):
        P.op("dve", lambda e, i_=i_: e.tensor_scalar(out=g.diag[:, i_, :], in0=mk16(g, "ident"),
                                                      scalar1=cpp(g, "cw_ssd", i_), scalar2=None, op0=OP.mult),
             [g.cmkb_t, g.cpp_t], [g.diag])


def modA(g, i, kb):
    return g.modv[:, i * 8 + kb:i * 8 + kb + 1]


def alloc_chunk_bufs(g, es, nfb):
    c = Ctx()
    c.xt = [sb(g, es, "xt%d" % i, [128, D], F32) for i in range(2)]
    c.junk = sb(g, es, "junk", [128, D], BF16)
    c.st = [sb(g, es, "st%d" % i, [128, 4], F32) for i in range(2)]
    c.xn = [sb(g, es, "xn%d" % i, [128, D], BF16) for i in range(2)]
    c.hTe = [sb(g, es, "hTe%d" % i, [128, 8, 130], BF16) for i in range(3)]
    c.pA = ps(g, es, "pA", [128, 8, 128], BF16)
    c.nfb = nfb
    return c


def prep_a(g, c, k, src_rows):
    P = g.P
    i2 = k % 2
    xt, st, xn = c.xt[i2], c.st[i2], c.xn[i2]
    P.dma("sp", xt[:], src_rows, writes=[xt])
    P.op("act", lambda e: e.activation(out=c.junk[:], in_=xt[:], func=AF.Square, accum_out=st[:, 0:1]),
         [xt], [c.junk, st])
    P.op("dve", lambda e: e.tensor_scalar(out=st[:, 1:2], in0=st[:, 0:1], scalar1=1.0 / D, scalar2=EPS,
                                           op0=OP.mult, op1=OP.add), [st], [st])
    P.op("act", lambda e: e.activation(out=st[:, 2:3], in_=st[:, 1:2], func=AF.Ln), [st], [st])
    P.op("act", lambda e: e.activation(out=st[:, 3:4], in_=st[:, 2:3], func=AF.Exp, scale=-0.5), [st], [st])
    P.op("act", lambda e: e.activation(out=xn[:], in_=xt[:], func=AF.Copy, scale=st[:, 3:4]), [xt, st], [xn])
    return xt


def prep_b(g, c, k, ai, bi, vmask=None, hT=None):
    P = g.P
    xn = c.xn[k % 2]
    if hT is None:
        hT = c.hTe[k % 3]

    def tr(e):
        ins = None
        for kb in range(8):
            ins = e.transpose(out=c.pA[:, kb, :], in_=xn[:, kb * 128:(kb + 1) * 128], identity=mk16(g, "ident"))
        return ins
    P.op("pe", tr, [xn, g.cmkb_t], [c.pA])
    P.op("dve", lambda e: e.tensor_tensor(out=hT[:, :, 1:129], in0=c.pA[:],
                                          in1=bc(g.modv[:, ai * 8:(ai + 1) * 8].unsqueeze(2), [128, 8, 128]),
                                          op=OP.mult), [c.pA, g.modv], [hT])
    P.op("dve", lambda e: e.tensor_tensor(out=hT[:, :, 1:129], in0=hT[:, :, 1:129],
                                          in1=bc(g.modv[:, bi * 8:(bi + 1) * 8].unsqueeze(2), [128, 8, 128]),
                                          op=OP.add), [hT, g.modv], [hT])
    if vmask is not None:
        P.op("pool", lambda e: e.tensor_tensor(out=hT[:, :, 1:129], in0=hT[:, :, 1:129],
                                               in1=bc(vmask.unsqueeze(1), [128, 8, 128]), op=OP.mult),
             [hT, g.cbc_t], [hT])


def prep(g, c, k, src_rows, ai, bi, vmask=None, hT=None):
    xt = prep_a(g, c, k, src_rows)
    prep_b(g, c, k, ai, bi, vmask=vmask, hT=hT)
    return xt


def halo_link(g, c, k, has_left):
    P = g.P
    cur = c.hTe[k % 3]
    if has_left:
        prv = c.hTe[(k - 1) % 3]
        P.op("pool", lambda e: e.tensor_copy(out=cur[:, :, 0:1], in_=prv[:, :, 128:129]), [prv], [cur])
        P.op("pool", lambda e: e.tensor_copy(out=prv[:, :, 129:130], in_=cur[:, :, 1:2]), [cur], [prv])
    else:
        P.op("pool", lambda e: e.memset(cur[:, :, 0:1], 0.0), [], [cur])


def halo_zero_right(g, c, k):
    cur = c.hTe[k % 3]
    g.P.op("pool", lambda e: e.memset(cur[:, :, 129:130], 0.0), [], [cur])


def fm_proj_conv(g, c, s, hT, W, nfb, cw_off, steps=None):
    P = g.P
    steps = steps if steps is not None else []
    groups = [list(range(a, min(a + 3, nfb))) for a in range(0, nfb, 3)]
    def e_proj(gi):
        fbs = groups[gi]
        pa = s.pxa[gi % len(s.pxa)]

        def mm(e, fbs=fbs, pa=pa):
            ins = None
            for j, fb in enumerate(fbs):
                for kb in range(8):
                    ins = e.matmul(pa[:, j * 130:(j + 1) * 130], lhsT=W[:, kb, fb * 128:(fb + 1) * 128], rhs=hT[:, kb, :],
                                   start=(kb == 0), stop=(kb == 7))
            return ins
        P.op("pe", mm, [W, hT], [pa])

    def e_rest(gi):
        fbs = groups[gi]
        n = len(fbs)
        pa = s.pxa[gi % len(s.pxa)]
        pb = s.pxc[gi % len(s.pxc)]
        pre = s.pre[gi % 2]
        P.op("dve", lambda e, n=n, pa=pa, pre=pre: e.tensor_copy(
            out=pre[:, 0:n, :], in_=pa[:, 0:n * 130].rearrange("p (j t) -> p j t", t=130)), [pa], [pre])

        def mc(e, fbs=fbs, pb=pb, pre=pre):
            ins = None
            for j, fb in enumerate(fbs):
                cf = cw_off + fb
                for k in range(3):
                    e.matmul(pb[:, j * 128:(j + 1) * 128], lhsT=g.diag[:, k * 24 + cf, :], rhs=pre[:, j, k:k + 128],
                             start=(k == 0), stop=False)
                ins = e.matmul(pb[:, j * 128:(j + 1) * 128], lhsT=g.brow[0:1, cf * 128:(cf + 1) * 128],
                               rhs=mk16(g, "ones")[0:1, :], start=False, stop=True)
            return ins
        P.op("pe", mc, [pre, g.diag, g.brow, g.cmkb_t], [pb])
        f0, f1 = fbs[0], fbs[-1] + 1
        P.op("act", lambda e, f0=f0, f1=f1, n=n, pb=pb: e.activation(
            out=s.xcs[:, f0:f1, :], in_=pb[:, 0:n * 128].rearrange("p (j t) -> p j t", t=128), func=AF.Silu),
            [pb], [(s.xcs, gi)])

    ng = len(groups)
    ahead = len(s.pxa) >= 2
    if ahead:
        e_proj(0)
    for gi in range(ng):
        if ahead:
            if gi + 1 < ng:
                e_proj(gi + 1)
        else:
            e_proj(gi)
        e_rest(gi)
        if steps:
            steps.pop(0)()
    while steps:
        steps.pop(0)()


def to_token_major(g, c, s, nblk, between=None):
    P = g.P
    between = between if between is not None else []
    for r0 in range(0, nblk, 8):
        n = min(8, nblk - r0)

        def tr(e, r0=r0, n=n):
            ins = None
            for j in range(n):
                ins = e.transpose(out=c.pA[:, j, :], in_=s.xcs[:, r0 + j, :], identity=mk16(g, "ident"))
            return ins
        P.op("pe", tr, [s.xcs, g.cmkb_t], [c.pA])
        P.op("act", lambda e, r0=r0, n=n: e.activation(
            out=s.xtok[:, r0 * 128:(r0 + n) * 128], in_=c.pA[:, 0:n, :], func=AF.Copy), [c.pA], [s.xtok])
        if between:
            between.pop(0)()
    while between:
        between.pop(0)()


def dt_steps(g, s, hT, Wdt, ncol, bias_ap, nega_ap, mask_ap):
    P = g.P

    def s1():
        def mm(e):
            ins = None
            for kb in range(8):
                ins = e.matmul(s.pD[:, 0:ncol], lhsT=hT[:, kb, 1:129], rhs=Wdt[:, kb, 0:ncol], start=(kb == 0), stop=(kb == 7))
            return ins
        P.op("pe", mm, [hT, Wdt], [s.pD])
        P.op("dve", lambda e: e.tensor_tensor(out=s.dtm[:, 0:ncol], in0=s.pD[:, 0:ncol], in1=bias_ap, op=OP.add),
             [s.pD, g.cbc_t], [s.dtm])

    def s2():
        P.op("act", lambda e: e.activation(out=s.dtm[:, 0:ncol], in_=s.dtm[:, 0:ncol], func=AF.Exp), [s.dtm], [s.dtm])

    def s3():
        P.op("act", lambda e: e.activation(out=s.dtm[:, 0:ncol], in_=s.dtm[:, 0:ncol], func=AF.Ln, bias=1.0), [s.dtm], [s.dtm])

    def s4():
        dv = s.dtm[:, 0:ncol].rearrange("p (a b) -> p a b", b=32)
        P.op("dve", lambda e: e.tensor_tensor(out=dv, in0=dv, in1=mask_ap, op=OP.mult), [s.dtm, g.cpp_t], [s.dtm])

    def s5():
        P.op("dve", lambda e: e.tensor_tensor(out=s.la[:, 0:ncol], in0=s.dtm[:, 0:ncol], in1=nega_ap, op=OP.mult),
             [s.dtm, g.nega], [s.la])
    return [s1, s2, s3, s4, s5]


def state_contrib(g, s, wexp_ap, xdd, on_group):
    P = g.P
    P.op("dve", lambda e: e.tensor_tensor(
        out=xdd[:].rearrange("p (h d) -> p h d", d=64), in0=s.xtok[:, 0:2048].rearrange("p (h d) -> p h d", d=64),
        in1=bc(wexp_ap.unsqueeze(2), [128, 32, 64]), op=OP.mult), [s.xtok, s.wx], [xdd])
    for gi in range(4):
        P.op("pe", lambda e, gi=gi: e.matmul(s.pH[:], lhsT=s.xtok[:, 2048 + gi * 128:2048 + (gi + 1) * 128],
                                             rhs=xdd[:, gi * 512:(gi + 1) * 512], start=True, stop=True),
             [s.xtok, xdd], [s.pH])
        on_group(gi, s.pH)


def load_w_cols(g, W, col0, ncols, dst0=0):
    wv = g.w_in.rearrange("(kb p) n -> p kb n", p=128)
    for a in range(0, ncols, 512):
        n = min(512, ncols - a)
        g.P.dma("pool", W[:, :, dst0 + a:dst0 + a + n], wv[:, :, col0 + a:col0 + a + n], writes=[W])


def phase_far(g):
    nc, P = g.nc, g.P
    with ExitStack() as es:
        c = alloc_chunk_bufs(g, es, 20)
        s = Ctx()
        Wf = sb(g, es, "Wf", [128, 8, 1024], BF16)
        Wxb = g.Wxbc
        Wdt = sb(g, es, "Wdt", [128, 8, 64], BF16)
        load_w_cols(g, Wf, 0, 1024)
        load_w_cols(g, Wdt, 6144, 64)
        s.pxa = [ps(g, es, "pxa%d" % i, [128, 512], F32) for i in range(2)]
        s.pxc = [ps(g, es, "pxc%d" % i, [128, 512], F32) for i in range(1)]
        s.pD = ps(g, es, "pD", [128, 512], F32)
        s.pH = ps(g, es, "pH", [128, 512], F32)
        pf = [ps(g, es, "pf%d" % i, [128, 512], F32) for i in range(2)]
        s.pre = [sb(g, es, "pre%d" % i, [128, 3, 130], BF16) for i in range(2)]
        s.xcs = sb(g, es, "xcs", [128, 20, 128], BF16, nparts=8)
        s.pD2 = sb(g, es, "pD2", [128, 192], F32)
        s.xtok = sb(g, es, "xtok", [128, 2560], BF16)
        s.dtm = sb(g, es, "dtm", [128, 64], F32)
        s.la = sb(g, es, "la", [128, 64], F32)
        s.wx = sb(g, es, "wx", [128, 64], F32)
        s.sg = sb(g, es, "sg", [128, 64], F32)
        Rb = sb(g, es, "Rb", [128, 32], F32)
        wxb = sb(g, es, "wxb", [128, 64], BF16)
        dec = sb(g, es, "dec", [128, 32], F32)
        xdd = [sb(g, es, "xdd%d" % i, [128, 2048], BF16) for i in range(2)]
        ub = [sb(g, es, "ub%d" % i, [128, D], BF16) for i in range(2)]
        P.op("dve", lambda e: e.memset(g.Sf[:], 0.0), [], [g.Sf])
        P.op("dve", lambda e: e.memset(g.Sb[:], 0.0), [], [g.Sb])
        P.op("dve", lambda e: e.memset(Rb[:], 0.0), [], [Rb])
        slots = [("c", 0), ("c", 1)] + [("l", i) for i in range(NCH)] + [("c", 0), ("c", 1)]
        first = {0, 2, 66}
        last = {1, 65, 67}

        def do_prep_a(k):
            kind, i = slots[k]
            src = g.ctxb[i * 128:(i + 1) * 128, :] if kind == "c" else g.xb[i * 128:(i + 1) * 128, :]
            prep_a(g, c, k, src)

        def do_prep_b(k):
            kind, i = slots[k]
            if kind == "c":
                prep_b(g, c, k, 2, 3)
            else:
                prep_b(g, c, k, 0, 1)
            halo_link(g, c, k, k not in first)
            if k in last:
                halo_zero_right(g, c, k)
        do_prep_a(0); do_prep_b(0)
        do_prep_a(1); do_prep_b(1)
        for k in range(NSLOT):
            kind, i = slots[k]
            hT = c.hTe[k % 3]
            fo = CPP["fmask"][0] + 2 * k
            mask_ap = bc(g.cpp_t[:, fo:fo + 2].unsqueeze(2), [128, 2, 32])
            steps = dt_steps(g, s, hT, Wdt, 64, cbcv(g, "dt_bias"), g.nega[:], mask_ap)

            def t1():
                def segs(e):
                    e.matmul(s.pD[:, 64:96], lhsT=mk32(g, "gt"), rhs=s.la[:, 0:32], start=True, stop=True)
                    e.matmul(s.pD[:, 96:128], lhsT=mk32(g, "lt"), rhs=s.la[:, 32:64], start=True, stop=True)
                    return e.matmul(s.pD[:, 128:192], lhsT=mk32(g, "ones"), rhs=s.la[:, 0:64], start=True, stop=True)
                P.op("pe", segs, [s.la, g.cmk_t], [s.pD])
                P.op("dve", lambda e: e.tensor_copy(out=s.pD2[:], in_=s.pD[:, 0:192]), [s.pD], [s.pD2])

            def t2b():
                P.op("pool", lambda e: e.tensor_copy(out=s.sg[:, 0:32], in_=s.pD2[:, 64:96]), [s.pD2], [s.sg])
                P.op("pool", lambda e: e.tensor_tensor(out=s.sg[:, 32:64], in0=s.pD2[:, 96:128], in1=Rb[:], op=OP.add),
                     [s.pD2, Rb], [s.sg])
                P.op("pool", lambda e: e.tensor_tensor(out=Rb[:], in0=Rb[:], in1=s.pD2[:, 160:192], op=OP.add),
                     [s.pD2, Rb], [Rb])

            def t3():
                P.op("act", lambda e: e.activation(out=s.wx[:], in_=s.sg[:], func=AF.Exp), [s.sg], [s.wx])
                P.op("act", lambda e: e.activation(out=dec[:], in_=s.pD2[:, 128:160], func=AF.Exp), [s.pD2], [dec])

            def t4():
                P.op("pool", lambda e: e.tensor_tensor(out=wxb[:], in0=s.wx[:], in1=s.dtm[:], op=OP.mult),
                     [s.wx, s.dtm], [wxb])
                P.op("pool", lambda e: e.tensor_tensor(
                    out=g.Sf[:].rearrange("p (h d) -> p h d", d=64), in0=g.Sf[:].rearrange("p (h d) -> p h d", d=64),
                    in1=bc(dec[:].unsqueeze(2), [128, 32, 64]), op=OP.mult), [g.Sf, dec], [g.Sf])
            for st_ in steps + [t1, t2b, t3, t4]:
                st_()
            if k + 2 < NSLOT:
                do_prep_a(k + 2)
            fsteps = []
            if kind == "l":
                u = ub[i % 2]

                def fhalf(hf, u=u, hT=hT, i=i):
                    def mm(e):
                        ins = None
                        for kb in range(8):
                            ins = e.matmul(pf[hf][:], lhsT=hT[:, kb, 1:129], rhs=Wf[:, kb, hf * 512:(hf + 1) * 512],
                                           start=(kb == 0), stop=(kb == 7))
                        return ins
                    P.op("pe", mm, [hT, Wf], [pf[hf]])
                    P.op("dve", lambda e: e.tensor_copy(out=u[:, hf * 512:(hf + 1) * 512], in_=pf[hf][:]),
                         [pf[hf]], [u])
                    if hf == 1:
                        P.dma("sp", g.U[i], u[:], reads=[u])
                fsteps = [lambda: fhalf(0), lambda: fhalf(1)]
            fm_proj_conv(g, c, s, hT, Wxb, 20, 0, [])
            if k + 2 < NSLOT:
                do_prep_b(k + 2)
            to_token_major(g, c, s, 20, fsteps)
            x3 = s.xtok[:, 0:2048].rearrange("p (h d) -> p h d", d=64)
            P.op("pool", lambda e: e.tensor_tensor(out=xdd[1][:].rearrange("p (h d) -> p h d", d=64), in0=x3,
                                                   in1=bc(wxb[:, 32:64].unsqueeze(2), [128, 32, 64]), op=OP.mult),
                 [s.xtok, wxb], [xdd[1]])
            P.op("dve", lambda e: e.tensor_tensor(out=xdd[0][:].rearrange("p (h d) -> p h d", d=64), in0=x3,
                                                  in1=bc(wxb[:, 0:32].unsqueeze(2), [128, 32, 64]), op=OP.mult),
                 [s.xtok, wxb], [xdd[0]])
            banks = [s.pxa[0], s.pxa[1], s.pxc[0], s.pH]
            for di, (xd_, S_) in enumerate(((xdd[0], g.Sf), (xdd[1], g.Sb))):
                for gi in range(4):
                    pst = banks[gi]
                    P.op("pe", lambda e, gi=gi, pst=pst, xd_=xd_: e.matmul(
                        pst[:], lhsT=s.xtok[:, 2048 + gi * 128:2048 + (gi + 1) * 128],
                        rhs=xd_[:, gi * 512:(gi + 1) * 512], start=True, stop=True), [s.xtok, xd_], [pst])
                    P.op("dve", lambda e, gi=gi, pst=pst, S_=S_: e.tensor_tensor(
                        out=S_[:, gi * 512:(gi + 1) * 512], in0=S_[:, gi * 512:(gi + 1) * 512], in1=pst[:], op=OP.add),
                        [S_, pst], [S_])
        if DEBUG:
            P.dma("sp", g.SFB[0], g.Sf[:], reads=[g.Sf])
            P.dma("sp", g.SFB[1], g.Sb[:], reads=[g.Sb])
        P.barrier()


def load_w_gen(g, W, src, nkb, ncols):
    wv = src.rearrange("(kb p) n -> p kb n", p=128)
    for a in range(0, ncols, 512):
        n = min(512, ncols - a)
        g.P.dma("pool", W[:, :, a:a + n], wv[:, :, a:a + n], writes=[W])


def phase_fnet(g):
    nc, P = g.nc, g.P
    with ExitStack() as es:
        T1 = sb(g, es, "T1", [64, 128], BF16)
        P.dma("pool", T1[:], g.t1, writes=[T1])
        V = [sb(g, es, "V%d" % i, [64, 4, D], BF16) for i in range(3)]
        Yt = [sb(g, es, "Yt%d" % i, [128, 4, D], BF16) for i in range(2)]
        p1 = [ps(g, es, "p1_%d" % i, [128, 512], F32) for i in range(8)]
        cnt = 0

        def ldv(tg_):
            if tg_ < 32:
                v_ = V[tg_ % 3]
                P.dma("sp", v_[:], g.U[:, tg_ * 4:(tg_ + 1) * 4, :], writes=[v_])
        ldv(0)
        ldv(1)
        for tg in range(32):
            v, yt = V[tg % 3], Yt[tg % 2]
            ldv(tg + 2)
            for t in range(4):
                for hf in range(2):
                    pp = p1[cnt % 8]
                    P.op("pe", lambda e, pp=pp, t=t, hf=hf, v=v: e.matmul(
                        pp[:], lhsT=T1[:], rhs=v[:, t, hf * 512:(hf + 1) * 512], start=True, stop=True), [T1, v], [pp])
                    if cnt % 2:
                        P.op("act", lambda e, pp=pp, t=t, hf=hf, yt=yt: e.activation(
                            out=yt[:, t, hf * 512:(hf + 1) * 512], in_=pp[:], func=AF.Copy), [pp], [yt])
                    else:
                        P.op("dve", lambda e, pp=pp, t=t, hf=hf, yt=yt: e.tensor_copy(
                            out=yt[:, t, hf * 512:(hf + 1) * 512], in_=pp[:]), [pp], [yt])
                    cnt += 1
            P.dma("sp", g.Y[:, tg * 4:(tg + 1) * 4, :], yt[:], reads=[yt])
        P.barrier()
    with ExitStack() as es:
        T2 = sb(g, es, "T2", [128, 2 * 64 * 68], BF16)
        for a in range(0, 2 * 64 * 68, 1088):
            P.dma("pool", T2[:, a:a + 1088], g.t2[:, a:a + 1088], writes=[T2])
        Yk = [sb(g, es, "Yk%d" % i, [128, 2, D], BF16) for i in range(2)]
        XTs = sb(g, es, "XTs", [128, 8, 2, EXT], BF16)
        p2f = [ps(g, es, "p2_%d" % i, [128, 512], F32) for i in range(8)]
        yv = g.Y.rearrange("(ri k) t c -> k t ri c", ri=2)
        xv = XTs[:].rearrange("p c r (j k) -> p c r j k", k=64)
        for k1 in range(64):
            yk = Yk[k1 % 2]
            P.dma("sp", yk[:], yv[k1], writes=[yk])
            for cg in range(2):
                ppb = p2f[(k1 * 2 + cg) % 8]
                pp = ppb[:, 0:272].rearrange("p (c k) -> p c k", k=68)

                def mm(e, pp=pp, cg=cg, yk=yk, k1=k1):
                    ins = None
                    for cb in range(4):
                        cbx = cg * 4 + cb
                        e.matmul(pp[:, cb, :], lhsT=yk[:, 0, cbx * 128:(cbx + 1) * 128],
                                 rhs=T2[:, k1 * 68:(k1 + 1) * 68], start=True, stop=False)
                        ins = e.matmul(pp[:, cb, :], lhsT=yk[:, 1, cbx * 128:(cbx + 1) * 128],
                                       rhs=T2[:, (64 + k1) * 68:(64 + k1 + 1) * 68], start=False, stop=True)
                    return ins
                P.op("pe", mm, [yk, T2], [ppb])
                for ri in range(2):
                    if (k1 + cg) % 2:
                        P.op("act", lambda e, pp=pp, cg=cg, ri=ri, k1=k1: e.activation(
                            out=xv[:, cg * 4:(cg + 1) * 4, ri, :, k1], in_=pp[:, :, ri * 34:(ri + 1) * 34],
                            func=AF.Copy), [ppb], [XTs])
                    else:
                        P.op("dve", lambda e, pp=pp, cg=cg, ri=ri, k1=k1: e.tensor_copy(
                            out=xv[:, cg * 4:(cg + 1) * 4, ri, :, k1], in_=pp[:, :, ri * 34:(ri + 1) * 34]),
                            [ppb], [XTs])
        for cb in range(8):
            P.dma("sp", g.XT[:, cb * 2 * EXT:(cb + 1) * 2 * EXT].rearrange("p (r t) -> p r t", r=2), XTs[:, cb, :, :],
                  reads=[XTs])
        P.barrier()


def phase_own(g, d):
    nc, P = g.nc, g.P
    with ExitStack() as es:
        c = alloc_chunk_bufs(g, es, 24)
        s = Ctx()
        hH = sb(g, es, "hH", [128, 8, 130], BF16)
        W = g.Wxbc
        Wdt = sb(g, es, "Wdt", [128, 8, 32], BF16)
        load_w_cols(g, Wdt, 6144 + 32 * d, 32)
        s.pxa = [ps(g, es, "pxa%d" % i, [128, 512], F32) for i in range(1)]
        s.pxc = [ps(g, es, "pxc%d" % i, [128, 512], F32) for i in range(1)]
        s.pxb = [s.pxa[0], s.pxc[0]]
        s.pre = [sb(g, es, "pre%d" % i, [128, 3, 130], BF16) for i in range(2)]
        s.pD = ps(g, es, "pD", [128, 512], F32)
        s.pH = ps(g, es, "pH", [128, 512], F32)
        s.pxa = [s.pxa[0], s.pH]
        psc = ps(g, es, "psc", [128, 4, 128], F32)
        pL = [ps(g, es, "pL%d" % i, [128, 4, 128], F32) for i in range(2)]
        s.xcs = sb(g, es, "xcs", [128, 24, 128], BF16, nparts=8)
        s.xtok = sb(g, es, "xtok", [128, 2560], BF16)
        s.dtm = sb(g, es, "dtm", [128, 32], F32)
        s.la = sb(g, es, "la", [128, 32], F32)
        s.wx = sb(g, es, "wx", [128, 32], F32)
        lab = sb(g, es, "lab", [128, 32], BF16)
        nlab = sb(g, es, "nlab", [128, 32], BF16)
        ecum = sb(g, es, "ecum", [128, 32], F32)
        dec = sb(g, es, "dec", [128, 32], F32)
        xd = sb(g, es, "xd", [128, 2048], BF16)
        xdd = sb(g, es, "xdd", [128, 2048], BF16)
        Sbf = sb(g, es, "Sbf", [128, 2048], BF16)
        Dt = [sb(g, es, "Dt%d" % i, [128, 8, 128], BF16) for i in range(2)]
        Lx = [sb(g, es, "Lx%d" % i, [128, 8, 128], BF16) for i in range(2)]
        G = [sb(g, es, "G%d" % i, [128, 8, 128], BF16) for i in range(2)]
        yo = sb(g, es, "yo", [128, 512], F32)
        ytile = sb(g, es, "ytile", [128, 2048], BF16)
        tmp = sb(g, es, "tmp", [128, 2048], BF16)
        yfl = sb(g, es, "yfl", [128, 2048], BF16)
        S = g.Sf if d == 0 else g.Sb
        mxk = "le" if d == 0 else "ge"
        sgk = "gt" if d == 0 else "lt"
        penk = "pen_f" if d == 0 else "pen_b"
        order = list(range(NEXT)) if d == 0 else list(range(NEXT - 1, -1, -1))

        prep(g, c, 0, g.xext[EXT:EXT + 128, :], 0, 1, vmask=cbcv(g, "halo_v"), hT=hH)

        def do_prep_a(ci):
            prep_a(g, c, ci, g.xext[ci * 128:(ci + 1) * 128, :])

        def do_prep_b(ci, prev_ci):
            hT = c.hTe[ci % 3]
            vm = None
            if ci == 0:
                vm = cbcv(g, "emask_bc", 0, 128)
            if ci == NEXT - 1:
                vm = cbcv(g, "emask_bc", 128, 128)
            prep_b(g, c, ci, 0, 1, vmask=vm)
            if prev_ci is not None:
                nb = c.hTe[prev_ci % 3]
                if ci == prev_ci + 1:
                    P.op("pool", lambda e: e.tensor_copy(out=hT[:, :, 0:1], in_=nb[:, :, 128:129]), [nb], [hT])
                    P.op("pool", lambda e: e.tensor_copy(out=nb[:, :, 129:130], in_=hT[:, :, 1:2]), [hT], [nb])
                else:
                    P.op("pool", lambda e: e.tensor_copy(out=hT[:, :, 129:130], in_=nb[:, :, 1:2]), [nb], [hT])
                    P.op("pool", lambda e: e.tensor_copy(out=nb[:, :, 0:1], in_=hT[:, :, 128:129]), [hT], [nb])
            if ci == 0:
                P.op("pool", lambda e: e.tensor_copy(out=hT[:, :, 0:1], in_=hH[:, :, 1:2]), [hH], [hT])
            if ci == NEXT - 1:
                P.op("pool", lambda e: e.tensor_copy(out=hT[:, :, 129:130], in_=hH[:, :, 2:3]), [hH], [hT])

        do_prep_a(order[0]); do_prep_b(order[0], None)
        do_prep_a(order[1]); do_prep_b(order[1], order[0])
        for oi, ci in enumerate(order):
            hT = c.hTe[ci % 3]
            if d == 1:
                P.dma("sp", yfl[:], g.YF[ci], writes=[yfl])
            mask_ap = bc(cpp(g, "emask", ci).unsqueeze(2), [128, 1, 32])
            steps = dt_steps(g, s, hT, Wdt, 32, cbcv(g, "dt_bias", 32 * d, 32), g.nega[:, 32 * d:32 * (d + 1)], mask_ap)

            def u1():
                P.op("pool", lambda e: e.tensor_copy(out=lab[:], in_=s.la[:]), [s.la], [lab])
                P.op("pool", lambda e: e.tensor_scalar(out=nlab[:], in0=lab[:], scalar1=-1.0, scalar2=None, op0=OP.mult),
                     [lab], [nlab])

                def segs(e):
                    e.matmul(s.pD[:, 64:96], lhsT=mk32(g, mxk), rhs=s.la[:], start=True, stop=True)
                    e.matmul(s.pD[:, 96:128], lhsT=mk32(g, sgk), rhs=s.la[:], start=True, stop=True)
                    return e.matmul(s.pD[:, 128:160], lhsT=mk32(g, "ones"), rhs=s.la[:], start=True, stop=True)
                P.op("pe", segs, [s.la, g.cmk_t], [s.pD])

            def u2():
                P.op("act", lambda e: e.activation(out=ecum[:], in_=s.pD[:, 64:96], func=AF.Exp), [s.pD], [ecum])
                P.op("act", lambda e: e.activation(out=s.wx[:], in_=s.pD[:, 96:128], func=AF.Exp), [s.pD], [s.wx])
                P.op("act", lambda e: e.activation(out=dec[:], in_=s.pD[:, 128:160], func=AF.Exp), [s.pD], [dec])

            def u3():
                P.op("pool", lambda e: e.tensor_tensor(out=s.wx[:], in0=s.wx[:], in1=s.dtm[:], op=OP.mult),
                     [s.wx, s.dtm], [s.wx])
            for st_ in steps + [u1, u2, u3]:
                st_()
            if oi + 2 < NEXT:
                do_prep_a(order[oi + 2])
            fm_proj_conv(g, c, s, hT, W, 24, 0, [])
            if oi + 2 < NEXT:
                do_prep_b(order[oi + 2], order[oi + 1])
            to_token_major(g, c, s, 20)
            x3 = s.xtok[:, 0:2048].rearrange("p (h d) -> p h d", d=64)
            P.op("dve", lambda e: e.tensor_tensor(out=xd[:].rearrange("p (h d) -> p h d", d=64), in0=x3,
                                                   in1=bc(s.dtm[:].unsqueeze(2), [128, 32, 64]), op=OP.mult),
                 [s.xtok, s.dtm], [xd])
            P.op("dve", lambda e: e.tensor_tensor(out=xdd[:].rearrange("p (h d) -> p h d", d=64), in0=x3,
                                                   in1=bc(s.wx[:].unsqueeze(2), [128, 32, 64]), op=OP.mult),
                 [s.xtok, s.wx], [xdd])
            P.op("act", lambda e: e.activation(out=Sbf[:], in_=S[:], func=AF.Copy), [S], [Sbf])

            def sc(e):
                ins = None
                for gi in range(4):
                    ins = e.matmul(psc[:, gi, :], lhsT=s.xcs[:, 16 + gi, :], rhs=s.xcs[:, 20 + gi, :], start=True, stop=True)
                return ins
            P.op("pe", sc, [s.xcs], [psc])
            for gi in range(4):
                dt_, lx, gg = Dt[gi % 2], Lx[gi % 2], G[gi % 2]
                P.op("pool", lambda e, gi=gi, dt_=dt_: e.tensor_tensor(
                    out=dt_[:], in0=bc(lab[:, gi * 8:(gi + 1) * 8].unsqueeze(2), [128, 8, 128]),
                    in1=bc(mk16(g, mxk).unsqueeze(1), [128, 8, 128]), op=OP.mult), [lab, g.cmkb_t], [dt_])
                for hh in range(2):
                    def mmL(e, gi=gi, hh=hh, dt_=dt_):
                        e.matmul(pL[hh][:], lhsT=mk16(g, "ones"), rhs=dt_[:, hh * 4:(hh + 1) * 4, :], start=True, stop=False)
                        e.matmul(pL[hh][:], lhsT=mk16(g, mxk),
                                 rhs=bc(nlab[:, gi * 8 + hh * 4:gi * 8 + hh * 4 + 4].unsqueeze(2), [128, 4, 128]),
                                 start=False, stop=False)
                        return e.matmul(pL[hh][:], lhsT=mk16(g, "ident"),
                                        rhs=bc(mk16(g, penk).unsqueeze(1), [128, 4, 128]), start=False, stop=True)
                    P.op("pe", mmL, [dt_, nlab, g.cmkb_t], [pL[hh]])
                    P.op("act", lambda e, hh=hh, lx=lx: e.activation(out=lx[:, hh * 4:(hh + 1) * 4, :], in_=pL[hh][:],
                                                                   func=AF.Exp), [pL[hh]], [lx])
                P.op("dve", lambda e, gi=gi, lx=lx, gg=gg: e.tensor_tensor(
                    out=gg[:], in0=lx[:], in1=bc(psc[:, gi, :].unsqueeze(1), [128, 8, 128]), op=OP.mult),
                    [lx, psc], [gg])

                def mmy(e, gi=gi, gg=gg):
                    ins = None
                    for h in range(8):
                        hh = gi * 8 + h
                        ins = e.matmul(s.pH[:, h * 64:(h + 1) * 64], lhsT=gg[:, h, :], rhs=xd[:, hh * 64:(hh + 1) * 64],
                                       start=True, stop=True)
                    return ins
                P.op("pe", mmy, [gg, xd], [s.pH])
                P.op("pe", lambda e, gi=gi: e.matmul(s.pxb[0][:], lhsT=s.xcs[:, 20 + gi, :],
                                                     rhs=Sbf[:, gi * 512:(gi + 1) * 512], start=True, stop=True),
                     [s.xcs, Sbf], [s.pxb[0]])
                P.op("dve", lambda e, gi=gi: e.tensor_tensor(
                    out=yo[:].rearrange("p (h d) -> p h d", d=64), in0=s.pxb[0][:].rearrange("p (h d) -> p h d", d=64),
                    in1=bc(ecum[:, gi * 8:(gi + 1) * 8].unsqueeze(2), [128, 8, 64]), op=OP.mult),
                    [s.pxb[0], ecum], [yo])
                P.op("dve", lambda e, gi=gi: e.tensor_tensor(out=ytile[:, gi * 512:(gi + 1) * 512], in0=yo[:],
                                                              in1=s.pH[:], op=OP.add), [yo, s.pH], [ytile])
                P.op("pe", lambda e, gi=gi: e.matmul(s.pxb[1][:], lhsT=s.xtok[:, 2048 + gi * 128:2048 + (gi + 1) * 128],
                                                     rhs=xdd[:, gi * 512:(gi + 1) * 512], start=True, stop=True),
                     [s.xtok, xdd], [s.pxb[1]])
                P.op("dve", lambda e, gi=gi: e.tensor_tensor(
                    out=S[:, gi * 512:(gi + 1) * 512].rearrange("p (h d) -> p h d", d=64),
                    in0=S[:, gi * 512:(gi + 1) * 512].rearrange("p (h d) -> p h d", d=64),
                    in1=bc(dec[:, gi * 8:(gi + 1) * 8].unsqueeze(2), [128, 8, 64]), op=OP.mult), [S, dec], [S])
                P.op("dve", lambda e, gi=gi: e.tensor_tensor(out=S[:, gi * 512:(gi + 1) * 512],
                                                              in0=S[:, gi * 512:(gi + 1) * 512], in1=s.pxb[1][:],
                                                              op=OP.add), [S, s.pxb[1]], [S])
            if d == 0:
                P.op("pool", lambda e: e.tensor_tensor(out=tmp[:].rearrange("p (h d) -> p h d", d=64), in0=x3,
                                                       in1=bc(cbcv(g, "d_skip").unsqueeze(2), [128, 32, 64]),
                                                       op=OP.mult), [s.xtok, g.cbc_t], [tmp])
                P.op("pool", lambda e: e.tensor_tensor(out=tmp[:], in0=tmp[:], in1=ytile[:], op=OP.add),
                     [tmp, ytile], [tmp])
                P.dma("sp", g.YF[ci], tmp[:], reads=[tmp])
            else:
                P.op("pool", lambda e: e.tensor_tensor(out=tmp[:], in0=yfl[:], in1=ytile[:], op=OP.add),
                     [yfl, ytile], [tmp])
                P.dma("sp", g.YT[ci], tmp[:], reads=[tmp])
        P.barrier()


def phase_merge(g):
    phase_merge_a(g)
    phase_merge_b(g)


def phase_merge_a(g):
    nc, P = g.nc, g.P
    with ExitStack() as es:
        c = alloc_chunk_bufs(g, es, 0)
        Wz = sb(g, es, "Wz", [128, 8, 2048], BF16)
        Wgs = sb(g, es, "Wgs", [128, 8, 1024], BF16)
        Wsb = sb(g, es, "Wsb", [128, 16, 1024], BF16)
        load_w_cols(g, Wz, 4096, 2048)
        load_w_cols(g, Wgs, 7232, 1024)
        load_w_gen(g, Wsb, g.w_sb, 16, 1024)
        sg = sb(g, es, "ssdg", [128, 2048], F32)
        P.dma("sp", sg[:], g.cbg[:, CBG["ssd_g"][0]:CBG["ssd_g"][0] + 2048], writes=[sg])
        pz = [ps(g, es, "pz%d" % i, [128, 512], F32) for i in range(4)]
        pbs = [ps(g, es, "pbs%d" % i, [128, 512], F32) for i in range(2)]
        yt = [sb(g, es, "yt%d" % i, [128, 2048], BF16) for i in range(2)]
        zs = sb(g, es, "zs", [128, 4, 512], BF16, nparts=4)
        t = sb(g, es, "t", [128, 4, 512], F32, nparts=4)
        jk = sb(g, es, "jk", [128, 512], BF16)
        st2 = sb(g, es, "st2", [128, 16], F32)
        ysn = sb(g, es, "ysn", [128, 4, 512], BF16, nparts=4)
        ysnT = sb(g, es, "ysnT", [128, 16, 128], BF16)
        sgs = sb(g, es, "sgs", [128, 2, 512], F32, nparts=2)
        ms = [sb(g, es, "ms%d" % i, [128, 1024], BF16) for i in range(2)]
        prep_a(g, c, 0, g.xext[0:128, :])
        prep_b(g, c, 0, 0, 1)
        for ci in range(NEXT):
            hT = c.hTe[ci % 3]
            y = yt[ci % 2]
            P.dma("sp", y[:], g.YT[ci], writes=[y])
            if ci + 1 < NEXT:
                prep_a(g, c, ci + 1, g.xext[(ci + 1) * 128:(ci + 2) * 128, :])
            for gi in range(4):
                def mm(e, gi=gi):
                    ins = None
                    for kb in range(8):
                        ins = e.matmul(pz[gi][:], lhsT=hT[:, kb, 1:129], rhs=Wz[:, kb, gi * 512:(gi + 1) * 512],
                                       start=(kb == 0), stop=(kb == 7))
                    return ins
                P.op("pe", mm, [hT, Wz], [pz[gi]])
            for gi in range(4):
                P.op("act", lambda e, gi=gi: e.activation(out=zs[:, gi, :], in_=pz[gi][:], func=AF.Silu),
                     [pz[gi]], [(zs, gi)])
            for gi in range(4):
                P.op("dve", lambda e, gi=gi: e.tensor_tensor(out=t[:, gi, :], in0=zs[:, gi, :],
                                                              in1=y[:, gi * 512:(gi + 1) * 512], op=OP.mult),
                     [(zs, gi), y], [(t, gi)])
            for gi in range(4):
                P.op("act", lambda e, gi=gi: e.activation(out=jk[:], in_=t[:, gi, :], func=AF.Square,
                                                          accum_out=st2[:, gi:gi + 1]), [(t, gi)], [jk, st2])
            P.op("dve", lambda e: e.tensor_scalar(out=st2[:, 4:8], in0=st2[:, 0:4], scalar1=1.0 / 512, scalar2=EPS,
                                                   op0=OP.mult, op1=OP.add), [st2], [st2])
            P.op("act", lambda e: e.activation(out=st2[:, 8:12], in_=st2[:, 4:8], func=AF.Ln), [st2], [st2])
            P.op("act", lambda e: e.activation(out=st2[:, 12:16], in_=st2[:, 8:12], func=AF.Exp, scale=-0.5), [st2], [st2])
            for gi in range(4):
                P.op("dve", lambda e, gi=gi: e.scalar_tensor_tensor(
                    out=ysn[:, gi, :], in0=t[:, gi, :], scalar=st2[:, 12 + gi:13 + gi], in1=sg[:, gi * 512:(gi + 1) * 512],
                    op0=OP.mult, op1=OP.mult), [(t, gi), st2, sg], [(ysn, gi)])
            for rnd in range(2):
                def tr(e, rnd=rnd):
                    ins = None
                    for j in range(8):
                        blk = rnd * 8 + j
                        ins = e.transpose(out=c.pA[:, j, :], in_=ysn[:, blk // 4, (blk % 4) * 128:(blk % 4 + 1) * 128],
                                          identity=mk16(g, "ident"))
                    return ins
                P.op("pe", tr, [ysn, g.cmkb_t], [c.pA])
                P.op("dve", lambda e, rnd=rnd: e.tensor_copy(out=ysnT[:, rnd * 8:(rnd + 1) * 8, :], in_=c.pA[:]),
                     [c.pA], [ysnT])
            if ci + 1 < NEXT:
                prep_b(g, c, ci + 1, 0, 1)
            for hf in range(2):
                def mmg(e, hf=hf):
                    ins = None
                    for kb in range(8):
                        ins = e.matmul(pz[hf][:], lhsT=hT[:, kb, 1:129], rhs=Wgs[:, kb, hf * 512:(hf + 1) * 512],
                                       start=(kb == 0), stop=(kb == 7))
                    return ins
                P.op("pe", mmg, [hT, Wgs], [pz[hf]])

                def mms(e, hf=hf):
                    ins = None
                    for kb in range(16):
                        ins = e.matmul(pbs[hf][:], lhsT=ysnT[:, kb, :], rhs=Wsb[:, kb, hf * 512:(hf + 1) * 512],
                                       start=(kb == 0), stop=(kb == 15))
                    return ins
                P.op("pe", mms, [ysnT, Wsb], [pbs[hf]])
            m = ms[ci % 2]
            for hf in range(2):
                P.op("act", lambda e, hf=hf: e.activation(out=sgs[:, hf, :], in_=pz[hf][:], func=AF.Sigmoid),
                     [pz[hf]], [(sgs, hf)])
                P.op("dve", lambda e, m=m, hf=hf: e.tensor_tensor(out=m[:, hf * 512:(hf + 1) * 512], in0=sgs[:, hf, :],
                                                                   in1=pbs[hf][:], op=OP.mult),
                     [(sgs, hf), pbs[hf]], [m])
            P.dma("sp", g.MS[ci], m[:], reads=[m])
        P.barrier()


def phase_merge_b(g):
    nc, P = g.nc, g.P
    with ExitStack() as es:
        c = alloc_chunk_bufs(g, es, 0)
        Wgf = sb(g, es, "Wgf", [128, 8, 1024], BF16)
        Wfa = sb(g, es, "Wfa", [128, 8, 1024], BF16)
        Wo = sb(g, es, "Wo", [128, 8, 1024], BF16)
        Tcs = sb(g, es, "Tcs", [128, 256], BF16)
        load_w_cols(g, Wgf, 6208, 1024)
        load_w_gen(g, Wfa, g.w_fa, 8, 1024)
        load_w_gen(g, Wo, g.w_o, 8, 1024)
        P.dma("pool", Tcs[:], g.tcs, writes=[Tcs])
        pb0 = ps(g, es, "pb0", [128, 2, 512], F32)
        pb1 = ps(g, es, "pb1", [128, 2, 512], F32)
        pb2 = ps(g, es, "pb2", [128, 2, 512], F32)
        xtc = [sb(g, es, "xtc%d" % i, [128, 8, 2, 128], BF16) for i in range(2)]
        msl = [sb(g, es, "msl%d" % i, [128, 1024], BF16) for i in range(2)]
        mixT = sb(g, es, "mixT", [128, 8, 128], BF16)
        sgf = sb(g, es, "sgf", [128, 1024], F32)
        tmp = sb(g, es, "tmpm", [128, 1024], F32)
        mrg = sb(g, es, "mrg", [128, 1024], BF16)
        mrgT = sb(g, es, "mrgT", [128, 8, 128], BF16)
        l1 = [sb(g, es, "l1_%d" % i, [128, 1024], F32) for i in range(2)]
        xtv = g.XT.rearrange("p (c r t) -> p c r t", c=8, r=2)
        prep_a(g, c, 0, g.xext[0:128, :])
        prep_b(g, c, 0, 0, 1)
        for ci in range(NEXT):
            hT = c.hTe[ci % 3]
            xt = c.xt[ci % 2]
            if ci + 1 < NEXT:
                prep_a(g, c, ci + 1, g.xext[(ci + 1) * 128:(ci + 2) * 128, :])
            xc_, m = xtc[ci % 2], msl[ci % 2]
            P.dma("sp", xc_[:], xtv[:, :, :, ci * 128:(ci + 1) * 128], writes=[xc_])
            P.dma("sp", m[:], g.MS[ci], writes=[m])
            for cg in range(2):
                def mmx(e, cg=cg):
                    e.matmul(pb2[:, cg, :], lhsT=Tcs[:, 0:128], rhs=xc_[:, cg * 4:(cg + 1) * 4, 0, :], start=True, stop=False)
                    return e.matmul(pb2[:, cg, :], lhsT=Tcs[:, 128:256], rhs=xc_[:, cg * 4:(cg + 1) * 4, 1, :],
                                    start=False, stop=True)
                P.op("pe", mmx, [Tcs, xc_], [pb2])
            P.op("act", lambda e: e.activation(out=mixT[:].rearrange("p a b -> p (a b)"),
                                               in_=pb2[:].rearrange("p a b -> p (a b)"), func=AF.Copy), [pb2], [mixT])
            for hf in range(2):
                def mmf(e, hf=hf):
                    ins = None
                    for kb in range(8):
                        ins = e.matmul(pb0[:, hf, :], lhsT=mixT[:, kb, :], rhs=Wfa[:, kb, hf * 512:(hf + 1) * 512],
                                       start=(kb == 0), stop=(kb == 7))
                    return ins
                P.op("pe", mmf, [mixT, Wfa], [pb0])

                def mmg(e, hf=hf):
                    ins = None
                    for kb in range(8):
                        ins = e.matmul(pb1[:, hf, :], lhsT=hT[:, kb, 1:129], rhs=Wgf[:, kb, hf * 512:(hf + 1) * 512],
                                       start=(kb == 0), stop=(kb == 7))
                    return ins
                P.op("pe", mmg, [hT, Wgf], [pb1])
            P.op("act", lambda e: e.activation(out=sgf[:], in_=pb1[:].rearrange("p a b -> p (a b)"), func=AF.Sigmoid),
                 [pb1], [sgf])
            P.op("dve", lambda e: e.tensor_tensor(out=tmp[:], in0=sgf[:], in1=pb0[:].rearrange("p a b -> p (a b)"),
                                                   op=OP.mult), [sgf, pb0], [tmp])
            P.op("dve", lambda e, m=m: e.tensor_tensor(out=mrg[:], in0=tmp[:], in1=m[:], op=OP.add), [tmp, m], [mrg])

            def tr(e):
                ins = None
                for kb in range(8):
                    ins = e.transpose(out=c.pA[:, kb, :], in_=mrg[:, kb * 128:(kb + 1) * 128], identity=mk16(g, "ident"))
                return ins
            P.op("pe", tr, [mrg, g.cmkb_t], [c.pA])
            P.op("act", lambda e: e.activation(out=mrgT[:], in_=c.pA[:], func=AF.Copy), [c.pA], [mrgT])
            if ci + 1 < NEXT:
                prep_b(g, c, ci + 1, 0, 1)
            for hf in range(2):
                def mmo(e, hf=hf):
                    ins = None
                    for kb in range(8):
                        ins = e.matmul(pb2[:, hf, :], lhsT=mrgT[:, kb, :], rhs=Wo[:, kb, hf * 512:(hf + 1) * 512],
                                       start=(kb == 0), stop=(kb == 7))
                    return ins
                P.op("pe", mmo, [mrgT, Wo], [pb2])
            l = l1[ci % 2]
            P.op("dve", lambda e, l=l: e.tensor_tensor(out=l[:], in0=pb2[:].rearrange("p a b -> p (a b)"),
                                                        in1=g.gbc[:, 0:D], op=OP.mult), [pb2, g.gbc], [l])
            P.op("pool", lambda e, l=l, xt=xt: e.tensor_tensor(out=l[:], in0=l[:], in1=xt[:], op=OP.add), [l, xt], [l])
            P.dma("sp", g.L1[ci], l[:], reads=[l])
        P.barrier()


def phase_ffn(g):
    nc, P = g.nc, g.P
    with ExitStack() as es:
        c = alloc_chunk_bufs(g, es, 0)
        h2T = sb(g, es, "h2T", [128, 8, EXT], BF16, nparts=NEXT)
        Wd = sb(g, es, "Wd", [128, NFB, 1024], BF16)
        load_w_gen(g, Wd, g.w_down, NFB, 1024)
        fg = sb(g, es, "fg", [128, 1024], F32)
        P.dma("sp", fg[:], g.cbg[:, CBG["final_g"][0]:CBG["final_g"][0] + 1024], writes=[fg])
        for ci in range(NEXT):
            i2 = ci % 2
            xt, st, xn = c.xt[i2], c.st[i2], c.xn[i2]
            P.dma("sp", xt[:], g.L1[ci], writes=[xt])
            P.op("act", lambda e: e.activation(out=c.junk[:], in_=xt[:], func=AF.Square, accum_out=st[:, 0:1]),
                 [xt], [c.junk, st])
            P.op("dve", lambda e: e.tensor_scalar(out=st[:, 1:2], in0=st[:, 0:1], scalar1=1.0 / D, scalar2=EPS,
                                                   op0=OP.mult, op1=OP.add), [st], [st])
            P.op("act", lambda e: e.activation(out=st[:, 2:3], in_=st[:, 1:2], func=AF.Sqrt), [st], [st])
            P.op("dve", lambda e: e.reciprocal(out=st[:, 3:4], in_=st[:, 2:3]), [st], [st])
            P.op("act", lambda e: e.activation(out=xn[:], in_=xt[:], func=AF.Copy, scale=st[:, 3:4]), [xt, st], [xn])

            def tr(e):
                ins = None
                for kb in range(8):
                    ins = e.transpose(out=c.pA[:, kb, :], in_=xn[:, kb * 128:(kb + 1) * 128], identity=mk16(g, "ident"))
                return ins
            P.op("pe", tr, [xn, g.cmkb_t], [c.pA])
            for kb in range(8):
                P.op("dve", lambda e, kb=kb, ci=ci: e.tensor_scalar(
                    out=h2T[:, kb, ci * 128:(ci + 1) * 128], in0=c.pA[:, kb, :], scalar1=modA(g, 4, kb),
                    scalar2=modA(g, 5, kb), op0=OP.mult, op1=OP.add), [c.pA, g.modv], [(h2T, ci)])
            if ci in (0, NEXT - 1):
                vm = cbcv(g, "emask_bc", 0 if ci == 0 else 128, 128)
                P.op("pool", lambda e, ci=ci, vm=vm: e.tensor_tensor(
                    out=h2T[:, :, ci * 128:(ci + 1) * 128], in0=h2T[:, :, ci * 128:(ci + 1) * 128],
                    in1=bc(vm.unsqueeze(1), [128, 8, 128]), op=OP.mult), [(h2T, ci), g.cbc_t], [(h2T, ci)])
        NB = 4
        pu = [ps(g, es, "pu%d" % i, [128, 512], F32) for i in range(2)]
        pd = ps(g, es, "pd", [128, 2, 512], F32)
        aT = sb(g, es, "aT", [128, NFB, 512], BF16, nparts=NFB)
        wu = [sb(g, es, "wu%d" % i, [128, 8, 2, 128], BF16) for i in range(3)]
        ug = [sb(g, es, "ug%d" % i, [128, 10, 64], BF16) for i in range(2)]
        dg = [sb(g, es, "dg%d" % i, [128, 4, 128], BF16) for i in range(4)]
        pcv = [ps(g, es, "pcv%d" % i, [128, 512], F32) for i in range(2)]
        acc = [sb(g, es, "acc%d" % i, [128, 8, 64], F32) for i in range(2)]
        sgl = sb(g, es, "sgl", [128, 512], F32)
        lt = [sb(g, es, "lt%d" % i, [128, 1024], F32) for i in range(2)]
        yy = [sb(g, es, "yy%d" % i, [128, 1024], F32) for i in range(2)]
        jk = c.junk
        st = [sb(g, es, "stf%d" % i, [128, 4], F32) for i in range(2)]
        wuv = g.w_up.rearrange("(kb p) (gv n) -> p kb gv n", p=128, gv=2)
        l1f = g.L1.rearrange("c p d -> (c p) d")
        cnt = 0
        nitem = NB * NFB

        def issue_w(i):
            if i < nitem:
                fb_ = i % NFB
                w_ = wu[i % 3]
                for gv_ in range(2):
                    P.dma("pool", w_[:, :, gv_, :], wuv[:, :, gv_, fb_ * 128:(fb_ + 1) * 128], writes=[w_])
        issue_w(0)
        issue_w(1)
        for blk in range(NB):
            base = blk * 512
            hparts = [(h2T, i) for i in range(base // 128, (base + 640 + 127) // 128)]
            for fb in range(NFB):
                w = wu[cnt % 3]
                issue_w(cnt + 2)
                cnt += 1
                for gv in range(2):
                    u = ug[gv]
                    a = acc[gv]
                    for j in range(2):
                        def mm(e, j=j, gv=gv, w=w):
                            ins = None
                            for kb in range(8):
                                ins = e.matmul(pu[j][:, 0:320], lhsT=w[:, kb, gv, :],
                                               rhs=h2T[:, kb, base + j * 320:base + (j + 1) * 320],
                                               start=(kb == 0), stop=(kb == 7))
                            return ins
                        P.op("pe", mm, [w] + hparts, [pu[j]])
                        P.op("act", lambda e, j=j, u=u: e.activation(
                            out=u[:].rearrange("p r c -> p (r c)")[:, j * 320:(j + 1) * 320], in_=pu[j][:, 0:320],
                            func=AF.Copy), [pu[j]], [u])
                    cf = gv * NFB + fb
                    wt = lambda t: cpp(g, "cw_ffn", t * 44 + cf)
                    P.op("act", lambda e, u=u, a=a, cf=cf: e.activation(
                        out=a[:], in_=u[:, 1:9, :], func=AF.Identity, scale=cpp(g, "cw_ffn", 4 * 44 + cf),
                        bias=cpp(g, "cb_ffn", cf)), [u, g.cpp_t], [a])
                    dgt = dg[(cnt * 2 + gv) % 4]
                    for i_, t_ in enumerate((1, 7, 3, 5)):
                        P.op("pool", lambda e, i_=i_, t_=t_, dgt=dgt, cf=cf: e.tensor_scalar(
                            out=dgt[:, i_, :], in0=mk16(g, "ident"), scalar1=cpp(g, "cw_ffn", t_ * 44 + cf), scalar2=0.0,
                            op0=OP.mult, op1=OP.add), [g.cmkb_t, g.cpp_t], [dgt])
                    pc = pcv[gv]
                    pc3 = pc[:].rearrange("p (r c) -> p r c", c=64)

                    def mcv(e, u=u, dgt=dgt, pc3=pc3):
                        e.matmul(pc3, lhsT=dgt[:, 0, :], rhs=u[:, 0:8, :], start=True, stop=False)
                        e.matmul(pc3, lhsT=dgt[:, 1, :], rhs=u[:, 2:10, :], start=False, stop=False)
                        e.matmul(pc3[:, :, 1:64], lhsT=dgt[:, 2, :], rhs=u[:, 1:9, 0:63], start=False, stop=False)
                        return e.matmul(pc3[:, :, 0:63], lhsT=dgt[:, 3, :], rhs=u[:, 1:9, 1:64], start=False, stop=True)
                    P.op("pe", mcv, [u, dgt], [pc])
                    for (kh, kw) in ((0, 0), (0, 2), (2, 0), (2, 2)):
                        dy, dx = kh - 1, kw - 1
                        c0, c1 = max(0, -dx), 64 - max(0, dx)
                        P.op("dve", lambda e, u=u, a=a, dy=dy, dx=dx, c0=c0, c1=c1, t=kh * 3 + kw, cf=cf:
                             e.scalar_tensor_tensor(out=a[:, :, c0:c1], in0=u[:, 1 + dy:9 + dy, c0 + dx:c1 + dx],
                                                    scalar=cpp(g, "cw_ffn", t * 44 + cf), in1=a[:, :, c0:c1],
                                                    op0=OP.mult, op1=OP.add), [u, a, g.cpp_t], [a])
                    P.op("dve", lambda e, a=a, pc3=pc3: e.tensor_tensor(out=a[:], in0=a[:], in1=pc3, op=OP.add),
                         [a, pc], [a])
                P.op("act", lambda e: e.activation(out=sgl[:], in_=acc[0][:].rearrange("p r c -> p (r c)"), func=AF.Silu),
                     [acc[0]], [sgl])
                P.op("dve", lambda e, fb=fb: e.tensor_tensor(out=aT[:, fb, :], in0=sgl[:],
                                                              in1=acc[1][:].rearrange("p r c -> p (r c)"), op=OP.mult),
                     [sgl, acc[1]], [(aT, fb)])
            for tcn in range(4):
                o0 = blk * 512 + tcn * 128
                i2 = (blk * 4 + tcn) % 2
                l, y, s4 = lt[i2], yy[i2], st[i2]
                o = y
                P.dma("sp", l[:], l1f[o0 + 64:o0 + 64 + 128, :], writes=[l])
                for hf in range(2):
                    def mmd(e, hf=hf, tcn=tcn):
                        ins = None
                        for fb in range(NFB):
                            ins = e.matmul(pd[:, hf, :], lhsT=aT[:, fb, tcn * 128:(tcn + 1) * 128],
                                           rhs=Wd[:, fb, hf * 512:(hf + 1) * 512], start=(fb == 0), stop=(fb == NFB - 1))
                        return ins
                    P.op("pe", mmd, [aT, Wd], [pd])
                P.op("dve", lambda e, y=y: e.tensor_tensor(out=y[:], in0=pd[:].rearrange("p a b -> p (a b)"),
                                                            in1=g.gbc[:, D:2 * D], op=OP.mult), [pd, g.gbc], [y])
                P.op("pool", lambda e, y=y, l=l: e.tensor_tensor(out=y[:], in0=y[:], in1=l[:], op=OP.add), [y, l], [y])
                P.op("act", lambda e, y=y, s4=s4: e.activation(out=jk[:], in_=y[:], func=AF.Square,
                                                               accum_out=s4[:, 0:1]), [y], [jk, s4])
                P.op("dve", lambda e, s4=s4: e.tensor_scalar(out=s4[:, 1:2], in0=s4[:, 0:1], scalar1=1.0 / D,
                                                              scalar2=EPS, op0=OP.mult, op1=OP.add), [s4], [s4])
                P.op("act", lambda e, s4=s4: e.activation(out=s4[:, 2:3], in_=s4[:, 1:2], func=AF.Sqrt), [s4], [s4])
                P.op("dve", lambda e, s4=s4: e.reciprocal(out=s4[:, 3:4], in_=s4[:, 2:3]), [s4], [s4])
                P.op("dve", lambda e, y=y, s4=s4, o=o: e.scalar_tensor_tensor(
                    out=o[:], in0=y[:], scalar=s4[:, 3:4], in1=fg[:], op0=OP.mult, op1=OP.mult), [y, s4, fg], [y])
                P.dma("sp", g.out[o0:o0 + 128, :], o[:], reads=[o])
        P.barrier()


def _pm(v):
    v = np.asarray(v, np.float32)
    return np.ascontiguousarray(v.reshape(-1, 128).T)


def _rb(v):
    v = np.asarray(v, np.float32).reshape(1, -1)
    return np.ascontiguousarray(np.broadcast_to(v, (128, v.shape[1])))


def _const_tables():
    k = np.arange(128)[:, None]
    m = np.arange(128)[None, :]
    mats = [np.ones((128, 128)), k <= m, k >= m, k > m, k < m, k == m,
            np.where(m < k, -BIG, 0.0), np.where(m > k, -BIG, 0.0)]
    cmk = np.concatenate([np.asarray(a, np.float32) for a in mats], axis=1)
    t1i = np.arange(64)[:, None] * np.arange(64)[None, :]
    th = 2 * np.pi * t1i / 64.0
    t1 = np.concatenate([np.cos(th), -np.sin(th)], axis=1).astype(np.float32)
    j = np.arange(128)[:, None] * np.arange(128)[None, :]
    thc = 2 * np.pi * j / 128.0
    tcs = (np.concatenate([np.cos(thc), np.sin(thc)], axis=1) / 1024.0).astype(np.float32)
    return cmk, t1, tcs


def _t2_tables(q):
    t2 = np.arange(128, dtype=np.float64)[:, None, None]
    k1 = np.arange(64, dtype=np.float64)[None, :, None]
    k2 = (32 * q - 1 + np.arange(34, dtype=np.float64))[None, None, :]
    kk = np.mod(k1 + 64 * k2, 8192)
    th = 2 * np.pi * np.mod(kk * t2, 8192) / 8192.0
    Mr, Mi = np.cos(th), -np.sin(th)
    ta = np.concatenate([Mr, Mi], axis=2)
    tb = np.concatenate([-Mi, Mr], axis=2)
    return np.concatenate([ta.reshape(128, -1), tb.reshape(128, -1)], axis=1).astype(np.float32)


_CACHE = {}


def kernel(x, c, ctx, c_ctx, w_mod, b_mod, norm1_g, w_in, conv_ssd_w, conv_ssd_b, dt_bias, a_log,
           d_skip, ssd_norm_g, w_fa, w_sb, w_o, norm2_g, w_up, conv_ffn_w, conv_ffn_b, w_down, final_g):
    f = lambda a: np.asarray(a, np.float32)
    x, c, ctx, c_ctx = f(x), f(c), f(ctx), f(c_ctx)
    cmk, t1, tcs = _const_tables()
    in_maps = []
    bm = f(b_mod)[0]
    for core in range(8):
        b, q = divmod(core, 4)
        e0 = 2048 * q - 64
        xext = np.zeros((EXT + 128, D), np.float32)
        lo, hi = max(e0, 0), min(e0 + EXT, SEQ)
        xext[lo - e0:hi - e0] = x[b, lo:hi]
        hv = np.zeros(2, np.float32)
        if e0 - 1 >= 0:
            xext[EXT] = x[b, e0 - 1]; hv[0] = 1
        if e0 + EXT < SEQ:
            xext[EXT + 1] = x[b, e0 + EXT]; hv[1] = 1
        tok = e0 + np.arange(EXT)
        valid = ((tok >= 0) & (tok < SEQ)).astype(np.float32)
        fmask = np.zeros((NSLOT, 128, 2), np.float32)
        fmask[0:2, :, 0] = 1
        fmask[66:68, :, 1] = 1
        lt = np.arange(SEQ).reshape(NCH, 128)
        fmask[2:66, :, 0] = (lt < e0)
        fmask[2:66, :, 1] = (lt >= e0 + EXT)
        cpp_a = np.zeros((128, CPP_N), np.float32)

        def put(key, arr):
            o, n = CPP[key]
            cpp_a[:, o:o + n] = arr
        put("c", _pm(c[b])); put("cctx", _pm(c_ctx)); put("bmod", _pm(bm))
        put("n1g", _pm(f(norm1_g)[0])); put("n2g", _pm(f(norm2_g)[0]))
        put("cw_ssd", np.concatenate([_pm(f(conv_ssd_w)[0, t]) for t in range(3)], axis=1))
        put("cb_ssd", _pm(f(conv_ssd_b)[0]))
        cfw = f(conv_ffn_w)[0].reshape(9, 2 * DFF)
        put("cw_ffn", np.concatenate([_pm(cfw[t]) for t in range(9)], axis=1))
        put("cb_ffn", _pm(f(conv_ffn_b)[0]))
        put("emask", valid.reshape(NEXT, 128).T)
        put("fmask", fmask.transpose(1, 0, 2).reshape(128, NSLOT * 2))
        cbc_a = np.zeros((128, CBC_N), np.float32)

        def putb(key, arr):
            o, n = CBC[key]
            cbc_a[:, o:o + n] = arr
        putb("dt_bias", _rb(f(dt_bias)[0].reshape(-1))); putb("a_log", _rb(f(a_log)[0].reshape(-1)))
        putb("d_skip", _rb(f(d_skip)[0]))
        cbg_a = np.concatenate([_rb(f(ssd_norm_g)[0]), _rb(f(final_g)), _rb(bm[2048:3072]), _rb(bm[5120:6144])], axis=1)
        putb("emask_bc", _rb(np.concatenate([valid[:128], valid[-128:]])))
        hvb = np.zeros(128, np.float32); hvb[0:2] = hv
        putb("halo_v", _rb(hvb))
        in_maps.append(dict(
            xb=np.ascontiguousarray(x[b]), ctxb=np.ascontiguousarray(ctx[b]), xext=xext,
            w_mod=f(w_mod)[0], w_in=f(w_in)[0], w_fa=f(w_fa)[0], w_sb=f(w_sb)[0], w_o=f(w_o)[0],
            w_up=f(w_up)[0], w_down=f(w_down)[0], cpp=cpp_a, cbc=cbc_a, cbg=cbg_a, cbrow=f(conv_ssd_b)[0].reshape(1, 3072).copy(), cmk=cmk, t1=t1, t2=_t2_tables(q), tcs=tcs))
    if "nc" not in _CACHE:
        _CACHE["nc"] = build_program()
    res = run_bass_kernel_spmd(_CACHE["nc"], in_maps, core_ids=list(range(8)))
    if DEBUG:
        _CACHE["res"] = res
    out = np.zeros((2, SEQ, D), np.float32)
    for core in range(8):
        b, q = divmod(core, 4)
        out[b, 2048 * q:2048 * (q + 1)] = res.results[core]["out"]
    return out
```
